# Optimizing a Trainium2 kernel written in Bass

```python
import jax, jax.numpy as jnp
from jax import lax
import numpy as np

D_MODEL = 2048
BATCH = 1
SEQ = 16384
DEPTH = 1

HEAD_DIM = 128
N_HEADS_A = D_MODEL // 256
N_KV_A = max(1, N_HEADS_A // 4)
N_HEADS_B = D_MODEL // 256
GRID_W = 64
NA_ROWS = 8
NA_COLS = 16
Q_BLOCK = 128
ROPE_THETA = 10000.0
ROPE_AXIS_DIM = HEAD_DIM // 2
D_FF = -(-(8 * D_MODEL) // (3 * 256)) * 256
NORM_EPS = 1e-6
IN_SPLITS = (
    N_HEADS_A * HEAD_DIM,
    N_KV_A * HEAD_DIM,
    N_KV_A * HEAD_DIM,
    N_HEADS_B * HEAD_DIM,
    N_HEADS_B * HEAD_DIM,
    N_HEADS_B * HEAD_DIM,
    D_MODEL,
    D_MODEL,
)
D_IN = sum(IN_SPLITS)

kernel_name = "hybrid_gqa_natten_gated_encoder_block"


def rms_norm(x, w):
    xf = x.astype(jnp.float32)
    y = xf * lax.rsqrt(jnp.mean(xf * xf, axis=-1, keepdims=True) + NORM_EPS)
    return (y * w.astype(jnp.float32)).astype(x.dtype)


def axial_rope_tables(S):
    t = jnp.arange(S, dtype=jnp.int32)
    row = (t // GRID_W).astype(jnp.float32)
    col = (t % GRID_W).astype(jnp.float32)
    inv = ROPE_THETA ** (-jnp.arange(0, ROPE_AXIS_DIM, 2, dtype=jnp.float32) / ROPE_AXIS_DIM)
    ang = jnp.concatenate([row[:, None] * inv[None], col[:, None] * inv[None]], axis=-1)
    return jnp.cos(ang), jnp.sin(ang)


def apply_rope(x, cos, sin):
    xf = x.astype(jnp.float32).reshape(*x.shape[:-1], HEAD_DIM // 2, 2)
    x0, x1 = xf[..., 0], xf[..., 1]
    c = cos[None, :, None, :]
    s = sin[None, :, None, :]
    out = jnp.stack([x0 * c - x1 * s, x0 * s + x1 * c], axis=-1)
    return out.reshape(x.shape).astype(x.dtype)


def global_gqa(q, k, v):
    B, S, HQ, hd = q.shape
    KVH = k.shape[2]
    G = HQ // KVH
    nb = S // Q_BLOCK
    scale = hd ** -0.5
    qb = q.reshape(B, nb, Q_BLOCK, KVH, G, hd).transpose(1, 0, 2, 3, 4, 5)

    def one_block(qblk):
        s = jnp.einsum('bqkgd,bskd->bkgqs', qblk, k).astype(jnp.float32) * scale
        p = jax.nn.softmax(s, axis=-1).astype(v.dtype)
        return jnp.einsum('bkgqs,bskd->bqkgd', p, v)

    o = lax.map(one_block, qb)
    return o.transpose(1, 0, 2, 3, 4, 5).reshape(B, S, HQ * hd)


def neighbourhood_attention(q, k, v, rpb):
    B, S, H, hd = q.shape
    rows = S // GRID_W
    kr = min(NA_ROWS, rows)
    nk = kr * NA_COLS
    nb = S // Q_BLOCK
    scale = hd ** -0.5
    t = jnp.arange(S, dtype=jnp.int32)
    r = t // GRID_W
    col = t % GRID_W
    rs = jnp.clip(r - kr // 2, 0, rows - kr)
    cs = jnp.clip(col - NA_COLS // 2, 0, GRID_W - NA_COLS)
    key_r = rs[:, None, None] + jnp.arange(kr, dtype=jnp.int32)[None, :, None]
    key_c = cs[:, None, None] + jnp.arange(NA_COLS, dtype=jnp.int32)[None, None, :]
    shape3 = (S, kr, NA_COLS)
    idx = (key_r * GRID_W + key_c).reshape(nb, Q_BLOCK, nk)
    rel_r = jnp.broadcast_to(key_r - r[:, None, None] + (NA_ROWS - 1), shape3).reshape(nb, Q_BLOCK, nk)
    rel_c = jnp.broadcast_to(key_c - col[:, None, None] + (NA_COLS - 1), shape3).reshape(nb, Q_BLOCK, nk)
    qb = q.reshape(B, nb, Q_BLOCK, H, hd).transpose(1, 0, 2, 3, 4)

    def one_block(args):
        qblk, ib, rr, rc = args
        kb = k[:, ib]
        vb = v[:, ib]
        bias = rpb[:, rr, rc].astype(jnp.float32)
        s = jnp.einsum('bqhd,bqnhd->bhqn', qblk, kb).astype(jnp.float32) * scale + bias[None]
        p = jax.nn.softmax(s, axis=-1).astype(v.dtype)
        return jnp.einsum('bhqn,bqnhd->bqhd', p, vb)

    o = lax.map(one_block, (qb, idx, rel_r, rel_c))
    return o.transpose(1, 0, 2, 3, 4).reshape(B, S, H * hd)


def setup_inputs(seed: int = 0) -> dict:
    key = jax.random.key(seed)
    ks = jax.random.split(key, 20)
    f32 = jnp.float32
    D = D_MODEL

    def w(k, shape, fan_in):
        return jax.random.normal(k, shape, f32) * (fan_in ** -0.5)

    def gain(k, shape):
        return 1.0 + 0.05 * jax.random.normal(k, shape, f32)

    return {
        "x": jax.random.normal(ks[0], (BATCH, SEQ, D), f32),
        "c": jax.random.normal(ks[1], (BATCH, D), f32),
        "w_ada": w(ks[2], (DEPTH, D, 6 * D), D) * 0.5,
        "b_ada": 0.02 * jax.random.normal(ks[3], (DEPTH, 6 * D), f32),
        "norm1_w": gain(ks[4], (DEPTH, D)),
        "w_in": w(ks[5], (DEPTH, D, D_IN), D),
        "q_norm_w": gain(ks[6], (DEPTH, HEAD_DIM)),
        "k_norm_w": gain(ks[7], (DEPTH, HEAD_DIM)),
        "nat_rpb": 0.1 * jax.random.normal(ks[8], (DEPTH, N_HEADS_B, 2 * NA_ROWS - 1, 2 * NA_COLS - 1), f32),
        "w_oa": w(ks[9], (DEPTH, N_HEADS_A * HEAD_DIM, D), N_HEADS_A * HEAD_DIM),
        "w_ob": w(ks[10], (DEPTH, N_HEADS_B * HEAD_DIM, D), N_HEADS_B * HEAD_DIM),
        "w_out": w(ks[11], (DEPTH, D, D), D),
        "norm2_w": gain(ks[12], (DEPTH, D)),
        "w_ffn_gate": w(ks[13], (DEPTH, D, D_FF), D),
        "w_ffn_up": w(ks[14], (DEPTH, D, D_FF), D),
        "w_ffn_down": w(ks[15], (DEPTH, D_FF, D), D_FF),
        "final_w": gain(ks[16], (D,)),
    }


def reference(x, c, w_ada, b_ada, norm1_w, w_in, q_norm_w, k_norm_w, nat_rpb,
              w_oa, w_ob, w_out, norm2_w, w_ffn_gate, w_ffn_up, w_ffn_down, final_w):
    B, S, D = x.shape
    cos, sin = axial_rope_tables(S)
    offs = np.cumsum(IN_SPLITS)[:-1].tolist()
    c_act = jax.nn.silu(c)
    for l in range(DEPTH):
        mod = c_act @ w_ada[l] + b_ada[l]
        sh1, sc1, g1, sh2, sc2, g2 = [m[:, None, :] for m in jnp.split(mod, 6, axis=-1)]

        h = rms_norm(x, norm1_w[l]) * (1.0 + sc1) + sh1
        proj = h @ w_in[l]
        qa, ka, va, qb, kb, vb, ga, gb = jnp.split(proj, offs, axis=-1)
        qa = qa.reshape(B, S, N_HEADS_A, HEAD_DIM)
        ka = ka.reshape(B, S, N_KV_A, HEAD_DIM)
        va = va.reshape(B, S, N_KV_A, HEAD_DIM)
        qa = apply_rope(rms_norm(qa, q_norm_w[l]), cos, sin)
        ka = apply_rope(rms_norm(ka, k_norm_w[l]), cos, sin)
        ya = global_gqa(qa, ka, va)
        yb = neighbourhood_attention(
            qb.reshape(B, S, N_HEADS_B, HEAD_DIM),
            kb.reshape(B, S, N_HEADS_B, HEAD_DIM),
            vb.reshape(B, S, N_HEADS_B, HEAD_DIM),
            nat_rpb[l])
        merged = jax.nn.sigmoid(ga) * (ya @ w_oa[l]) + jax.nn.sigmoid(gb) * (yb @ w_ob[l])
        x = x + g1 * (merged @ w_out[l])

        h2 = rms_norm(x, norm2_w[l]) * (1.0 + sc2) + sh2
        ff = (jax.nn.silu(h2 @ w_ffn_gate[l]) * (h2 @ w_ffn_up[l])) @ w_ffn_down[l]
        x = x + g2 * ff
    return rms_norm(x, final_w)
```

```python
import math
from contextlib import ExitStack

import numpy as np
import concourse.bass as bass
import concourse.mybir as mybir
from concourse.bass_utils import run_bass_kernel_spmd

F32 = mybir.dt.float32
BF16 = mybir.dt.bfloat16
AF = mybir.ActivationFunctionType
ALU = mybir.AluOpType

D = 2048
S_ALL = 16384
NCORE = 8
TOWN = 2048
THALO = 512
TLOC = TOWN + THALO
NTOK_IN = S_ALL + THALO
D_IN = 8704
D_FF = 5632
EPS = 1e-6
NV = 162
GRID_W = 64
NA_ROWS = 8
NA_COLS = 16
MASK_NEG = -30000.0


class Buf:
    __slots__ = ("name", "w", "rs", "rd")

    def __init__(self, name):
        self.name = name
        self.w = None
        self.rs = {}
        self.rd = []


class Op:
    __slots__ = ("eng", "fn", "deps", "signaled", "token", "key", "idx", "epoch")

    def __init__(self, eng, fn, key, idx, epoch):
        self.eng = eng
        self.fn = fn
        self.key = key
        self.deps = set()
        self.signaled = key is not None
        self.token = None
        self.idx = idx
        self.epoch = epoch


ENGS = ("pe", "act", "dve", "pool", "sp")


class Sched:
    def __init__(self, nc, es):
        self.nc = nc
        self.es = es
        self.sem = {e: es.enter_context(nc.semaphore("s_" + e)) for e in ENGS}
        self.cnt = {e: 0 for e in ENGS}
        self.keysem = {}
        self.keycnt = {}
        self.waited = {e: {} for e in ENGS}
        self.epoch = 0
        self.ops = {e: [] for e in ENGS}
        self.nidx = {e: 0 for e in ENGS}
        self.allops = []

    def _mk(self, eng, fn, reads, writes, key):
        op = Op(eng, fn, key, self.nidx[eng], self.epoch)
        self.nidx[eng] += 1
        deps = op.deps
        for b in reads:
            if b.w is not None:
                deps.add(b.w)
        for b in writes:
            if b.w is not None:
                deps.add(b.w)
            deps.update(b.rs.values())
            deps.update(b.rd)
        for d in list(deps):
            if d.key is None and key is None and d.eng == eng and (eng == "pe" or op.idx - d.idx > 2):
                deps.discard(d)
        for d in deps:
            d.signaled = True
        for b in reads:
            if key is not None:
                b.rd.append(op)
            else:
                b.rs[eng] = op
        for b in writes:
            b.w = op
            b.rs = {}
            b.rd = []
        self.ops[eng].append(op)
        self.allops.append(op)
        return op

    def op(self, eng, fn, reads=(), writes=()):
        return self._mk(eng, fn, reads, writes, None)

    def dma(self, eng, out, in_, reads=(), writes=(), key=None):
        assert key is not None
        if key not in self.keysem:
            self.keysem[key] = self.es.enter_context(self.nc.semaphore("k%d" % len(self.keysem)))
            self.keycnt[key] = 0
        return self._mk(eng, lambda e: e.dma_start(out=out, in_=in_), reads, writes, key)

    def flush(self, final=False):
        nc = self.nc
        last = {}
        for e in ENGS:
            for op in reversed(self.ops[e]):
                if op.key is None and op.fn is not None:
                    op.signaled = True
                    last[e] = op
                    break
        for op in self.allops:
            if op.key is not None:
                self.keycnt[op.key] += 16
                op.token = (self.keysem[op.key], self.keycnt[op.key])
            elif op.signaled:
                self.cnt[op.eng] += 1
                op.token = (self.sem[op.eng], self.cnt[op.eng])
        bar = [(self.sem[e], self.cnt[e]) for e in ENGS if self.cnt[e] > 0]
        bar += [(self.keysem[k], self.keycnt[k]) for k in self.keysem if self.keycnt[k] > 0]
        ops = self.ops
        waited = self.waited
        sems = self.sem
        epoch = self.epoch

        def emit(ename, e):
            w = waited[ename]
            for op in ops[ename]:
                for d in op.deps:
                    if d.epoch != epoch:
                        continue
                    if d.key is None and d.eng == ename:
                        if ename == "pe" or op.idx - d.idx > 2:
                            continue
                    s, v = d.token
                    if w.get(id(s), 0) < v:
                        e.wait_ge(s, v)
                        w[id(s)] = v
                if op.fn is None:
                    continue
                ins = op.fn(e)
                if op.key is not None:
                    ins.then_inc(op.token[0], 16)
                elif op.signaled:
                    ins.then_inc(sems[ename], 1)
            for s, v in bar:
                k = id(s)
                if w.get(k, 0) < v:
                    e.wait_ge(s, v)
                    w[k] = v

        with nc.Block() as block:
            @block.tensor
            def _(e):
                emit("pe", e)

            @block.scalar
            def _(e):
                emit("act", e)

            @block.vector
            def _(e):
                emit("dve", e)

            @block.gpsimd
            def _(e):
                emit("pool", e)

            @block.sync
            def _(e):
                emit("sp", e)

        self.ops = {e: [] for e in ENGS}
        self.allops = []
        self.epoch += 1


class Ring:
    def __init__(self, items):
        self.items = items
        self.i = 0

    def next(self):
        it = self.items[self.i % len(self.items)]
        self.i += 1
        return it


def v3(ap, k):
    return ap.rearrange("p (k t) -> p k t", k=k)


def build_program(debug_outs=(), stop_after=99):
    nc = bass.Bass("TRN2", target_bir_lowering=False)

    def din(name, shape, dt=F32):
        return nc.dram_tensor(name, list(shape), dt, kind="ExternalInput").ap()

    def dscr(name, shape, dt=BF16):
        kind = "ExternalOutput" if name in debug_outs else "Internal"
        return nc.dram_tensor(name, list(shape), dt, kind=kind).ap()

    xT = din("xT", [D, NTOK_IN])
    cosT = din("cosT", [128, S_ALL])
    sinT = din("sinT", [128, S_ALL])
    vecs = din("vecs", [128, NV])
    rmat = din("rmat", [128, 128])
    natab = din("natab", [8, 20, 128, 640])
    w_ada = din("w_ada", [D, 6 * D])
    w_in = din("w_in", [D, D_IN])
    w_oa = din("w_oa", [1024, D])
    w_ob = din("w_ob", [1024, D])
    w_out = din("w_out", [D, D])
    w_gate = din("w_gate", [D, D_FF])
    w_up = din("w_up", [D, D_FF])
    w_down = din("w_down", [D_FF, D])
    outT = nc.dram_tensor("outT", [16, 128, TOWN], F32, kind="ExternalOutput").ap()

    hT_scr = dscr("hT_scr", [128, 16 * TLOC])
    KT_scr = dscr("KT_scr", [2, 128, S_ALL])
    V_scr = dscr("V_scr", [2, 128, 128 * 128])
    qT_scr = dscr("qT_scr", [8, 128, TOWN])
    qBT_scr = dscr("qBT_scr", [8, 128, TOWN])
    kBT_scr = dscr("kBT_scr", [8, 128, TLOC])
    vB_scr = dscr("vB_scr", [8, 128, 20 * 128])
    sg_scr = dscr("sg_scr", [32, 128, TOWN])
    bv_scr = dscr("bv_scr", [128, 16], F32)
    yT_scr = dscr("yT_scr", [16, 128, TOWN])
    x1T_scr = dscr("x1T_scr", [16, 128, TOWN], F32)
    h2T_scr = dscr("h2T_scr", [128, 16 * TOWN])
    act_scr = dscr("act_scr", [44, 128, TOWN])
    x2T_scr = dscr("x2T_scr", [16, 128, TOWN], F32)

    w_ada_v = w_ada.rearrange("(k p) n -> p k n", p=128)
    w_in_v = w_in.rearrange("(k p) n -> p k n", p=128)
    xT_v = xT.rearrange("(k p) t -> p k t", p=128)

    with ExitStack() as ges:
        S = Sched(nc, ges)

        def sb(es, name, shape, dt):
            return es.enter_context(nc.sbuf_tensor(name, list(shape), dt))

        vecs_sb = sb(ges, "vecs_sb", [128, NV], F32)
        mod_sb = sb(ges, "mod_sb", [128, 96], F32)
        A1 = sb(ges, "A1", [128, 16], F32)
        A2 = sb(ges, "A2", [128, 16], F32)
        B1bf = sb(ges, "B1bf", [128, 16], BF16)
        B2bf = sb(ges, "B2bf", [128, 16], BF16)
        ones_bf = sb(ges, "ones_bf", [128, 128], BF16)
        rmat_bf = sb(ges, "rmat_bf", [128, 128], BF16)
        kw = sb(ges, "kw", [128, 1], F32)
        epsA = sb(ges, "epsA", [128, 1], F32)
        epsB = sb(ges, "epsB", [128, 1], F32)
        bvcol = sb(ges, "bvcol", [128, 16], F32)
        csil = sb(ges, "csil", [128, 16], BF16)
        dbl = [ges.enter_context(nc.psum_tensor("dbank%d" % i, [128, 1024], F32)) for i in range(4)]
        banks = [dbl[i // 2][:, (i % 2) * 512:(i % 2) * 512 + 512] for i in range(8)]
        bB = [Buf("bank%d" % i) for i in range(8)]
        Bvecs, Bmod, Bconst, Bcsil, Bbv = Buf("vecs"), Buf("mod"), Buf("const"), Buf("csil"), Buf("bv")

        C_C, C_BADA, C_N1, C_N2, C_FW, C_QW, C_KW = 0, 16, 112, 128, 144, 160, 161

        with ExitStack() as es:
            wts = [sb(es, "wada%d" % i, [128, 16 * 512], BF16) for i in range(3)]
            wB = [Buf("wada%d" % i) for i in range(3)]
            S.dma("sp", vecs_sb[:, :], vecs[:, :], writes=[Bvecs], key="vecs")
            S.dma("pool", rmat_bf[:, :], rmat[:, :], writes=[Bconst], key="rmat")
            S.op("dve", lambda e: e.memset(ones_bf[:, :], 1.0), writes=[Bconst])
            S.op("dve", lambda e: e.memset(epsA[:, :], D * EPS), writes=[Bconst])
            S.op("dve", lambda e: e.memset(epsB[:, :], 128 * EPS), writes=[Bconst])
            S.op("act", lambda e: e.activation(out=csil[:, :], in_=vecs_sb[:, C_C:C_C + 16], func=AF.Silu),
                 reads=[Bvecs], writes=[Bcsil])
            ps_mod = banks[7]
            for i in range(24):
                sl = i % 3
                wt = wts[sl]
                S.dma("pool", v3(wt[:, :], 16), w_ada_v[:, :, i * 512:(i + 1) * 512], writes=[wB[sl]],
                      key=("wada", sl))
                for jj in range(4):
                    j = i * 4 + jj
                    for k in range(16):
                        S.op("pe", (lambda e, wt=wt, k=k, jj=jj, j=j: e.matmul(
                            ps_mod[:, j:j + 1], lhsT=wt[:, k * 512 + jj * 128:k * 512 + (jj + 1) * 128],
                            rhs=csil[:, k:k + 1], start=(k == 0), stop=(k == 15))),
                            reads=[wB[sl], Bcsil], writes=[bB[7]])
            S.op("dve", lambda e: e.tensor_tensor(out=mod_sb[:, :], in0=ps_mod[:, 0:96],
                                                  in1=vecs_sb[:, C_BADA:C_BADA + 96], op=ALU.add),
                 reads=[bB[7], Bvecs], writes=[Bmod])
            for (A, sc0, nw0) in ((A1, 16, C_N1), (A2, 64, C_N2)):
                S.op("dve", (lambda e, A=A, sc0=sc0, nw0=nw0: e.scalar_tensor_tensor(
                    out=A[:, :], in0=mod_sb[:, sc0:sc0 + 16], scalar=1.0, in1=vecs_sb[:, nw0:nw0 + 16],
                    op0=ALU.add, op1=ALU.mult)), reads=[Bmod, Bvecs], writes=[Bmod])
                S.op("pool", (lambda e, A=A: e.tensor_scalar(out=A[:, :], in0=A[:, :], scalar1=math.sqrt(D),
                                                            scalar2=None, op0=ALU.mult)),
                     reads=[Bmod], writes=[Bmod])
            S.op("dve", lambda e: e.tensor_copy(out=B1bf[:, :], in_=mod_sb[:, 0:16]), reads=[Bmod], writes=[Bmod])
            S.op("dve", lambda e: e.tensor_copy(out=B2bf[:, :], in_=mod_sb[:, 48:64]), reads=[Bmod], writes=[Bmod])
            S.op("pool", lambda e: e.tensor_scalar(out=kw[:, :], in0=vecs_sb[:, C_KW:C_KW + 1],
                                                   scalar1=math.sqrt(128.0), scalar2=None, op0=ALU.mult),
                 reads=[Bvecs], writes=[Bmod])
            S.flush()
        if stop_after <= 0:
            return nc

        def norm_rope_multi(items, use_sqrt=False):
            for it_ in items:
                T = it_["T"]
                S.op("act", lambda e, it_=it_, T=T: e.activation(out=T["sq"][:, :], in_=it_["ps"][:, :], func=AF.Square,
                                                               bias=it_["bias"], scale=1.0),
                     reads=[it_["psB"]] + list(it_["extra"]), writes=[T["sqB"]])
                S.op("dve", lambda e, it_=it_, T=T: e.tensor_scalar(out=T["raw"][:, :], in0=it_["ps"][:, :], scalar1=it_["bias"],
                                                                  scalar2=None, op0=ALU.add),
                     reads=[it_["psB"], T["sqB"]] + list(it_["extra"]), writes=[T["rawB"]])
            for it_ in items:
                T = it_["T"]
                S.op("pe", lambda e, it_=it_, T=T: e.matmul(banks[it_["sumb"]][:, :], lhsT=ones_bf[:, :], rhs=T["sq"][:, :],
                                                          start=True, stop=True),
                     reads=[T["sqB"], Bconst], writes=[bB[it_["sumb"]]])
            for it_ in items:
                T = it_["T"]
                S.op("act", lambda e, it_=it_, T=T: e.activation(out=T["rs"][:, :], in_=banks[it_["sumb"]][:, :],
                                                               func=(AF.Sqrt if use_sqrt else AF.Ln),
                                                               bias=epsB[:, 0:1], scale=1.0),
                     reads=[bB[it_["sumb"]], Bconst], writes=[T["rsB"]])
            for it_ in items:
                T = it_["T"]
                if use_sqrt:
                    S.op("dve", lambda e, T=T: e.reciprocal(out=T["rs"][:, :], in_=T["rs"][:, :]),
                         reads=[T["rsB"]], writes=[T["rsB"]])
                else:
                    S.op("act", lambda e, T=T: e.activation(out=T["rs"][:, :], in_=T["rs"][:, :], func=AF.Exp, scale=-0.5),
                         reads=[T["rsB"]], writes=[T["rsB"]])
            for it_ in items:
                T = it_["T"]
                S.op("dve", lambda e, it_=it_, T=T: e.scalar_tensor_tensor(out=T["n"][:, :], in0=T["raw"][:, :], scalar=it_["w"],
                                                                         in1=T["rs"][:, :], op0=ALU.mult, op1=ALU.mult),
                     reads=[T["rawB"], T["rsB"], Bmod, Bvecs], writes=[T["nB"]])
            for it_ in items:
                T = it_["T"]
                S.op("pe", lambda e, it_=it_, T=T: e.matmul(banks[it_["rotb"]][:, :], lhsT=rmat_bf[:, :], rhs=T["n"][:, :],
                                                          start=True, stop=True),
                     reads=[T["nB"], Bconst], writes=[bB[it_["rotb"]]])
            for it_ in items:
                T = it_["T"]
                S.op("pool", lambda e, it_=it_, T=T: e.tensor_tensor(out=T["t1"][:, :], in0=T["n"][:, :], in1=it_["cos"], op=ALU.mult),
                     reads=[T["nB"], it_["tabB"]], writes=[T["t1B"]])
            for it_ in items:
                T = it_["T"]
                S.op("dve", lambda e, it_=it_, T=T: e.tensor_tensor(out=T["t2"][:, :], in0=banks[it_["rotb"]][:, :], in1=it_["sin"],
                                                                  op=ALU.mult),
                     reads=[bB[it_["rotb"]], it_["tabB"]], writes=[T["t2B"]])
            for it_ in items:
                T = it_["T"]
                S.op("pool", lambda e, it_=it_, T=T: e.tensor_tensor(out=it_["out"], in0=T["t1"][:, :], in1=T["t2"][:, :], op=ALU.add),
                     reads=[T["t1B"], T["t2B"]], writes=[it_["outB"]])
                if it_.get("after") is not None:
                    it_["after"]()

        def norm_rope(ps, psB, bias_ap, w_ap, cos_ap, sin_ap, tabB, T, out_ap, outB, extra_reads=()):
            norm_rope_multi([dict(ps=ps, psB=psB, bias=bias_ap, w=w_ap, cos=cos_ap, sin=sin_ap, tabB=tabB, T=T, out=out_ap,
                                  outB=outB, extra=extra_reads, sumb=3, rotb=4, after=None)])

        def mk_chain_tiles(es, pfx):
            T = {}
            T["raw"] = sb(es, pfx + "raw", [128, 512], F32)
            T["sq"] = sb(es, pfx + "sq", [128, 512], BF16)
            T["rs"] = sb(es, pfx + "rs", [128, 512], F32)
            T["n"] = sb(es, pfx + "n", [128, 512], BF16)
            T["t1"] = sb(es, pfx + "t1", [128, 512], F32)
            T["t2"] = sb(es, pfx + "t2", [128, 512], F32)
            for k in ("raw", "sq", "rs", "n", "t1", "t2"):
                T[k + "B"] = Buf(pfx + k)
            return T

        def bias_cols(wt, wB_, ncol, Bbf, dst_ap, dstB, col_stride):
            for c in range(ncol):
                for k in range(16):
                    S.op("pe", (lambda e, c=c, k=k: e.matmul(
                        banks[7][:, c:c + 1], lhsT=wt[:, k * col_stride + c * 128:k * col_stride + (c + 1) * 128],
                        rhs=Bbf[:, k:k + 1], start=(k == 0), stop=(k == 15))),
                        reads=[wB_, Bmod], writes=[bB[7]])
            S.op("dve", lambda e: e.tensor_copy(out=dst_ap, in_=banks[7][:, 0:ncol]), reads=[bB[7]], writes=[dstB])

        with ExitStack() as es:
            xs = [sb(es, "xs%d" % i, [128, 16 * 512], F32) for i in range(2)]
            xsB = [Buf("xs%d" % i) for i in range(2)]
            sq = sb(es, "sq", [128, 16 * 512], BF16)
            sqB = Buf("sq")
            hT = [sb(es, "hT%d" % i, [128, 16 * 512], BF16) for i in range(2)]
            hTB = [Buf("hT%d" % i) for i in range(2)]
            wkv = sb(es, "wkv", [128, 16 * 512], BF16)
            wkvB = Buf("wkv")
            rstd = sb(es, "rstd", [128, 512], F32)
            rstdB = Buf("rstd")
            cs = [sb(es, "cs%d" % i, [128, 1024], F32) for i in range(2)]
            csB = [Buf("cs%d" % i) for i in range(2)]
            kout = [sb(es, "kout%d" % i, [128, 512], BF16) for i in range(2)]
            koutB = [Buf("kout%d" % i) for i in range(2)]
            vout = [sb(es, "vout%d" % i, [128, 1024], BF16) for i in range(2)]
            voutB = [Buf("vout%d" % i) for i in range(2)]
            bk = sb(es, "bk", [128, 2], F32)
            bkB = Buf("bk")
            TT = [mk_chain_tiles(es, "c1a"), mk_chain_tiles(es, "c1b")]
            BKT = [Buf("KT0"), Buf("KT1")]
            BV = [Buf("V0"), Buf("V1")]
            BhT = Buf("hTscr")

            S.dma("pool", v3(wkv[:, :], 16), w_in_v[:, :, 1024:1536], writes=[wkvB], key="wkv")
            bias_cols(wkv, wkvB, 2, B1bf, bk[:, 0:2], bkB, 512)
            for c in range(2):
                for k in range(16):
                    S.op("pe", (lambda e, c=c, k=k: e.matmul(
                        banks[7][:, 8 + c:9 + c], lhsT=wkv[:, k * 512 + 256 + c * 128:k * 512 + 256 + (c + 1) * 128],
                        rhs=B1bf[:, k:k + 1], start=(k == 0), stop=(k == 15))),
                        reads=[wkvB, Bmod], writes=[bB[7]])
            S.op("dve", lambda e: e.tensor_copy(out=bvcol[:, 0:2], in_=banks[7][:, 8:10]), reads=[bB[7]], writes=[Bbv])

            NB1 = 33

            def stageA(tb):
                sl = tb % 2
                t0 = tb * 512
                S.dma("sp", v3(xs[sl][:, :], 16), xT_v[:, :, t0:t0 + 512], writes=[xsB[sl]], key=("xs", sl))
                S.op("act", lambda e: e.activation(out=sq[:, :], in_=xs[sl][:, :], func=AF.Square),
                     reads=[xsB[sl]], writes=[sqB])
                for k in range(16):
                    S.op("pe", (lambda e, k=k: e.matmul(banks[0][:, :], lhsT=ones_bf[:, :],
                                                       rhs=sq[:, k * 512:(k + 1) * 512],
                                                       start=(k == 0), stop=(k == 15))),
                         reads=[sqB, Bconst], writes=[bB[0]])
                S.op("act", lambda e: e.activation(out=rstd[:, :], in_=banks[0][:, :], func=AF.Ln,
                                                   bias=epsA[:, 0:1], scale=1.0),
                     reads=[bB[0], Bconst], writes=[rstdB])
                S.op("act", lambda e: e.activation(out=rstd[:, :], in_=rstd[:, :], func=AF.Exp, scale=-0.5),
                     reads=[rstdB], writes=[rstdB])
                for k in range(16):
                    eng = "dve"
                    S.op(eng, (lambda e, k=k: e.scalar_tensor_tensor(
                        out=hT[sl][:, k * 512:(k + 1) * 512], in0=xs[sl][:, k * 512:(k + 1) * 512],
                        scalar=A1[:, k:k + 1], in1=rstd[:, :], op0=ALU.mult, op1=ALU.mult)),
                        reads=[xsB[sl], rstdB, Bmod], writes=[hTB[sl]])

            def stageB(tb):
                sl = tb % 2
                t0 = tb * 512
                own = tb < 4 or tb == 32
                if own:
                    lt0 = t0 if tb < 4 else TOWN
                    S.dma("pool", v3(hT_scr[:, :], 16)[:, :, lt0:lt0 + 512], v3(hT[sl][:, :], 16),
                          reads=[hTB[sl]], writes=[BhT], key="hTst")
                if tb == 32:
                    return
                S.dma("sp", cs[sl][:, 0:512], cosT[:, t0:t0 + 512], writes=[csB[sl]], key=("cs", sl))
                S.dma("sp", cs[sl][:, 512:1024], sinT[:, t0:t0 + 512], writes=[csB[sl]], key=("cs", sl))
                for g in range(2):
                    for k in range(16):
                        S.op("pe", (lambda e, g=g, k=k: e.matmul(
                            banks[1 + g][:, :], lhsT=wkv[:, k * 512 + g * 128:k * 512 + (g + 1) * 128],
                            rhs=hT[sl][:, k * 512:(k + 1) * 512], start=(k == 0), stop=(k == 15))),
                            reads=[wkvB, hTB[sl]], writes=[bB[1 + g]])
                for s in range(4):
                    bank = 5 + s // 2
                    c0 = (s % 2) * 256
                    for k in range(16):
                        S.op("pe", (lambda e, s=s, k=k, bank=bank, c0=c0: e.matmul(
                            banks[bank][:, c0:c0 + 256], lhsT=hT[sl][:, k * 512 + s * 128:k * 512 + (s + 1) * 128],
                            rhs=wkv[:, k * 512 + 256:k * 512 + 512], start=(k == 0), stop=(k == 15))),
                            reads=[wkvB, hTB[sl]], writes=[bB[bank]])
                vo = vout[sl]
                S.op("act", lambda e: e.activation(out=vo[:, 0:512], in_=banks[5][:, :], func=AF.Copy),
                     reads=[bB[5]], writes=[voutB[sl]])
                S.op("dve", lambda e: e.tensor_copy(out=vo[:, 512:1024], in_=banks[6][:, :]),
                     reads=[bB[6]], writes=[voutB[sl]])
                for g in range(2):
                    S.dma("pool", v3(V_scr[g, :, tb * 512:(tb + 1) * 512], 4),
                          v3(vo[:, :], 4)[:, :, g * 128:(g + 1) * 128],
                          reads=[voutB[sl]], writes=[BV[g]], key=("vst", sl))
                items = []
                for g in range(2):
                    ko = kout[g]
                    items.append(dict(
                        ps=banks[1 + g], psB=bB[1 + g], bias=bk[:, g:g + 1], w=kw[:, 0:1], cos=cs[sl][:, 0:512],
                        sin=cs[sl][:, 512:1024], tabB=csB[sl], T=TT[g], out=ko[:, :], outB=koutB[g], extra=[bkB],
                        sumb=(3, 7)[g], rotb=(4, 3)[g],
                        after=(lambda g=g, ko=ko: S.dma("pool", KT_scr[g, :, t0:t0 + 512], ko[:, :], reads=[koutB[g]],
                                                        writes=[BKT[g]], key=("kst", g)))))
                norm_rope_multi(items)

            stageA(0)
            for tb in range(NB1):
                if tb + 1 < NB1:
                    stageA(tb + 1)
                stageB(tb)
            S.flush()
        if stop_after <= 1:
            return nc

        def load_wtile(dst, dstB, view, c0, ncols, nk, key):
            S.dma("pool", v3(dst[:, 0:nk * ncols], nk), view[:, :, c0:c0 + ncols], writes=[dstB], key=key)

        def mm_acc(bank_i, lhs_fn, rhs_fn, nk, reads, out_ap=None):
            for k in range(nk):
                S.op("pe", (lambda e, k=k: e.matmul(out_ap if out_ap is not None else banks[bank_i][:, :],
                                                   lhsT=lhs_fn(k), rhs=rhs_fn(k), start=(k == 0), stop=(k == nk - 1))),
                     reads=reads, writes=[bB[bank_i]])

        BqT, BqBT, BkBT, BvB, Bsg = Buf("qT"), Buf("qBT"), Buf("kBT"), Buf("vB"), Buf("sg")
        with ExitStack() as es:
            hTo = sb(es, "hTo", [128, 16 * TLOC], BF16)
            hToB = Buf("hTo")
            wts = [sb(es, "w2_%d" % i, [128, 16 * 512], BF16) for i in range(3)]
            wtB = [Buf("w2_%d" % i) for i in range(3)]
            cso = sb(es, "cso", [128, 2 * TOWN], F32)
            csoB = Buf("cso")
            TT2 = [mk_chain_tiles(es, "c2a"), mk_chain_tiles(es, "c2b")]
            qitems = []
            ot = [sb(es, "ot%d" % i, [128, 512], BF16) for i in range(4)]
            otB = [Buf("ot%d" % i) for i in range(4)]
            bc = [sb(es, "bc%d" % i, [128, 4], F32) for i in range(2)]
            bcB = [Buf("bc%d" % i) for i in range(2)]
            S.dma("sp", hTo[:, :], hT_scr[:, :], reads=[BhT], writes=[hToB], key="hTo")
            S.dma("sp", cso[:, 0:TOWN], cosT[:, 0:TOWN], writes=[csoB], key="cso")
            S.dma("sp", cso[:, TOWN:2 * TOWN], sinT[:, 0:TOWN], writes=[csoB], key="cso")
            order = [0, 1, 3, 4, 5, 6, 7, 8] + list(range(9, 17))
            mb = Ring([0, 1, 2, 5, 6])
            oti = 0
            for n_i, ti in enumerate(order):
                sl = n_i % 3
                wt, wB_ = wts[sl], wtB[sl]
                load_wtile(wt, wB_, w_in_v, ti * 512, 512, 16, ("w2", sl))
                if ti in (7, 8):
                    hv = ti - 7
                    bias_cols(wt, wB_, 4, B1bf, bvcol[:, 2 + 4 * hv:6 + 4 * hv], Bbv, 512)
                    for s_ in range(TLOC // 128):
                        bi = mb.next()
                        mm_acc(bi, lambda k, s_=s_: hTo[:, k * TLOC + s_ * 128:k * TLOC + (s_ + 1) * 128],
                               lambda k, wt=wt: wt[:, k * 512:(k + 1) * 512], 16, [hToB, wB_])
                        o_, oB_ = ot[oti % 4], otB[oti % 4]
                        oti += 1
                        if s_ % 2 == 0:
                            S.op("act", (lambda e, o_=o_, bi=bi: e.activation(out=o_[:, :], in_=banks[bi][:, :], func=AF.Copy)),
                                 reads=[bB[bi]], writes=[oB_])
                        else:
                            S.op("dve", (lambda e, o_=o_, bi=bi: e.tensor_copy(out=o_[:, :], in_=banks[bi][:, :])),
                                 reads=[bB[bi]], writes=[oB_])
                        S.dma("sp", vB_scr[4 * hv:4 * hv + 4, :, s_ * 128:(s_ + 1) * 128].rearrange("h p d -> p h d"),
                              v3(o_[:, :], 4), reads=[oB_], writes=[BvB], key=("ot", oti % 4))
                    continue
                bcs, bcsB = bc[n_i % 2], bcB[n_i % 2]
                bias_cols(wt, wB_, 4, B1bf, bcs[:, 0:4], bcsB, 512)
                for cc in range(4):
                    ntb = 5 if ti in (5, 6) else 4
                    for tb in range(ntb):
                        bi = mb.next()
                        mm_acc(bi, lambda k, wt=wt, cc=cc: wt[:, k * 512 + cc * 128:k * 512 + (cc + 1) * 128],
                               lambda k, tb=tb: hTo[:, k * TLOC + tb * 512:k * TLOC + (tb + 1) * 512], 16, [hToB, wB_])
                        o_, oB_ = ot[oti % 4], otB[oti % 4]
                        oti += 1
                        okey = ("ot", oti % 4)
                        if ti in (0, 1):
                            h = ti * 4 + cc
                            qitems.append(dict(
                                ps=banks[bi], psB=bB[bi], bias=bcs[:, cc:cc + 1], w=vecs_sb[:, C_QW:C_QW + 1],
                                cos=cso[:, tb * 512:(tb + 1) * 512], sin=cso[:, TOWN + tb * 512:TOWN + (tb + 1) * 512],
                                tabB=csoB, T=TT2[tb % 2], out=o_[:, :], outB=oB_, extra=[bcsB],
                                sumb=(3, 7)[tb % 2], rotb=(4, 3)[tb % 2],
                                after=(lambda h=h, tb=tb, o_=o_, oB_=oB_, okey=okey: S.dma(
                                    "sp", qT_scr[h, :, tb * 512:(tb + 1) * 512], o_[:, :], reads=[oB_], writes=[BqT], key=okey))))
                            if len(qitems) == 2:
                                norm_rope_multi(qitems)
                                qitems = []
                        elif ti in (3, 4, 5, 6):
                            S.op("act", (lambda e, o_=o_, bi=bi, cc=cc, bcs=bcs: e.activation(
                                out=o_[:, :], in_=banks[bi][:, :], func=AF.Identity, bias=bcs[:, cc:cc + 1], scale=1.0)),
                                reads=[bB[bi], bcsB], writes=[oB_])
                            if ti in (3, 4):
                                h = (ti - 3) * 4 + cc
                                S.dma("sp", qBT_scr[h, :, tb * 512:(tb + 1) * 512], o_[:, :], reads=[oB_], writes=[BqBT], key=okey)
                            else:
                                h = (ti - 5) * 4 + cc
                                S.dma("sp", kBT_scr[h, :, tb * 512:(tb + 1) * 512], o_[:, :], reads=[oB_], writes=[BkBT], key=okey)
                        else:
                            gi = (ti - 9) * 4 + cc
                            S.op("act", (lambda e, o_=o_, bi=bi, cc=cc, bcs=bcs: e.activation(
                                out=o_[:, :], in_=banks[bi][:, :], func=AF.Sigmoid, bias=bcs[:, cc:cc + 1], scale=1.0)),
                                reads=[bB[bi], bcsB], writes=[oB_])
                            S.dma("sp", sg_scr[gi, :, tb * 512:(tb + 1) * 512], o_[:, :], reads=[oB_], writes=[Bsg], key=okey)
            S.flush()
        if stop_after <= 2:
            return nc

        ByT = Buf("yT")

        def attn_epilogue(Obank, Lbank, bias_ap, E, dst_ap, key):
            S.op("act", lambda e: e.activation(out=E["rec"][:, :], in_=banks[Lbank][:, :], func=AF.Ln), reads=[bB[Lbank]], writes=[E["recB"]])
            S.op("act", lambda e: e.activation(out=E["rec"][:, :], in_=E["rec"][:, :], func=AF.Exp, scale=-1.0), reads=[E["recB"]], writes=[E["recB"]])
            S.op("dve", lambda e: e.tensor_tensor(out=E["o"][:, :], in0=banks[Obank][:, :], in1=E["rec"][:, :], op=ALU.mult),
                 reads=[bB[Obank], E["recB"]], writes=[E["oB"]])
            y_, yB_ = E["y"][E["i"] % 2], E["yB"][E["i"] % 2]
            E["i"] += 1
            S.op("dve", lambda e: e.tensor_scalar(out=y_[:, :], in0=E["o"][:, :], scalar1=bias_ap, scalar2=None, op0=ALU.add),
                 reads=[E["oB"], Bbv], writes=[yB_])
            S.dma("pool", dst_ap, y_[:, :], reads=[yB_], writes=[ByT], key=(key, E["i"] % 2))

        def mk_epi(es, pfx):
            E = {"rec": sb(es, pfx + "rec", [128, 512], F32), "o": sb(es, pfx + "o", [128, 512], F32),
                 "y": [sb(es, pfx + "y%d" % i, [128, 512], BF16) for i in range(2)],
                 "recB": Buf("rec"), "oB": Buf("o"), "yB": [Buf("y0"), Buf("y1")], "i": 0}
            return E

        with ExitStack() as es:
            KTs = [sb(es, "KTs%d" % g, [128, S_ALL], BF16) for g in range(2)]
            Vs = [sb(es, "Vs%d" % g, [128, S_ALL], BF16) for g in range(2)]
            qs = [sb(es, "qs%d" % g, [128, 4 * TOWN], BF16) for g in range(2)]
            KTsB = [Buf("KTs0"), Buf("KTs1")]
            VsB = [Buf("Vs0"), Buf("Vs1")]
            qsB = [Buf("qs0"), Buf("qs1")]
            Pt = [sb(es, "Pt%d" % i, [128, 512], BF16) for i in range(4)]
            PtB = [Buf("Pt%d" % i) for i in range(4)]
            E = mk_epi(es, "e3")
            for g in range(2):
                S.dma("sp", qs[g][:, :].rearrange("p (h t) -> p h t", h=4), qT_scr[4 * g:4 * g + 4].rearrange("h p t -> p h t"),
                      reads=[BqT], writes=[qsB[g]], key=("qs", g))
                for part in range(4):
                    c0 = part * 4096
                    S.dma("sp", KTs[g][:, c0:c0 + 4096], KT_scr[g, :, c0:c0 + 4096], reads=[BKT[g]], writes=[KTsB[g]], key=("KTs", g))
                    S.dma("sp", Vs[g][:, c0:c0 + 4096], V_scr[g, :, c0:c0 + 4096], reads=[BV[g]], writes=[VsB[g]], key=("Vs", g))
            item = 0
            NKC = S_ALL // 128
            for g in range(2):
                for qb in range(4):
                    for hh in range(4):
                        h = 4 * g + hh
                        Ob, Lb = 4 + item % 2, 6 + item % 2
                        item += 1
                        q_ap = qs[g][:, hh * TOWN + qb * 512:hh * TOWN + (qb + 1) * 512]

                        def s_mm(kc, g=g, q_ap=q_ap):
                            bi = kc % 4
                            S.op("pe", (lambda e: e.matmul(banks[bi][:, :], lhsT=KTs[g][:, kc * 128:(kc + 1) * 128], rhs=q_ap,
                                                          start=True, stop=True)),
                                 reads=[KTsB[g], qsB[g]], writes=[bB[bi]])
                            S.op("act", (lambda e: e.activation(out=Pt[bi][:, :], in_=banks[bi][:, :], func=AF.Exp)),
                                 reads=[bB[bi]], writes=[PtB[bi]])

                        def pv_mm(kc, g=g, Ob=Ob, Lb=Lb):
                            bi = kc % 4
                            S.op("pe", (lambda e: e.matmul(banks[Ob][:, :], lhsT=Vs[g][:, kc * 128:(kc + 1) * 128], rhs=Pt[bi][:, :],
                                                          start=(kc == 0), stop=(kc == NKC - 1))),
                                 reads=[VsB[g], PtB[bi]], writes=[bB[Ob]])
                            S.op("pe", (lambda e: e.matmul(banks[Lb][:, :], lhsT=ones_bf[:, :], rhs=Pt[bi][:, :],
                                                          start=(kc == 0), stop=(kc == NKC - 1))),
                                 reads=[PtB[bi], Bconst], writes=[bB[Lb]])

                        s_mm(0)
                        s_mm(1)
                        for kc in range(NKC):
                            if kc + 2 < NKC:
                                s_mm(kc + 2)
                            pv_mm(kc)
                        attn_epilogue(Ob, Lb, bvcol[:, g:g + 1], E, yT_scr[h, :, qb * 512:(qb + 1) * 512], "y3")
            S.flush()
        if stop_after <= 3:
            return nc

        def lc_of(kc):
            return kc if kc < 16 else (kc - 18 if kc < 18 else kc - 2)

        def kc_of(lc):
            return lc if 0 <= lc < 16 else (lc + 18 if lc < 0 else lc + 2)

        with ExitStack() as es:
            kBs = [sb(es, "kBs%d" % i, [128, TLOC], BF16) for i in range(2)]
            vBs = [sb(es, "vBs%d" % i, [128, TLOC], BF16) for i in range(2)]
            qBs = [sb(es, "qBs%d" % i, [128, TOWN], BF16) for i in range(2)]
            hdB = [Buf("hd0"), Buf("hd1")]
            tab = sb(es, "tab", [128, 20 * 640], F32)
            tabB = [Buf("tab%d" % i) for i in range(20)]
            Pa = [sb(es, "Pa%d" % i, [128, 20 * 640], BF16) for i in range(2)]
            PaB = [[Buf("Pa%d_%d" % (i, j)) for j in range(20)] for i in range(2)]
            tmp = [sb(es, "natmp%d" % i, [128, 640], F32) for i in range(2)]
            tmpB = [Buf("natmp0"), Buf("natmp1")]
            E = mk_epi(es, "e4")
            scale = 1.0 / math.sqrt(128.0)
            item = 0
            for h in range(8):
                sl = h % 2
                S.dma("sp", kBs[sl][:, :], kBT_scr[h], reads=[BkBT], writes=[hdB[sl]], key=("hd", sl))
                S.dma("sp", vBs[sl][:, :], vB_scr[h], reads=[BvB], writes=[hdB[sl]], key=("hd", sl))
                S.dma("sp", qBs[sl][:, :], qBT_scr[h], reads=[BqBT], writes=[hdB[sl]], key=("hd", sl))
                for kc in range(20):
                    S.dma("sp", tab[:, kc * 640:(kc + 1) * 640], natab[h, kc], writes=[tabB[kc]], key=("tab", kc % 4))
                for kc in range(20):
                    lc = lc_of(kc)
                    blo, bhi = max(0, lc - 2), min(15, lc + 2)
                    nq = (bhi - blo + 1) * 128
                    q0 = blo * 128
                    dd = dbl[kc % 2]
                    n1 = min(nq, 512)
                    S.op("pe", (lambda e, kc=kc, dd=dd, n1=n1, q0=q0, sl=sl: e.matmul(
                        dd[:, 0:n1], lhsT=kBs[sl][:, kc * 128:(kc + 1) * 128], rhs=qBs[sl][:, q0:q0 + n1], start=True, stop=True)),
                        reads=[hdB[sl]], writes=[bB[2 * (kc % 2)]])
                    rd = [bB[2 * (kc % 2)]]
                    if nq > 512:
                        S.op("pe", (lambda e, kc=kc, dd=dd, nq=nq, q0=q0, sl=sl: e.matmul(
                            dd[:, 512:nq], lhsT=kBs[sl][:, kc * 128:(kc + 1) * 128], rhs=qBs[sl][:, q0 + 512:q0 + nq],
                            start=True, stop=True)),
                            reads=[hdB[sl]], writes=[bB[2 * (kc % 2) + 1]])
                        rd.append(bB[2 * (kc % 2) + 1])
                    tm, tmB_ = tmp[kc % 2], tmpB[kc % 2]
                    S.op("dve", (lambda e, kc=kc, dd=dd, n1=n1, tm=tm: e.scalar_tensor_tensor(
                        out=tm[:, 0:n1], in0=dd[:, 0:n1], scalar=scale, in1=tab[:, kc * 640:kc * 640 + n1],
                        op0=ALU.mult, op1=ALU.add)), reads=[rd[0], tabB[kc]], writes=[tmB_])
                    if nq > 512:
                        S.op("dve", (lambda e, kc=kc, dd=dd, nq=nq, tm=tm: e.scalar_tensor_tensor(
                            out=tm[:, 512:nq], in0=dd[:, 512:nq], scalar=scale, in1=tab[:, kc * 640 + 512:kc * 640 + nq],
                            op0=ALU.mult, op1=ALU.add)), reads=[rd[1], tabB[kc]], writes=[tmB_])
                    S.op("act", (lambda e, kc=kc, nq=nq, tm=tm, sl=sl: e.activation(
                        out=Pa[sl][:, kc * 640:kc * 640 + nq], in_=tm[:, 0:nq], func=AF.Exp)),
                        reads=[tmB_], writes=[PaB[sl][kc]])
                for qb in range(4):
                    Ob, Lb = 4 + item % 2, 6 + item % 2
                    item += 1
                    for bq in range(4):
                        b = qb * 4 + bq
                        lcs = list(range(b - 2, b + 3))
                        for i_, lc in enumerate(lcs):
                            kc = kc_of(lc)
                            j0 = (b - max(0, lc - 2)) * 128
                            for (bank_i, lhs) in ((Ob, None), (Lb, ones_bf)):
                                S.op("pe", (lambda e, kc=kc, j0=j0, bank_i=bank_i, lhs=lhs, bq=bq, i_=i_, sl=sl: e.matmul(
                                    banks[bank_i][:, bq * 128:(bq + 1) * 128],
                                    lhsT=(vBs[sl][:, kc * 128:(kc + 1) * 128] if lhs is None else lhs[:, :]),
                                    rhs=Pa[sl][:, kc * 640 + j0:kc * 640 + j0 + 128], start=(i_ == 0), stop=(i_ == 4))),
                                    reads=[hdB[sl], PaB[sl][kc], Bconst], writes=[bB[bank_i]])
                    attn_epilogue(Ob, Lb, bvcol[:, 2 + h:3 + h], E, yT_scr[8 + h, :, qb * 512:(qb + 1) * 512], "y4")
            S.flush()
        if stop_after <= 4:
            return nc

        m_scr = dscr("m_scr", [16, 128, TOWN])
        Bm = Buf("m")
        Bx1, Bh2 = Buf("x1"), Buf("h2")
        with ExitStack() as es:
            woa = sb(es, "woa", [128, 8 * D], BF16)
            wob = sb(es, "wob", [128, 8 * D], BF16)
            wB_ = Buf("woab")
            yab = [sb(es, "yab%d" % i, [128, 16 * 512], BF16) for i in range(2)]
            yabB = [Buf("yab0"), Buf("yab1")]
            sgt = [sb(es, "sgt%d" % i, [128, 1024], BF16) for i in range(3)]
            sgtB = [Buf("sgt%d" % i) for i in range(3)]
            t1 = [sb(es, "m_t1%d" % i, [128, 512], F32) for i in range(2)]
            t2 = [sb(es, "m_t2%d" % i, [128, 512], F32) for i in range(2)]
            t1B = [Buf("t1a"), Buf("t1b")]
            t2B = [Buf("t2a"), Buf("t2b")]
            mo = [sb(es, "mo%d" % i, [128, 512], BF16) for i in range(3)]
            moB = [Buf("mo%d" % i) for i in range(3)]
            for part in range(4):
                S.dma("pool", v3(woa[:, :], 8)[:, :, part * 512:(part + 1) * 512],
                      w_oa.rearrange("(k p) n -> p k n", p=128)[:, :, part * 512:(part + 1) * 512], writes=[wB_], key="woab")
                S.dma("pool", v3(wob[:, :], 8)[:, :, part * 512:(part + 1) * 512],
                      w_ob.rearrange("(k p) n -> p k n", p=128)[:, :, part * 512:(part + 1) * 512], writes=[wB_], key="woab")
            it = 0
            for tb in range(4):
                sl = tb % 2
                S.dma("sp", v3(yab[sl][:, :], 16), yT_scr.rearrange("k p t -> p k t")[:, :, tb * 512:(tb + 1) * 512],
                      reads=[ByT], writes=[yabB[sl]], key=("yab", sl))
                for n in range(16):
                    s3 = it % 3
                    S.dma("sp", sgt[s3][:, 0:512], sg_scr[n, :, tb * 512:(tb + 1) * 512], reads=[Bsg], writes=[sgtB[s3]], key=("sgt", s3))
                    S.dma("sp", sgt[s3][:, 512:1024], sg_scr[16 + n, :, tb * 512:(tb + 1) * 512], reads=[Bsg], writes=[sgtB[s3]], key=("sgt", s3))
                    ba, bb_ = (0, 1) if it % 2 == 0 else (2, 5)
                    mm_acc(ba, lambda k, n=n: woa[:, k * D + n * 128:k * D + (n + 1) * 128],
                           lambda k, sl=sl: yab[sl][:, k * 512:(k + 1) * 512], 8, [wB_, yabB[sl]])
                    mm_acc(bb_, lambda k, n=n: wob[:, k * D + n * 128:k * D + (n + 1) * 128],
                           lambda k, sl=sl: yab[sl][:, (8 + k) * 512:(9 + k) * 512], 8, [wB_, yabB[sl]])
                    i2 = it % 2
                    S.op("dve", (lambda e, i2=i2, ba=ba, s3=s3: e.tensor_tensor(out=t1[i2][:, :], in0=banks[ba][:, :], in1=sgt[s3][:, 0:512], op=ALU.mult)),
                         reads=[bB[ba], sgtB[s3]], writes=[t1B[i2]])
                    S.op("dve", (lambda e, i2=i2, bb_=bb_, s3=s3: e.tensor_tensor(out=t2[i2][:, :], in0=banks[bb_][:, :], in1=sgt[s3][:, 512:1024], op=ALU.mult)),
                         reads=[bB[bb_], sgtB[s3]], writes=[t2B[i2]])
                    S.op("pool", (lambda e, i2=i2, s3=s3: e.tensor_tensor(out=mo[s3][:, :], in0=t1[i2][:, :], in1=t2[i2][:, :], op=ALU.add)),
                         reads=[t1B[i2], t2B[i2]], writes=[moB[s3]])
                    S.dma("pool", m_scr[n, :, tb * 512:(tb + 1) * 512], mo[s3][:, :], reads=[moB[s3]], writes=[Bm], key=("mo", s3))
                    it += 1
            S.flush()

        def stats_to_rstd(bank_i, rs_tile, rsB):
            S.op("act", lambda e: e.activation(out=rs_tile[:, :], in_=banks[bank_i][:, :], func=AF.Ln, bias=epsA[:, 0:1], scale=1.0),
                 reads=[bB[bank_i], Bconst], writes=[rsB])
            S.op("act", lambda e: e.activation(out=rs_tile[:, :], in_=rs_tile[:, :], func=AF.Exp, scale=-0.5), reads=[rsB], writes=[rsB])

        with ExitStack() as es:
            wout = sb(es, "wout", [128, 16 * D], BF16)
            woutB = Buf("wout")
            mb_ = [sb(es, "mblk%d" % i, [128, 16 * 512], BF16) for i in range(2)]
            mbB = [Buf("mblk0"), Buf("mblk1")]
            xin = [sb(es, "xin%d" % i, [128, 512], F32) for i in range(3)]
            xinB = [Buf("xin%d" % i) for i in range(3)]
            x1b = sb(es, "x1b", [128, 16 * 512], F32)
            x1bB = Buf("x1b")
            sq5 = [sb(es, "sq5_%d" % i, [128, 512], BF16) for i in range(2)]
            sq5B = [Buf("sq5a"), Buf("sq5b")]
            rs5 = sb(es, "rs5", [128, 512], F32)
            rs5B = Buf("rs5")
            h2o = sb(es, "h2o", [128, 16 * 512], BF16)
            h2oB = Buf("h2o")
            for part in range(4):
                S.dma("pool", v3(wout[:, :], 16)[:, :, part * 512:(part + 1) * 512],
                      w_out.rearrange("(k p) n -> p k n", p=128)[:, :, part * 512:(part + 1) * 512], writes=[woutB], key="wout")
            it = 0
            mr = Ring([0, 1, 2, 3])
            for tb in range(4):
                sl = tb % 2
                S.dma("sp", v3(mb_[sl][:, :], 16), m_scr.rearrange("k p t -> p k t")[:, :, tb * 512:(tb + 1) * 512],
                      reads=[Bm], writes=[mbB[sl]], key=("mblk", sl))
                for n in range(16):
                    s3 = it % 3
                    it += 1
                    S.dma("sp", xin[s3][:, :], xT_v[:, n, tb * 512:(tb + 1) * 512], writes=[xinB[s3]], key=("xin", s3))
                    bi = mr.next()
                    mm_acc(bi, lambda k, n=n: wout[:, k * D + n * 128:k * D + (n + 1) * 128],
                           lambda k, sl=sl: mb_[sl][:, k * 512:(k + 1) * 512], 16, [woutB, mbB[sl]])
                    S.op("dve", (lambda e, bi=bi, n=n, s3=s3: e.scalar_tensor_tensor(
                        out=x1b[:, n * 512:(n + 1) * 512], in0=banks[bi][:, :], scalar=mod_sb[:, 32 + n:33 + n], in1=xin[s3][:, :],
                        op0=ALU.mult, op1=ALU.add)), reads=[bB[bi], xinB[s3], Bmod], writes=[x1bB])
                    q2 = n % 2
                    S.op("act", (lambda e, n=n, q2=q2: e.activation(out=sq5[q2][:, :], in_=x1b[:, n * 512:(n + 1) * 512], func=AF.Square)),
                         reads=[x1bB], writes=[sq5B[q2]])
                    S.op("pe", (lambda e, n=n, q2=q2: e.matmul(banks[7][:, :], lhsT=ones_bf[:, :], rhs=sq5[q2][:, :], start=(n == 0), stop=(n == 15))),
                         reads=[sq5B[q2], Bconst], writes=[bB[7]])
                stats_to_rstd(7, rs5, rs5B)
                for k in range(16):
                    S.op("dve", (lambda e, k=k: e.scalar_tensor_tensor(
                        out=h2o[:, k * 512:(k + 1) * 512], in0=x1b[:, k * 512:(k + 1) * 512], scalar=A2[:, k:k + 1], in1=rs5[:, :],
                        op0=ALU.mult, op1=ALU.mult)), reads=[x1bB, rs5B, Bmod], writes=[h2oB])
                S.dma("pool", v3(h2T_scr[:, :], 16)[:, :, tb * 512:(tb + 1) * 512], v3(h2o[:, :], 16), reads=[h2oB], writes=[Bh2], key="h2st")
                S.dma("pool", x1T_scr.rearrange("k p t -> p k t")[:, :, tb * 512:(tb + 1) * 512], v3(x1b[:, :], 16),
                      reads=[x1bB], writes=[Bx1], key="x1st")
            S.flush()
        if stop_after <= 5:
            return nc

        Bact = Buf("act")
        w_gate_v = w_gate.rearrange("(k p) n -> p k n", p=128)
        w_up_v = w_up.rearrange("(k p) n -> p k n", p=128)
        with ExitStack() as es:
            h2 = sb(es, "h2", [128, 16 * TOWN], BF16)
            h2B = Buf("h2")
            wg = [sb(es, "wg%d" % i, [128, 16 * 512], BF16) for i in range(2)]
            wu = [sb(es, "wu%d" % i, [128, 16 * 512], BF16) for i in range(2)]
            wgB = [Buf("wg0"), Buf("wg1")]
            wuB = [Buf("wu0"), Buf("wu1")]
            bg = [sb(es, "bg%d" % i, [128, 4], F32) for i in range(2)]
            bu = [sb(es, "bu%d" % i, [128, 4], F32) for i in range(2)]
            bgB = [Buf("bg0"), Buf("bg1")]
            buB = [Buf("bu0"), Buf("bu1")]
            sgl = [sb(es, "sgl%d" % i, [128, 512], F32) for i in range(2)]
            sglB = [Buf("sgl0"), Buf("sgl1")]
            ao = [sb(es, "ao%d" % i, [128, 512], BF16) for i in range(3)]
            aoB = [Buf("ao%d" % i) for i in range(3)]
            S.dma("sp", h2[:, :], h2T_scr[:, :], reads=[Bh2], writes=[h2B], key="h2ld")
            it = 0
            for ti in range(11):
                sl = ti % 2
                load_wtile(wg[sl], wgB[sl], w_gate_v, ti * 512, 512, 16, ("wg", sl))
                load_wtile(wu[sl], wuB[sl], w_up_v, ti * 512, 512, 16, ("wu", sl))
                bias_cols(wg[sl], wgB[sl], 4, B2bf, bg[sl][:, 0:4], bgB[sl], 512)
                bias_cols(wu[sl], wuB[sl], 4, B2bf, bu[sl][:, 0:4], buB[sl], 512)
                for cc in range(4):
                    for tb in range(4):
                        bG, bU = ((0, 1), (2, 3), (4, 5))[it % 3]
                        i2, i3 = it % 2, it % 3
                        it += 1
                        mm_acc(bG, lambda k, sl=sl, cc=cc: wg[sl][:, k * 512 + cc * 128:k * 512 + (cc + 1) * 128],
                               lambda k, tb=tb: h2[:, k * TOWN + tb * 512:k * TOWN + (tb + 1) * 512], 16, [wgB[sl], h2B])
                        mm_acc(bU, lambda k, sl=sl, cc=cc: wu[sl][:, k * 512 + cc * 128:k * 512 + (cc + 1) * 128],
                               lambda k, tb=tb: h2[:, k * TOWN + tb * 512:k * TOWN + (tb + 1) * 512], 16, [wuB[sl], h2B])
                        S.op("act", (lambda e, bG=bG, i2=i2, sl=sl, cc=cc: e.activation(
                            out=sgl[i2][:, :], in_=banks[bG][:, :], func=AF.Silu, bias=bg[sl][:, cc:cc + 1], scale=1.0)),
                            reads=[bB[bG], bgB[sl]], writes=[sglB[i2]])
                        S.op("dve", (lambda e, bU=bU, i2=i2, i3=i3, sl=sl, cc=cc: e.scalar_tensor_tensor(
                            out=ao[i3][:, :], in0=banks[bU][:, :], scalar=bu[sl][:, cc:cc + 1], in1=sgl[i2][:, :],
                            op0=ALU.add, op1=ALU.mult)), reads=[bB[bU], buB[sl], sglB[i2]], writes=[aoB[i3]])
                        S.dma("sp", act_scr[ti * 4 + cc, :, tb * 512:(tb + 1) * 512], ao[i3][:, :], reads=[aoB[i3]], writes=[Bact],
                              key=("ao", i3))
            S.flush()
        if stop_after <= 6:
            return nc

        Bx2 = Buf("x2")
        w_down_v = w_down.rearrange("(k p) n -> p k n", p=128)
        with ExitStack() as es:
            acth = sb(es, "acth", [128, 44 * 1024], BF16)
            acthB = Buf("acth")
            wd = [sb(es, "wd%d" % i, [128, 44 * 256], BF16) for i in range(2)]
            wdB = [Buf("wd0"), Buf("wd1")]
            x1i = [sb(es, "x1i%d" % i, [128, 512], F32) for i in range(3)]
            x1iB = [Buf("x1i%d" % i) for i in range(3)]
            x2o = [sb(es, "x2o%d" % i, [128, 512], F32) for i in range(3)]
            x2oB = [Buf("x2o%d" % i) for i in range(3)]
            sq7 = [sb(es, "sq7_%d" % i, [128, 512], BF16) for i in range(2)]
            sq7B = [Buf("sq7a"), Buf("sq7b")]
            rs7 = [sb(es, "rs7_%d" % i, [128, 512], F32) for i in range(2)]
            rs7B = [Buf("rs7a"), Buf("rs7b")]
            Fw = sb(es, "Fw", [128, 16], F32)
            FwB = Buf("Fw")
            yo = [sb(es, "yo%d" % i, [128, 512], F32) for i in range(3)]
            yoB = [Buf("yo%d" % i) for i in range(3)]
            Bout = Buf("out")
            S.op("pool", lambda e: e.tensor_scalar(out=Fw[:, :], in0=vecs_sb[:, C_FW:C_FW + 16], scalar1=math.sqrt(D), scalar2=None, op0=ALU.mult),
                 reads=[Bvecs], writes=[FwB])
            it = 0
            wi = 0
            mr = Ring([0, 1, 2, 3, 4, 5])
            for half in range(2):
                h0 = half * 1024
                for part in range(4):
                    S.dma("sp", v3(acth[:, :], 44)[:, part * 11:(part + 1) * 11, :],
                          act_scr.rearrange("k p t -> p k t")[:, part * 11:(part + 1) * 11, h0:h0 + 1024],
                          reads=[Bact], writes=[acthB], key="acth")
                for nt in range(8):
                    sl = wi % 2
                    wi += 1
                    for part in range(4):
                        S.dma("pool", v3(wd[sl][:, :], 44)[:, part * 11:(part + 1) * 11, :],
                              w_down_v[:, part * 11:(part + 1) * 11, nt * 256:(nt + 1) * 256], writes=[wdB[sl]], key=("wd", sl))
                    for nn in range(2):
                        n = nt * 2 + nn
                        for tb2 in range(2):
                            s3 = it % 3
                            q2 = it % 2
                            it += 1
                            c0 = h0 + tb2 * 512
                            S.dma("sp", x1i[s3][:, :], x1T_scr[n, :, c0:c0 + 512], reads=[Bx1], writes=[x1iB[s3]], key=("x1i", s3))
                            bi = mr.next()
                            mm_acc(bi, lambda k, sl=sl, nn=nn: wd[sl][:, k * 256 + nn * 128:k * 256 + (nn + 1) * 128],
                                   lambda k, tb2=tb2: acth[:, k * 1024 + tb2 * 512:k * 1024 + (tb2 + 1) * 512], 44, [wdB[sl], acthB])
                            S.op("dve", (lambda e, bi=bi, n=n, s3=s3: e.scalar_tensor_tensor(
                                out=x2o[s3][:, :], in0=banks[bi][:, :], scalar=mod_sb[:, 80 + n:81 + n], in1=x1i[s3][:, :],
                                op0=ALU.mult, op1=ALU.add)), reads=[bB[bi], x1iB[s3], Bmod], writes=[x2oB[s3]])
                            S.op("act", (lambda e, s3=s3, q2=q2: e.activation(out=sq7[q2][:, :], in_=x2o[s3][:, :], func=AF.Square)),
                                 reads=[x2oB[s3]], writes=[sq7B[q2]])
                            S.op("pe", (lambda e, q2=q2, tb2=tb2, n=n: e.matmul(banks[6 + tb2][:, :], lhsT=ones_bf[:, :], rhs=sq7[q2][:, :],
                                                                       start=(n == 0), stop=(n == 15))),
                                 reads=[sq7B[q2], Bconst], writes=[bB[6 + tb2]])
                            S.dma("act", x2T_scr[n, :, c0:c0 + 512], x2o[s3][:, :], reads=[x2oB[s3]], writes=[Bx2], key=("x2o", s3))
                for tb2 in range(2):
                    stats_to_rstd(6 + tb2, rs7[tb2], rs7B[tb2])
                for tb2 in range(2):
                    c0 = h0 + tb2 * 512
                    for n in range(16):
                        s3 = it % 3
                        it += 1
                        S.dma("sp", x1i[s3][:, :], x2T_scr[n, :, c0:c0 + 512], reads=[Bx2], writes=[x1iB[s3]], key=("x1i", s3))
                        S.op("dve", (lambda e, s3=s3, n=n, tb2=tb2: e.scalar_tensor_tensor(
                            out=yo[s3][:, :], in0=x1i[s3][:, :], scalar=Fw[:, n:n + 1], in1=rs7[tb2][:, :],
                            op0=ALU.mult, op1=ALU.mult)), reads=[x1iB[s3], FwB, rs7B[tb2]], writes=[yoB[s3]])
                        S.dma("act", outT[n, :, c0:c0 + 512], yo[s3][:, :], reads=[yoB[s3]], writes=[Bout], key=("yo", s3))
            S.flush()
        return nc


def _pack_cols(v):
    v = np.asarray(v, np.float32).reshape(-1)
    return np.ascontiguousarray(v.reshape(-1, 128).T)


def _token_order(c):
    own = np.arange(TOWN * c, TOWN * (c + 1))
    others = np.concatenate([np.arange(0, TOWN * c), np.arange(TOWN * (c + 1), S_ALL)])
    r0 = 32 * c
    rows_before = np.arange(r0 - 4, r0) if c > 0 else np.arange(4, 8)
    rows_after = np.arange(r0 + 32, r0 + 36) if c < NCORE - 1 else np.arange(248, 252)
    halo_rows = np.concatenate([rows_before, rows_after])
    halo = (halo_rows[:, None] * GRID_W + np.arange(GRID_W)[None, :]).reshape(-1)
    return own, others, halo, halo_rows


def _rope_tables():
    t = np.arange(S_ALL)
    row = (t // GRID_W).astype(np.float32)
    col = (t % GRID_W).astype(np.float32)
    inv = (np.float32(10000.0) ** (-(np.arange(0, 64, 2, dtype=np.float32) / np.float32(64.0)))).astype(np.float32)
    ang = np.concatenate([row[:, None] * inv[None], col[:, None] * inv[None]], axis=-1).astype(np.float32)
    cos = np.cos(ang).astype(np.float32)
    sin = np.sin(ang).astype(np.float32)
    cosT = np.repeat(cos.T, 2, axis=0)
    sinT = np.repeat(sin.T, 2, axis=0)
    sinT[0::2] *= -1.0
    return np.ascontiguousarray(cosT), np.ascontiguousarray(sinT)


def _na_table(c, rpb, halo_rows):
    tab = np.full((8, 20, 128, 640), MASK_NEG, np.float32)
    rows_tot = S_ALL // GRID_W
    for kc in range(20):
        if kc < 16:
            lc = kc
            grow = 32 * c + 2 * kc + np.arange(2)
        elif kc < 18:
            lc = kc - 18
            grow = halo_rows[(kc - 16) * 2:(kc - 16) * 2 + 2]
        else:
            lc = kc - 2
            grow = halo_rows[4 + (kc - 18) * 2:4 + (kc - 18) * 2 + 2]
        key_r = np.repeat(grow, GRID_W)
        key_c = np.tile(np.arange(GRID_W), 2)
        blo = max(0, lc - 2)
        bhi = min(15, lc + 2)
        for b in range(blo, bhi + 1):
            qr = np.repeat(32 * c + 2 * b + np.arange(2), GRID_W)
            qc = np.tile(np.arange(GRID_W), 2)
            rs = np.clip(qr - NA_ROWS // 2, 0, rows_tot - NA_ROWS)
            cs_ = np.clip(qc - NA_COLS // 2, 0, GRID_W - NA_COLS)
            inr = (key_r[:, None] >= rs[None, :]) & (key_r[:, None] < rs[None, :] + NA_ROWS)
            inc = (key_c[:, None] >= cs_[None, :]) & (key_c[:, None] < cs_[None, :] + NA_COLS)
            valid = inr & inc
            if kc >= 16:
                own_lo = 32 * c + 2 * (b - 2)
                own_hi = 32 * c + 2 * (b + 2) + 1
                lo = max(own_lo, 32 * c)
                hi = min(own_hi, 32 * c + 31)
                dup = (key_r >= lo) & (key_r <= hi)
                valid &= ~dup[:, None]
            rel_r = np.clip(key_r[:, None] - qr[None, :] + (NA_ROWS - 1), 0, 2 * NA_ROWS - 2)
            rel_c = np.clip(key_c[:, None] - qc[None, :] + (NA_COLS - 1), 0, 2 * NA_COLS - 2)
            j0 = (b - blo) * 128
            for h in range(8):
                bias = rpb[h][rel_r, rel_c]
                tab[h, kc, :, j0:j0 + 128] = np.where(valid, bias, np.float32(MASK_NEG))
    return tab


def prep_inputs(inp):
    x = np.asarray(inp["x"], np.float32)[0]
    xTfull = np.ascontiguousarray(x.T)
    cosT, sinT = _rope_tables()
    rm = np.zeros((128, 128), np.float32)
    for k in range(128):
        rm[k, k ^ 1] = 1.0
    vec = np.zeros((128, NV), np.float32)
    vec[:, 0:16] = _pack_cols(inp["c"])
    vec[:, 16:112] = _pack_cols(inp["b_ada"])
    vec[:, 112:128] = _pack_cols(inp["norm1_w"])
    vec[:, 128:144] = _pack_cols(inp["norm2_w"])
    vec[:, 144:160] = _pack_cols(inp["final_w"])
    vec[:, 160:161] = _pack_cols(inp["q_norm_w"])
    vec[:, 161:162] = _pack_cols(inp["k_norm_w"])
    rpb = np.asarray(inp["nat_rpb"], np.float32)[0]
    shared = {
        "vecs": vec, "rmat": rm,
        "w_ada": np.ascontiguousarray(np.asarray(inp["w_ada"], np.float32)[0]),
        "w_in": np.ascontiguousarray(np.asarray(inp["w_in"], np.float32)[0]),
        "w_oa": np.ascontiguousarray(np.asarray(inp["w_oa"], np.float32)[0]),
        "w_ob": np.ascontiguousarray(np.asarray(inp["w_ob"], np.float32)[0]),
        "w_out": np.ascontiguousarray(np.asarray(inp["w_out"], np.float32)[0]),
        "w_gate": np.ascontiguousarray(np.asarray(inp["w_ffn_gate"], np.float32)[0]),
        "w_up": np.ascontiguousarray(np.asarray(inp["w_ffn_up"], np.float32)[0]),
        "w_down": np.ascontiguousarray(np.asarray(inp["w_ffn_down"], np.float32)[0]),
    }
    maps = []
    for c in range(NCORE):
        own, others, halo, halo_rows = _token_order(c)
        order = np.concatenate([own, others, halo])
        m = dict(shared)
        m["xT"] = np.ascontiguousarray(xTfull[:, order])
        o2 = order[:S_ALL]
        m["cosT"] = np.ascontiguousarray(cosT[:, o2])
        m["sinT"] = np.ascontiguousarray(sinT[:, o2])
        m["natab"] = _na_table(c, rpb, halo_rows)
        maps.append(m)
    return maps


_NC_CACHE = {}


def kernel(**inputs):
    maps = prep_inputs(inputs)
    if "nc" not in _NC_CACHE:
        _NC_CACHE["nc"] = build_program()
    nc = _NC_CACHE["nc"]
    res = run_bass_kernel_spmd(nc, maps, core_ids=list(range(NCORE)))
    outs = []
    for c in range(NCORE):
        o = np.asarray(res.results[c]["outT"], np.float32).reshape(D, TOWN)
        outs.append(o.T)
    return np.ascontiguousarray(np.concatenate(outs, axis=0)[None].astype(np.float32))
```

```python
import math
from contextlib import ExitStack

import numpy as np
import concourse.bass as bass
import concourse.mybir as mybir
from concourse.bass_utils import run_bass_kernel_spmd

F32 = mybir.dt.float32
BF16 = mybir.dt.bfloat16
AF = mybir.ActivationFunctionType
ALU = mybir.AluOpType

D = 2048
S_ALL = 16384
NCORE = 8
TOWN = 2048
THALO = 512
TLOC = TOWN + THALO
NTOK_IN = S_ALL + THALO
D_IN = 8704
D_FF = 5632
EPS = 1e-6
NV = 162
GRID_W = 64
NA_ROWS = 8
NA_COLS = 16
MASK_NEG = -30000.0


class Buf:
    __slots__ = ("name", "w", "rs", "rd")

    def __init__(self, name):
        self.name = name
        self.w = None
        self.rs = {}
        self.rd = []


class Op:
    __slots__ = ("eng", "fn", "deps", "signaled", "token", "key", "idx", "epoch")

    def __init__(self, eng, fn, key, idx, epoch):
        self.eng = eng
        self.fn = fn
        self.key = key
        self.deps = set()
        self.signaled = key is not None
        self.token = None
        self.idx = idx
        self.epoch = epoch


ENGS = ("pe", "act", "dve", "pool", "sp")


class Sched:
    def __init__(self, nc, es):
        self.nc = nc
        self.es = es
        self.sem = {e: es.enter_context(nc.semaphore("s_" + e)) for e in ENGS}
        self.cnt = {e: 0 for e in ENGS}
        self.keysem = {}
        self.keyidx = {}
        self.sempool = []
        self.poolcnt = []
        self.waited = {e: {} for e in ENGS}
        self.epoch = 0
        self.ops = {e: [] for e in ENGS}
        self.nidx = {e: 0 for e in ENGS}
        self.allops = []

    def _mk(self, eng, fn, reads, writes, key):
        op = Op(eng, fn, key, self.nidx[eng], self.epoch)
        self.nidx[eng] += 1
        deps = op.deps
        for b in reads:
            if b.w is not None:
                deps.add(b.w)
        for b in writes:
            if b.w is not None:
                deps.add(b.w)
            deps.update(b.rs.values())
            deps.update(b.rd)
        for d in list(deps):
            if d.key is None and key is None and d.eng == eng and (eng == "pe" or op.idx - d.idx > 2):
                deps.discard(d)
        for d in deps:
            d.signaled = True
        for b in reads:
            if key is not None:
                b.rd.append(op)
            else:
                b.rs[eng] = op
        for b in writes:
            b.w = op
            b.rs = {}
            b.rd = []
        self.ops[eng].append(op)
        self.allops.append(op)
        return op

    def op(self, eng, fn, reads=(), writes=()):
        return self._mk(eng, fn, reads, writes, None)

    def dma(self, eng, out, in_, reads=(), writes=(), key=None):
        assert key is not None
        if key not in self.keysem:
            i = len(self.keysem)
            if i >= len(self.sempool):
                self.sempool.append(self.es.enter_context(self.nc.semaphore("k%d" % i)))
                self.poolcnt.append(0)
            self.keysem[key] = self.sempool[i]
            self.keyidx[key] = i
        return self._mk(eng, lambda e: e.dma_start(out=out, in_=in_), reads, writes, key)

    def flush(self, final=False):
        nc = self.nc
        last = {}
        for e in ENGS:
            for op in reversed(self.ops[e]):
                if op.key is None and op.fn is not None:
                    op.signaled = True
                    last[e] = op
                    break
        for op in self.allops:
            if op.key is not None:
                i = self.keyidx[op.key]
                self.poolcnt[i] += 16
                op.token = (self.sempool[i], self.poolcnt[i])
            elif op.signaled:
                self.cnt[op.eng] += 1
                op.token = (self.sem[op.eng], self.cnt[op.eng])
        bar = [(self.sem[e], self.cnt[e]) for e in ENGS if self.cnt[e] > 0]
        bar += [(self.sempool[i], self.poolcnt[i]) for i in range(len(self.sempool)) if self.poolcnt[i] > 0]
        ops = self.ops
        waited = self.waited
        sems = self.sem
        epoch = self.epoch

        def emit(ename, e):
            w = waited[ename]
            for op in ops[ename]:
                for d in op.deps:
                    if d.epoch != epoch:
                        continue
                    if d.key is None and d.eng == ename:
                        if ename == "pe" or op.idx - d.idx > 2:
                            continue
                    s, v = d.token
                    if w.get(id(s), 0) < v:
                        e.wait_ge(s, v)
                        w[id(s)] = v
                if op.fn is None:
                    continue
                ins = op.fn(e)
                if op.key is not None:
                    ins.then_inc(op.token[0], 16)
                elif op.signaled:
                    ins.then_inc(sems[ename], 1)
            for s, v in bar:
                k = id(s)
                if w.get(k, 0) < v:
                    e.wait_ge(s, v)
                    w[k] = v

        with nc.Block() as block:
            @block.tensor
            def _(e):
                emit("pe", e)

            @block.scalar
            def _(e):
                emit("act", e)

            @block.vector
            def _(e):
                emit("dve", e)

            @block.gpsimd
            def _(e):
                emit("pool", e)

            @block.sync
            def _(e):
                emit("sp", e)

        self.ops = {e: [] for e in ENGS}
        self.allops = []
        self.epoch += 1
        self.keysem = {}
        self.keyidx = {}


class Ring:
    def __init__(self, items):
        self.items = items
        self.i = 0

    def next(self):
        it = self.items[self.i % len(self.items)]
        self.i += 1
        return it


def v3(ap, k):
    return ap.rearrange("p (k t) -> p k t", k=k)


def build_program(debug_outs=(), stop_after=99):
    nc = bass.Bass("TRN2", target_bir_lowering=False)

    def din(name, shape, dt=F32):
        return nc.dram_tensor(name, list(shape), dt, kind="ExternalInput").ap()

    def dscr(name, shape, dt=BF16):
        kind = "ExternalOutput" if name in debug_outs else "Internal"
        return nc.dram_tensor(name, list(shape), dt, kind=kind).ap()

    xT = din("xT", [D, NTOK_IN])
    cosT = din("cosT", [128, S_ALL])
    sinT = din("sinT", [128, S_ALL])
    vecs = din("vecs", [128, NV])
    rmat = din("rmat", [128, 128])
    natab = din("natab", [8, 20, 128, 640])
    w_ada = din("w_ada", [D, 6 * D])
    w_in = din("w_in", [D, D_IN])
    w_oa = din("w_oa", [1024, D])
    w_ob = din("w_ob", [1024, D])
    w_out = din("w_out", [D, D])
    w_gate = din("w_gate", [D, D_FF])
    w_up = din("w_up", [D, D_FF])
    w_down = din("w_down", [D_FF, D])
    outT = nc.dram_tensor("outT", [16, 128, TOWN], F32, kind="ExternalOutput").ap()

    hT_scr = dscr("hT_scr", [128, 16 * TLOC])
    KT_scr = dscr("KT_scr", [2, 128, S_ALL])
    V_scr = dscr("V_scr", [2, 128, 128 * 128])
    qT_scr = dscr("qT_scr", [8, 128, TOWN])
    qBT_scr = dscr("qBT_scr", [8, 128, TOWN])
    kBT_scr = dscr("kBT_scr", [8, 128, TLOC])
    vB_scr = dscr("vB_scr", [8, 128, 20 * 128])
    sg_scr = dscr("sg_scr", [32, 128, TOWN])
    bv_scr = dscr("bv_scr", [128, 16], F32)
    yT_scr = dscr("yT_scr", [16, 128, TOWN])
    x1T_scr = dscr("x1T_scr", [16, 128, TOWN], F32)
    h2T_scr = dscr("h2T_scr", [128, 16 * TOWN])
    act_scr = dscr("act_scr", [44, 128, TOWN])
    x2T_scr = dscr("x2T_scr", [16, 128, TOWN], F32)

    w_ada_v = w_ada.rearrange("(k p) n -> p k n", p=128)
    w_in_v = w_in.rearrange("(k p) n -> p k n", p=128)
    xT_v = xT.rearrange("(k p) t -> p k t", p=128)

    with ExitStack() as ges:
        S = Sched(nc, ges)

        def sb(es, name, shape, dt):
            return es.enter_context(nc.sbuf_tensor(name, list(shape), dt))

        vecs_sb = sb(ges, "vecs_sb", [128, NV], F32)
        mod_sb = sb(ges, "mod_sb", [128, 96], F32)
        A1 = sb(ges, "A1", [128, 16], F32)
        A2 = sb(ges, "A2", [128, 16], F32)
        B1bf = sb(ges, "B1bf", [128, 16], BF16)
        B2bf = sb(ges, "B2bf", [128, 16], BF16)
        ones_bf = sb(ges, "ones_bf", [128, 128], BF16)
        rmat_bf = sb(ges, "rmat_bf", [128, 128], BF16)
        kw = sb(ges, "kw", [128, 1], F32)
        epsA = sb(ges, "epsA", [128, 1], F32)
        epsB = sb(ges, "epsB", [128, 1], F32)
        bvcol = sb(ges, "bvcol", [128, 16], F32)
        csil = sb(ges, "csil", [128, 16], BF16)
        dbl = [ges.enter_context(nc.psum_tensor("dbank%d" % i, [128, 1024], F32)) for i in range(4)]
        banks = [dbl[i // 2][:, (i % 2) * 512:(i % 2) * 512 + 512] for i in range(8)]
        bB = [Buf("bank%d" % i) for i in range(8)]
        Bvecs, Bmod, Bconst, Bcsil, Bbv = Buf("vecs"), Buf("mod"), Buf("const"), Buf("csil"), Buf("bv")

        C_C, C_BADA, C_N1, C_N2, C_FW, C_QW, C_KW = 0, 16, 112, 128, 144, 160, 161

        with ExitStack() as es:
            wts = [sb(es, "wada%d" % i, [128, 16 * 512], BF16) for i in range(3)]
            wB = [Buf("wada%d" % i) for i in range(3)]
            S.dma("sp", vecs_sb[:, :], vecs[:, :], writes=[Bvecs], key="vecs")
            S.dma("pool", rmat_bf[:, :], rmat[:, :], writes=[Bconst], key="rmat")
            S.op("dve", lambda e: e.memset(ones_bf[:, :], 1.0), writes=[Bconst])
            S.op("dve", lambda e: e.memset(epsA[:, :], D * EPS), writes=[Bconst])
            S.op("dve", lambda e: e.memset(epsB[:, :], 128 * EPS), writes=[Bconst])
            S.op("act", lambda e: e.activation(out=csil[:, :], in_=vecs_sb[:, C_C:C_C + 16], func=AF.Silu),
                 reads=[Bvecs], writes=[Bcsil])
            ps_mod = banks[7]
            for i in range(24):
                sl = i % 3
                wt = wts[sl]
                S.dma("pool", v3(wt[:, :], 16), w_ada_v[:, :, i * 512:(i + 1) * 512], writes=[wB[sl]],
                      key=("wada", sl))
                for jj in range(4):
                    j = i * 4 + jj
                    for k in range(16):
                        S.op("pe", (lambda e, wt=wt, k=k, jj=jj, j=j: e.matmul(
                            ps_mod[:, j:j + 1], lhsT=wt[:, k * 512 + jj * 128:k * 512 + (jj + 1) * 128],
                            rhs=csil[:, k:k + 1], start=(k == 0), stop=(k == 15))),
                            reads=[wB[sl], Bcsil], writes=[bB[7]])
            S.op("dve", lambda e: e.tensor_tensor(out=mod_sb[:, :], in0=ps_mod[:, 0:96],
                                                  in1=vecs_sb[:, C_BADA:C_BADA + 96], op=ALU.add),
                 reads=[bB[7], Bvecs], writes=[Bmod])
            for (A, sc0, nw0) in ((A1, 16, C_N1), (A2, 64, C_N2)):
                S.op("dve", (lambda e, A=A, sc0=sc0, nw0=nw0: e.scalar_tensor_tensor(
                    out=A[:, :], in0=mod_sb[:, sc0:sc0 + 16], scalar=1.0, in1=vecs_sb[:, nw0:nw0 + 16],
                    op0=ALU.add, op1=ALU.mult)), reads=[Bmod, Bvecs], writes=[Bmod])
                S.op("pool", (lambda e, A=A: e.tensor_scalar(out=A[:, :], in0=A[:, :], scalar1=math.sqrt(D),
                                                            scalar2=None, op0=ALU.mult)),
                     reads=[Bmod], writes=[Bmod])
            S.op("dve", lambda e: e.tensor_copy(out=B1bf[:, :], in_=mod_sb[:, 0:16]), reads=[Bmod], writes=[Bmod])
            S.op("dve", lambda e: e.tensor_copy(out=B2bf[:, :], in_=mod_sb[:, 48:64]), reads=[Bmod], writes=[Bmod])
            S.op("pool", lambda e: e.tensor_scalar(out=kw[:, :], in0=vecs_sb[:, C_KW:C_KW + 1],
                                                   scalar1=math.sqrt(128.0), scalar2=None, op0=ALU.mult),
                 reads=[Bvecs], writes=[Bmod])
            S.flush()
        if stop_after <= 0:
            return nc

        def norm_rope_multi(items, use_sqrt=False):
            chain_part1(items)
            chain_part2(items, use_sqrt)

        def chain_part1(items):
            for it_ in items:
                T = it_["T"]
                S.op("act", lambda e, it_=it_, T=T: e.activation(out=T["sq"][:, :], in_=it_["ps"][:, :], func=AF.Square,
                                                               bias=it_["bias"], scale=1.0),
                     reads=[it_["psB"]] + list(it_["extra"]), writes=[T["sqB"]])
                S.op("dve", lambda e, it_=it_, T=T: e.tensor_scalar(out=T["raw"][:, :], in0=it_["ps"][:, :], scalar1=it_["bias"],
                                                                  scalar2=None, op0=ALU.add),
                     reads=[it_["psB"], T["sqB"]] + list(it_["extra"]), writes=[T["rawB"]])

        def chain_part2(items, use_sqrt=False):
            for it_ in items:
                T = it_["T"]
                S.op("pe", lambda e, it_=it_, T=T: e.matmul(banks[it_["sumb"]][:, :], lhsT=ones_bf[:, :], rhs=T["sq"][:, :],
                                                          start=True, stop=True),
                     reads=[T["sqB"], Bconst], writes=[bB[it_["sumb"]]])
            for it_ in items:
                T = it_["T"]
                S.op("act", lambda e, it_=it_, T=T: e.activation(out=T["rs"][:, :], in_=banks[it_["sumb"]][:, :],
                                                               func=(AF.Sqrt if use_sqrt else AF.Ln),
                                                               bias=epsB[:, 0:1], scale=1.0),
                     reads=[bB[it_["sumb"]], Bconst], writes=[T["rsB"]])
            for it_ in items:
                T = it_["T"]
                if use_sqrt:
                    S.op("dve", lambda e, T=T: e.reciprocal(out=T["rs"][:, :], in_=T["rs"][:, :]),
                         reads=[T["rsB"]], writes=[T["rsB"]])
                else:
                    S.op("act", lambda e, T=T: e.activation(out=T["rs"][:, :], in_=T["rs"][:, :], func=AF.Exp, scale=-0.5),
                         reads=[T["rsB"]], writes=[T["rsB"]])
            for it_ in items:
                T = it_["T"]
                S.op("dve", lambda e, it_=it_, T=T: e.scalar_tensor_tensor(out=T["n"][:, :], in0=T["raw"][:, :], scalar=it_["w"],
                                                                         in1=T["rs"][:, :], op0=ALU.mult, op1=ALU.mult),
                     reads=[T["rawB"], T["rsB"], Bmod, Bvecs], writes=[T["nB"]])
            for it_ in items:
                T = it_["T"]
                S.op("pe", lambda e, it_=it_, T=T: e.matmul(banks[it_["rotb"]][:, :], lhsT=rmat_bf[:, :], rhs=T["n"][:, :],
                                                          start=True, stop=True),
                     reads=[T["nB"], Bconst], writes=[bB[it_["rotb"]]])
            for it_ in items:
                T = it_["T"]
                S.op("pool", lambda e, it_=it_, T=T: e.tensor_tensor(out=T["t1"][:, :], in0=T["n"][:, :], in1=it_["cos"], op=ALU.mult),
                     reads=[T["nB"], it_["tabB"]], writes=[T["t1B"]])
            for it_ in items:
                T = it_["T"]
                S.op("dve", lambda e, it_=it_, T=T: e.tensor_tensor(out=T["t2"][:, :], in0=banks[it_["rotb"]][:, :], in1=it_["sin"],
                                                                  op=ALU.mult),
                     reads=[bB[it_["rotb"]], it_["tabB"]], writes=[T["t2B"]])
            for it_ in items:
                T = it_["T"]
                S.op("pool", lambda e, it_=it_, T=T: e.tensor_tensor(out=it_["out"], in0=T["t1"][:, :], in1=T["t2"][:, :], op=ALU.add),
                     reads=[T["t1B"], T["t2B"]], writes=[it_["outB"]])
                if it_.get("after") is not None:
                    it_["after"]()

        def norm_rope(ps, psB, bias_ap, w_ap, cos_ap, sin_ap, tabB, T, out_ap, outB, extra_reads=()):
            norm_rope_multi([dict(ps=ps, psB=psB, bias=bias_ap, w=w_ap, cos=cos_ap, sin=sin_ap, tabB=tabB, T=T, out=out_ap,
                                  outB=outB, extra=extra_reads, sumb=3, rotb=4, after=None)])

        def chain_tiles_alt(es, T, pfx):
            T2 = dict(T)
            T2["raw"] = sb(es, pfx + "raw", [128, 512], F32)
            T2["sq"] = sb(es, pfx + "sq", [128, 512], BF16)
            T2["rawB"] = Buf(pfx + "raw")
            T2["sqB"] = Buf(pfx + "sq")
            return T2

        def mk_chain_tiles(es, pfx):
            T = {}
            T["raw"] = sb(es, pfx + "raw", [128, 512], F32)
            T["sq"] = sb(es, pfx + "sq", [128, 512], BF16)
            T["rs"] = sb(es, pfx + "rs", [128, 512], F32)
            T["n"] = sb(es, pfx + "n", [128, 512], BF16)
            T["t1"] = sb(es, pfx + "t1", [128, 512], F32)
            T["t2"] = sb(es, pfx + "t2", [128, 512], F32)
            for k in ("raw", "sq", "rs", "n", "t1", "t2"):
                T[k + "B"] = Buf(pfx + k)
            return T

        def bias_cols(wt, wB_, ncol, Bbf, dst_ap, dstB, col_stride):
            for c in range(ncol):
                for k in range(16):
                    S.op("pe", (lambda e, c=c, k=k: e.matmul(
                        banks[7][:, c:c + 1], lhsT=wt[:, k * col_stride + c * 128:k * col_stride + (c + 1) * 128],
                        rhs=Bbf[:, k:k + 1], start=(k == 0), stop=(k == 15))),
                        reads=[wB_, Bmod], writes=[bB[7]])
            S.op("dve", lambda e: e.tensor_copy(out=dst_ap, in_=banks[7][:, 0:ncol]), reads=[bB[7]], writes=[dstB])

        with ExitStack() as es:
            xs = [sb(es, "xs%d" % i, [128, 16 * 512], F32) for i in range(2)]
            xsB = [Buf("xs%d" % i) for i in range(2)]
            sq = sb(es, "sq", [128, 16 * 512], BF16)
            sqB = Buf("sq")
            xsP = [[Buf("xs%d_%d" % (i, j)) for j in range(4)] for i in range(2)]
            sqP = [Buf("sq_%d" % j) for j in range(4)]
            hT = [sb(es, "hT%d" % i, [128, 16 * 512], BF16) for i in range(2)]
            hTB = [Buf("hT%d" % i) for i in range(2)]
            wkv = sb(es, "wkv", [128, 16 * 512], BF16)
            wkvB = Buf("wkv")
            rstd = sb(es, "rstd", [128, 512], F32)
            rstdB = Buf("rstd")
            cs = [sb(es, "cs%d" % i, [128, 1024], F32) for i in range(2)]
            csB = [Buf("cs%d" % i) for i in range(2)]
            kout = [sb(es, "kout%d" % i, [128, 512], BF16) for i in range(2)]
            koutB = [Buf("kout%d" % i) for i in range(2)]
            vout = [sb(es, "vout%d" % i, [128, 1024], BF16) for i in range(2)]
            voutB = [Buf("vout%d" % i) for i in range(2)]
            bk = sb(es, "bk", [128, 2], F32)
            bkB = Buf("bk")
            TT = [mk_chain_tiles(es, "c1a"), mk_chain_tiles(es, "c1b")]
            TTp = [TT, [chain_tiles_alt(es, TT[0], "c1c"), chain_tiles_alt(es, TT[1], "c1d")]]
            BKT = [Buf("KT0"), Buf("KT1")]
            BV = [Buf("V0"), Buf("V1")]
            BhT = Buf("hTscr")

            S.dma("pool", v3(wkv[:, :], 16), w_in_v[:, :, 1024:1536], writes=[wkvB], key="wkv")
            bias_cols(wkv, wkvB, 2, B1bf, bk[:, 0:2], bkB, 512)
            for c in range(2):
                for k in range(16):
                    S.op("pe", (lambda e, c=c, k=k: e.matmul(
                        banks[7][:, 8 + c:9 + c], lhsT=wkv[:, k * 512 + 256 + c * 128:k * 512 + 256 + (c + 1) * 128],
                        rhs=B1bf[:, k:k + 1], start=(k == 0), stop=(k == 15))),
                        reads=[wkvB, Bmod], writes=[bB[7]])
            S.op("dve", lambda e: e.tensor_copy(out=bvcol[:, 0:2], in_=banks[7][:, 8:10]), reads=[bB[7]], writes=[Bbv])

            NB1 = 33

            def stageA1a(tb):
                sl = tb % 2
                t0 = tb * 512
                for p_ in range(4):
                    S.dma("sp", v3(xs[sl][:, :], 16)[:, 4 * p_:4 * p_ + 4, :], xT_v[:, 4 * p_:4 * p_ + 4, t0:t0 + 512],
                          writes=[xsP[sl][p_]], key=("xs", sl, p_))
                    S.op("act", (lambda e, p_=p_: e.activation(out=sq[:, p_ * 2048:(p_ + 1) * 2048],
                                                              in_=xs[sl][:, p_ * 2048:(p_ + 1) * 2048], func=AF.Square)),
                         reads=[xsP[sl][p_]], writes=[sqP[p_]])

            def stageA1b(tb):
                for k in range(16):
                    S.op("pe", (lambda e, k=k: e.matmul(banks[0][:, :], lhsT=ones_bf[:, :],
                                                       rhs=sq[:, k * 512:(k + 1) * 512],
                                                       start=(k == 0), stop=(k == 15))),
                         reads=[sqP[k // 4], Bconst], writes=[bB[0]])

            def stageA2(tb):
                sl = tb % 2
                S.op("act", lambda e: e.activation(out=rstd[:, :], in_=banks[0][:, :], func=AF.Ln,
                                                   bias=epsA[:, 0:1], scale=1.0),
                     reads=[bB[0], Bconst], writes=[rstdB])
                S.op("act", lambda e: e.activation(out=rstd[:, :], in_=rstd[:, :], func=AF.Exp, scale=-0.5),
                     reads=[rstdB], writes=[rstdB])
                for k in range(16):
                    S.op("dve", (lambda e, k=k: e.scalar_tensor_tensor(
                        out=hT[sl][:, k * 512:(k + 1) * 512], in0=xs[sl][:, k * 512:(k + 1) * 512],
                        scalar=A1[:, k:k + 1], in1=rstd[:, :], op0=ALU.mult, op1=ALU.mult)),
                        reads=[xsP[sl][k // 4], rstdB, Bmod], writes=[hTB[sl]])

            def stageB(tb):
                sl = tb % 2
                t0 = tb * 512
                own = tb < 4 or tb == 32
                if own:
                    lt0 = t0 if tb < 4 else TOWN
                    S.dma("pool", v3(hT_scr[:, :], 16)[:, :, lt0:lt0 + 512], v3(hT[sl][:, :], 16),
                          reads=[hTB[sl]], writes=[BhT], key="hTst")
                if tb == 32:
                    return None
                S.dma("sp", cs[sl][:, 0:512], cosT[:, t0:t0 + 512], writes=[csB[sl]], key=("cs", sl))
                S.dma("sp", cs[sl][:, 512:1024], sinT[:, t0:t0 + 512], writes=[csB[sl]], key=("cs", sl))
                for g in range(2):
                    for k in range(16):
                        S.op("pe", (lambda e, g=g, k=k: e.matmul(
                            banks[1 + g][:, :], lhsT=wkv[:, k * 512 + g * 128:k * 512 + (g + 1) * 128],
                            rhs=hT[sl][:, k * 512:(k + 1) * 512], start=(k == 0), stop=(k == 15))),
                            reads=[wkvB, hTB[sl]], writes=[bB[1 + g]])
                for s in range(4):
                    bank = 5 + s // 2
                    c0 = (s % 2) * 256
                    for k in range(16):
                        S.op("pe", (lambda e, s=s, k=k, bank=bank, c0=c0: e.matmul(
                            banks[bank][:, c0:c0 + 256], lhsT=hT[sl][:, k * 512 + s * 128:k * 512 + (s + 1) * 128],
                            rhs=wkv[:, k * 512 + 256:k * 512 + 512], start=(k == 0), stop=(k == 15))),
                            reads=[wkvB, hTB[sl]], writes=[bB[bank]])
                vo = vout[sl]
                S.op("act", lambda e: e.activation(out=vo[:, 0:512], in_=banks[5][:, :], func=AF.Copy),
                     reads=[bB[5]], writes=[voutB[sl]])
                S.op("dve", lambda e: e.tensor_copy(out=vo[:, 512:1024], in_=banks[6][:, :]),
                     reads=[bB[6]], writes=[voutB[sl]])
                for g in range(2):
                    S.dma("pool", v3(V_scr[g, :, tb * 512:(tb + 1) * 512], 4),
                          v3(vo[:, :], 4)[:, :, g * 128:(g + 1) * 128],
                          reads=[voutB[sl]], writes=[BV[g]], key=("vst", sl))
                items = []
                for g in range(2):
                    ko = kout[g]
                    items.append(dict(
                        ps=banks[1 + g], psB=bB[1 + g], bias=bk[:, g:g + 1], w=kw[:, 0:1], cos=cs[sl][:, 0:512],
                        sin=cs[sl][:, 512:1024], tabB=csB[sl], T=TTp[sl][g], out=ko[:, :], outB=koutB[g], extra=[bkB],
                        sumb=(3, 7)[g], rotb=(4, 3)[g],
                        after=(lambda g=g, ko=ko: S.dma("pool", KT_scr[g, :, t0:t0 + 512], ko[:, :], reads=[koutB[g]],
                                                        writes=[BKT[g]], key=("kst", g)))))
                chain_part1(items)
                return items

            stageA1a(0)
            stageA1b(0)
            stageA2(0)
            stageA1a(1)
            stageA1b(1)
            pending = None
            for tb in range(NB1):
                if tb + 1 < NB1:
                    stageA2(tb + 1)
                if tb + 2 < NB1:
                    stageA1a(tb + 2)
                items = stageB(tb)
                if tb + 2 < NB1:
                    stageA1b(tb + 2)
                if pending is not None:
                    chain_part2(pending)
                pending = items
            if pending is not None:
                chain_part2(pending)
            S.flush()
        if stop_after <= 1:
            return nc

        def load_wtile(dst, dstB, view, c0, ncols, nk, key):
            S.dma("pool", v3(dst[:, 0:nk * ncols], nk), view[:, :, c0:c0 + ncols], writes=[dstB], key=key)

        def mm_acc(bank_i, lhs_fn, rhs_fn, nk, reads, out_ap=None):
            for k in range(nk):
                S.op("pe", (lambda e, k=k: e.matmul(out_ap if out_ap is not None else banks[bank_i][:, :],
                                                   lhsT=lhs_fn(k), rhs=rhs_fn(k), start=(k == 0), stop=(k == nk - 1))),
                     reads=reads, writes=[bB[bank_i]])

        BqT, BqBT, BkBT, BvB, Bsg = Buf("qT"), Buf("qBT"), Buf("kBT"), Buf("vB"), Buf("sg")
        with ExitStack() as es:
            hTo = sb(es, "hTo", [128, 16 * TLOC], BF16)
            hToB = Buf("hTo")
            wts = [sb(es, "w2_%d" % i, [128, 16 * 512], BF16) for i in range(3)]
            wtB = [Buf("w2_%d" % i) for i in range(3)]
            cso = sb(es, "cso", [128, 2 * TOWN], F32)
            csoB = Buf("cso")
            TT2 = [mk_chain_tiles(es, "c2a"), mk_chain_tiles(es, "c2b")]
            TT2p = [TT2, [chain_tiles_alt(es, TT2[0], "c2c"), chain_tiles_alt(es, TT2[1], "c2d")]]
            qitems = []
            qpend = [None]
            qpair = [0]
            ot = [sb(es, "ot%d" % i, [128, 512], BF16) for i in range(4)]
            otB = [Buf("ot%d" % i) for i in range(4)]
            bc = [sb(es, "bc%d" % i, [128, 4], F32) for i in range(2)]
            bcB = [Buf("bc%d" % i) for i in range(2)]
            hToP = [Buf("hTo%d" % i) for i in range(5)]
            for tbp in range(5):
                S.dma("sp", v3(hTo[:, :], 16)[:, :, tbp * 512:(tbp + 1) * 512], v3(hT_scr[:, :], 16)[:, :, tbp * 512:(tbp + 1) * 512],
                      reads=[BhT], writes=[hToP[tbp]], key=("hTo", tbp))
            S.dma("sp", cso[:, 0:TOWN], cosT[:, 0:TOWN], writes=[csoB], key="cso")
            S.dma("sp", cso[:, TOWN:2 * TOWN], sinT[:, 0:TOWN], writes=[csoB], key="cso")
            order = [0, 1, 3, 4, 5, 6, 7, 8] + list(range(9, 17))
            mb = Ring([0, 1, 2, 5, 6])
            oti = 0
            for n_i, ti in enumerate(order):
                sl = n_i % 3
                wt, wB_ = wts[sl], wtB[sl]
                load_wtile(wt, wB_, w_in_v, ti * 512, 512, 16, ("w2", sl))
                if ti in (7, 8):
                    hv = ti - 7
                    bias_cols(wt, wB_, 4, B1bf, bvcol[:, 2 + 4 * hv:6 + 4 * hv], Bbv, 512)
                    for s_ in range(TLOC // 128):
                        bi = mb.next()
                        mm_acc(bi, lambda k, s_=s_: hTo[:, k * TLOC + s_ * 128:k * TLOC + (s_ + 1) * 128],
                               lambda k, wt=wt: wt[:, k * 512:(k + 1) * 512], 16, [hToP[s_ // 4], wB_])
                        o_, oB_ = ot[oti % 4], otB[oti % 4]
                        oti += 1
                        if s_ % 2 == 0:
                            S.op("act", (lambda e, o_=o_, bi=bi: e.activation(out=o_[:, :], in_=banks[bi][:, :], func=AF.Copy)),
                                 reads=[bB[bi]], writes=[oB_])
                        else:
                            S.op("dve", (lambda e, o_=o_, bi=bi: e.tensor_copy(out=o_[:, :], in_=banks[bi][:, :])),
                                 reads=[bB[bi]], writes=[oB_])
                        S.dma("sp", vB_scr[4 * hv:4 * hv + 4, :, s_ * 128:(s_ + 1) * 128].rearrange("h p d -> p h d"),
                              v3(o_[:, :], 4), reads=[oB_], writes=[BvB], key=("ot", oti % 4))
                    continue
                bcs, bcsB = bc[n_i % 2], bcB[n_i % 2]
                bias_cols(wt, wB_, 4, B1bf, bcs[:, 0:4], bcsB, 512)
                for cc in range(4):
                    ntb = 5 if ti in (5, 6) else 4
                    for tb in range(ntb):
                        bi = mb.next()
                        mm_acc(bi, lambda k, wt=wt, cc=cc: wt[:, k * 512 + cc * 128:k * 512 + (cc + 1) * 128],
                               lambda k, tb=tb: hTo[:, k * TLOC + tb * 512:k * TLOC + (tb + 1) * 512], 16, [hToP[tb], wB_])
                        o_, oB_ = ot[oti % 4], otB[oti % 4]
                        oti += 1
                        okey = ("ot", oti % 4)
                        if ti in (0, 1):
                            h = ti * 4 + cc
                            qitems.append(dict(
                                ps=banks[bi], psB=bB[bi], bias=bcs[:, cc:cc + 1], w=vecs_sb[:, C_QW:C_QW + 1],
                                cos=cso[:, tb * 512:(tb + 1) * 512], sin=cso[:, TOWN + tb * 512:TOWN + (tb + 1) * 512],
                                tabB=csoB, T=TT2p[qpair[0] % 2][tb % 2], out=o_[:, :], outB=oB_, extra=[bcsB],
                                sumb=(3, 7)[tb % 2], rotb=(4, 3)[tb % 2],
                                after=(lambda h=h, tb=tb, o_=o_, oB_=oB_, okey=okey: S.dma(
                                    "sp", qT_scr[h, :, tb * 512:(tb + 1) * 512], o_[:, :], reads=[oB_], writes=[BqT], key=okey))))
                            if len(qitems) == 2:
                                chain_part1(qitems)
                                if qpend[0] is not None:
                                    chain_part2(qpend[0])
                                qpend[0] = qitems
                                qitems = []
                                qpair[0] += 1
                                if ti == 1 and cc == 3 and tb == 3:
                                    chain_part2(qpend[0])
                                    qpend[0] = None
                        elif ti in (3, 4, 5, 6):
                            S.op("act", (lambda e, o_=o_, bi=bi, cc=cc, bcs=bcs: e.activation(
                                out=o_[:, :], in_=banks[bi][:, :], func=AF.Identity, bias=bcs[:, cc:cc + 1], scale=1.0)),
                                reads=[bB[bi], bcsB], writes=[oB_])
                            if ti in (3, 4):
                                h = (ti - 3) * 4 + cc
                                S.dma("sp", qBT_scr[h, :, tb * 512:(tb + 1) * 512], o_[:, :], reads=[oB_], writes=[BqBT], key=okey)
                            else:
                                h = (ti - 5) * 4 + cc
                                S.dma("sp", kBT_scr[h, :, tb * 512:(tb + 1) * 512], o_[:, :], reads=[oB_], writes=[BkBT], key=okey)
                        else:
                            gi = (ti - 9) * 4 + cc
                            S.op("act", (lambda e, o_=o_, bi=bi, cc=cc, bcs=bcs: e.activation(
                                out=o_[:, :], in_=banks[bi][:, :], func=AF.Sigmoid, bias=bcs[:, cc:cc + 1], scale=1.0)),
                                reads=[bB[bi], bcsB], writes=[oB_])
                            S.dma("sp", sg_scr[gi, :, tb * 512:(tb + 1) * 512], o_[:, :], reads=[oB_], writes=[Bsg], key=okey)
            S.flush()
        if stop_after <= 2:
            return nc

        ByT = Buf("yT")

        def attn_epilogue(Obank, Lbank, bias_ap, E, dst_ap, key):
            S.op("act", lambda e: e.activation(out=E["rec"][:, :], in_=banks[Lbank][:, :], func=AF.Ln), reads=[bB[Lbank]], writes=[E["recB"]])
            S.op("act", lambda e: e.activation(out=E["rec"][:, :], in_=E["rec"][:, :], func=AF.Exp, scale=-1.0), reads=[E["recB"]], writes=[E["recB"]])
            S.op("dve", lambda e: e.tensor_tensor(out=E["o"][:, :], in0=banks[Obank][:, :], in1=E["rec"][:, :], op=ALU.mult),
                 reads=[bB[Obank], E["recB"]], writes=[E["oB"]])
            y_, yB_ = E["y"][E["i"] % 2], E["yB"][E["i"] % 2]
            E["i"] += 1
            S.op("dve", lambda e: e.tensor_scalar(out=y_[:, :], in0=E["o"][:, :], scalar1=bias_ap, scalar2=None, op0=ALU.add),
                 reads=[E["oB"], Bbv], writes=[yB_])
            S.dma("pool", dst_ap, y_[:, :], reads=[yB_], writes=[ByT], key=(key, E["i"] % 2))

        def mk_epi(es, pfx):
            E = {"rec": sb(es, pfx + "rec", [128, 512], F32), "o": sb(es, pfx + "o", [128, 512], F32),
                 "y": [sb(es, pfx + "y%d" % i, [128, 512], BF16) for i in range(2)],
                 "recB": Buf("rec"), "oB": Buf("o"), "yB": [Buf("y0"), Buf("y1")], "i": 0}
            return E

        with ExitStack() as es:
            KTs = [sb(es, "KTs%d" % g, [128, S_ALL], BF16) for g in range(2)]
            Vs = [sb(es, "Vs%d" % g, [128, S_ALL], BF16) for g in range(2)]
            qs = [sb(es, "qs%d" % g, [128, 4 * TOWN], BF16) for g in range(2)]
            KTsB = [[Buf("KTs%d_%d" % (g_, p_)) for p_ in range(4)] for g_ in range(2)]
            VsB = [[Buf("Vs%d_%d" % (g_, p_)) for p_ in range(4)] for g_ in range(2)]
            qsB = [Buf("qs0"), Buf("qs1")]
            Pt = [sb(es, "Pt%d" % i, [128, 512], BF16) for i in range(4)]
            PtB = [Buf("Pt%d" % i) for i in range(4)]
            E = mk_epi(es, "e3")
            for g in range(2):
                S.dma("sp", qs[g][:, :].rearrange("p (h t) -> p h t", h=4), qT_scr[4 * g:4 * g + 4].rearrange("h p t -> p h t"),
                      reads=[BqT], writes=[qsB[g]], key=("qs", g))
                for part in range(4):
                    c0 = part * 4096
                    S.dma("sp", KTs[g][:, c0:c0 + 4096], KT_scr[g, :, c0:c0 + 4096], reads=[BKT[g]], writes=[KTsB[g][part]], key=("KTs", g, part))
                    S.dma("sp", Vs[g][:, c0:c0 + 4096], V_scr[g, :, c0:c0 + 4096], reads=[BV[g]], writes=[VsB[g][part]], key=("Vs", g, part))
            item = 0
            NKC = S_ALL // 128
            for g in range(2):
                for qb in range(4):
                    for hh in range(4):
                        h = 4 * g + hh
                        Ob, Lb = 4 + item % 2, 6 + item % 2
                        item += 1
                        q_ap = qs[g][:, hh * TOWN + qb * 512:hh * TOWN + (qb + 1) * 512]

                        def s_mm(kc, g=g, q_ap=q_ap):
                            bi = kc % 4
                            S.op("pe", (lambda e: e.matmul(banks[bi][:, :], lhsT=KTs[g][:, kc * 128:(kc + 1) * 128], rhs=q_ap,
                                                          start=True, stop=True)),
                                 reads=[KTsB[g][kc // 32], qsB[g]], writes=[bB[bi]])
                            S.op("act", (lambda e: e.activation(out=Pt[bi][:, :], in_=banks[bi][:, :], func=AF.Exp)),
                                 reads=[bB[bi]], writes=[PtB[bi]])

                        def pv_mm(kc, g=g, Ob=Ob, Lb=Lb):
                            bi = kc % 4
                            S.op("pe", (lambda e: e.matmul(banks[Ob][:, :], lhsT=Vs[g][:, kc * 128:(kc + 1) * 128], rhs=Pt[bi][:, :],
                                                          start=(kc == 0), stop=(kc == NKC - 1))),
                                 reads=[VsB[g][kc // 32], PtB[bi]], writes=[bB[Ob]])
                            S.op("pe", (lambda e: e.matmul(banks[Lb][:, :], lhsT=ones_bf[:, :], rhs=Pt[bi][:, :],
                                                          start=(kc == 0), stop=(kc == NKC - 1))),
                                 reads=[PtB[bi], Bconst], writes=[bB[Lb]])

                        s_mm(0)
                        s_mm(1)
                        for kc in range(NKC):
                            if kc + 2 < NKC:
                                s_mm(kc + 2)
                            pv_mm(kc)
                        attn_epilogue(Ob, Lb, bvcol[:, g:g + 1], E, yT_scr[h, :, qb * 512:(qb + 1) * 512], "y3")
            S.flush()
        if stop_after <= 3:
            return nc

        def lc_of(kc):
            return kc if kc < 16 else (kc - 18 if kc < 18 else kc - 2)

        def kc_of(lc):
            return lc if 0 <= lc < 16 else (lc + 18 if lc < 0 else lc + 2)

        with ExitStack() as es:
            kBs = [sb(es, "kBs%d" % i, [128, TLOC], BF16) for i in range(2)]
            vBs = [sb(es, "vBs%d" % i, [128, TLOC], BF16) for i in range(2)]
            qBs = [sb(es, "qBs%d" % i, [128, TOWN], BF16) for i in range(2)]
            hdB = [Buf("hd0"), Buf("hd1")]
            tab = sb(es, "tab", [128, 20 * 640], F32)
            tabB = [Buf("tab%d" % i) for i in range(20)]
            Pa = [sb(es, "Pa%d" % i, [128, 20 * 640], BF16) for i in range(2)]
            PaB = [[Buf("Pa%d_%d" % (i, j)) for j in range(20)] for i in range(2)]
            tmp = [sb(es, "natmp%d" % i, [128, 640], F32) for i in range(2)]
            tmpB = [Buf("natmp0"), Buf("natmp1")]
            E = mk_epi(es, "e4")
            scale = 1.0 / math.sqrt(128.0)
            item = 0
            for h in range(8):
                sl = h % 2
                S.dma("sp", kBs[sl][:, :], kBT_scr[h], reads=[BkBT], writes=[hdB[sl]], key=("hd", sl))
                S.dma("sp", vBs[sl][:, :], vB_scr[h], reads=[BvB], writes=[hdB[sl]], key=("hd", sl))
                S.dma("sp", qBs[sl][:, :], qBT_scr[h], reads=[BqBT], writes=[hdB[sl]], key=("hd", sl))
                for kc in range(20):
                    S.dma("sp", tab[:, kc * 640:(kc + 1) * 640], natab[h, kc], writes=[tabB[kc]], key=("tab", kc % 4))
                for kc in range(20):
                    lc = lc_of(kc)
                    blo, bhi = max(0, lc - 2), min(15, lc + 2)
                    nq = (bhi - blo + 1) * 128
                    q0 = blo * 128
                    dd = dbl[kc % 2]
                    n1 = min(nq, 512)
                    S.op("pe", (lambda e, kc=kc, dd=dd, n1=n1, q0=q0, sl=sl: e.matmul(
                        dd[:, 0:n1], lhsT=kBs[sl][:, kc * 128:(kc + 1) * 128], rhs=qBs[sl][:, q0:q0 + n1], start=True, stop=True)),
                        reads=[hdB[sl]], writes=[bB[2 * (kc % 2)]])
                    rd = [bB[2 * (kc % 2)]]
                    if nq > 512:
                        S.op("pe", (lambda e, kc=kc, dd=dd, nq=nq, q0=q0, sl=sl: e.matmul(
                            dd[:, 512:nq], lhsT=kBs[sl][:, kc * 128:(kc + 1) * 128], rhs=qBs[sl][:, q0 + 512:q0 + nq],
                            start=True, stop=True)),
                            reads=[hdB[sl]], writes=[bB[2 * (kc % 2) + 1]])
                        rd.append(bB[2 * (kc % 2) + 1])
                    tm, tmB_ = tmp[kc % 2], tmpB[kc % 2]
                    S.op("dve", (lambda e, kc=kc, dd=dd, n1=n1, tm=tm: e.scalar_tensor_tensor(
                        out=tm[:, 0:n1], in0=dd[:, 0:n1], scalar=scale, in1=tab[:, kc * 640:kc * 640 + n1],
                        op0=ALU.mult, op1=ALU.add)), reads=[rd[0], tabB[kc]], writes=[tmB_])
                    if nq > 512:
                        S.op("dve", (lambda e, kc=kc, dd=dd, nq=nq, tm=tm: e.scalar_tensor_tensor(
                            out=tm[:, 512:nq], in0=dd[:, 512:nq], scalar=scale, in1=tab[:, kc * 640 + 512:kc * 640 + nq],
                            op0=ALU.mult, op1=ALU.add)), reads=[rd[1], tabB[kc]], writes=[tmB_])
                    S.op("act", (lambda e, kc=kc, nq=nq, tm=tm, sl=sl: e.activation(
                        out=Pa[sl][:, kc * 640:kc * 640 + nq], in_=tm[:, 0:nq], func=AF.Exp)),
                        reads=[tmB_], writes=[PaB[sl][kc]])
                for qb in range(4):
                    Ob, Lb = 4 + item % 2, 6 + item % 2
                    item += 1
                    for bq in range(4):
                        b = qb * 4 + bq
                        lcs = list(range(b - 2, b + 3))
                        for i_, lc in enumerate(lcs):
                            kc = kc_of(lc)
                            j0 = (b - max(0, lc - 2)) * 128
                            for (bank_i, lhs) in ((Ob, None), (Lb, ones_bf)):
                                S.op("pe", (lambda e, kc=kc, j0=j0, bank_i=bank_i, lhs=lhs, bq=bq, i_=i_, sl=sl: e.matmul(
                                    banks[bank_i][:, bq * 128:(bq + 1) * 128],
                                    lhsT=(vBs[sl][:, kc * 128:(kc + 1) * 128] if lhs is None else lhs[:, :]),
                                    rhs=Pa[sl][:, kc * 640 + j0:kc * 640 + j0 + 128], start=(i_ == 0), stop=(i_ == 4))),
                                    reads=[hdB[sl], PaB[sl][kc], Bconst], writes=[bB[bank_i]])
                    attn_epilogue(Ob, Lb, bvcol[:, 2 + h:3 + h], E, yT_scr[8 + h, :, qb * 512:(qb + 1) * 512], "y4")
            S.flush()
        if stop_after <= 4:
            return nc

        m_scr = dscr("m_scr", [16, 128, TOWN])
        Bm = Buf("m")
        Bx1, Bh2 = Buf("x1"), Buf("h2")
        with ExitStack() as es:
            woa = sb(es, "woa", [128, 8 * D], BF16)
            wob = sb(es, "wob", [128, 8 * D], BF16)
            woP = [Buf("woab%d" % i) for i in range(4)]
            yab = [sb(es, "yab%d" % i, [128, 16 * 512], BF16) for i in range(2)]
            yabB = [Buf("yab0"), Buf("yab1")]
            sgt = [sb(es, "sgt%d" % i, [128, 1024], BF16) for i in range(3)]
            sgtB = [Buf("sgt%d" % i) for i in range(3)]
            t1 = [sb(es, "m_t1%d" % i, [128, 512], F32) for i in range(2)]
            t2 = [sb(es, "m_t2%d" % i, [128, 512], F32) for i in range(2)]
            t1B = [Buf("t1a"), Buf("t1b")]
            t2B = [Buf("t2a"), Buf("t2b")]
            mo = [sb(es, "mo%d" % i, [128, 512], BF16) for i in range(3)]
            moB = [Buf("mo%d" % i) for i in range(3)]
            for part in range(4):
                S.dma("pool", v3(woa[:, :], 8)[:, :, part * 512:(part + 1) * 512],
                      w_oa.rearrange("(k p) n -> p k n", p=128)[:, :, part * 512:(part + 1) * 512], writes=[woP[part]], key=("woab", part))
                S.dma("pool", v3(wob[:, :], 8)[:, :, part * 512:(part + 1) * 512],
                      w_ob.rearrange("(k p) n -> p k n", p=128)[:, :, part * 512:(part + 1) * 512], writes=[woP[part]], key=("woab", part))
            it = 0
            for tb in range(4):
                sl = tb % 2
                S.dma("sp", v3(yab[sl][:, :], 16), yT_scr.rearrange("k p t -> p k t")[:, :, tb * 512:(tb + 1) * 512],
                      reads=[ByT], writes=[yabB[sl]], key=("yab", sl))
                for n in range(16):
                    s3 = it % 3
                    S.dma("sp", sgt[s3][:, 0:512], sg_scr[n, :, tb * 512:(tb + 1) * 512], reads=[Bsg], writes=[sgtB[s3]], key=("sgt", s3))
                    S.dma("sp", sgt[s3][:, 512:1024], sg_scr[16 + n, :, tb * 512:(tb + 1) * 512], reads=[Bsg], writes=[sgtB[s3]], key=("sgt", s3))
                    ba, bb_ = (0, 1) if it % 2 == 0 else (2, 5)
                    mm_acc(ba, lambda k, n=n: woa[:, k * D + n * 128:k * D + (n + 1) * 128],
                           lambda k, sl=sl: yab[sl][:, k * 512:(k + 1) * 512], 8, [woP[n // 4], yabB[sl]])
                    mm_acc(bb_, lambda k, n=n: wob[:, k * D + n * 128:k * D + (n + 1) * 128],
                           lambda k, sl=sl: yab[sl][:, (8 + k) * 512:(9 + k) * 512], 8, [woP[n // 4], yabB[sl]])
                    i2 = it % 2
                    S.op("dve", (lambda e, i2=i2, ba=ba, s3=s3: e.tensor_tensor(out=t1[i2][:, :], in0=banks[ba][:, :], in1=sgt[s3][:, 0:512], op=ALU.mult)),
                         reads=[bB[ba], sgtB[s3]], writes=[t1B[i2]])
                    S.op("dve", (lambda e, i2=i2, bb_=bb_, s3=s3: e.tensor_tensor(out=t2[i2][:, :], in0=banks[bb_][:, :], in1=sgt[s3][:, 512:1024], op=ALU.mult)),
                         reads=[bB[bb_], sgtB[s3]], writes=[t2B[i2]])
                    S.op("pool", (lambda e, i2=i2, s3=s3: e.tensor_tensor(out=mo[s3][:, :], in0=t1[i2][:, :], in1=t2[i2][:, :], op=ALU.add)),
                         reads=[t1B[i2], t2B[i2]], writes=[moB[s3]])
                    S.dma("pool", m_scr[n, :, tb * 512:(tb + 1) * 512], mo[s3][:, :], reads=[moB[s3]], writes=[Bm], key=("mo", s3))
                    it += 1
            S.flush()

        def stats_to_rstd(bank_i, rs_tile, rsB):
            S.op("act", lambda e: e.activation(out=rs_tile[:, :], in_=banks[bank_i][:, :], func=AF.Ln, bias=epsA[:, 0:1], scale=1.0),
                 reads=[bB[bank_i], Bconst], writes=[rsB])
            S.op("act", lambda e: e.activation(out=rs_tile[:, :], in_=rs_tile[:, :], func=AF.Exp, scale=-0.5), reads=[rsB], writes=[rsB])

        with ExitStack() as es:
            wout = sb(es, "wout", [128, 16 * D], BF16)
            woutP = [Buf("wout%d" % i) for i in range(4)]
            x1cB = [Buf("x1c%d" % i) for i in range(16)]
            mb_ = [sb(es, "mblk%d" % i, [128, 16 * 512], BF16) for i in range(2)]
            mbB = [Buf("mblk0"), Buf("mblk1")]
            xin = [sb(es, "xin%d" % i, [128, 512], F32) for i in range(3)]
            xinB = [Buf("xin%d" % i) for i in range(3)]
            x1b = sb(es, "x1b", [128, 16 * 512], F32)
            x1bB = Buf("x1b")
            sq5 = [sb(es, "sq5_%d" % i, [128, 512], BF16) for i in range(2)]
            sq5B = [Buf("sq5a"), Buf("sq5b")]
            rs5 = sb(es, "rs5", [128, 512], F32)
            rs5B = Buf("rs5")
            h2o = sb(es, "h2o", [128, 16 * 512], BF16)
            h2oB = Buf("h2o")
            for part in range(4):
                S.dma("pool", v3(wout[:, :], 16)[:, :, part * 512:(part + 1) * 512],
                      w_out.rearrange("(k p) n -> p k n", p=128)[:, :, part * 512:(part + 1) * 512], writes=[woutP[part]], key=("wout", part))
            it = 0
            mr = Ring([0, 1, 2, 3])
            pend5 = None

            def stat5(n, q2):
                S.op("pe", (lambda e: e.matmul(banks[7][:, :], lhsT=ones_bf[:, :], rhs=sq5[q2][:, :], start=(n == 0), stop=(n == 15))),
                     reads=[sq5B[q2], Bconst], writes=[bB[7]])

            for tb in range(4):
                sl = tb % 2
                S.dma("sp", v3(mb_[sl][:, :], 16), m_scr.rearrange("k p t -> p k t")[:, :, tb * 512:(tb + 1) * 512],
                      reads=[Bm], writes=[mbB[sl]], key=("mblk", sl))
                for n in range(16):
                    s3 = it % 3
                    it += 1
                    S.dma("sp", xin[s3][:, :], xT_v[:, n, tb * 512:(tb + 1) * 512], writes=[xinB[s3]], key=("xin", s3))
                    bi = mr.next()
                    mm_acc(bi, lambda k, n=n: wout[:, k * D + n * 128:k * D + (n + 1) * 128],
                           lambda k, sl=sl: mb_[sl][:, k * 512:(k + 1) * 512], 16, [woutP[n // 4], mbB[sl]])
                    S.op("dve", (lambda e, bi=bi, n=n, s3=s3: e.scalar_tensor_tensor(
                        out=x1b[:, n * 512:(n + 1) * 512], in0=banks[bi][:, :], scalar=mod_sb[:, 32 + n:33 + n], in1=xin[s3][:, :],
                        op0=ALU.mult, op1=ALU.add)), reads=[bB[bi], xinB[s3], Bmod], writes=[x1cB[n]])
                    q2 = n % 2
                    S.op("act", (lambda e, n=n, q2=q2: e.activation(out=sq5[q2][:, :], in_=x1b[:, n * 512:(n + 1) * 512], func=AF.Square)),
                         reads=[x1cB[n]], writes=[sq5B[q2]])
                    if pend5 is not None:
                        stat5(*pend5)
                    pend5 = (n, q2)
                stat5(*pend5)
                pend5 = None
                stats_to_rstd(7, rs5, rs5B)
                for k in range(16):
                    S.op("dve", (lambda e, k=k: e.scalar_tensor_tensor(
                        out=h2o[:, k * 512:(k + 1) * 512], in0=x1b[:, k * 512:(k + 1) * 512], scalar=A2[:, k:k + 1], in1=rs5[:, :],
                        op0=ALU.mult, op1=ALU.mult)), reads=[x1cB[k], rs5B, Bmod], writes=[h2oB])
                S.dma("pool", v3(h2T_scr[:, :], 16)[:, :, tb * 512:(tb + 1) * 512], v3(h2o[:, :], 16), reads=[h2oB], writes=[Bh2], key="h2st")
                for j4 in range(4):
                    S.dma("pool", x1T_scr.rearrange("k p t -> p k t")[:, 4 * j4:4 * j4 + 4, tb * 512:(tb + 1) * 512],
                          v3(x1b[:, :], 16)[:, 4 * j4:4 * j4 + 4, :],
                          reads=x1cB[4 * j4:4 * j4 + 4], writes=[Bx1], key=("x1st", j4))
            S.flush()
        if stop_after <= 5:
            return nc

        Bact = Buf("act")
        w_gate_v = w_gate.rearrange("(k p) n -> p k n", p=128)
        w_up_v = w_up.rearrange("(k p) n -> p k n", p=128)
        with ExitStack() as es:
            h2 = sb(es, "h2", [128, 16 * TOWN], BF16)
            h2B = Buf("h2")
            wg = [sb(es, "wg%d" % i, [128, 16 * 512], BF16) for i in range(2)]
            wu = [sb(es, "wu%d" % i, [128, 16 * 512], BF16) for i in range(2)]
            wgB = [Buf("wg0"), Buf("wg1")]
            wuB = [Buf("wu0"), Buf("wu1")]
            bg = [sb(es, "bg%d" % i, [128, 4], F32) for i in range(2)]
            bu = [sb(es, "bu%d" % i, [128, 4], F32) for i in range(2)]
            bgB = [Buf("bg0"), Buf("bg1")]
            buB = [Buf("bu0"), Buf("bu1")]
            sgl = [sb(es, "sgl%d" % i, [128, 512], F32) for i in range(2)]
            sglB = [Buf("sgl0"), Buf("sgl1")]
            ao = [sb(es, "ao%d" % i, [128, 512], BF16) for i in range(3)]
            aoB = [Buf("ao%d" % i) for i in range(3)]
            h2P = [Buf("h2_%d" % i) for i in range(4)]
            for tbp in range(4):
                S.dma("sp", v3(h2[:, :], 16)[:, :, tbp * 512:(tbp + 1) * 512], v3(h2T_scr[:, :], 16)[:, :, tbp * 512:(tbp + 1) * 512],
                      reads=[Bh2], writes=[h2P[tbp]], key=("h2ld", tbp))
            it = 0
            for ti in range(11):
                sl = ti % 2
                load_wtile(wg[sl], wgB[sl], w_gate_v, ti * 512, 512, 16, ("wg", sl))
                load_wtile(wu[sl], wuB[sl], w_up_v, ti * 512, 512, 16, ("wu", sl))
                bias_cols(wg[sl], wgB[sl], 4, B2bf, bg[sl][:, 0:4], bgB[sl], 512)
                bias_cols(wu[sl], wuB[sl], 4, B2bf, bu[sl][:, 0:4], buB[sl], 512)
                for cc in range(4):
                    for tb in range(4):
                        bG, bU = ((0, 1), (2, 3), (4, 5))[it % 3]
                        i2, i3 = it % 2, it % 3
                        it += 1
                        mm_acc(bG, lambda k, sl=sl, cc=cc: wg[sl][:, k * 512 + cc * 128:k * 512 + (cc + 1) * 128],
                               lambda k, tb=tb: h2[:, k * TOWN + tb * 512:k * TOWN + (tb + 1) * 512], 16, [wgB[sl], h2P[tb]])
                        mm_acc(bU, lambda k, sl=sl, cc=cc: wu[sl][:, k * 512 + cc * 128:k * 512 + (cc + 1) * 128],
                               lambda k, tb=tb: h2[:, k * TOWN + tb * 512:k * TOWN + (tb + 1) * 512], 16, [wuB[sl], h2P[tb]])
                        S.op("act", (lambda e, bG=bG, i2=i2, sl=sl, cc=cc: e.activation(
                            out=sgl[i2][:, :], in_=banks[bG][:, :], func=AF.Silu, bias=bg[sl][:, cc:cc + 1], scale=1.0)),
                            reads=[bB[bG], bgB[sl]], writes=[sglB[i2]])
                        S.op("dve", (lambda e, bU=bU, i2=i2, i3=i3, sl=sl, cc=cc: e.scalar_tensor_tensor(
                            out=ao[i3][:, :], in0=banks[bU][:, :], scalar=bu[sl][:, cc:cc + 1], in1=sgl[i2][:, :],
                            op0=ALU.add, op1=ALU.mult)), reads=[bB[bU], buB[sl], sglB[i2]], writes=[aoB[i3]])
                        S.dma("sp", act_scr[ti * 4 + cc, :, tb * 512:(tb + 1) * 512], ao[i3][:, :], reads=[aoB[i3]], writes=[Bact],
                              key=("ao", i3))
            S.flush()
        if stop_after <= 6:
            return nc

        Bx2h = [Buf("x2a"), Buf("x2b")]
        w_down_v = w_down.rearrange("(k p) n -> p k n", p=128)
        with ExitStack() as es:
            acth = sb(es, "acth", [128, 44 * 1024], BF16)
            acthB = Buf("acth")
            wd = [sb(es, "wd%d" % i, [128, 44 * 256], BF16) for i in range(2)]
            wdB = [Buf("wd0"), Buf("wd1")]
            x1i = [sb(es, "x1i%d" % i, [128, 512], F32) for i in range(3)]
            x1iB = [Buf("x1i%d" % i) for i in range(3)]
            x2o = [sb(es, "x2o%d" % i, [128, 512], F32) for i in range(3)]
            x2oB = [Buf("x2o%d" % i) for i in range(3)]
            sq7 = [sb(es, "sq7_%d" % i, [128, 512], BF16) for i in range(2)]
            sq7B = [Buf("sq7a"), Buf("sq7b")]
            rsh = [[sb(es, "rsh%d_%d" % (h_, i), [128, 512], F32) for i in range(2)] for h_ in range(2)]
            rshB = [[Buf("rsh%d_%d" % (h_, i)) for i in range(2)] for h_ in range(2)]
            x2i = [sb(es, "x2i%d" % i, [128, 512], F32) for i in range(4)]
            x2iB = [Buf("x2i%d" % i) for i in range(4)]
            Fw = sb(es, "Fw", [128, 16], F32)
            FwB = Buf("Fw")
            yo = [sb(es, "yo%d" % i, [128, 512], F32) for i in range(4)]
            yoB = [Buf("yo%d" % i) for i in range(4)]
            Bout = Buf("out")
            S.op("pool", lambda e: e.tensor_scalar(out=Fw[:, :], in0=vecs_sb[:, C_FW:C_FW + 16], scalar1=math.sqrt(D), scalar2=None, op0=ALU.mult),
                 reads=[Bvecs], writes=[FwB])
            it = 0
            wi = 0
            mr = Ring([0, 1, 2, 3, 4, 5])
            pend7 = None

            def stat7(q2, tb2, n):
                S.op("pe", (lambda e: e.matmul(banks[6 + tb2][:, :], lhsT=ones_bf[:, :], rhs=sq7[q2][:, :],
                                              start=(n == 0), stop=(n == 15))),
                     reads=[sq7B[q2], Bconst], writes=[bB[6 + tb2]])

            def load_act(half):
                h0_ = half * 1024
                for part in range(4):
                    S.dma("sp", v3(acth[:, :], 44)[:, part * 11:(part + 1) * 11, :],
                          act_scr.rearrange("k p t -> p k t")[:, part * 11:(part + 1) * 11, h0_:h0_ + 1024],
                          reads=[Bact], writes=[acthB], key="acth")

            fctr = [0]

            def final_pass(half):
                h0_ = half * 1024
                for tb2 in range(2):
                    stats_to_rstd(6 + tb2, rsh[half][tb2], rshB[half][tb2])
                yield
                for tb2 in range(2):
                    c0 = h0_ + tb2 * 512
                    for n in range(16):
                        s3 = fctr[0] % 4
                        fctr[0] += 1
                        S.dma("sp", x2i[s3][:, :], x2T_scr[n, :, c0:c0 + 512], reads=[Bx2h[half]], writes=[x2iB[s3]], key=("x2i", s3))
                        S.op("dve", (lambda e, s3=s3, n=n, tb2=tb2: e.scalar_tensor_tensor(
                            out=yo[s3][:, :], in0=x2i[s3][:, :], scalar=Fw[:, n:n + 1], in1=rsh[half][tb2][:, :],
                            op0=ALU.mult, op1=ALU.mult)), reads=[x2iB[s3], FwB, rshB[half][tb2]], writes=[yoB[s3]])
                        S.dma("act", outT[n, :, c0:c0 + 512], yo[s3][:, :], reads=[yoB[s3]], writes=[Bout], key=("yo", s3))
                        yield

            load_act(0)
            fp_gen = None
            for half in range(2):
                h0 = half * 1024
                for nt in range(8):
                    sl = wi % 2
                    wi += 1
                    for part in range(4):
                        S.dma("pool", v3(wd[sl][:, :], 44)[:, part * 11:(part + 1) * 11, :],
                              w_down_v[:, part * 11:(part + 1) * 11, nt * 256:(nt + 1) * 256], writes=[wdB[sl]], key=("wd", sl))
                    for nn in range(2):
                        n = nt * 2 + nn
                        for tb2 in range(2):
                            s3 = it % 3
                            q2 = it % 2
                            it += 1
                            c0 = h0 + tb2 * 512
                            S.dma("sp", x1i[s3][:, :], x1T_scr[n, :, c0:c0 + 512], reads=[Bx1], writes=[x1iB[s3]], key=("x1i", s3))
                            bi = mr.next()
                            mm_acc(bi, lambda k, sl=sl, nn=nn: wd[sl][:, k * 256 + nn * 128:k * 256 + (nn + 1) * 128],
                                   lambda k, tb2=tb2: acth[:, k * 1024 + tb2 * 512:k * 1024 + (tb2 + 1) * 512], 44, [wdB[sl], acthB])
                            S.op("dve", (lambda e, bi=bi, n=n, s3=s3: e.scalar_tensor_tensor(
                                out=x2o[s3][:, :], in0=banks[bi][:, :], scalar=mod_sb[:, 80 + n:81 + n], in1=x1i[s3][:, :],
                                op0=ALU.mult, op1=ALU.add)), reads=[bB[bi], x1iB[s3], Bmod], writes=[x2oB[s3]])
                            S.op("act", (lambda e, s3=s3, q2=q2: e.activation(out=sq7[q2][:, :], in_=x2o[s3][:, :], func=AF.Square)),
                                 reads=[x2oB[s3]], writes=[sq7B[q2]])
                            if pend7 is not None:
                                stat7(*pend7)
                            pend7 = (q2, tb2, n)
                            S.dma("act", x2T_scr[n, :, c0:c0 + 512], x2o[s3][:, :], reads=[x2oB[s3]], writes=[Bx2h[half]], key=("x2o", s3))
                            if fp_gen is not None:
                                next(fp_gen, None)
                stat7(*pend7)
                pend7 = None
                if half == 0:
                    load_act(1)
                    fp_gen = final_pass(0)
                    next(fp_gen)
                else:
                    for _ in fp_gen:
                        pass
                    for _ in final_pass(1):
                        pass
            S.flush()
        return nc


def _pack_cols(v):
    v = np.asarray(v, np.float32).reshape(-1)
    return np.ascontiguousarray(v.reshape(-1, 128).T)


def _token_order(c):
    own = np.arange(TOWN * c, TOWN * (c + 1))
    others = np.concatenate([np.arange(0, TOWN * c), np.arange(TOWN * (c + 1), S_ALL)])
    r0 = 32 * c
    rows_before = np.arange(r0 - 4, r0) if c > 0 else np.arange(4, 8)
    rows_after = np.arange(r0 + 32, r0 + 36) if c < NCORE - 1 else np.arange(248, 252)
    halo_rows = np.concatenate([rows_before, rows_after])
    halo = (halo_rows[:, None] * GRID_W + np.arange(GRID_W)[None, :]).reshape(-1)
    return own, others, halo, halo_rows


def _rope_tables():
    t = np.arange(S_ALL)
    row = (t // GRID_W).astype(np.float32)
    col = (t % GRID_W).astype(np.float32)
    inv = (np.float32(10000.0) ** (-(np.arange(0, 64, 2, dtype=np.float32) / np.float32(64.0)))).astype(np.float32)
    ang = np.concatenate([row[:, None] * inv[None], col[:, None] * inv[None]], axis=-1).astype(np.float32)
    cos = np.cos(ang).astype(np.float32)
    sin = np.sin(ang).astype(np.float32)
    cosT = np.repeat(cos.T, 2, axis=0)
    sinT = np.repeat(sin.T, 2, axis=0)
    sinT[0::2] *= -1.0
    return np.ascontiguousarray(cosT), np.ascontiguousarray(sinT)


def _na_table(c, rpb, halo_rows):
    tab = np.full((8, 20, 128, 640), MASK_NEG, np.float32)
    rows_tot = S_ALL // GRID_W
    for kc in range(20):
        if kc < 16:
            lc = kc
            grow = 32 * c + 2 * kc + np.arange(2)
        elif kc < 18:
            lc = kc - 18
            grow = halo_rows[(kc - 16) * 2:(kc - 16) * 2 + 2]
        else:
            lc = kc - 2
            grow = halo_rows[4 + (kc - 18) * 2:4 + (kc - 18) * 2 + 2]
        key_r = np.repeat(grow, GRID_W)
        key_c = np.tile(np.arange(GRID_W), 2)
        blo = max(0, lc - 2)
        bhi = min(15, lc + 2)
        for b in range(blo, bhi + 1):
            qr = np.repeat(32 * c + 2 * b + np.arange(2), GRID_W)
            qc = np.tile(np.arange(GRID_W), 2)
            rs = np.clip(qr - NA_ROWS // 2, 0, rows_tot - NA_ROWS)
            cs_ = np.clip(qc - NA_COLS // 2, 0, GRID_W - NA_COLS)
            inr = (key_r[:, None] >= rs[None, :]) & (key_r[:, None] < rs[None, :] + NA_ROWS)
            inc = (key_c[:, None] >= cs_[None, :]) & (key_c[:, None] < cs_[None, :] + NA_COLS)
            valid = inr & inc
            if kc >= 16:
                own_lo = 32 * c + 2 * (b - 2)
                own_hi = 32 * c + 2 * (b + 2) + 1
                lo = max(own_lo, 32 * c)
                hi = min(own_hi, 32 * c + 31)
                dup = (key_r >= lo) & (key_r <= hi)
                valid &= ~dup[:, None]
            rel_r = np.clip(key_r[:, None] - qr[None, :] + (NA_ROWS - 1), 0, 2 * NA_ROWS - 2)
            rel_c = np.clip(key_c[:, None] - qc[None, :] + (NA_COLS - 1), 0, 2 * NA_COLS - 2)
            j0 = (b - blo) * 128
            for h in range(8):
                bias = rpb[h][rel_r, rel_c]
                tab[h, kc, :, j0:j0 + 128] = np.where(valid, bias, np.float32(MASK_NEG))
    return tab


def prep_inputs(inp):
    x = np.asarray(inp["x"], np.float32)[0]
    xTfull = np.ascontiguousarray(x.T)
    cosT, sinT = _rope_tables()
    rm = np.zeros((128, 128), np.float32)
    for k in range(128):
        rm[k, k ^ 1] = 1.0
    vec = np.zeros((128, NV), np.float32)
    vec[:, 0:16] = _pack_cols(inp["c"])
    vec[:, 16:112] = _pack_cols(inp["b_ada"])
    vec[:, 112:128] = _pack_cols(inp["norm1_w"])
    vec[:, 128:144] = _pack_cols(inp["norm2_w"])
    vec[:, 144:160] = _pack_cols(inp["final_w"])
    vec[:, 160:161] = _pack_cols(inp["q_norm_w"])
    vec[:, 161:162] = _pack_cols(inp["k_norm_w"])
    rpb = np.asarray(inp["nat_rpb"], np.float32)[0]
    shared = {
        "vecs": vec, "rmat": rm,
        "w_ada": np.ascontiguousarray(np.asarray(inp["w_ada"], np.float32)[0]),
        "w_in": np.ascontiguousarray(np.asarray(inp["w_in"], np.float32)[0]),
        "w_oa": np.ascontiguousarray(np.asarray(inp["w_oa"], np.float32)[0]),
        "w_ob": np.ascontiguousarray(np.asarray(inp["w_ob"], np.float32)[0]),
        "w_out": np.ascontiguousarray(np.asarray(inp["w_out"], np.float32)[0]),
        "w_gate": np.ascontiguousarray(np.asarray(inp["w_ffn_gate"], np.float32)[0]),
        "w_up": np.ascontiguousarray(np.asarray(inp["w_ffn_up"], np.float32)[0]),
        "w_down": np.ascontiguousarray(np.asarray(inp["w_ffn_down"], np.float32)[0]),
    }
    maps = []
    for c in range(NCORE):
        own, others, halo, halo_rows = _token_order(c)
        order = np.concatenate([own, others, halo])
        m = dict(shared)
        m["xT"] = np.ascontiguousarray(xTfull[:, order])
        o2 = order[:S_ALL]
        m["cosT"] = np.ascontiguousarray(cosT[:, o2])
        m["sinT"] = np.ascontiguousarray(sinT[:, o2])
        m["natab"] = _na_table(c, rpb, halo_rows)
        maps.append(m)
    return maps


_NC_CACHE = {}


def kernel(**inputs):
    maps = prep_inputs(inputs)
    if "nc" not in _NC_CACHE:
        _NC_CACHE["nc"] = build_program()
    nc = _NC_CACHE["nc"]
    res = run_bass_kernel_spmd(nc, maps, core_ids=list(range(NCORE)))
    outs = []
    for c in range(NCORE):
        o = np.asarray(res.results[c]["outT"], np.float32).reshape(D, TOWN)
        outs.append(o.T)
    return np.ascontiguousarray(np.concatenate(outs, axis=0)[None].astype(np.float32))
```

```python
import math
from contextlib import ExitStack

import numpy as np
import concourse.bass as bass
import concourse.mybir as mybir
from concourse.bass_utils import run_bass_kernel_spmd

F32 = mybir.dt.float32
BF16 = mybir.dt.bfloat16
AF = mybir.ActivationFunctionType
ALU = mybir.AluOpType

D = 2048
S_ALL = 16384
NCORE = 8
TOWN = 2048
THALO = 512
TLOC = TOWN + THALO
NTOK_IN = S_ALL + THALO
D_IN = 8704
D_FF = 5632
EPS = 1e-6
NV = 162
GRID_W = 64
NA_ROWS = 8
NA_COLS = 16
MASK_NEG = -30000.0


class Buf:
    __slots__ = ("name", "w", "rs", "rd")

    def __init__(self, name):
        self.name = name
        self.w = None
        self.rs = {}
        self.rd = []


class Op:
    __slots__ = ("eng", "fn", "deps", "signaled", "token", "key", "idx", "epoch")

    def __init__(self, eng, fn, key, idx, epoch):
        self.eng = eng
        self.fn = fn
        self.key = key
        self.deps = set()
        self.signaled = key is not None
        self.token = None
        self.idx = idx
        self.epoch = epoch


ENGS = ("pe", "act", "dve", "pool", "sp")


class Sched:
    def __init__(self, nc, es):
        self.nc = nc
        self.es = es
        self.sem = {e: es.enter_context(nc.semaphore("s_" + e)) for e in ENGS}
        self.cnt = {e: 0 for e in ENGS}
        self.keysem = {}
        self.keyidx = {}
        self.sempool = []
        self.poolcnt = []
        self.waited = {e: {} for e in ENGS}
        self.epoch = 0
        self.ops = {e: [] for e in ENGS}
        self.nidx = {e: 0 for e in ENGS}
        self.allops = []

    def _mk(self, eng, fn, reads, writes, key):
        op = Op(eng, fn, key, self.nidx[eng], self.epoch)
        self.nidx[eng] += 1
        deps = op.deps
        for b in reads:
            if b.w is not None:
                deps.add(b.w)
        for b in writes:
            if b.w is not None:
                deps.add(b.w)
            deps.update(b.rs.values())
            deps.update(b.rd)
        for d in list(deps):
            if d.key is None and key is None and d.eng == eng and (eng == "pe" or op.idx - d.idx > 2):
                deps.discard(d)
        for d in deps:
            d.signaled = True
        for b in reads:
            if key is not None:
                b.rd.append(op)
            else:
                b.rs[eng] = op
        for b in writes:
            b.w = op
            b.rs = {}
            b.rd = []
        self.ops[eng].append(op)
        self.allops.append(op)
        return op

    def op(self, eng, fn, reads=(), writes=()):
        return self._mk(eng, fn, reads, writes, None)

    def dma(self, eng, out, in_, reads=(), writes=(), key=None):
        assert key is not None
        if key not in self.keysem:
            i = len(self.keysem)
            if i >= len(self.sempool):
                self.sempool.append(self.es.enter_context(self.nc.semaphore("k%d" % i)))
                self.poolcnt.append(0)
            self.keysem[key] = self.sempool[i]
            self.keyidx[key] = i
        return self._mk(eng, lambda e: e.dma_start(out=out, in_=in_), reads, writes, key)

    def flush(self, final=False):
        nc = self.nc
        last = {}
        for e in ENGS:
            for op in reversed(self.ops[e]):
                if op.key is None and op.fn is not None:
                    op.signaled = True
                    last[e] = op
                    break
        for op in self.allops:
            if op.key is not None:
                i = self.keyidx[op.key]
                self.poolcnt[i] += 16
                op.token = (self.sempool[i], self.poolcnt[i])
            elif op.signaled:
                self.cnt[op.eng] += 1
                op.token = (self.sem[op.eng], self.cnt[op.eng])
        bar = [(self.sem[e], self.cnt[e]) for e in ENGS if self.cnt[e] > 0]
        bar += [(self.sempool[i], self.poolcnt[i]) for i in range(len(self.sempool)) if self.poolcnt[i] > 0]
        ops = self.ops
        waited = self.waited
        sems = self.sem
        epoch = self.epoch

        def emit(ename, e):
            w = waited[ename]
            for op in ops[ename]:
                for d in op.deps:
                    if d.epoch != epoch:
                        continue
                    if d.key is None and d.eng == ename:
                        if ename == "pe" or op.idx - d.idx > 2:
                            continue
                    s, v = d.token
                    if w.get(id(s), 0) < v:
                        e.wait_ge(s, v)
                        w[id(s)] = v
                if op.fn is None:
                    continue
                ins = op.fn(e)
                if op.key is not None:
                    ins.then_inc(op.token[0], 16)
                elif op.signaled:
                    ins.then_inc(sems[ename], 1)
            for s, v in bar:
                k = id(s)
                if w.get(k, 0) < v:
                    e.wait_ge(s, v)
                    w[k] = v

        with nc.Block() as block:
            @block.tensor
            def _(e):
                emit("pe", e)

            @block.scalar
            def _(e):
                emit("act", e)

            @block.vector
            def _(e):
                emit("dve", e)

            @block.gpsimd
            def _(e):
                emit("pool", e)

            @block.sync
            def _(e):
                emit("sp", e)

        self.ops = {e: [] for e in ENGS}
        self.allops = []
        self.epoch += 1
        self.keysem = {}
        self.keyidx = {}


class Ring:
    def __init__(self, items):
        self.items = items
        self.i = 0

    def next(self):
        it = self.items[self.i % len(self.items)]
        self.i += 1
        return it


def v3(ap, k):
    return ap.rearrange("p (k t) -> p k t", k=k)


def build_program(debug_outs=(), stop_after=99):
    nc = bass.Bass("TRN2", target_bir_lowering=False)

    def din(name, shape, dt=F32):
        return nc.dram_tensor(name, list(shape), dt, kind="ExternalInput").ap()

    def dscr(name, shape, dt=BF16):
        kind = "ExternalOutput" if name in debug_outs else "Internal"
        return nc.dram_tensor(name, list(shape), dt, kind=kind).ap()

    xT = din("xT", [D, NTOK_IN])
    cosT = din("cosT", [128, S_ALL])
    sinT = din("sinT", [128, S_ALL])
    vecs = din("vecs", [128, NV])
    rmat = din("rmat", [128, 128])
    natab = din("natab", [8, 20, 128, 640])
    w_ada = din("w_ada", [D, 6 * D])
    w_in = din("w_in", [D, D_IN])
    w_oa = din("w_oa", [1024, D])
    w_ob = din("w_ob", [1024, D])
    w_out = din("w_out", [D, D])
    w_gate = din("w_gate", [D, D_FF])
    w_up = din("w_up", [D, D_FF])
    w_down = din("w_down", [D_FF, D])
    outT = nc.dram_tensor("outT", [16, 128, TOWN], F32, kind="ExternalOutput").ap()

    hT_scr = dscr("hT_scr", [128, 16 * TLOC])
    KT_scr = dscr("KT_scr", [2, 128, S_ALL])
    V_scr = dscr("V_scr", [2, 128, 128 * 128])
    qT_scr = dscr("qT_scr", [8, 128, TOWN])
    qBT_scr = dscr("qBT_scr", [8, 128, TOWN])
    kBT_scr = dscr("kBT_scr", [8, 128, TLOC])
    vB_scr = dscr("vB_scr", [8, 128, 20 * 128])
    sg_scr = dscr("sg_scr", [32, 128, TOWN])
    bv_scr = dscr("bv_scr", [128, 16], F32)
    yT_scr = dscr("yT_scr", [16, 128, TOWN])
    x1T_scr = dscr("x1T_scr", [16, 128, TOWN], F32)
    h2T_scr = dscr("h2T_scr", [128, 16 * TOWN])
    act_scr = dscr("act_scr", [44, 128, TOWN])
    x2T_scr = dscr("x2T_scr", [16, 128, TOWN], F32)

    w_ada_v = w_ada.rearrange("(k p) n -> p k n", p=128)
    w_in_v = w_in.rearrange("(k p) n -> p k n", p=128)
    xT_v = xT.rearrange("(k p) t -> p k t", p=128)

    with ExitStack() as ges:
        S = Sched(nc, ges)

        def sb(es, name, shape, dt):
            return es.enter_context(nc.sbuf_tensor(name, list(shape), dt))

        vecs_sb = sb(ges, "vecs_sb", [128, NV], F32)
        mod_sb = sb(ges, "mod_sb", [128, 96], F32)
        A1 = sb(ges, "A1", [128, 16], F32)
        A2 = sb(ges, "A2", [128, 16], F32)
        B1bf = sb(ges, "B1bf", [128, 16], BF16)
        B2bf = sb(ges, "B2bf", [128, 16], BF16)
        ones_bf = sb(ges, "ones_bf", [128, 128], BF16)
        rmat_bf = sb(ges, "rmat_bf", [128, 128], BF16)
        kw = sb(ges, "kw", [128, 1], F32)
        epsA = sb(ges, "epsA", [128, 1], F32)
        epsB = sb(ges, "epsB", [128, 1], F32)
        bvcol = sb(ges, "bvcol", [128, 16], F32)
        csil = sb(ges, "csil", [128, 16], BF16)
        dbl = [ges.enter_context(nc.psum_tensor("dbank%d" % i, [128, 1024], F32)) for i in range(4)]
        banks = [dbl[i // 2][:, (i % 2) * 512:(i % 2) * 512 + 512] for i in range(8)]
        bB = [Buf("bank%d" % i) for i in range(8)]
        Bvecs, Bmod, Bconst, Bcsil, Bbv = Buf("vecs"), Buf("mod"), Buf("const"), Buf("csil"), Buf("bv")

        C_C, C_BADA, C_N1, C_N2, C_FW, C_QW, C_KW = 0, 16, 112, 128, 144, 160, 161

        with ExitStack() as es:
            wts = [sb(es, "wada%d" % i, [128, 16 * 512], BF16) for i in range(3)]
            wB = [Buf("wada%d" % i) for i in range(3)]
            S.dma("sp", vecs_sb[:, :], vecs[:, :], writes=[Bvecs], key="vecs")
            S.dma("pool", rmat_bf[:, :], rmat[:, :], writes=[Bconst], key="rmat")
            S.op("dve", lambda e: e.memset(ones_bf[:, :], 1.0), writes=[Bconst])
            S.op("dve", lambda e: e.memset(epsA[:, :], D * EPS), writes=[Bconst])
            S.op("dve", lambda e: e.memset(epsB[:, :], 128 * EPS), writes=[Bconst])
            S.op("act", lambda e: e.activation(out=csil[:, :], in_=vecs_sb[:, C_C:C_C + 16], func=AF.Silu),
                 reads=[Bvecs], writes=[Bcsil])
            ps_mod = banks[7]
            for i in range(8):
                sl = i % 3
                wt = wts[sl]
                S.dma("pool", v3(wt[:, :], 16), w_ada_v[:, :, i * 512:(i + 1) * 512], writes=[wB[sl]],
                      key=("wada", sl))
                for jj in range(4):
                    j = i * 4 + jj
                    for k in range(16):
                        S.op("pe", (lambda e, wt=wt, k=k, jj=jj, j=j: e.matmul(
                            ps_mod[:, j:j + 1], lhsT=wt[:, k * 512 + jj * 128:k * 512 + (jj + 1) * 128],
                            rhs=csil[:, k:k + 1], start=(k == 0), stop=(k == 15))),
                            reads=[wB[sl], Bcsil], writes=[bB[7]])
            S.op("dve", lambda e: e.tensor_tensor(out=mod_sb[:, 0:32], in0=ps_mod[:, 0:32],
                                                  in1=vecs_sb[:, C_BADA:C_BADA + 32], op=ALU.add),
                 reads=[bB[7], Bvecs], writes=[Bmod])
            for (A, sc0, nw0) in ((A1, 16, C_N1),):
                S.op("dve", (lambda e, A=A, sc0=sc0, nw0=nw0: e.scalar_tensor_tensor(
                    out=A[:, :], in0=mod_sb[:, sc0:sc0 + 16], scalar=1.0, in1=vecs_sb[:, nw0:nw0 + 16],
                    op0=ALU.add, op1=ALU.mult)), reads=[Bmod, Bvecs], writes=[Bmod])
                S.op("pool", (lambda e, A=A: e.tensor_scalar(out=A[:, :], in0=A[:, :], scalar1=math.sqrt(D),
                                                            scalar2=None, op0=ALU.mult)),
                     reads=[Bmod], writes=[Bmod])
            S.op("dve", lambda e: e.tensor_copy(out=B1bf[:, :], in_=mod_sb[:, 0:16]), reads=[Bmod], writes=[Bmod])
            S.op("pool", lambda e: e.tensor_scalar(out=kw[:, :], in0=vecs_sb[:, C_KW:C_KW + 1],
                                                   scalar1=math.sqrt(128.0), scalar2=None, op0=ALU.mult),
                 reads=[Bvecs], writes=[Bmod])
            S.flush()
        if stop_after <= 0:
            return nc

        def norm_rope_multi(items, use_sqrt=False):
            chain_part1(items)
            chain_part2(items, use_sqrt)

        def chain_part1(items):
            for it_ in items:
                T = it_["T"]
                S.op("act", lambda e, it_=it_, T=T: e.activation(out=T["sq"][:, :], in_=it_["ps"][:, :], func=AF.Square,
                                                               bias=it_["bias"], scale=1.0),
                     reads=[it_["psB"]] + list(it_["extra"]), writes=[T["sqB"]])
                S.op("dve", lambda e, it_=it_, T=T: e.tensor_scalar(out=T["raw"][:, :], in0=it_["ps"][:, :], scalar1=it_["bias"],
                                                                  scalar2=None, op0=ALU.add),
                     reads=[it_["psB"], T["sqB"]] + list(it_["extra"]), writes=[T["rawB"]])

        def chain_part2(items, use_sqrt=False):
            for it_ in items:
                T = it_["T"]
                S.op("pe", lambda e, it_=it_, T=T: e.matmul(banks[it_["sumb"]][:, :], lhsT=ones_bf[:, :], rhs=T["sq"][:, :],
                                                          start=True, stop=True),
                     reads=[T["sqB"], Bconst], writes=[bB[it_["sumb"]]])
            for it_ in items:
                T = it_["T"]
                S.op("act", lambda e, it_=it_, T=T: e.activation(out=T["rs"][:, :], in_=banks[it_["sumb"]][:, :],
                                                               func=(AF.Sqrt if use_sqrt else AF.Ln),
                                                               bias=epsB[:, 0:1], scale=1.0),
                     reads=[bB[it_["sumb"]], Bconst], writes=[T["rsB"]])
            for it_ in items:
                T = it_["T"]
                if use_sqrt:
                    S.op("dve", lambda e, T=T: e.reciprocal(out=T["rs"][:, :], in_=T["rs"][:, :]),
                         reads=[T["rsB"]], writes=[T["rsB"]])
                else:
                    S.op("act", lambda e, T=T: e.activation(out=T["rs"][:, :], in_=T["rs"][:, :], func=AF.Exp, scale=-0.5),
                         reads=[T["rsB"]], writes=[T["rsB"]])
            for it_ in items:
                T = it_["T"]
                S.op("dve", lambda e, it_=it_, T=T: e.scalar_tensor_tensor(out=T["n"][:, :], in0=T["raw"][:, :], scalar=it_["w"],
                                                                         in1=T["rs"][:, :], op0=ALU.mult, op1=ALU.mult),
                     reads=[T["rawB"], T["rsB"], Bmod, Bvecs], writes=[T["nB"]])
            for it_ in items:
                T = it_["T"]
                S.op("pe", lambda e, it_=it_, T=T: e.matmul(banks[it_["rotb"]][:, :], lhsT=rmat_bf[:, :], rhs=T["n"][:, :],
                                                          start=True, stop=True),
                     reads=[T["nB"], Bconst], writes=[bB[it_["rotb"]]])
            for it_ in items:
                T = it_["T"]
                S.op("pool", lambda e, it_=it_, T=T: e.tensor_tensor(out=T["t1"][:, :], in0=T["n"][:, :], in1=it_["cos"], op=ALU.mult),
                     reads=[T["nB"], it_["tabB"]], writes=[T["t1B"]])
            for it_ in items:
                T = it_["T"]
                S.op("dve", lambda e, it_=it_, T=T: e.tensor_tensor(out=T["t2"][:, :], in0=banks[it_["rotb"]][:, :], in1=it_["sin"],
                                                                  op=ALU.mult),
                     reads=[bB[it_["rotb"]], it_["tabB"]], writes=[T["t2B"]])
            for it_ in items:
                T = it_["T"]
                S.op("pool", lambda e, it_=it_, T=T: e.tensor_tensor(out=it_["out"], in0=T["t1"][:, :], in1=T["t2"][:, :], op=ALU.add),
                     reads=[T["t1B"], T["t2B"]], writes=[it_["outB"]])
                if it_.get("after") is not None:
                    it_["after"]()

        def norm_rope(ps, psB, bias_ap, w_ap, cos_ap, sin_ap, tabB, T, out_ap, outB, extra_reads=()):
            norm_rope_multi([dict(ps=ps, psB=psB, bias=bias_ap, w=w_ap, cos=cos_ap, sin=sin_ap, tabB=tabB, T=T, out=out_ap,
                                  outB=outB, extra=extra_reads, sumb=3, rotb=4, after=None)])

        def chain_tiles_alt(es, T, pfx):
            T2 = dict(T)
            T2["raw"] = sb(es, pfx + "raw", [128, 512], F32)
            T2["sq"] = sb(es, pfx + "sq", [128, 512], BF16)
            T2["rawB"] = Buf(pfx + "raw")
            T2["sqB"] = Buf(pfx + "sq")
            return T2

        def mk_chain_tiles(es, pfx):
            T = {}
            T["raw"] = sb(es, pfx + "raw", [128, 512], F32)
            T["sq"] = sb(es, pfx + "sq", [128, 512], BF16)
            T["rs"] = sb(es, pfx + "rs", [128, 512], F32)
            T["n"] = sb(es, pfx + "n", [128, 512], BF16)
            T["t1"] = sb(es, pfx + "t1", [128, 512], F32)
            T["t2"] = sb(es, pfx + "t2", [128, 512], F32)
            for k in ("raw", "sq", "rs", "n", "t1", "t2"):
                T[k + "B"] = Buf(pfx + k)
            return T

        def bias_cols(wt, wB_, ncol, Bbf, dst_ap, dstB, col_stride):
            for c in range(ncol):
                for k in range(16):
                    S.op("pe", (lambda e, c=c, k=k: e.matmul(
                        banks[7][:, c:c + 1], lhsT=wt[:, k * col_stride + c * 128:k * col_stride + (c + 1) * 128],
                        rhs=Bbf[:, k:k + 1], start=(k == 0), stop=(k == 15))),
                        reads=[wB_, Bmod], writes=[bB[7]])
            S.op("dve", lambda e: e.tensor_copy(out=dst_ap, in_=banks[7][:, 0:ncol]), reads=[bB[7]], writes=[dstB])

        with ExitStack() as es:
            xs = [sb(es, "xs%d" % i, [128, 16 * 512], F32) for i in range(2)]
            xsB = [Buf("xs%d" % i) for i in range(2)]
            sq = sb(es, "sq", [128, 16 * 512], BF16)
            sqB = Buf("sq")
            xsP = [[Buf("xs%d_%d" % (i, j)) for j in range(4)] for i in range(2)]
            sqP = [Buf("sq_%d" % j) for j in range(4)]
            hT = [sb(es, "hT%d" % i, [128, 16 * 512], BF16) for i in range(2)]
            hTB = [Buf("hT%d" % i) for i in range(2)]
            wkv = sb(es, "wkv", [128, 16 * 512], BF16)
            wkvB = Buf("wkv")
            rstd = sb(es, "rstd", [128, 512], F32)
            rstdB = Buf("rstd")
            cs = [sb(es, "cs%d" % i, [128, 1024], F32) for i in range(2)]
            csB = [Buf("cs%d" % i) for i in range(2)]
            kout = [sb(es, "kout%d" % i, [128, 512], BF16) for i in range(2)]
            koutB = [Buf("kout%d" % i) for i in range(2)]
            vout = [sb(es, "vout%d" % i, [128, 1024], BF16) for i in range(2)]
            voutB = [Buf("vout%d" % i) for i in range(2)]
            bk = sb(es, "bk", [128, 2], F32)
            bkB = Buf("bk")
            TT = [mk_chain_tiles(es, "c1a"), mk_chain_tiles(es, "c1b")]
            TTp = [TT, [chain_tiles_alt(es, TT[0], "c1c"), chain_tiles_alt(es, TT[1], "c1d")]]
            BKT = [Buf("KT0"), Buf("KT1")]
            BV = [Buf("V0"), Buf("V1")]
            BhT = Buf("hTscr")

            S.dma("pool", v3(wkv[:, :], 16), w_in_v[:, :, 1024:1536], writes=[wkvB], key="wkv")
            bias_cols(wkv, wkvB, 2, B1bf, bk[:, 0:2], bkB, 512)
            for c in range(2):
                for k in range(16):
                    S.op("pe", (lambda e, c=c, k=k: e.matmul(
                        banks[7][:, 8 + c:9 + c], lhsT=wkv[:, k * 512 + 256 + c * 128:k * 512 + 256 + (c + 1) * 128],
                        rhs=B1bf[:, k:k + 1], start=(k == 0), stop=(k == 15))),
                        reads=[wkvB, Bmod], writes=[bB[7]])
            S.op("dve", lambda e: e.tensor_copy(out=bvcol[:, 0:2], in_=banks[7][:, 8:10]), reads=[bB[7]], writes=[Bbv])

            NB1 = 33

            def stageA1a(tb):
                sl = tb % 2
                t0 = tb * 512
                for p_ in range(4):
                    S.dma("sp", v3(xs[sl][:, :], 16)[:, 4 * p_:4 * p_ + 4, :], xT_v[:, 4 * p_:4 * p_ + 4, t0:t0 + 512],
                          writes=[xsP[sl][p_]], key=("xs", sl, p_))
                    S.op("act", (lambda e, p_=p_: e.activation(out=sq[:, p_ * 2048:(p_ + 1) * 2048],
                                                              in_=xs[sl][:, p_ * 2048:(p_ + 1) * 2048], func=AF.Square)),
                         reads=[xsP[sl][p_]], writes=[sqP[p_]])

            def stageA1b(tb):
                for k in range(16):
                    S.op("pe", (lambda e, k=k: e.matmul(banks[0][:, :], lhsT=ones_bf[:, :],
                                                       rhs=sq[:, k * 512:(k + 1) * 512],
                                                       start=(k == 0), stop=(k == 15))),
                         reads=[sqP[k // 4], Bconst], writes=[bB[0]])

            def stageA2(tb):
                sl = tb % 2
                S.op("act", lambda e: e.activation(out=rstd[:, :], in_=banks[0][:, :], func=AF.Ln,
                                                   bias=epsA[:, 0:1], scale=1.0),
                     reads=[bB[0], Bconst], writes=[rstdB])
                S.op("act", lambda e: e.activation(out=rstd[:, :], in_=rstd[:, :], func=AF.Exp, scale=-0.5),
                     reads=[rstdB], writes=[rstdB])
                for k in range(16):
                    S.op("dve", (lambda e, k=k: e.scalar_tensor_tensor(
                        out=hT[sl][:, k * 512:(k + 1) * 512], in0=xs[sl][:, k * 512:(k + 1) * 512],
                        scalar=A1[:, k:k + 1], in1=rstd[:, :], op0=ALU.mult, op1=ALU.mult)),
                        reads=[xsP[sl][k // 4], rstdB, Bmod], writes=[hTB[sl]])

            def stageB(tb):
                sl = tb % 2
                t0 = tb * 512
                own = tb < 4 or tb == 32
                if own:
                    lt0 = t0 if tb < 4 else TOWN
                    S.dma("pool", v3(hT_scr[:, :], 16)[:, :, lt0:lt0 + 512], v3(hT[sl][:, :], 16),
                          reads=[hTB[sl]], writes=[BhT], key="hTst")
                if tb == 32:
                    return None
                S.dma("sp", cs[sl][:, 0:512], cosT[:, t0:t0 + 512], writes=[csB[sl]], key=("cs", sl))
                S.dma("sp", cs[sl][:, 512:1024], sinT[:, t0:t0 + 512], writes=[csB[sl]], key=("cs", sl))
                for g in range(2):
                    for k in range(16):
                        S.op("pe", (lambda e, g=g, k=k: e.matmul(
                            banks[1 + g][:, :], lhsT=wkv[:, k * 512 + g * 128:k * 512 + (g + 1) * 128],
                            rhs=hT[sl][:, k * 512:(k + 1) * 512], start=(k == 0), stop=(k == 15))),
                            reads=[wkvB, hTB[sl]], writes=[bB[1 + g]])
                for s in range(4):
                    bank = 5 + s // 2
                    c0 = (s % 2) * 256
                    for k in range(16):
                        S.op("pe", (lambda e, s=s, k=k, bank=bank, c0=c0: e.matmul(
                            banks[bank][:, c0:c0 + 256], lhsT=hT[sl][:, k * 512 + s * 128:k * 512 + (s + 1) * 128],
                            rhs=wkv[:, k * 512 + 256:k * 512 + 512], start=(k == 0), stop=(k == 15))),
                            reads=[wkvB, hTB[sl]], writes=[bB[bank]])
                vo = vout[sl]
                S.op("act", lambda e: e.activation(out=vo[:, 0:512], in_=banks[5][:, :], func=AF.Copy),
                     reads=[bB[5]], writes=[voutB[sl]])
                S.op("dve", lambda e: e.tensor_copy(out=vo[:, 512:1024], in_=banks[6][:, :]),
                     reads=[bB[6]], writes=[voutB[sl]])
                for g in range(2):
                    S.dma("pool", v3(V_scr[g, :, tb * 512:(tb + 1) * 512], 4),
                          v3(vo[:, :], 4)[:, :, g * 128:(g + 1) * 128],
                          reads=[voutB[sl]], writes=[BV[g]], key=("vst", sl))
                items = []
                for g in range(2):
                    ko = kout[g]
                    items.append(dict(
                        ps=banks[1 + g], psB=bB[1 + g], bias=bk[:, g:g + 1], w=kw[:, 0:1], cos=cs[sl][:, 0:512],
                        sin=cs[sl][:, 512:1024], tabB=csB[sl], T=TTp[sl][g], out=ko[:, :], outB=koutB[g], extra=[bkB],
                        sumb=(3, 4)[g], rotb=(3, 4)[g],
                        after=(lambda g=g, ko=ko: S.dma("pool", KT_scr[g, :, t0:t0 + 512], ko[:, :], reads=[koutB[g]],
                                                        writes=[BKT[g]], key=("kst", g)))))
                chain_part1(items)
                return items

            wad = [sb(es, "wad%d" % i, [128, 16 * 256], BF16) for i in range(2)]
            wadB = [Buf("wad0"), Buf("wad1")]
            Bmod2 = Buf("mod2")

            def mod_rest(i):
                sl_ = i % 2
                c0 = 4096 + 256 * i
                S.dma("pool", v3(wad[sl_][:, :], 16), w_ada_v[:, :, c0:c0 + 256], writes=[wadB[sl_]], key=("wad", sl_))
                for jj in range(2):
                    j = 32 + 2 * i + jj
                    for k in range(16):
                        S.op("pe", (lambda e, k=k, jj=jj, j=j: e.matmul(
                            banks[7][:, j:j + 1], lhsT=wad[sl_][:, k * 256 + jj * 128:k * 256 + (jj + 1) * 128],
                            rhs=csil[:, k:k + 1], start=(k == 0), stop=(k == 15))),
                            reads=[wadB[sl_], Bcsil], writes=[bB[7]])

            stageA1a(0)
            stageA1b(0)
            stageA2(0)
            stageA1a(1)
            stageA1b(1)
            pending = None
            for tb in range(NB1):
                if tb + 1 < NB1:
                    stageA2(tb + 1)
                if tb + 2 < NB1:
                    stageA1a(tb + 2)
                items = stageB(tb)
                if tb + 2 < NB1:
                    stageA1b(tb + 2)
                if pending is not None:
                    chain_part2(pending)
                pending = items
                if tb < 32:
                    mod_rest(tb)
            if pending is not None:
                chain_part2(pending)
            S.op("dve", lambda e: e.tensor_tensor(out=mod_sb[:, 32:96], in0=banks[7][:, 32:96],
                                                  in1=vecs_sb[:, C_BADA + 32:C_BADA + 96], op=ALU.add),
                 reads=[bB[7], Bvecs], writes=[Bmod2])
            S.op("dve", lambda e: e.scalar_tensor_tensor(out=A2[:, :], in0=mod_sb[:, 64:80], scalar=1.0,
                                                         in1=vecs_sb[:, C_N2:C_N2 + 16], op0=ALU.add, op1=ALU.mult),
                 reads=[Bmod2, Bvecs], writes=[Bmod2])
            S.op("pool", lambda e: e.tensor_scalar(out=A2[:, :], in0=A2[:, :], scalar1=math.sqrt(D), scalar2=None, op0=ALU.mult),
                 reads=[Bmod2], writes=[Bmod2])
            S.op("dve", lambda e: e.tensor_copy(out=B2bf[:, :], in_=mod_sb[:, 48:64]), reads=[Bmod2], writes=[Bmod2])
            S.flush()
        if stop_after <= 1:
            return nc

        def load_wtile(dst, dstB, view, c0, ncols, nk, key):
            S.dma("pool", v3(dst[:, 0:nk * ncols], nk), view[:, :, c0:c0 + ncols], writes=[dstB], key=key)

        def mm_acc(bank_i, lhs_fn, rhs_fn, nk, reads, out_ap=None):
            for k in range(nk):
                S.op("pe", (lambda e, k=k: e.matmul(out_ap if out_ap is not None else banks[bank_i][:, :],
                                                   lhsT=lhs_fn(k), rhs=rhs_fn(k), start=(k == 0), stop=(k == nk - 1))),
                     reads=reads, writes=[bB[bank_i]])

        BqT, BqBT, BkBT, BvB, Bsg = Buf("qT"), Buf("qBT"), Buf("kBT"), Buf("vB"), Buf("sg")
        with ExitStack() as es:
            hTo = sb(es, "hTo", [128, 16 * TLOC], BF16)
            hToB = Buf("hTo")
            wts = [sb(es, "w2_%d" % i, [128, 16 * 512], BF16) for i in range(3)]
            wtB = [Buf("w2_%d" % i) for i in range(3)]
            cso = sb(es, "cso", [128, 2 * TOWN], F32)
            csoB = Buf("cso")
            TT2 = [mk_chain_tiles(es, "c2a"), mk_chain_tiles(es, "c2b")]
            TT2p = [TT2, [chain_tiles_alt(es, TT2[0], "c2c"), chain_tiles_alt(es, TT2[1], "c2d")]]
            qitems = []
            qpend = [None]
            qpair = [0]
            ot = [sb(es, "ot%d" % i, [128, 512], BF16) for i in range(4)]
            otB = [Buf("ot%d" % i) for i in range(4)]
            bc = [sb(es, "bc%d" % i, [128, 4], F32) for i in range(2)]
            bcB = [Buf("bc%d" % i) for i in range(2)]
            hToP = [Buf("hTo%d" % i) for i in range(5)]
            for tbp in range(5):
                S.dma("sp", v3(hTo[:, :], 16)[:, :, tbp * 512:(tbp + 1) * 512], v3(hT_scr[:, :], 16)[:, :, tbp * 512:(tbp + 1) * 512],
                      reads=[BhT], writes=[hToP[tbp]], key=("hTo", tbp))
            S.dma("sp", cso[:, 0:TOWN], cosT[:, 0:TOWN], writes=[csoB], key="cso")
            S.dma("sp", cso[:, TOWN:2 * TOWN], sinT[:, 0:TOWN], writes=[csoB], key="cso")
            order = [0, 1, 3, 4, 5, 6, 7, 8] + list(range(9, 17))
            mb = Ring([0, 1, 2, 5, 6])
            oti = 0
            for n_i, ti in enumerate(order):
                sl = n_i % 3
                wt, wB_ = wts[sl], wtB[sl]
                load_wtile(wt, wB_, w_in_v, ti * 512, 512, 16, ("w2", sl))
                if ti in (7, 8):
                    hv = ti - 7
                    bias_cols(wt, wB_, 4, B1bf, bvcol[:, 2 + 4 * hv:6 + 4 * hv], Bbv, 512)
                    for s_ in range(TLOC // 128):
                        bi = mb.next()
                        mm_acc(bi, lambda k, s_=s_: hTo[:, k * TLOC + s_ * 128:k * TLOC + (s_ + 1) * 128],
                               lambda k, wt=wt: wt[:, k * 512:(k + 1) * 512], 16, [hToP[s_ // 4], wB_])
                        o_, oB_ = ot[oti % 4], otB[oti % 4]
                        oti += 1
                        if s_ % 2 == 0:
                            S.op("act", (lambda e, o_=o_, bi=bi: e.activation(out=o_[:, :], in_=banks[bi][:, :], func=AF.Copy)),
                                 reads=[bB[bi]], writes=[oB_])
                        else:
                            S.op("dve", (lambda e, o_=o_, bi=bi: e.tensor_copy(out=o_[:, :], in_=banks[bi][:, :])),
                                 reads=[bB[bi]], writes=[oB_])
                        S.dma("sp", vB_scr[4 * hv:4 * hv + 4, :, s_ * 128:(s_ + 1) * 128].rearrange("h p d -> p h d"),
                              v3(o_[:, :], 4), reads=[oB_], writes=[BvB], key=("ot", oti % 4))
                    continue
                bcs, bcsB = bc[n_i % 2], bcB[n_i % 2]
                bias_cols(wt, wB_, 4, B1bf, bcs[:, 0:4], bcsB, 512)
                for cc in range(4):
                    ntb = 5 if ti in (5, 6) else 4
                    for tb in range(ntb):
                        bi = mb.next()
                        mm_acc(bi, lambda k, wt=wt, cc=cc: wt[:, k * 512 + cc * 128:k * 512 + (cc + 1) * 128],
                               lambda k, tb=tb: hTo[:, k * TLOC + tb * 512:k * TLOC + (tb + 1) * 512], 16, [hToP[tb], wB_])
                        o_, oB_ = ot[oti % 4], otB[oti % 4]
                        oti += 1
                        okey = ("ot", oti % 4)
                        if ti in (0, 1):
                            h = ti * 4 + cc
                            qitems.append(dict(
                                ps=banks[bi], psB=bB[bi], bias=bcs[:, cc:cc + 1], w=vecs_sb[:, C_QW:C_QW + 1],
                                cos=cso[:, tb * 512:(tb + 1) * 512], sin=cso[:, TOWN + tb * 512:TOWN + (tb + 1) * 512],
                                tabB=csoB, T=TT2p[qpair[0] % 2][tb % 2], out=o_[:, :], outB=oB_, extra=[bcsB],
                                sumb=(3, 7)[tb % 2], rotb=(4, 3)[tb % 2],
                                after=(lambda h=h, tb=tb, o_=o_, oB_=oB_, okey=okey: S.dma(
                                    "sp", qT_scr[h, :, tb * 512:(tb + 1) * 512], o_[:, :], reads=[oB_], writes=[BqT], key=okey))))
                            if len(qitems) == 2:
                                chain_part1(qitems)
                                if qpend[0] is not None:
                                    chain_part2(qpend[0])
                                qpend[0] = qitems
                                qitems = []
                                qpair[0] += 1
                                if ti == 1 and cc == 3 and tb == 3:
                                    chain_part2(qpend[0])
                                    qpend[0] = None
                        elif ti in (3, 4, 5, 6):
                            S.op("act", (lambda e, o_=o_, bi=bi, cc=cc, bcs=bcs: e.activation(
                                out=o_[:, :], in_=banks[bi][:, :], func=AF.Identity, bias=bcs[:, cc:cc + 1], scale=1.0)),
                                reads=[bB[bi], bcsB], writes=[oB_])
                            if ti in (3, 4):
                                h = (ti - 3) * 4 + cc
                                S.dma("sp", qBT_scr[h, :, tb * 512:(tb + 1) * 512], o_[:, :], reads=[oB_], writes=[BqBT], key=okey)
                            else:
                                h = (ti - 5) * 4 + cc
                                S.dma("sp", kBT_scr[h, :, tb * 512:(tb + 1) * 512], o_[:, :], reads=[oB_], writes=[BkBT], key=okey)
                        else:
                            gi = (ti - 9) * 4 + cc
                            S.op("act", (lambda e, o_=o_, bi=bi, cc=cc, bcs=bcs: e.activation(
                                out=o_[:, :], in_=banks[bi][:, :], func=AF.Sigmoid, bias=bcs[:, cc:cc + 1], scale=1.0)),
                                reads=[bB[bi], bcsB], writes=[oB_])
                            S.dma("sp", sg_scr[gi, :, tb * 512:(tb + 1) * 512], o_[:, :], reads=[oB_], writes=[Bsg], key=okey)
            S.flush()
        if stop_after <= 2:
            return nc

        ByT = Buf("yT")

        def attn_epilogue(Obank, Lbank, bias_ap, E, dst_ap, key):
            S.op("act", lambda e: e.activation(out=E["rec"][:, :], in_=banks[Lbank][:, :], func=AF.Ln), reads=[bB[Lbank]], writes=[E["recB"]])
            S.op("act", lambda e: e.activation(out=E["rec"][:, :], in_=E["rec"][:, :], func=AF.Exp, scale=-1.0), reads=[E["recB"]], writes=[E["recB"]])
            S.op("dve", lambda e: e.tensor_tensor(out=E["o"][:, :], in0=banks[Obank][:, :], in1=E["rec"][:, :], op=ALU.mult),
                 reads=[bB[Obank], E["recB"]], writes=[E["oB"]])
            y_, yB_ = E["y"][E["i"] % 2], E["yB"][E["i"] % 2]
            E["i"] += 1
            S.op("dve", lambda e: e.tensor_scalar(out=y_[:, :], in0=E["o"][:, :], scalar1=bias_ap, scalar2=None, op0=ALU.add),
                 reads=[E["oB"], Bbv], writes=[yB_])
            S.dma("pool", dst_ap, y_[:, :], reads=[yB_], writes=[ByT], key=(key, E["i"] % 2))

        def mk_epi(es, pfx):
            E = {"rec": sb(es, pfx + "rec", [128, 512], F32), "o": sb(es, pfx + "o", [128, 512], F32),
                 "y": [sb(es, pfx + "y%d" % i, [128, 512], BF16) for i in range(2)],
                 "recB": Buf("rec"), "oB": Buf("o"), "yB": [Buf("y0"), Buf("y1")], "i": 0}
            return E

        with ExitStack() as es:
            KTs = [sb(es, "KTs%d" % g, [128, S_ALL], BF16) for g in range(2)]
            Vs = [sb(es, "Vs%d" % g, [128, S_ALL], BF16) for g in range(2)]
            qs = [sb(es, "qs%d" % g, [128, 4 * TOWN], BF16) for g in range(2)]
            KTsB = [[Buf("KTs%d_%d" % (g_, p_)) for p_ in range(4)] for g_ in range(2)]
            VsB = [[Buf("Vs%d_%d" % (g_, p_)) for p_ in range(4)] for g_ in range(2)]
            qsB = [Buf("qs0"), Buf("qs1")]
            Pt = [sb(es, "Pt%d" % i, [128, 512], BF16) for i in range(4)]
            PtB = [Buf("Pt%d" % i) for i in range(4)]
            E = mk_epi(es, "e3")
            for g in range(2):
                S.dma("sp", qs[g][:, :].rearrange("p (h t) -> p h t", h=4), qT_scr[4 * g:4 * g + 4].rearrange("h p t -> p h t"),
                      reads=[BqT], writes=[qsB[g]], key=("qs", g))
                for part in range(4):
                    c0 = part * 4096
                    S.dma("sp", KTs[g][:, c0:c0 + 4096], KT_scr[g, :, c0:c0 + 4096], reads=[BKT[g]], writes=[KTsB[g][part]], key=("KTs", g, part))
                    S.dma("sp", Vs[g][:, c0:c0 + 4096], V_scr[g, :, c0:c0 + 4096], reads=[BV[g]], writes=[VsB[g][part]], key=("Vs", g, part))
            item = 0
            NKC = S_ALL // 128
            for g in range(2):
                for qb in range(4):
                    for hh in range(4):
                        h = 4 * g + hh
                        Ob, Lb = 4 + item % 2, 6 + item % 2
                        item += 1
                        q_ap = qs[g][:, hh * TOWN + qb * 512:hh * TOWN + (qb + 1) * 512]

                        def s_mm(kc, g=g, q_ap=q_ap):
                            bi = kc % 4
                            S.op("pe", (lambda e: e.matmul(banks[bi][:, :], lhsT=KTs[g][:, kc * 128:(kc + 1) * 128], rhs=q_ap,
                                                          start=True, stop=True)),
                                 reads=[KTsB[g][kc // 32], qsB[g]], writes=[bB[bi]])
                            S.op("act", (lambda e: e.activation(out=Pt[bi][:, :], in_=banks[bi][:, :], func=AF.Exp)),
                                 reads=[bB[bi]], writes=[PtB[bi]])

                        def pv_mm(kc, g=g, Ob=Ob, Lb=Lb):
                            bi = kc % 4
                            S.op("pe", (lambda e: e.matmul(banks[Ob][:, :], lhsT=Vs[g][:, kc * 128:(kc + 1) * 128], rhs=Pt[bi][:, :],
                                                          start=(kc == 0), stop=(kc == NKC - 1))),
                                 reads=[VsB[g][kc // 32], PtB[bi]], writes=[bB[Ob]])
                            S.op("pe", (lambda e: e.matmul(banks[Lb][:, :], lhsT=ones_bf[:, :], rhs=Pt[bi][:, :],
                                                          start=(kc == 0), stop=(kc == NKC - 1))),
                                 reads=[PtB[bi], Bconst], writes=[bB[Lb]])

                        s_mm(0)
                        s_mm(1)
                        for kc in range(NKC):
                            if kc + 2 < NKC:
                                s_mm(kc + 2)
                            pv_mm(kc)
                        attn_epilogue(Ob, Lb, bvcol[:, g:g + 1], E, yT_scr[h, :, qb * 512:(qb + 1) * 512], "y3")
            S.flush()
        if stop_after <= 3:
            return nc

        def lc_of(kc):
            return kc if kc < 16 else (kc - 18 if kc < 18 else kc - 2)

        def kc_of(lc):
            return lc if 0 <= lc < 16 else (lc + 18 if lc < 0 else lc + 2)

        with ExitStack() as es:
            kBs = [sb(es, "kBs%d" % i, [128, TLOC], BF16) for i in range(2)]
            vBs = [sb(es, "vBs%d" % i, [128, TLOC], BF16) for i in range(2)]
            qBs = [sb(es, "qBs%d" % i, [128, TOWN], BF16) for i in range(2)]
            hdB = [Buf("hd0"), Buf("hd1")]
            tab = sb(es, "tab", [128, 20 * 640], F32)
            tabB = [Buf("tab%d" % i) for i in range(20)]
            Pa = [sb(es, "Pa%d" % i, [128, 20 * 640], BF16) for i in range(2)]
            PaB = [[Buf("Pa%d_%d" % (i, j)) for j in range(20)] for i in range(2)]
            tmp = [sb(es, "natmp%d" % i, [128, 640], F32) for i in range(2)]
            tmpB = [Buf("natmp0"), Buf("natmp1")]
            E = mk_epi(es, "e4")
            scale = 1.0 / math.sqrt(128.0)
            item = 0
            for h in range(8):
                sl = h % 2
                S.dma("sp", kBs[sl][:, :], kBT_scr[h], reads=[BkBT], writes=[hdB[sl]], key=("hd", sl))
                S.dma("sp", vBs[sl][:, :], vB_scr[h], reads=[BvB], writes=[hdB[sl]], key=("hd", sl))
                S.dma("sp", qBs[sl][:, :], qBT_scr[h], reads=[BqBT], writes=[hdB[sl]], key=("hd", sl))
                for kc in range(20):
                    S.dma("sp", tab[:, kc * 640:(kc + 1) * 640], natab[h, kc], writes=[tabB[kc]], key=("tab", kc % 4))
                for kc in range(20):
                    lc = lc_of(kc)
                    blo, bhi = max(0, lc - 2), min(15, lc + 2)
                    nq = (bhi - blo + 1) * 128
                    q0 = blo * 128
                    dd = dbl[kc % 2]
                    n1 = min(nq, 512)
                    S.op("pe", (lambda e, kc=kc, dd=dd, n1=n1, q0=q0, sl=sl: e.matmul(
                        dd[:, 0:n1], lhsT=kBs[sl][:, kc * 128:(kc + 1) * 128], rhs=qBs[sl][:, q0:q0 + n1], start=True, stop=True)),
                        reads=[hdB[sl]], writes=[bB[2 * (kc % 2)]])
                    rd = [bB[2 * (kc % 2)]]
                    if nq > 512:
                        S.op("pe", (lambda e, kc=kc, dd=dd, nq=nq, q0=q0, sl=sl: e.matmul(
                            dd[:, 512:nq], lhsT=kBs[sl][:, kc * 128:(kc + 1) * 128], rhs=qBs[sl][:, q0 + 512:q0 + nq],
                            start=True, stop=True)),
                            reads=[hdB[sl]], writes=[bB[2 * (kc % 2) + 1]])
                        rd.append(bB[2 * (kc % 2) + 1])
                    tm, tmB_ = tmp[kc % 2], tmpB[kc % 2]
                    S.op("dve", (lambda e, kc=kc, dd=dd, n1=n1, tm=tm: e.scalar_tensor_tensor(
                        out=tm[:, 0:n1], in0=dd[:, 0:n1], scalar=scale, in1=tab[:, kc * 640:kc * 640 + n1],
                        op0=ALU.mult, op1=ALU.add)), reads=[rd[0], tabB[kc]], writes=[tmB_])
                    if nq > 512:
                        S.op("dve", (lambda e, kc=kc, dd=dd, nq=nq, tm=tm: e.scalar_tensor_tensor(
                            out=tm[:, 512:nq], in0=dd[:, 512:nq], scalar=scale, in1=tab[:, kc * 640 + 512:kc * 640 + nq],
                            op0=ALU.mult, op1=ALU.add)), reads=[rd[1], tabB[kc]], writes=[tmB_])
                    S.op("act", (lambda e, kc=kc, nq=nq, tm=tm, sl=sl: e.activation(
                        out=Pa[sl][:, kc * 640:kc * 640 + nq], in_=tm[:, 0:nq], func=AF.Exp)),
                        reads=[tmB_], writes=[PaB[sl][kc]])
                for qb in range(4):
                    Ob, Lb = 4 + item % 2, 6 + item % 2
                    item += 1
                    for bq in range(4):
                        b = qb * 4 + bq
                        lcs = list(range(b - 2, b + 3))
                        for i_, lc in enumerate(lcs):
                            kc = kc_of(lc)
                            j0 = (b - max(0, lc - 2)) * 128
                            for (bank_i, lhs) in ((Ob, None), (Lb, ones_bf)):
                                S.op("pe", (lambda e, kc=kc, j0=j0, bank_i=bank_i, lhs=lhs, bq=bq, i_=i_, sl=sl: e.matmul(
                                    banks[bank_i][:, bq * 128:(bq + 1) * 128],
                                    lhsT=(vBs[sl][:, kc * 128:(kc + 1) * 128] if lhs is None else lhs[:, :]),
                                    rhs=Pa[sl][:, kc * 640 + j0:kc * 640 + j0 + 128], start=(i_ == 0), stop=(i_ == 4))),
                                    reads=[hdB[sl], PaB[sl][kc], Bconst], writes=[bB[bank_i]])
                    attn_epilogue(Ob, Lb, bvcol[:, 2 + h:3 + h], E, yT_scr[8 + h, :, qb * 512:(qb + 1) * 512], "y4")
            S.flush()
        if stop_after <= 4:
            return nc

        m_scr = dscr("m_scr", [16, 128, TOWN])
        Bm = Buf("m")
        Bx1, Bh2 = Buf("x1"), Buf("h2")
        with ExitStack() as es:
            woa = sb(es, "woa", [128, 8 * D], BF16)
            wob = sb(es, "wob", [128, 8 * D], BF16)
            woP = [Buf("woab%d" % i) for i in range(4)]
            yab = [sb(es, "yab%d" % i, [128, 16 * 512], BF16) for i in range(2)]
            yabB = [Buf("yab0"), Buf("yab1")]
            sgt = [sb(es, "sgt%d" % i, [128, 1024], BF16) for i in range(3)]
            sgtB = [Buf("sgt%d" % i) for i in range(3)]
            t1 = [sb(es, "m_t1%d" % i, [128, 512], F32) for i in range(2)]
            t2 = [sb(es, "m_t2%d" % i, [128, 512], F32) for i in range(2)]
            t1B = [Buf("t1a"), Buf("t1b")]
            t2B = [Buf("t2a"), Buf("t2b")]
            mo = [sb(es, "mo%d" % i, [128, 512], BF16) for i in range(3)]
            moB = [Buf("mo%d" % i) for i in range(3)]
            for part in range(4):
                S.dma("pool", v3(woa[:, :], 8)[:, :, part * 512:(part + 1) * 512],
                      w_oa.rearrange("(k p) n -> p k n", p=128)[:, :, part * 512:(part + 1) * 512], writes=[woP[part]], key=("woab", part))
                S.dma("pool", v3(wob[:, :], 8)[:, :, part * 512:(part + 1) * 512],
                      w_ob.rearrange("(k p) n -> p k n", p=128)[:, :, part * 512:(part + 1) * 512], writes=[woP[part]], key=("woab", part))
            it = 0
            for tb in range(4):
                sl = tb % 2
                S.dma("sp", v3(yab[sl][:, :], 16), yT_scr.rearrange("k p t -> p k t")[:, :, tb * 512:(tb + 1) * 512],
                      reads=[ByT], writes=[yabB[sl]], key=("yab", sl))
                for n in range(16):
                    s3 = it % 3
                    S.dma("sp", sgt[s3][:, 0:512], sg_scr[n, :, tb * 512:(tb + 1) * 512], reads=[Bsg], writes=[sgtB[s3]], key=("sgt", s3))
                    S.dma("sp", sgt[s3][:, 512:1024], sg_scr[16 + n, :, tb * 512:(tb + 1) * 512], reads=[Bsg], writes=[sgtB[s3]], key=("sgt", s3))
                    ba, bb_ = (0, 1) if it % 2 == 0 else (2, 5)
                    mm_acc(ba, lambda k, n=n: woa[:, k * D + n * 128:k * D + (n + 1) * 128],
                           lambda k, sl=sl: yab[sl][:, k * 512:(k + 1) * 512], 8, [woP[n // 4], yabB[sl]])
                    mm_acc(bb_, lambda k, n=n: wob[:, k * D + n * 128:k * D + (n + 1) * 128],
                           lambda k, sl=sl: yab[sl][:, (8 + k) * 512:(9 + k) * 512], 8, [woP[n // 4], yabB[sl]])
                    i2 = it % 2
                    S.op("dve", (lambda e, i2=i2, ba=ba, s3=s3: e.tensor_tensor(out=t1[i2][:, :], in0=banks[ba][:, :], in1=sgt[s3][:, 0:512], op=ALU.mult)),
                         reads=[bB[ba], sgtB[s3]], writes=[t1B[i2]])
                    S.op("dve", (lambda e, i2=i2, bb_=bb_, s3=s3: e.tensor_tensor(out=t2[i2][:, :], in0=banks[bb_][:, :], in1=sgt[s3][:, 512:1024], op=ALU.mult)),
                         reads=[bB[bb_], sgtB[s3]], writes=[t2B[i2]])
                    S.op("pool", (lambda e, i2=i2, s3=s3: e.tensor_tensor(out=mo[s3][:, :], in0=t1[i2][:, :], in1=t2[i2][:, :], op=ALU.add)),
                         reads=[t1B[i2], t2B[i2]], writes=[moB[s3]])
                    S.dma("pool", m_scr[n, :, tb * 512:(tb + 1) * 512], mo[s3][:, :], reads=[moB[s3]], writes=[Bm], key=("mo", s3))
                    it += 1
            S.flush()

        def stats_to_rstd(bank_i, rs_tile, rsB):
            S.op("act", lambda e: e.activation(out=rs_tile[:, :], in_=banks[bank_i][:, :], func=AF.Ln, bias=epsA[:, 0:1], scale=1.0),
                 reads=[bB[bank_i], Bconst], writes=[rsB])
            S.op("act", lambda e: e.activation(out=rs_tile[:, :], in_=rs_tile[:, :], func=AF.Exp, scale=-0.5), reads=[rsB], writes=[rsB])

        with ExitStack() as es:
            wout = sb(es, "wout", [128, 16 * D], BF16)
            woutP = [Buf("wout%d" % i) for i in range(4)]
            x1cB = [Buf("x1c%d" % i) for i in range(16)]
            mb_ = [sb(es, "mblk%d" % i, [128, 16 * 512], BF16) for i in range(2)]
            mbB = [Buf("mblk0"), Buf("mblk1")]
            xin = [sb(es, "xin%d" % i, [128, 512], F32) for i in range(3)]
            xinB = [Buf("xin%d" % i) for i in range(3)]
            x1b = sb(es, "x1b", [128, 16 * 512], F32)
            x1bB = Buf("x1b")
            sq5 = [sb(es, "sq5_%d" % i, [128, 512], BF16) for i in range(2)]
            sq5B = [Buf("sq5a"), Buf("sq5b")]
            rs5 = sb(es, "rs5", [128, 512], F32)
            rs5B = Buf("rs5")
            h2o = sb(es, "h2o", [128, 16 * 512], BF16)
            h2oB = Buf("h2o")
            for part in range(4):
                S.dma("pool", v3(wout[:, :], 16)[:, :, part * 512:(part + 1) * 512],
                      w_out.rearrange("(k p) n -> p k n", p=128)[:, :, part * 512:(part + 1) * 512], writes=[woutP[part]], key=("wout", part))
            it = 0
            mr = Ring([0, 1, 2, 3])
            pend5 = None

            def stat5(n, q2):
                S.op("pe", (lambda e: e.matmul(banks[7][:, :], lhsT=ones_bf[:, :], rhs=sq5[q2][:, :], start=(n == 0), stop=(n == 15))),
                     reads=[sq5B[q2], Bconst], writes=[bB[7]])

            for tb in range(4):
                sl = tb % 2
                S.dma("sp", v3(mb_[sl][:, :], 16), m_scr.rearrange("k p t -> p k t")[:, :, tb * 512:(tb + 1) * 512],
                      reads=[Bm], writes=[mbB[sl]], key=("mblk", sl))
                for n in range(16):
                    s3 = it % 3
                    it += 1
                    S.dma("sp", xin[s3][:, :], xT_v[:, n, tb * 512:(tb + 1) * 512], writes=[xinB[s3]], key=("xin", s3))
                    bi = mr.next()
                    mm_acc(bi, lambda k, n=n: wout[:, k * D + n * 128:k * D + (n + 1) * 128],
                           lambda k, sl=sl: mb_[sl][:, k * 512:(k + 1) * 512], 16, [woutP[n // 4], mbB[sl]])
                    S.op("dve", (lambda e, bi=bi, n=n, s3=s3: e.scalar_tensor_tensor(
                        out=x1b[:, n * 512:(n + 1) * 512], in0=banks[bi][:, :], scalar=mod_sb[:, 32 + n:33 + n], in1=xin[s3][:, :],
                        op0=ALU.mult, op1=ALU.add)), reads=[bB[bi], xinB[s3], Bmod], writes=[x1cB[n]])
                    q2 = n % 2
                    S.op("act", (lambda e, n=n, q2=q2: e.activation(out=sq5[q2][:, :], in_=x1b[:, n * 512:(n + 1) * 512], func=AF.Square)),
                         reads=[x1cB[n]], writes=[sq5B[q2]])
                    if pend5 is not None:
                        stat5(*pend5)
                    pend5 = (n, q2)
                stat5(*pend5)
                pend5 = None
                stats_to_rstd(7, rs5, rs5B)
                for k in range(16):
                    S.op("dve", (lambda e, k=k: e.scalar_tensor_tensor(
                        out=h2o[:, k * 512:(k + 1) * 512], in0=x1b[:, k * 512:(k + 1) * 512], scalar=A2[:, k:k + 1], in1=rs5[:, :],
                        op0=ALU.mult, op1=ALU.mult)), reads=[x1cB[k], rs5B, Bmod], writes=[h2oB])
                S.dma("pool", v3(h2T_scr[:, :], 16)[:, :, tb * 512:(tb + 1) * 512], v3(h2o[:, :], 16), reads=[h2oB], writes=[Bh2], key="h2st")
                for j4 in range(4):
                    S.dma("pool", x1T_scr.rearrange("k p t -> p k t")[:, 4 * j4:4 * j4 + 4, tb * 512:(tb + 1) * 512],
                          v3(x1b[:, :], 16)[:, 4 * j4:4 * j4 + 4, :],
                          reads=x1cB[4 * j4:4 * j4 + 4], writes=[Bx1], key=("x1st", j4))
            S.flush()
        if stop_after <= 5:
            return nc

        Bact = Buf("act")
        w_gate_v = w_gate.rearrange("(k p) n -> p k n", p=128)
        w_up_v = w_up.rearrange("(k p) n -> p k n", p=128)
        with ExitStack() as es:
            h2 = sb(es, "h2", [128, 16 * TOWN], BF16)
            h2B = Buf("h2")
            wg = [sb(es, "wg%d" % i, [128, 16 * 512], BF16) for i in range(2)]
            wu = [sb(es, "wu%d" % i, [128, 16 * 512], BF16) for i in range(2)]
            wgB = [Buf("wg0"), Buf("wg1")]
            wuB = [Buf("wu0"), Buf("wu1")]
            bg = [sb(es, "bg%d" % i, [128, 4], F32) for i in range(2)]
            bu = [sb(es, "bu%d" % i, [128, 4], F32) for i in range(2)]
            bgB = [Buf("bg0"), Buf("bg1")]
            buB = [Buf("bu0"), Buf("bu1")]
            sgl = [sb(es, "sgl%d" % i, [128, 512], F32) for i in range(2)]
            sglB = [Buf("sgl0"), Buf("sgl1")]
            ao = [sb(es, "ao%d" % i, [128, 512], BF16) for i in range(3)]
            aoB = [Buf("ao%d" % i) for i in range(3)]
            h2P = [Buf("h2_%d" % i) for i in range(4)]
            for tbp in range(4):
                S.dma("sp", v3(h2[:, :], 16)[:, :, tbp * 512:(tbp + 1) * 512], v3(h2T_scr[:, :], 16)[:, :, tbp * 512:(tbp + 1) * 512],
                      reads=[Bh2], writes=[h2P[tbp]], key=("h2ld", tbp))
            it = 0
            for ti in range(11):
                sl = ti % 2
                load_wtile(wg[sl], wgB[sl], w_gate_v, ti * 512, 512, 16, ("wg", sl))
                load_wtile(wu[sl], wuB[sl], w_up_v, ti * 512, 512, 16, ("wu", sl))
                bias_cols(wg[sl], wgB[sl], 4, B2bf, bg[sl][:, 0:4], bgB[sl], 512)
                bias_cols(wu[sl], wuB[sl], 4, B2bf, bu[sl][:, 0:4], buB[sl], 512)
                for cc in range(4):
                    for tb in range(4):
                        bG, bU = ((0, 1), (2, 3), (4, 5))[it % 3]
                        i2, i3 = it % 2, it % 3
                        it += 1
                        mm_acc(bG, lambda k, sl=sl, cc=cc: wg[sl][:, k * 512 + cc * 128:k * 512 + (cc + 1) * 128],
                               lambda k, tb=tb: h2[:, k * TOWN + tb * 512:k * TOWN + (tb + 1) * 512], 16, [wgB[sl], h2P[tb]])
                        mm_acc(bU, lambda k, sl=sl, cc=cc: wu[sl][:, k * 512 + cc * 128:k * 512 + (cc + 1) * 128],
                               lambda k, tb=tb: h2[:, k * TOWN + tb * 512:k * TOWN + (tb + 1) * 512], 16, [wuB[sl], h2P[tb]])
                        S.op("act", (lambda e, bG=bG, i2=i2, sl=sl, cc=cc: e.activation(
                            out=sgl[i2][:, :], in_=banks[bG][:, :], func=AF.Silu, bias=bg[sl][:, cc:cc + 1], scale=1.0)),
                            reads=[bB[bG], bgB[sl]], writes=[sglB[i2]])
                        S.op("dve", (lambda e, bU=bU, i2=i2, i3=i3, sl=sl, cc=cc: e.scalar_tensor_tensor(
                            out=ao[i3][:, :], in0=banks[bU][:, :], scalar=bu[sl][:, cc:cc + 1], in1=sgl[i2][:, :],
                            op0=ALU.add, op1=ALU.mult)), reads=[bB[bU], buB[sl], sglB[i2]], writes=[aoB[i3]])
                        S.dma("sp", act_scr[ti * 4 + cc, :, tb * 512:(tb + 1) * 512], ao[i3][:, :], reads=[aoB[i3]], writes=[Bact],
                              key=("ao", i3))
            S.flush()
        if stop_after <= 6:
            return nc

        Bx2h = [Buf("x2a"), Buf("x2b")]
        w_down_v = w_down.rearrange("(k p) n -> p k n", p=128)
        with ExitStack() as es:
            acth = sb(es, "acth", [128, 44 * 1024], BF16)
            acthB = Buf("acth")
            wd = [sb(es, "wd%d" % i, [128, 44 * 256], BF16) for i in range(2)]
            wdB = [Buf("wd0"), Buf("wd1")]
            x1i = [sb(es, "x1i%d" % i, [128, 512], F32) for i in range(3)]
            x1iB = [Buf("x1i%d" % i) for i in range(3)]
            x2o = [sb(es, "x2o%d" % i, [128, 512], F32) for i in range(3)]
            x2oB = [Buf("x2o%d" % i) for i in range(3)]
            sq7 = [sb(es, "sq7_%d" % i, [128, 512], BF16) for i in range(2)]
            sq7B = [Buf("sq7a"), Buf("sq7b")]
            rsh = [[sb(es, "rsh%d_%d" % (h_, i), [128, 512], F32) for i in range(2)] for h_ in range(2)]
            rshB = [[Buf("rsh%d_%d" % (h_, i)) for i in range(2)] for h_ in range(2)]
            x2i = [sb(es, "x2i%d" % i, [128, 512], F32) for i in range(4)]
            x2iB = [Buf("x2i%d" % i) for i in range(4)]
            Fw = sb(es, "Fw", [128, 16], F32)
            FwB = Buf("Fw")
            yo = [sb(es, "yo%d" % i, [128, 512], F32) for i in range(4)]
            yoB = [Buf("yo%d" % i) for i in range(4)]
            Bout = Buf("out")
            S.op("pool", lambda e: e.tensor_scalar(out=Fw[:, :], in0=vecs_sb[:, C_FW:C_FW + 16], scalar1=math.sqrt(D), scalar2=None, op0=ALU.mult),
                 reads=[Bvecs], writes=[FwB])
            it = 0
            wi = 0
            mr = Ring([0, 1, 2, 3, 4, 5])
            pend7 = None

            def stat7(q2, tb2, n):
                S.op("pe", (lambda e: e.matmul(banks[6 + tb2][:, :], lhsT=ones_bf[:, :], rhs=sq7[q2][:, :],
                                              start=(n == 0), stop=(n == 15))),
                     reads=[sq7B[q2], Bconst], writes=[bB[6 + tb2]])

            def load_act(half):
                h0_ = half * 1024
                for part in range(4):
                    S.dma("sp", v3(acth[:, :], 44)[:, part * 11:(part + 1) * 11, :],
                          act_scr.rearrange("k p t -> p k t")[:, part * 11:(part + 1) * 11, h0_:h0_ + 1024],
                          reads=[Bact], writes=[acthB], key="acth")

            fctr = [0]

            def final_pass(half):
                h0_ = half * 1024
                for tb2 in range(2):
                    stats_to_rstd(6 + tb2, rsh[half][tb2], rshB[half][tb2])
                yield
                for tb2 in range(2):
                    c0 = h0_ + tb2 * 512
                    for n in range(16):
                        s3 = fctr[0] % 4
                        fctr[0] += 1
                        S.dma("sp", x2i[s3][:, :], x2T_scr[n, :, c0:c0 + 512], reads=[Bx2h[half]], writes=[x2iB[s3]], key=("x2i", s3))
                        S.op("dve", (lambda e, s3=s3, n=n, tb2=tb2: e.scalar_tensor_tensor(
                            out=yo[s3][:, :], in0=x2i[s3][:, :], scalar=Fw[:, n:n + 1], in1=rsh[half][tb2][:, :],
                            op0=ALU.mult, op1=ALU.mult)), reads=[x2iB[s3], FwB, rshB[half][tb2]], writes=[yoB[s3]])
                        S.dma("act", outT[n, :, c0:c0 + 512], yo[s3][:, :], reads=[yoB[s3]], writes=[Bout], key=("yo", s3))
                        yield

            load_act(0)
            fp_gen = None
            for half in range(2):
                h0 = half * 1024
                for nt in range(8):
                    sl = wi % 2
                    wi += 1
                    for part in range(4):
                        S.dma("pool", v3(wd[sl][:, :], 44)[:, part * 11:(part + 1) * 11, :],
                              w_down_v[:, part * 11:(part + 1) * 11, nt * 256:(nt + 1) * 256], writes=[wdB[sl]], key=("wd", sl))
                    for nn in range(2):
                        n = nt * 2 + nn
                        for tb2 in range(2):
                            s3 = it % 3
                            q2 = it % 2
                            it += 1
                            c0 = h0 + tb2 * 512
                            S.dma("sp", x1i[s3][:, :], x1T_scr[n, :, c0:c0 + 512], reads=[Bx1], writes=[x1iB[s3]], key=("x1i", s3))
                            bi = mr.next()
                            mm_acc(bi, lambda k, sl=sl, nn=nn: wd[sl][:, k * 256 + nn * 128:k * 256 + (nn + 1) * 128],
                                   lambda k, tb2=tb2: acth[:, k * 1024 + tb2 * 512:k * 1024 + (tb2 + 1) * 512], 44, [wdB[sl], acthB])
                            S.op("dve", (lambda e, bi=bi, n=n, s3=s3: e.scalar_tensor_tensor(
                                out=x2o[s3][:, :], in0=banks[bi][:, :], scalar=mod_sb[:, 80 + n:81 + n], in1=x1i[s3][:, :],
                                op0=ALU.mult, op1=ALU.add)), reads=[bB[bi], x1iB[s3], Bmod], writes=[x2oB[s3]])
                            S.op("act", (lambda e, s3=s3, q2=q2: e.activation(out=sq7[q2][:, :], in_=x2o[s3][:, :], func=AF.Square)),
                                 reads=[x2oB[s3]], writes=[sq7B[q2]])
                            if pend7 is not None:
                                stat7(*pend7)
                            pend7 = (q2, tb2, n)
                            S.dma("act", x2T_scr[n, :, c0:c0 + 512], x2o[s3][:, :], reads=[x2oB[s3]], writes=[Bx2h[half]], key=("x2o", s3))
                            if fp_gen is not None:
                                next(fp_gen, None)
                stat7(*pend7)
                pend7 = None
                if half == 0:
                    load_act(1)
                    fp_gen = final_pass(0)
                    next(fp_gen)
                else:
                    for _ in fp_gen:
                        pass
                    for _ in final_pass(1):
                        pass
            S.flush()
        return nc


def _pack_cols(v):
    v = np.asarray(v, np.float32).reshape(-1)
    return np.ascontiguousarray(v.reshape(-1, 128).T)


def _token_order(c):
    own = np.arange(TOWN * c, TOWN * (c + 1))
    others = np.concatenate([np.arange(0, TOWN * c), np.arange(TOWN * (c + 1), S_ALL)])
    r0 = 32 * c
    rows_before = np.arange(r0 - 4, r0) if c > 0 else np.arange(4, 8)
    rows_after = np.arange(r0 + 32, r0 + 36) if c < NCORE - 1 else np.arange(248, 252)
    halo_rows = np.concatenate([rows_before, rows_after])
    halo = (halo_rows[:, None] * GRID_W + np.arange(GRID_W)[None, :]).reshape(-1)
    return own, others, halo, halo_rows


def _rope_tables():
    t = np.arange(S_ALL)
    row = (t // GRID_W).astype(np.float32)
    col = (t % GRID_W).astype(np.float32)
    inv = (np.float32(10000.0) ** (-(np.arange(0, 64, 2, dtype=np.float32) / np.float32(64.0)))).astype(np.float32)
    ang = np.concatenate([row[:, None] * inv[None], col[:, None] * inv[None]], axis=-1).astype(np.float32)
    cos = np.cos(ang).astype(np.float32)
    sin = np.sin(ang).astype(np.float32)
    cosT = np.repeat(cos.T, 2, axis=0)
    sinT = np.repeat(sin.T, 2, axis=0)
    sinT[0::2] *= -1.0
    return np.ascontiguousarray(cosT), np.ascontiguousarray(sinT)


def _na_table(c, rpb, halo_rows):
    tab = np.full((8, 20, 128, 640), MASK_NEG, np.float32)
    rows_tot = S_ALL // GRID_W
    for kc in range(20):
        if kc < 16:
            lc = kc
            grow = 32 * c + 2 * kc + np.arange(2)
        elif kc < 18:
            lc = kc - 18
            grow = halo_rows[(kc - 16) * 2:(kc - 16) * 2 + 2]
        else:
            lc = kc - 2
            grow = halo_rows[4 + (kc - 18) * 2:4 + (kc - 18) * 2 + 2]
        key_r = np.repeat(grow, GRID_W)
        key_c = np.tile(np.arange(GRID_W), 2)
        blo = max(0, lc - 2)
        bhi = min(15, lc + 2)
        for b in range(blo, bhi + 1):
            qr = np.repeat(32 * c + 2 * b + np.arange(2), GRID_W)
            qc = np.tile(np.arange(GRID_W), 2)
            rs = np.clip(qr - NA_ROWS // 2, 0, rows_tot - NA_ROWS)
            cs_ = np.clip(qc - NA_COLS // 2, 0, GRID_W - NA_COLS)
            inr = (key_r[:, None] >= rs[None, :]) & (key_r[:, None] < rs[None, :] + NA_ROWS)
            inc = (key_c[:, None] >= cs_[None, :]) & (key_c[:, None] < cs_[None, :] + NA_COLS)
            valid = inr & inc
            if kc >= 16:
                own_lo = 32 * c + 2 * (b - 2)
                own_hi = 32 * c + 2 * (b + 2) + 1
                lo = max(own_lo, 32 * c)
                hi = min(own_hi, 32 * c + 31)
                dup = (key_r >= lo) & (key_r <= hi)
                valid &= ~dup[:, None]
            rel_r = np.clip(key_r[:, None] - qr[None, :] + (NA_ROWS - 1), 0, 2 * NA_ROWS - 2)
            rel_c = np.clip(key_c[:, None] - qc[None, :] + (NA_COLS - 1), 0, 2 * NA_COLS - 2)
            j0 = (b - blo) * 128
            for h in range(8):
                bias = rpb[h][rel_r, rel_c]
                tab[h, kc, :, j0:j0 + 128] = np.where(valid, bias, np.float32(MASK_NEG))
    return tab


def prep_inputs(inp):
    x = np.asarray(inp["x"], np.float32)[0]
    xTfull = np.ascontiguousarray(x.T)
    cosT, sinT = _rope_tables()
    rm = np.zeros((128, 128), np.float32)
    for k in range(128):
        rm[k, k ^ 1] = 1.0
    vec = np.zeros((128, NV), np.float32)
    vec[:, 0:16] = _pack_cols(inp["c"])
    vec[:, 16:112] = _pack_cols(inp["b_ada"])
    vec[:, 112:128] = _pack_cols(inp["norm1_w"])
    vec[:, 128:144] = _pack_cols(inp["norm2_w"])
    vec[:, 144:160] = _pack_cols(inp["final_w"])
    vec[:, 160:161] = _pack_cols(inp["q_norm_w"])
    vec[:, 161:162] = _pack_cols(inp["k_norm_w"])
    rpb = np.asarray(inp["nat_rpb"], np.float32)[0]
    shared = {
        "vecs": vec, "rmat": rm,
        "w_ada": np.ascontiguousarray(np.asarray(inp["w_ada"], np.float32)[0]),
        "w_in": np.ascontiguousarray(np.asarray(inp["w_in"], np.float32)[0]),
        "w_oa": np.ascontiguousarray(np.asarray(inp["w_oa"], np.float32)[0]),
        "w_ob": np.ascontiguousarray(np.asarray(inp["w_ob"], np.float32)[0]),
        "w_out": np.ascontiguousarray(np.asarray(inp["w_out"], np.float32)[0]),
        "w_gate": np.ascontiguousarray(np.asarray(inp["w_ffn_gate"], np.float32)[0]),
        "w_up": np.ascontiguousarray(np.asarray(inp["w_ffn_up"], np.float32)[0]),
        "w_down": np.ascontiguousarray(np.asarray(inp["w_ffn_down"], np.float32)[0]),
    }
    maps = []
    for c in range(NCORE):
        own, others, halo, halo_rows = _token_order(c)
        order = np.concatenate([own, others, halo])
        m = dict(shared)
        m["xT"] = np.ascontiguousarray(xTfull[:, order])
        o2 = order[:S_ALL]
        m["cosT"] = np.ascontiguousarray(cosT[:, o2])
        m["sinT"] = np.ascontiguousarray(sinT[:, o2])
        m["natab"] = _na_table(c, rpb, halo_rows)
        maps.append(m)
    return maps


_NC_CACHE = {}


def kernel(**inputs):
    maps = prep_inputs(inputs)
    if "nc" not in _NC_CACHE:
        _NC_CACHE["nc"] = build_program()
    nc = _NC_CACHE["nc"]
    res = run_bass_kernel_spmd(nc, maps, core_ids=list(range(NCORE)))
    outs = []
    for c in range(NCORE):
        o = np.asarray(res.results[c]["outT"], np.float32).reshape(D, TOWN)
        outs.append(o.T)
    return np.ascontiguousarray(np.concatenate(outs, axis=0)[None].astype(np.float32))
```

```python
import math
from contextlib import ExitStack

import numpy as np
import concourse.bass as bass
import concourse.mybir as mybir
from concourse.bass_utils import run_bass_kernel_spmd

F32 = mybir.dt.float32
BF16 = mybir.dt.bfloat16
AF = mybir.ActivationFunctionType
ALU = mybir.AluOpType

D = 2048
S_ALL = 16384
NCORE = 8
TOWN = 2048
THALO = 512
TLOC = TOWN + THALO
NTOK_IN = S_ALL + THALO
D_IN = 8704
D_FF = 5632
EPS = 1e-6
NV = 162
GRID_W = 64
NA_ROWS = 8
NA_COLS = 16
MASK_NEG = -30000.0


class Buf:
    __slots__ = ("name", "w", "rs", "rd")

    def __init__(self, name):
        self.name = name
        self.w = None
        self.rs = {}
        self.rd = []


class Op:
    __slots__ = ("eng", "fn", "deps", "signaled", "token", "key", "idx", "epoch")

    def __init__(self, eng, fn, key, idx, epoch):
        self.eng = eng
        self.fn = fn
        self.key = key
        self.deps = set()
        self.signaled = key is not None
        self.token = None
        self.idx = idx
        self.epoch = epoch


ENGS = ("pe", "act", "dve", "pool", "sp")


class Sched:
    def __init__(self, nc, es):
        self.nc = nc
        self.es = es
        self.sem = {e: es.enter_context(nc.semaphore("s_" + e)) for e in ENGS}
        self.cnt = {e: 0 for e in ENGS}
        self.keysem = {}
        self.keyidx = {}
        self.sempool = []
        self.poolcnt = []
        self.poolkind = []
        self.waited = {e: {} for e in ENGS}
        self.epoch = 0
        self.ops = {e: [] for e in ENGS}
        self.nidx = {e: 0 for e in ENGS}
        self.allops = []

    def _mk(self, eng, fn, reads, writes, key):
        op = Op(eng, fn, key, self.nidx[eng], self.epoch)
        self.nidx[eng] += 1
        deps = op.deps
        for b in reads:
            if b.w is not None:
                deps.add(b.w)
        for b in writes:
            if b.w is not None:
                deps.add(b.w)
            deps.update(b.rs.values())
            deps.update(b.rd)
        for d in list(deps):
            if d.key is None and key is None and d.eng == eng and (eng == "pe" or op.idx - d.idx > 2):
                deps.discard(d)
        for d in deps:
            d.signaled = True
        for b in reads:
            if key is not None:
                b.rd.append(op)
            else:
                b.rs[eng] = op
        for b in writes:
            b.w = op
            b.rs = {}
            b.rd = []
        self.ops[eng].append(op)
        self.allops.append(op)
        return op

    def op(self, eng, fn, reads=(), writes=()):
        return self._mk(eng, fn, reads, writes, None)

    def dma(self, eng, out, in_, reads=(), writes=(), key=None):
        assert key is not None
        kind = "sw" if eng == "pool" else "hw"
        key = (kind, key)
        if key not in self.keysem:
            free = [i for i in range(len(self.sempool)) if self.poolkind[i] == kind and i not in self.keyidx.values()]
            if free:
                i = free[0]
            else:
                i = len(self.sempool)
                self.sempool.append(self.es.enter_context(self.nc.semaphore("k%s%d" % (kind, i))))
                self.poolcnt.append(0)
                self.poolkind.append(kind)
            self.keysem[key] = self.sempool[i]
            self.keyidx[key] = i
        return self._mk(eng, lambda e: e.dma_start(out=out, in_=in_), reads, writes, key)

    def flush(self, final=False):
        nc = self.nc
        last = {}
        for e in ENGS:
            for op in reversed(self.ops[e]):
                if op.key is None and op.fn is not None:
                    op.signaled = True
                    last[e] = op
                    break
        for op in self.allops:
            if op.key is not None:
                i = self.keyidx[op.key]
                self.poolcnt[i] += 16
                op.token = (self.sempool[i], self.poolcnt[i])
            elif op.signaled:
                self.cnt[op.eng] += 1
                op.token = (self.sem[op.eng], self.cnt[op.eng])
        bar = [(self.sem[e], self.cnt[e]) for e in ENGS if self.cnt[e] > 0]
        bar += [(self.sempool[i], self.poolcnt[i]) for i in range(len(self.sempool)) if self.poolcnt[i] > 0]
        ops = self.ops
        waited = self.waited
        sems = self.sem
        epoch = self.epoch

        def emit(ename, e):
            w = waited[ename]
            for op in ops[ename]:
                for d in op.deps:
                    if d.epoch != epoch:
                        continue
                    if d.key is None and d.eng == ename:
                        if ename == "pe" or op.idx - d.idx > 2:
                            continue
                    s, v = d.token
                    if w.get(id(s), 0) < v:
                        e.wait_ge(s, v)
                        w[id(s)] = v
                if op.fn is None:
                    continue
                ins = op.fn(e)
                if op.key is not None:
                    ins.then_inc(op.token[0], 16)
                elif op.signaled:
                    ins.then_inc(sems[ename], 1)
            for s, v in bar:
                k = id(s)
                if w.get(k, 0) < v:
                    e.wait_ge(s, v)
                    w[k] = v

        with nc.Block() as block:
            @block.tensor
            def _(e):
                emit("pe", e)

            @block.scalar
            def _(e):
                emit("act", e)

            @block.vector
            def _(e):
                emit("dve", e)

            @block.gpsimd
            def _(e):
                emit("pool", e)

            @block.sync
            def _(e):
                emit("sp", e)

        self.ops = {e: [] for e in ENGS}
        self.allops = []
        self.epoch += 1
        self.keysem = {}
        self.keyidx = {}


class Ring:
    def __init__(self, items):
        self.items = items
        self.i = 0

    def next(self):
        it = self.items[self.i % len(self.items)]
        self.i += 1
        return it


def v3(ap, k):
    return ap.rearrange("p (k t) -> p k t", k=k)


def build_program(debug_outs=(), stop_after=99):
    nc = bass.Bass("TRN2", target_bir_lowering=False)

    def din(name, shape, dt=F32):
        return nc.dram_tensor(name, list(shape), dt, kind="ExternalInput").ap()

    def dscr(name, shape, dt=BF16):
        kind = "ExternalOutput" if name in debug_outs else "Internal"
        return nc.dram_tensor(name, list(shape), dt, kind=kind).ap()

    xT = din("xT", [D, NTOK_IN])
    cosT = din("cosT", [128, S_ALL])
    sinT = din("sinT", [128, S_ALL])
    vecs = din("vecs", [128, NV])
    rmat = din("rmat", [128, 128])
    natab = din("natab", [8, 20, 128, 640])
    w_ada = din("w_ada", [D, 6 * D])
    w_in = din("w_in", [D, D_IN])
    w_oa = din("w_oa", [1024, D])
    w_ob = din("w_ob", [1024, D])
    w_out = din("w_out", [D, D])
    w_gate = din("w_gate", [D, D_FF])
    w_up = din("w_up", [D, D_FF])
    w_down = din("w_down", [D_FF, D])
    outT = nc.dram_tensor("outT", [16, 128, TOWN], F32, kind="ExternalOutput").ap()

    hT_scr = dscr("hT_scr", [128, 16 * TLOC])
    KT_scr = dscr("KT_scr", [2, 128, S_ALL])
    V_scr = dscr("V_scr", [2, 128, 128 * 128])
    qT_scr = dscr("qT_scr", [8, 128, TOWN])
    qBT_scr = dscr("qBT_scr", [8, 128, TOWN])
    kBT_scr = dscr("kBT_scr", [8, 128, TLOC])
    vB_scr = dscr("vB_scr", [8, 128, 20 * 128])
    sg_scr = dscr("sg_scr", [32, 128, TOWN])
    bv_scr = dscr("bv_scr", [128, 16], F32)
    yT_scr = dscr("yT_scr", [16, 128, TOWN])
    x1T_scr = dscr("x1T_scr", [16, 128, TOWN], F32)
    h2T_scr = dscr("h2T_scr", [128, 16 * TOWN])
    act_scr = dscr("act_scr", [44, 128, TOWN])
    x2T_scr = dscr("x2T_scr", [16, 128, TOWN], F32)

    w_ada_v = w_ada.rearrange("(k p) n -> p k n", p=128)
    w_in_v = w_in.rearrange("(k p) n -> p k n", p=128)
    xT_v = xT.rearrange("(k p) t -> p k t", p=128)

    with ExitStack() as ges:
        S = Sched(nc, ges)

        def sb(es, name, shape, dt):
            return es.enter_context(nc.sbuf_tensor(name, list(shape), dt))

        vecs_sb = sb(ges, "vecs_sb", [128, NV], F32)
        mod_sb = sb(ges, "mod_sb", [128, 96], F32)
        A1 = sb(ges, "A1", [128, 16], F32)
        A2 = sb(ges, "A2", [128, 16], F32)
        B1bf = sb(ges, "B1bf", [128, 16], BF16)
        B2bf = sb(ges, "B2bf", [128, 16], BF16)
        ones_bf = sb(ges, "ones_bf", [128, 128], BF16)
        rmat_bf = sb(ges, "rmat_bf", [128, 128], BF16)
        kw = sb(ges, "kw", [128, 1], F32)
        epsA = sb(ges, "epsA", [128, 1], F32)
        epsB = sb(ges, "epsB", [128, 1], F32)
        bvcol = sb(ges, "bvcol", [128, 16], F32)
        csil = sb(ges, "csil", [128, 16], BF16)
        dbl = [ges.enter_context(nc.psum_tensor("dbank%d" % i, [128, 1024], F32)) for i in range(4)]
        banks = [dbl[i // 2][:, (i % 2) * 512:(i % 2) * 512 + 512] for i in range(8)]
        bB = [Buf("bank%d" % i) for i in range(8)]
        Bvecs, Bmod, Bconst, Bcsil, Bbv = Buf("vecs"), Buf("mod"), Buf("const"), Buf("csil"), Buf("bv")

        C_C, C_BADA, C_N1, C_N2, C_FW, C_QW, C_KW = 0, 16, 112, 128, 144, 160, 161

        with ExitStack() as es:
            wts = [sb(es, "wada%d" % i, [128, 16 * 512], BF16) for i in range(3)]
            wB = [Buf("wada%d" % i) for i in range(3)]
            S.dma("sp", vecs_sb[:, :], vecs[:, :], writes=[Bvecs], key="vecs")
            S.dma("pool", rmat_bf[:, :], rmat[:, :], writes=[Bconst], key="rmat")
            S.op("dve", lambda e: e.memset(ones_bf[:, :], 1.0), writes=[Bconst])
            S.op("dve", lambda e: e.memset(epsA[:, :], D * EPS), writes=[Bconst])
            S.op("dve", lambda e: e.memset(epsB[:, :], 128 * EPS), writes=[Bconst])
            S.op("act", lambda e: e.activation(out=csil[:, :], in_=vecs_sb[:, C_C:C_C + 16], func=AF.Silu),
                 reads=[Bvecs], writes=[Bcsil])
            ps_mod = banks[7]
            for i in range(8):
                sl = i % 3
                wt = wts[sl]
                S.dma("pool", v3(wt[:, :], 16), w_ada_v[:, :, i * 512:(i + 1) * 512], writes=[wB[sl]],
                      key=("wada", sl))
                for jj in range(4):
                    j = i * 4 + jj
                    for k in range(16):
                        S.op("pe", (lambda e, wt=wt, k=k, jj=jj, j=j: e.matmul(
                            ps_mod[:, j:j + 1], lhsT=wt[:, k * 512 + jj * 128:k * 512 + (jj + 1) * 128],
                            rhs=csil[:, k:k + 1], start=(k == 0), stop=(k == 15))),
                            reads=[wB[sl], Bcsil], writes=[bB[7]])
            S.op("dve", lambda e: e.tensor_tensor(out=mod_sb[:, 0:32], in0=ps_mod[:, 0:32],
                                                  in1=vecs_sb[:, C_BADA:C_BADA + 32], op=ALU.add),
                 reads=[bB[7], Bvecs], writes=[Bmod])
            for (A, sc0, nw0) in ((A1, 16, C_N1),):
                S.op("dve", (lambda e, A=A, sc0=sc0, nw0=nw0: e.scalar_tensor_tensor(
                    out=A[:, :], in0=mod_sb[:, sc0:sc0 + 16], scalar=1.0, in1=vecs_sb[:, nw0:nw0 + 16],
                    op0=ALU.add, op1=ALU.mult)), reads=[Bmod, Bvecs], writes=[Bmod])
                S.op("pool", (lambda e, A=A: e.tensor_scalar(out=A[:, :], in0=A[:, :], scalar1=math.sqrt(D),
                                                            scalar2=None, op0=ALU.mult)),
                     reads=[Bmod], writes=[Bmod])
            S.op("dve", lambda e: e.tensor_copy(out=B1bf[:, :], in_=mod_sb[:, 0:16]), reads=[Bmod], writes=[Bmod])
            S.op("pool", lambda e: e.tensor_scalar(out=kw[:, :], in0=vecs_sb[:, C_KW:C_KW + 1],
                                                   scalar1=math.sqrt(128.0), scalar2=None, op0=ALU.mult),
                 reads=[Bvecs], writes=[Bmod])
            S.flush()
        if stop_after <= 0:
            return nc

        def norm_rope_multi(items, use_sqrt=False):
            chain_part1(items)
            chain_part2(items, use_sqrt)

        def chain_part1(items):
            for it_ in items:
                T = it_["T"]
                S.op("act", lambda e, it_=it_, T=T: e.activation(out=T["sq"][:, :], in_=it_["ps"][:, :], func=AF.Square,
                                                               bias=it_["bias"], scale=1.0),
                     reads=[it_["psB"]] + list(it_["extra"]), writes=[T["sqB"]])
                S.op("dve", lambda e, it_=it_, T=T: e.tensor_scalar(out=T["raw"][:, :], in0=it_["ps"][:, :], scalar1=it_["bias"],
                                                                  scalar2=None, op0=ALU.add),
                     reads=[it_["psB"], T["sqB"]] + list(it_["extra"]), writes=[T["rawB"]])

        def chain_part2(items, use_sqrt=False):
            for it_ in items:
                T = it_["T"]
                S.op("pe", lambda e, it_=it_, T=T: e.matmul(banks[it_["sumb"]][:, :], lhsT=ones_bf[:, :], rhs=T["sq"][:, :],
                                                          start=True, stop=True),
                     reads=[T["sqB"], Bconst], writes=[bB[it_["sumb"]]])
            for it_ in items:
                T = it_["T"]
                S.op("act", lambda e, it_=it_, T=T: e.activation(out=T["rs"][:, :], in_=banks[it_["sumb"]][:, :],
                                                               func=(AF.Sqrt if use_sqrt else AF.Ln),
                                                               bias=epsB[:, 0:1], scale=1.0),
                     reads=[bB[it_["sumb"]], Bconst], writes=[T["rsB"]])
            for it_ in items:
                T = it_["T"]
                if use_sqrt:
                    S.op("dve", lambda e, T=T: e.reciprocal(out=T["rs"][:, :], in_=T["rs"][:, :]),
                         reads=[T["rsB"]], writes=[T["rsB"]])
                else:
                    S.op("act", lambda e, T=T: e.activation(out=T["rs"][:, :], in_=T["rs"][:, :], func=AF.Exp, scale=-0.5),
                         reads=[T["rsB"]], writes=[T["rsB"]])
            for it_ in items:
                T = it_["T"]
                S.op("dve", lambda e, it_=it_, T=T: e.scalar_tensor_tensor(out=T["n"][:, :], in0=T["raw"][:, :], scalar=it_["w"],
                                                                         in1=T["rs"][:, :], op0=ALU.mult, op1=ALU.mult),
                     reads=[T["rawB"], T["rsB"], Bmod, Bvecs], writes=[T["nB"]])
            for it_ in items:
                T = it_["T"]
                S.op("pe", lambda e, it_=it_, T=T: e.matmul(banks[it_["rotb"]][:, :], lhsT=rmat_bf[:, :], rhs=T["n"][:, :],
                                                          start=True, stop=True),
                     reads=[T["nB"], Bconst], writes=[bB[it_["rotb"]]])
            for it_ in items:
                T = it_["T"]
                S.op("pool", lambda e, it_=it_, T=T: e.tensor_tensor(out=T["t1"][:, :], in0=T["n"][:, :], in1=it_["cos"], op=ALU.mult),
                     reads=[T["nB"], it_["tabB"]], writes=[T["t1B"]])
            for it_ in items:
                T = it_["T"]
                S.op("dve", lambda e, it_=it_, T=T: e.tensor_tensor(out=T["t2"][:, :], in0=banks[it_["rotb"]][:, :], in1=it_["sin"],
                                                                  op=ALU.mult),
                     reads=[bB[it_["rotb"]], it_["tabB"]], writes=[T["t2B"]])
            for it_ in items:
                T = it_["T"]
                S.op("pool", lambda e, it_=it_, T=T: e.tensor_tensor(out=it_["out"], in0=T["t1"][:, :], in1=T["t2"][:, :], op=ALU.add),
                     reads=[T["t1B"], T["t2B"]], writes=[it_["outB"]])
                if it_.get("after") is not None:
                    it_["after"]()

        def norm_rope(ps, psB, bias_ap, w_ap, cos_ap, sin_ap, tabB, T, out_ap, outB, extra_reads=()):
            norm_rope_multi([dict(ps=ps, psB=psB, bias=bias_ap, w=w_ap, cos=cos_ap, sin=sin_ap, tabB=tabB, T=T, out=out_ap,
                                  outB=outB, extra=extra_reads, sumb=3, rotb=4, after=None)])

        def chain_tiles_alt(es, T, pfx):
            T2 = dict(T)
            T2["raw"] = sb(es, pfx + "raw", [128, 512], F32)
            T2["sq"] = sb(es, pfx + "sq", [128, 512], BF16)
            T2["rawB"] = Buf(pfx + "raw")
            T2["sqB"] = Buf(pfx + "sq")
            return T2

        def mk_chain_tiles(es, pfx):
            T = {}
            T["raw"] = sb(es, pfx + "raw", [128, 512], F32)
            T["sq"] = sb(es, pfx + "sq", [128, 512], BF16)
            T["rs"] = sb(es, pfx + "rs", [128, 512], F32)
            T["n"] = sb(es, pfx + "n", [128, 512], BF16)
            T["t1"] = sb(es, pfx + "t1", [128, 512], F32)
            T["t2"] = sb(es, pfx + "t2", [128, 512], F32)
            for k in ("raw", "sq", "rs", "n", "t1", "t2"):
                T[k + "B"] = Buf(pfx + k)
            return T

        def bias_cols(wt, wB_, ncol, Bbf, dst_ap, dstB, col_stride):
            for c in range(ncol):
                for k in range(16):
                    S.op("pe", (lambda e, c=c, k=k: e.matmul(
                        banks[7][:, c:c + 1], lhsT=wt[:, k * col_stride + c * 128:k * col_stride + (c + 1) * 128],
                        rhs=Bbf[:, k:k + 1], start=(k == 0), stop=(k == 15))),
                        reads=[wB_, Bmod], writes=[bB[7]])
            S.op("dve", lambda e: e.tensor_copy(out=dst_ap, in_=banks[7][:, 0:ncol]), reads=[bB[7]], writes=[dstB])

        with ExitStack() as es:
            xs = [sb(es, "xs%d" % i, [128, 16 * 512], F32) for i in range(2)]
            xsB = [Buf("xs%d" % i) for i in range(2)]
            sq = sb(es, "sq", [128, 16 * 512], BF16)
            sqB = Buf("sq")
            xsP = [[Buf("xs%d_%d" % (i, j)) for j in range(4)] for i in range(2)]
            sqP = [Buf("sq_%d" % j) for j in range(4)]
            hT = [sb(es, "hT%d" % i, [128, 16 * 512], BF16) for i in range(2)]
            hTB = [Buf("hT%d" % i) for i in range(2)]
            wkv = sb(es, "wkv", [128, 16 * 512], BF16)
            wkvB = Buf("wkv")
            rstd = sb(es, "rstd", [128, 512], F32)
            rstdB = Buf("rstd")
            cs = [sb(es, "cs%d" % i, [128, 1024], F32) for i in range(2)]
            csB = [Buf("cs%d" % i) for i in range(2)]
            kout = [sb(es, "kout%d" % i, [128, 512], BF16) for i in range(2)]
            koutB = [Buf("kout%d" % i) for i in range(2)]
            vout = [sb(es, "vout%d" % i, [128, 1024], BF16) for i in range(2)]
            voutB = [Buf("vout%d" % i) for i in range(2)]
            bk = sb(es, "bk", [128, 2], F32)
            bkB = Buf("bk")
            TT = [mk_chain_tiles(es, "c1a"), mk_chain_tiles(es, "c1b")]
            TTp = [TT, [chain_tiles_alt(es, TT[0], "c1c"), chain_tiles_alt(es, TT[1], "c1d")]]
            BKT = [Buf("KT0"), Buf("KT1")]
            BV = [Buf("V0"), Buf("V1")]
            BhT = Buf("hTscr")

            S.dma("pool", v3(wkv[:, :], 16), w_in_v[:, :, 1024:1536], writes=[wkvB], key="wkv")
            bias_cols(wkv, wkvB, 2, B1bf, bk[:, 0:2], bkB, 512)
            for c in range(2):
                for k in range(16):
                    S.op("pe", (lambda e, c=c, k=k: e.matmul(
                        banks[7][:, 8 + c:9 + c], lhsT=wkv[:, k * 512 + 256 + c * 128:k * 512 + 256 + (c + 1) * 128],
                        rhs=B1bf[:, k:k + 1], start=(k == 0), stop=(k == 15))),
                        reads=[wkvB, Bmod], writes=[bB[7]])
            S.op("dve", lambda e: e.tensor_copy(out=bvcol[:, 0:2], in_=banks[7][:, 8:10]), reads=[bB[7]], writes=[Bbv])

            NB1 = 33

            def stageA1a(tb):
                sl = tb % 2
                t0 = tb * 512
                for p_ in range(4):
                    S.dma("sp", v3(xs[sl][:, :], 16)[:, 4 * p_:4 * p_ + 4, :], xT_v[:, 4 * p_:4 * p_ + 4, t0:t0 + 512],
                          writes=[xsP[sl][p_]], key=("xs", sl, p_))
                    S.op("act", (lambda e, p_=p_: e.activation(out=sq[:, p_ * 2048:(p_ + 1) * 2048],
                                                              in_=xs[sl][:, p_ * 2048:(p_ + 1) * 2048], func=AF.Square)),
                         reads=[xsP[sl][p_]], writes=[sqP[p_]])

            def stageA1b(tb):
                for k in range(16):
                    S.op("pe", (lambda e, k=k: e.matmul(banks[0][:, :], lhsT=ones_bf[:, :],
                                                       rhs=sq[:, k * 512:(k + 1) * 512],
                                                       start=(k == 0), stop=(k == 15))),
                         reads=[sqP[k // 4], Bconst], writes=[bB[0]])

            def stageA2(tb):
                sl = tb % 2
                S.op("act", lambda e: e.activation(out=rstd[:, :], in_=banks[0][:, :], func=AF.Ln,
                                                   bias=epsA[:, 0:1], scale=1.0),
                     reads=[bB[0], Bconst], writes=[rstdB])
                S.op("act", lambda e: e.activation(out=rstd[:, :], in_=rstd[:, :], func=AF.Exp, scale=-0.5),
                     reads=[rstdB], writes=[rstdB])
                for k in range(16):
                    S.op("dve", (lambda e, k=k: e.scalar_tensor_tensor(
                        out=hT[sl][:, k * 512:(k + 1) * 512], in0=xs[sl][:, k * 512:(k + 1) * 512],
                        scalar=A1[:, k:k + 1], in1=rstd[:, :], op0=ALU.mult, op1=ALU.mult)),
                        reads=[xsP[sl][k // 4], rstdB, Bmod], writes=[hTB[sl]])

            def stageB(tb):
                sl = tb % 2
                t0 = tb * 512
                own = tb < 4 or tb == 32
                if own:
                    lt0 = t0 if tb < 4 else TOWN
                    S.dma("pool", v3(hT_scr[:, :], 16)[:, :, lt0:lt0 + 512], v3(hT[sl][:, :], 16),
                          reads=[hTB[sl]], writes=[BhT], key="hTst")
                if tb == 32:
                    return None
                S.dma("sp", cs[sl][:, 0:512], cosT[:, t0:t0 + 512], writes=[csB[sl]], key=("cs", sl))
                S.dma("sp", cs[sl][:, 512:1024], sinT[:, t0:t0 + 512], writes=[csB[sl]], key=("cs", sl))
                for g in range(2):
                    for k in range(16):
                        S.op("pe", (lambda e, g=g, k=k: e.matmul(
                            banks[1 + g][:, :], lhsT=wkv[:, k * 512 + g * 128:k * 512 + (g + 1) * 128],
                            rhs=hT[sl][:, k * 512:(k + 1) * 512], start=(k == 0), stop=(k == 15))),
                            reads=[wkvB, hTB[sl]], writes=[bB[1 + g]])
                for s in range(4):
                    bank = 5 + s // 2
                    c0 = (s % 2) * 256
                    for k in range(16):
                        S.op("pe", (lambda e, s=s, k=k, bank=bank, c0=c0: e.matmul(
                            banks[bank][:, c0:c0 + 256], lhsT=hT[sl][:, k * 512 + s * 128:k * 512 + (s + 1) * 128],
                            rhs=wkv[:, k * 512 + 256:k * 512 + 512], start=(k == 0), stop=(k == 15))),
                            reads=[wkvB, hTB[sl]], writes=[bB[bank]])
                vo = vout[sl]
                S.op("act", lambda e: e.activation(out=vo[:, 0:512], in_=banks[5][:, :], func=AF.Copy),
                     reads=[bB[5]], writes=[voutB[sl]])
                S.op("dve", lambda e: e.tensor_copy(out=vo[:, 512:1024], in_=banks[6][:, :]),
                     reads=[bB[6]], writes=[voutB[sl]])
                for g in range(2):
                    S.dma("pool", v3(V_scr[g, :, tb * 512:(tb + 1) * 512], 4),
                          v3(vo[:, :], 4)[:, :, g * 128:(g + 1) * 128],
                          reads=[voutB[sl]], writes=[BV[g]], key=("vst", sl))
                items = []
                for g in range(2):
                    ko = kout[g]
                    items.append(dict(
                        ps=banks[1 + g], psB=bB[1 + g], bias=bk[:, g:g + 1], w=kw[:, 0:1], cos=cs[sl][:, 0:512],
                        sin=cs[sl][:, 512:1024], tabB=csB[sl], T=TTp[sl][g], out=ko[:, :], outB=koutB[g], extra=[bkB],
                        sumb=(3, 4)[g], rotb=(3, 4)[g],
                        after=(lambda g=g, ko=ko: S.dma("pool", KT_scr[g, :, t0:t0 + 512], ko[:, :], reads=[koutB[g]],
                                                        writes=[BKT[g]], key=("kst", g)))))
                chain_part1(items)
                return items

            wad = [sb(es, "wad%d" % i, [128, 16 * 256], BF16) for i in range(2)]
            wadB = [Buf("wad0"), Buf("wad1")]
            Bmod2 = Buf("mod2")

            def mod_rest(i):
                sl_ = i % 2
                c0 = 4096 + 256 * i
                S.dma("pool", v3(wad[sl_][:, :], 16), w_ada_v[:, :, c0:c0 + 256], writes=[wadB[sl_]], key=("wad", sl_))
                for jj in range(2):
                    j = 32 + 2 * i + jj
                    for k in range(16):
                        S.op("pe", (lambda e, k=k, jj=jj, j=j: e.matmul(
                            banks[7][:, j:j + 1], lhsT=wad[sl_][:, k * 256 + jj * 128:k * 256 + (jj + 1) * 128],
                            rhs=csil[:, k:k + 1], start=(k == 0), stop=(k == 15))),
                            reads=[wadB[sl_], Bcsil], writes=[bB[7]])

            stageA1a(0)
            stageA1b(0)
            stageA2(0)
            stageA1a(1)
            stageA1b(1)
            pending = None
            for tb in range(NB1):
                if tb + 1 < NB1:
                    stageA2(tb + 1)
                if tb + 2 < NB1:
                    stageA1a(tb + 2)
                items = stageB(tb)
                if tb + 2 < NB1:
                    stageA1b(tb + 2)
                if pending is not None:
                    chain_part2(pending)
                pending = items
                if tb < 32:
                    mod_rest(tb)
            if pending is not None:
                chain_part2(pending)
            S.op("dve", lambda e: e.tensor_tensor(out=mod_sb[:, 32:96], in0=banks[7][:, 32:96],
                                                  in1=vecs_sb[:, C_BADA + 32:C_BADA + 96], op=ALU.add),
                 reads=[bB[7], Bvecs], writes=[Bmod2])
            S.op("dve", lambda e: e.scalar_tensor_tensor(out=A2[:, :], in0=mod_sb[:, 64:80], scalar=1.0,
                                                         in1=vecs_sb[:, C_N2:C_N2 + 16], op0=ALU.add, op1=ALU.mult),
                 reads=[Bmod2, Bvecs], writes=[Bmod2])
            S.op("pool", lambda e: e.tensor_scalar(out=A2[:, :], in0=A2[:, :], scalar1=math.sqrt(D), scalar2=None, op0=ALU.mult),
                 reads=[Bmod2], writes=[Bmod2])
            S.op("dve", lambda e: e.tensor_copy(out=B2bf[:, :], in_=mod_sb[:, 48:64]), reads=[Bmod2], writes=[Bmod2])
            S.flush()
        if stop_after <= 1:
            return nc

        def load_wtile(dst, dstB, view, c0, ncols, nk, key):
            S.dma("pool", v3(dst[:, 0:nk * ncols], nk), view[:, :, c0:c0 + ncols], writes=[dstB], key=key)

        def mm_acc(bank_i, lhs_fn, rhs_fn, nk, reads, out_ap=None):
            for k in range(nk):
                S.op("pe", (lambda e, k=k: e.matmul(out_ap if out_ap is not None else banks[bank_i][:, :],
                                                   lhsT=lhs_fn(k), rhs=rhs_fn(k), start=(k == 0), stop=(k == nk - 1))),
                     reads=reads, writes=[bB[bank_i]])

        BqT, BqBT, BkBT, BvB, Bsg = Buf("qT"), Buf("qBT"), Buf("kBT"), Buf("vB"), Buf("sg")
        with ExitStack() as es:
            hTo = sb(es, "hTo", [128, 16 * TLOC], BF16)
            hToB = Buf("hTo")
            wts = [sb(es, "w2_%d" % i, [128, 16 * 512], BF16) for i in range(3)]
            wtB = [Buf("w2_%d" % i) for i in range(3)]
            cso = sb(es, "cso", [128, 2 * TOWN], F32)
            csoB = Buf("cso")
            TT2 = [mk_chain_tiles(es, "c2a"), mk_chain_tiles(es, "c2b")]
            TT2p = [TT2, [chain_tiles_alt(es, TT2[0], "c2c"), chain_tiles_alt(es, TT2[1], "c2d")]]
            qitems = []
            qpend = [None]
            qpair = [0]
            ot = [sb(es, "ot%d" % i, [128, 512], BF16) for i in range(4)]
            otB = [Buf("ot%d" % i) for i in range(4)]
            bc = [sb(es, "bc%d" % i, [128, 4], F32) for i in range(2)]
            bcB = [Buf("bc%d" % i) for i in range(2)]
            hToP = [Buf("hTo%d" % i) for i in range(5)]
            for tbp in range(5):
                S.dma("sp", v3(hTo[:, :], 16)[:, :, tbp * 512:(tbp + 1) * 512], v3(hT_scr[:, :], 16)[:, :, tbp * 512:(tbp + 1) * 512],
                      reads=[BhT], writes=[hToP[tbp]], key=("hTo", tbp))
            S.dma("sp", cso[:, 0:TOWN], cosT[:, 0:TOWN], writes=[csoB], key="cso")
            S.dma("sp", cso[:, TOWN:2 * TOWN], sinT[:, 0:TOWN], writes=[csoB], key="cso")
            order = [0, 1, 3, 4, 5, 6, 7, 8] + list(range(9, 17))
            mb = Ring([0, 1, 2, 5, 6])
            oti = 0
            for n_i, ti in enumerate(order):
                sl = n_i % 3
                wt, wB_ = wts[sl], wtB[sl]
                load_wtile(wt, wB_, w_in_v, ti * 512, 512, 16, ("w2", sl))
                if ti in (7, 8):
                    hv = ti - 7
                    bias_cols(wt, wB_, 4, B1bf, bvcol[:, 2 + 4 * hv:6 + 4 * hv], Bbv, 512)
                    for s_ in range(TLOC // 128):
                        bi = mb.next()
                        mm_acc(bi, lambda k, s_=s_: hTo[:, k * TLOC + s_ * 128:k * TLOC + (s_ + 1) * 128],
                               lambda k, wt=wt: wt[:, k * 512:(k + 1) * 512], 16, [hToP[s_ // 4], wB_])
                        o_, oB_ = ot[oti % 4], otB[oti % 4]
                        oti += 1
                        if s_ % 2 == 0:
                            S.op("act", (lambda e, o_=o_, bi=bi: e.activation(out=o_[:, :], in_=banks[bi][:, :], func=AF.Copy)),
                                 reads=[bB[bi]], writes=[oB_])
                        else:
                            S.op("dve", (lambda e, o_=o_, bi=bi: e.tensor_copy(out=o_[:, :], in_=banks[bi][:, :])),
                                 reads=[bB[bi]], writes=[oB_])
                        S.dma("sp", vB_scr[4 * hv:4 * hv + 4, :, s_ * 128:(s_ + 1) * 128].rearrange("h p d -> p h d"),
                              v3(o_[:, :], 4), reads=[oB_], writes=[BvB], key=("ot", oti % 4))
                    continue
                bcs, bcsB = bc[n_i % 2], bcB[n_i % 2]
                bias_cols(wt, wB_, 4, B1bf, bcs[:, 0:4], bcsB, 512)
                for cc in range(4):
                    ntb = 5 if ti in (5, 6) else 4
                    for tb in range(ntb):
                        bi = mb.next()
                        mm_acc(bi, lambda k, wt=wt, cc=cc: wt[:, k * 512 + cc * 128:k * 512 + (cc + 1) * 128],
                               lambda k, tb=tb: hTo[:, k * TLOC + tb * 512:k * TLOC + (tb + 1) * 512], 16, [hToP[tb], wB_])
                        o_, oB_ = ot[oti % 4], otB[oti % 4]
                        oti += 1
                        okey = ("ot", oti % 4)
                        if ti in (0, 1):
                            h = ti * 4 + cc
                            qitems.append(dict(
                                ps=banks[bi], psB=bB[bi], bias=bcs[:, cc:cc + 1], w=vecs_sb[:, C_QW:C_QW + 1],
                                cos=cso[:, tb * 512:(tb + 1) * 512], sin=cso[:, TOWN + tb * 512:TOWN + (tb + 1) * 512],
                                tabB=csoB, T=TT2p[qpair[0] % 2][tb % 2], out=o_[:, :], outB=oB_, extra=[bcsB],
                                sumb=(3, 7)[tb % 2], rotb=(4, 3)[tb % 2],
                                after=(lambda h=h, tb=tb, o_=o_, oB_=oB_, okey=okey: S.dma(
                                    "sp", qT_scr[h, :, tb * 512:(tb + 1) * 512], o_[:, :], reads=[oB_], writes=[BqT], key=okey))))
                            if len(qitems) == 2:
                                chain_part1(qitems)
                                if qpend[0] is not None:
                                    chain_part2(qpend[0])
                                qpend[0] = qitems
                                qitems = []
                                qpair[0] += 1
                                if ti == 1 and cc == 3 and tb == 3:
                                    chain_part2(qpend[0])
                                    qpend[0] = None
                        elif ti in (3, 4, 5, 6):
                            S.op("act", (lambda e, o_=o_, bi=bi, cc=cc, bcs=bcs: e.activation(
                                out=o_[:, :], in_=banks[bi][:, :], func=AF.Identity, bias=bcs[:, cc:cc + 1], scale=1.0)),
                                reads=[bB[bi], bcsB], writes=[oB_])
                            if ti in (3, 4):
                                h = (ti - 3) * 4 + cc
                                S.dma("sp", qBT_scr[h, :, tb * 512:(tb + 1) * 512], o_[:, :], reads=[oB_], writes=[BqBT], key=okey)
                            else:
                                h = (ti - 5) * 4 + cc
                                S.dma("sp", kBT_scr[h, :, tb * 512:(tb + 1) * 512], o_[:, :], reads=[oB_], writes=[BkBT], key=okey)
                        else:
                            gi = (ti - 9) * 4 + cc
                            S.op("act", (lambda e, o_=o_, bi=bi, cc=cc, bcs=bcs: e.activation(
                                out=o_[:, :], in_=banks[bi][:, :], func=AF.Sigmoid, bias=bcs[:, cc:cc + 1], scale=1.0)),
                                reads=[bB[bi], bcsB], writes=[oB_])
                            S.dma("sp", sg_scr[gi, :, tb * 512:(tb + 1) * 512], o_[:, :], reads=[oB_], writes=[Bsg], key=okey)
            S.flush()
        if stop_after <= 2:
            return nc

        ByT = Buf("yT")

        def attn_epilogue(Obank, Lbank, bias_ap, E, dst_ap, key):
            S.op("act", lambda e: e.activation(out=E["rec"][:, :], in_=banks[Lbank][:, :], func=AF.Ln), reads=[bB[Lbank]], writes=[E["recB"]])
            S.op("act", lambda e: e.activation(out=E["rec"][:, :], in_=E["rec"][:, :], func=AF.Exp, scale=-1.0), reads=[E["recB"]], writes=[E["recB"]])
            S.op("dve", lambda e: e.tensor_tensor(out=E["o"][:, :], in0=banks[Obank][:, :], in1=E["rec"][:, :], op=ALU.mult),
                 reads=[bB[Obank], E["recB"]], writes=[E["oB"]])
            y_, yB_ = E["y"][E["i"] % 2], E["yB"][E["i"] % 2]
            E["i"] += 1
            S.op("dve", lambda e: e.tensor_scalar(out=y_[:, :], in0=E["o"][:, :], scalar1=bias_ap, scalar2=None, op0=ALU.add),
                 reads=[E["oB"], Bbv], writes=[yB_])
            S.dma("pool", dst_ap, y_[:, :], reads=[yB_], writes=[ByT], key=(key, E["i"] % 2))

        def mk_epi(es, pfx):
            E = {"rec": sb(es, pfx + "rec", [128, 512], F32), "o": sb(es, pfx + "o", [128, 512], F32),
                 "y": [sb(es, pfx + "y%d" % i, [128, 512], BF16) for i in range(2)],
                 "recB": Buf("rec"), "oB": Buf("o"), "yB": [Buf("y0"), Buf("y1")], "i": 0}
            return E

        with ExitStack() as es:
            KTs = [sb(es, "KTs%d" % g, [128, S_ALL], BF16) for g in range(2)]
            Vs = [sb(es, "Vs%d" % g, [128, S_ALL], BF16) for g in range(2)]
            qs = [sb(es, "qs%d" % g, [128, 4 * TOWN], BF16) for g in range(2)]
            KTsB = [[Buf("KTs%d_%d" % (g_, p_)) for p_ in range(4)] for g_ in range(2)]
            VsB = [[Buf("Vs%d_%d" % (g_, p_)) for p_ in range(4)] for g_ in range(2)]
            qsB = [Buf("qs0"), Buf("qs1")]
            Pt = [sb(es, "Pt%d" % i, [128, 512], BF16) for i in range(4)]
            PtB = [Buf("Pt%d" % i) for i in range(4)]
            E = mk_epi(es, "e3")
            for g in range(2):
                S.dma("sp", qs[g][:, :].rearrange("p (h t) -> p h t", h=4), qT_scr[4 * g:4 * g + 4].rearrange("h p t -> p h t"),
                      reads=[BqT], writes=[qsB[g]], key=("qs", g))
                for part in range(4):
                    c0 = part * 4096
                    S.dma("sp", KTs[g][:, c0:c0 + 4096], KT_scr[g, :, c0:c0 + 4096], reads=[BKT[g]], writes=[KTsB[g][part]], key=("KTs", g, part))
                    S.dma("sp", Vs[g][:, c0:c0 + 4096], V_scr[g, :, c0:c0 + 4096], reads=[BV[g]], writes=[VsB[g][part]], key=("Vs", g, part))
            item = 0
            NKC = S_ALL // 128
            for g in range(2):
                for qb in range(4):
                    for hh in range(4):
                        h = 4 * g + hh
                        Ob, Lb = 4 + item % 2, 6 + item % 2
                        item += 1
                        q_ap = qs[g][:, hh * TOWN + qb * 512:hh * TOWN + (qb + 1) * 512]

                        def s_mm(kc, g=g, q_ap=q_ap):
                            bi = kc % 4
                            S.op("pe", (lambda e: e.matmul(banks[bi][:, :], lhsT=KTs[g][:, kc * 128:(kc + 1) * 128], rhs=q_ap,
                                                          start=True, stop=True)),
                                 reads=[KTsB[g][kc // 32], qsB[g]], writes=[bB[bi]])
                            S.op("act", (lambda e: e.activation(out=Pt[bi][:, :], in_=banks[bi][:, :], func=AF.Exp)),
                                 reads=[bB[bi]], writes=[PtB[bi]])

                        def pv_mm(kc, g=g, Ob=Ob, Lb=Lb):
                            bi = kc % 4
                            S.op("pe", (lambda e: e.matmul(banks[Ob][:, :], lhsT=Vs[g][:, kc * 128:(kc + 1) * 128], rhs=Pt[bi][:, :],
                                                          start=(kc == 0), stop=(kc == NKC - 1))),
                                 reads=[VsB[g][kc // 32], PtB[bi]], writes=[bB[Ob]])
                            S.op("pe", (lambda e: e.matmul(banks[Lb][:, :], lhsT=ones_bf[:, :], rhs=Pt[bi][:, :],
                                                          start=(kc == 0), stop=(kc == NKC - 1))),
                                 reads=[PtB[bi], Bconst], writes=[bB[Lb]])

                        s_mm(0)
                        s_mm(1)
                        for kc in range(NKC):
                            if kc + 2 < NKC:
                                s_mm(kc + 2)
                            pv_mm(kc)
                        attn_epilogue(Ob, Lb, bvcol[:, g:g + 1], E, yT_scr[h, :, qb * 512:(qb + 1) * 512], "y3")
            S.flush()
        if stop_after <= 3:
            return nc

        def lc_of(kc):
            return kc if kc < 16 else (kc - 18 if kc < 18 else kc - 2)

        def kc_of(lc):
            return lc if 0 <= lc < 16 else (lc + 18 if lc < 0 else lc + 2)

        with ExitStack() as es:
            kBs = [sb(es, "kBs%d" % i, [128, TLOC], BF16) for i in range(2)]
            vBs = [sb(es, "vBs%d" % i, [128, TLOC], BF16) for i in range(2)]
            qBs = [sb(es, "qBs%d" % i, [128, TOWN], BF16) for i in range(2)]
            hdB = [Buf("hd0"), Buf("hd1")]
            tab = sb(es, "tab", [128, 20 * 640], F32)
            tabB = [Buf("tab%d" % i) for i in range(20)]
            Pa = [sb(es, "Pa%d" % i, [128, 20 * 640], BF16) for i in range(2)]
            PaB = [[Buf("Pa%d_%d" % (i, j)) for j in range(20)] for i in range(2)]
            tmp = [sb(es, "natmp%d" % i, [128, 640], F32) for i in range(2)]
            tmpB = [Buf("natmp0"), Buf("natmp1")]
            E = mk_epi(es, "e4")
            scale = 1.0 / math.sqrt(128.0)
            item = 0
            for h in range(8):
                sl = h % 2
                S.dma("sp", kBs[sl][:, :], kBT_scr[h], reads=[BkBT], writes=[hdB[sl]], key=("hd", sl))
                S.dma("sp", vBs[sl][:, :], vB_scr[h], reads=[BvB], writes=[hdB[sl]], key=("hd", sl))
                S.dma("sp", qBs[sl][:, :], qBT_scr[h], reads=[BqBT], writes=[hdB[sl]], key=("hd", sl))
                for kc in range(20):
                    S.dma("sp", tab[:, kc * 640:(kc + 1) * 640], natab[h, kc], writes=[tabB[kc]], key=("tab", kc % 4))
                for kc in range(20):
                    lc = lc_of(kc)
                    blo, bhi = max(0, lc - 2), min(15, lc + 2)
                    nq = (bhi - blo + 1) * 128
                    q0 = blo * 128
                    dd = dbl[kc % 2]
                    n1 = min(nq, 512)
                    S.op("pe", (lambda e, kc=kc, dd=dd, n1=n1, q0=q0, sl=sl: e.matmul(
                        dd[:, 0:n1], lhsT=kBs[sl][:, kc * 128:(kc + 1) * 128], rhs=qBs[sl][:, q0:q0 + n1], start=True, stop=True)),
                        reads=[hdB[sl]], writes=[bB[2 * (kc % 2)]])
                    rd = [bB[2 * (kc % 2)]]
                    if nq > 512:
                        S.op("pe", (lambda e, kc=kc, dd=dd, nq=nq, q0=q0, sl=sl: e.matmul(
                            dd[:, 512:nq], lhsT=kBs[sl][:, kc * 128:(kc + 1) * 128], rhs=qBs[sl][:, q0 + 512:q0 + nq],
                            start=True, stop=True)),
                            reads=[hdB[sl]], writes=[bB[2 * (kc % 2) + 1]])
                        rd.append(bB[2 * (kc % 2) + 1])
                    tm, tmB_ = tmp[kc % 2], tmpB[kc % 2]
                    S.op("dve", (lambda e, kc=kc, dd=dd, n1=n1, tm=tm: e.scalar_tensor_tensor(
                        out=tm[:, 0:n1], in0=dd[:, 0:n1], scalar=scale, in1=tab[:, kc * 640:kc * 640 + n1],
                        op0=ALU.mult, op1=ALU.add)), reads=[rd[0], tabB[kc]], writes=[tmB_])
                    if nq > 512:
                        S.op("dve", (lambda e, kc=kc, dd=dd, nq=nq, tm=tm: e.scalar_tensor_tensor(
                            out=tm[:, 512:nq], in0=dd[:, 512:nq], scalar=scale, in1=tab[:, kc * 640 + 512:kc * 640 + nq],
                            op0=ALU.mult, op1=ALU.add)), reads=[rd[1], tabB[kc]], writes=[tmB_])
                    S.op("act", (lambda e, kc=kc, nq=nq, tm=tm, sl=sl: e.activation(
                        out=Pa[sl][:, kc * 640:kc * 640 + nq], in_=tm[:, 0:nq], func=AF.Exp)),
                        reads=[tmB_], writes=[PaB[sl][kc]])
                for qb in range(4):
                    Ob, Lb = 4 + item % 2, 6 + item % 2
                    item += 1
                    for bq in range(4):
                        b = qb * 4 + bq
                        lcs = list(range(b - 2, b + 3))
                        for i_, lc in enumerate(lcs):
                            kc = kc_of(lc)
                            j0 = (b - max(0, lc - 2)) * 128
                            for (bank_i, lhs) in ((Ob, None), (Lb, ones_bf)):
                                S.op("pe", (lambda e, kc=kc, j0=j0, bank_i=bank_i, lhs=lhs, bq=bq, i_=i_, sl=sl: e.matmul(
                                    banks[bank_i][:, bq * 128:(bq + 1) * 128],
                                    lhsT=(vBs[sl][:, kc * 128:(kc + 1) * 128] if lhs is None else lhs[:, :]),
                                    rhs=Pa[sl][:, kc * 640 + j0:kc * 640 + j0 + 128], start=(i_ == 0), stop=(i_ == 4))),
                                    reads=[hdB[sl], PaB[sl][kc], Bconst], writes=[bB[bank_i]])
                    attn_epilogue(Ob, Lb, bvcol[:, 2 + h:3 + h], E, yT_scr[8 + h, :, qb * 512:(qb + 1) * 512], "y4")
            S.flush()
        if stop_after <= 4:
            return nc

        m_scr = dscr("m_scr", [16, 128, TOWN])
        Bm = Buf("m")
        Bx1, Bh2 = Buf("x1"), Buf("h2")
        with ExitStack() as es:
            woa = sb(es, "woa", [128, 8 * D], BF16)
            wob = sb(es, "wob", [128, 8 * D], BF16)
            woP = [Buf("woab%d" % i) for i in range(4)]
            yab = [sb(es, "yab%d" % i, [128, 16 * 512], BF16) for i in range(2)]
            yabB = [Buf("yab0"), Buf("yab1")]
            sgt = [sb(es, "sgt%d" % i, [128, 1024], BF16) for i in range(3)]
            sgtB = [Buf("sgt%d" % i) for i in range(3)]
            t1 = [sb(es, "m_t1%d" % i, [128, 512], F32) for i in range(2)]
            t2 = [sb(es, "m_t2%d" % i, [128, 512], F32) for i in range(2)]
            t1B = [Buf("t1a"), Buf("t1b")]
            t2B = [Buf("t2a"), Buf("t2b")]
            mo = [sb(es, "mo%d" % i, [128, 512], BF16) for i in range(3)]
            moB = [Buf("mo%d" % i) for i in range(3)]
            for part in range(4):
                S.dma("pool", v3(woa[:, :], 8)[:, :, part * 512:(part + 1) * 512],
                      w_oa.rearrange("(k p) n -> p k n", p=128)[:, :, part * 512:(part + 1) * 512], writes=[woP[part]], key=("woab", part))
                S.dma("pool", v3(wob[:, :], 8)[:, :, part * 512:(part + 1) * 512],
                      w_ob.rearrange("(k p) n -> p k n", p=128)[:, :, part * 512:(part + 1) * 512], writes=[woP[part]], key=("woab", part))
            it = 0
            for tb in range(4):
                sl = tb % 2
                S.dma("sp", v3(yab[sl][:, :], 16), yT_scr.rearrange("k p t -> p k t")[:, :, tb * 512:(tb + 1) * 512],
                      reads=[ByT], writes=[yabB[sl]], key=("yab", sl))
                for n in range(16):
                    s3 = it % 3
                    S.dma("sp", sgt[s3][:, 0:512], sg_scr[n, :, tb * 512:(tb + 1) * 512], reads=[Bsg], writes=[sgtB[s3]], key=("sgt", s3))
                    S.dma("sp", sgt[s3][:, 512:1024], sg_scr[16 + n, :, tb * 512:(tb + 1) * 512], reads=[Bsg], writes=[sgtB[s3]], key=("sgt", s3))
                    ba, bb_ = (0, 1) if it % 2 == 0 else (2, 5)
                    mm_acc(ba, lambda k, n=n: woa[:, k * D + n * 128:k * D + (n + 1) * 128],
                           lambda k, sl=sl: yab[sl][:, k * 512:(k + 1) * 512], 8, [woP[n // 4], yabB[sl]])
                    mm_acc(bb_, lambda k, n=n: wob[:, k * D + n * 128:k * D + (n + 1) * 128],
                           lambda k, sl=sl: yab[sl][:, (8 + k) * 512:(9 + k) * 512], 8, [woP[n // 4], yabB[sl]])
                    i2 = it % 2
                    S.op("dve", (lambda e, i2=i2, ba=ba, s3=s3: e.tensor_tensor(out=t1[i2][:, :], in0=banks[ba][:, :], in1=sgt[s3][:, 0:512], op=ALU.mult)),
                         reads=[bB[ba], sgtB[s3]], writes=[t1B[i2]])
                    S.op("dve", (lambda e, i2=i2, bb_=bb_, s3=s3: e.tensor_tensor(out=t2[i2][:, :], in0=banks[bb_][:, :], in1=sgt[s3][:, 512:1024], op=ALU.mult)),
                         reads=[bB[bb_], sgtB[s3]], writes=[t2B[i2]])
                    S.op("pool", (lambda e, i2=i2, s3=s3: e.tensor_tensor(out=mo[s3][:, :], in0=t1[i2][:, :], in1=t2[i2][:, :], op=ALU.add)),
                         reads=[t1B[i2], t2B[i2]], writes=[moB[s3]])
                    S.dma("pool", m_scr[n, :, tb * 512:(tb + 1) * 512], mo[s3][:, :], reads=[moB[s3]], writes=[Bm], key=("mo", s3))
                    it += 1
            S.flush()

        def stats_to_rstd(bank_i, rs_tile, rsB):
            S.op("act", lambda e: e.activation(out=rs_tile[:, :], in_=banks[bank_i][:, :], func=AF.Ln, bias=epsA[:, 0:1], scale=1.0),
                 reads=[bB[bank_i], Bconst], writes=[rsB])
            S.op("act", lambda e: e.activation(out=rs_tile[:, :], in_=rs_tile[:, :], func=AF.Exp, scale=-0.5), reads=[rsB], writes=[rsB])

        with ExitStack() as es:
            wout = sb(es, "wout", [128, 16 * D], BF16)
            woutP = [Buf("wout%d" % i) for i in range(4)]
            x1cB = [Buf("x1c%d" % i) for i in range(16)]
            mb_ = [sb(es, "mblk%d" % i, [128, 16 * 512], BF16) for i in range(2)]
            mbB = [Buf("mblk0"), Buf("mblk1")]
            xin = [sb(es, "xin%d" % i, [128, 512], F32) for i in range(3)]
            xinB = [Buf("xin%d" % i) for i in range(3)]
            x1b = sb(es, "x1b", [128, 16 * 512], F32)
            x1bB = Buf("x1b")
            sq5 = [sb(es, "sq5_%d" % i, [128, 512], BF16) for i in range(2)]
            sq5B = [Buf("sq5a"), Buf("sq5b")]
            rs5 = sb(es, "rs5", [128, 512], F32)
            rs5B = Buf("rs5")
            h2o = sb(es, "h2o", [128, 16 * 512], BF16)
            h2oB = Buf("h2o")
            for part in range(4):
                S.dma("pool", v3(wout[:, :], 16)[:, :, part * 512:(part + 1) * 512],
                      w_out.rearrange("(k p) n -> p k n", p=128)[:, :, part * 512:(part + 1) * 512], writes=[woutP[part]], key=("wout", part))
            it = 0
            mr = Ring([0, 1, 2, 3])
            pend5 = None

            def stat5(n, q2):
                S.op("pe", (lambda e: e.matmul(banks[7][:, :], lhsT=ones_bf[:, :], rhs=sq5[q2][:, :], start=(n == 0), stop=(n == 15))),
                     reads=[sq5B[q2], Bconst], writes=[bB[7]])

            for tb in range(4):
                sl = tb % 2
                S.dma("sp", v3(mb_[sl][:, :], 16), m_scr.rearrange("k p t -> p k t")[:, :, tb * 512:(tb + 1) * 512],
                      reads=[Bm], writes=[mbB[sl]], key=("mblk", sl))
                for n in range(16):
                    s3 = it % 3
                    it += 1
                    S.dma("sp", xin[s3][:, :], xT_v[:, n, tb * 512:(tb + 1) * 512], writes=[xinB[s3]], key=("xin", s3))
                    bi = mr.next()
                    mm_acc(bi, lambda k, n=n: wout[:, k * D + n * 128:k * D + (n + 1) * 128],
                           lambda k, sl=sl: mb_[sl][:, k * 512:(k + 1) * 512], 16, [woutP[n // 4], mbB[sl]])
                    S.op("dve", (lambda e, bi=bi, n=n, s3=s3: e.scalar_tensor_tensor(
                        out=x1b[:, n * 512:(n + 1) * 512], in0=banks[bi][:, :], scalar=mod_sb[:, 32 + n:33 + n], in1=xin[s3][:, :],
                        op0=ALU.mult, op1=ALU.add)), reads=[bB[bi], xinB[s3], Bmod], writes=[x1cB[n]])
                    q2 = n % 2
                    S.op("act", (lambda e, n=n, q2=q2: e.activation(out=sq5[q2][:, :], in_=x1b[:, n * 512:(n + 1) * 512], func=AF.Square)),
                         reads=[x1cB[n]], writes=[sq5B[q2]])
                    if pend5 is not None:
                        stat5(*pend5)
                    pend5 = (n, q2)
                stat5(*pend5)
                pend5 = None
                stats_to_rstd(7, rs5, rs5B)
                for k in range(16):
                    S.op("dve", (lambda e, k=k: e.scalar_tensor_tensor(
                        out=h2o[:, k * 512:(k + 1) * 512], in0=x1b[:, k * 512:(k + 1) * 512], scalar=A2[:, k:k + 1], in1=rs5[:, :],
                        op0=ALU.mult, op1=ALU.mult)), reads=[x1cB[k], rs5B, Bmod], writes=[h2oB])
                S.dma("pool", v3(h2T_scr[:, :], 16)[:, :, tb * 512:(tb + 1) * 512], v3(h2o[:, :], 16), reads=[h2oB], writes=[Bh2], key="h2st")
                for j4 in range(4):
                    S.dma("pool", x1T_scr.rearrange("k p t -> p k t")[:, 4 * j4:4 * j4 + 4, tb * 512:(tb + 1) * 512],
                          v3(x1b[:, :], 16)[:, 4 * j4:4 * j4 + 4, :],
                          reads=x1cB[4 * j4:4 * j4 + 4], writes=[Bx1], key=("x1st", j4))
            S.flush()
        if stop_after <= 5:
            return nc

        Bact = Buf("act")
        w_gate_v = w_gate.rearrange("(k p) n -> p k n", p=128)
        w_up_v = w_up.rearrange("(k p) n -> p k n", p=128)
        with ExitStack() as es:
            h2 = sb(es, "h2", [128, 16 * TOWN], BF16)
            h2B = Buf("h2")
            wg = [sb(es, "wg%d" % i, [128, 16 * 512], BF16) for i in range(2)]
            wu = [sb(es, "wu%d" % i, [128, 16 * 512], BF16) for i in range(2)]
            wgB = [Buf("wg0"), Buf("wg1")]
            wuB = [Buf("wu0"), Buf("wu1")]
            bg = [sb(es, "bg%d" % i, [128, 4], F32) for i in range(2)]
            bu = [sb(es, "bu%d" % i, [128, 4], F32) for i in range(2)]
            bgB = [Buf("bg0"), Buf("bg1")]
            buB = [Buf("bu0"), Buf("bu1")]
            sgl = [sb(es, "sgl%d" % i, [128, 512], F32) for i in range(2)]
            sglB = [Buf("sgl0"), Buf("sgl1")]
            ao = [sb(es, "ao%d" % i, [128, 512], BF16) for i in range(3)]
            aoB = [Buf("ao%d" % i) for i in range(3)]
            h2P = [Buf("h2_%d" % i) for i in range(4)]
            for tbp in range(4):
                S.dma("sp", v3(h2[:, :], 16)[:, :, tbp * 512:(tbp + 1) * 512], v3(h2T_scr[:, :], 16)[:, :, tbp * 512:(tbp + 1) * 512],
                      reads=[Bh2], writes=[h2P[tbp]], key=("h2ld", tbp))
            it = 0
            for ti in range(11):
                sl = ti % 2
                load_wtile(wg[sl], wgB[sl], w_gate_v, ti * 512, 512, 16, ("wg", sl))
                load_wtile(wu[sl], wuB[sl], w_up_v, ti * 512, 512, 16, ("wu", sl))
                bias_cols(wg[sl], wgB[sl], 4, B2bf, bg[sl][:, 0:4], bgB[sl], 512)
                bias_cols(wu[sl], wuB[sl], 4, B2bf, bu[sl][:, 0:4], buB[sl], 512)
                for cc in range(4):
                    for tb in range(4):
                        bG, bU = ((0, 1), (2, 3), (4, 5))[it % 3]
                        i2, i3 = it % 2, it % 3
                        it += 1
                        mm_acc(bG, lambda k, sl=sl, cc=cc: wg[sl][:, k * 512 + cc * 128:k * 512 + (cc + 1) * 128],
                               lambda k, tb=tb: h2[:, k * TOWN + tb * 512:k * TOWN + (tb + 1) * 512], 16, [wgB[sl], h2P[tb]])
                        mm_acc(bU, lambda k, sl=sl, cc=cc: wu[sl][:, k * 512 + cc * 128:k * 512 + (cc + 1) * 128],
                               lambda k, tb=tb: h2[:, k * TOWN + tb * 512:k * TOWN + (tb + 1) * 512], 16, [wuB[sl], h2P[tb]])
                        S.op("act", (lambda e, bG=bG, i2=i2, sl=sl, cc=cc: e.activation(
                            out=sgl[i2][:, :], in_=banks[bG][:, :], func=AF.Silu, bias=bg[sl][:, cc:cc + 1], scale=1.0)),
                            reads=[bB[bG], bgB[sl]], writes=[sglB[i2]])
                        S.op("dve", (lambda e, bU=bU, i2=i2, i3=i3, sl=sl, cc=cc: e.scalar_tensor_tensor(
                            out=ao[i3][:, :], in0=banks[bU][:, :], scalar=bu[sl][:, cc:cc + 1], in1=sgl[i2][:, :],
                            op0=ALU.add, op1=ALU.mult)), reads=[bB[bU], buB[sl], sglB[i2]], writes=[aoB[i3]])
                        S.dma("sp", act_scr[ti * 4 + cc, :, tb * 512:(tb + 1) * 512], ao[i3][:, :], reads=[aoB[i3]], writes=[Bact],
                              key=("ao", i3))
            S.flush()
        if stop_after <= 6:
            return nc

        Bx2h = [Buf("x2a"), Buf("x2b")]
        w_down_v = w_down.rearrange("(k p) n -> p k n", p=128)
        with ExitStack() as es:
            acth = sb(es, "acth", [128, 44 * 1024], BF16)
            acthB = Buf("acth")
            wd = [sb(es, "wd%d" % i, [128, 44 * 256], BF16) for i in range(2)]
            wdB = [Buf("wd0"), Buf("wd1")]
            x1i = [sb(es, "x1i%d" % i, [128, 512], F32) for i in range(3)]
            x1iB = [Buf("x1i%d" % i) for i in range(3)]
            x2o = [sb(es, "x2o%d" % i, [128, 512], F32) for i in range(3)]
            x2oB = [Buf("x2o%d" % i) for i in range(3)]
            sq7 = [sb(es, "sq7_%d" % i, [128, 512], BF16) for i in range(2)]
            sq7B = [Buf("sq7a"), Buf("sq7b")]
            rsh = [[sb(es, "rsh%d_%d" % (h_, i), [128, 512], F32) for i in range(2)] for h_ in range(2)]
            rshB = [[Buf("rsh%d_%d" % (h_, i)) for i in range(2)] for h_ in range(2)]
            x2i = [sb(es, "x2i%d" % i, [128, 512], F32) for i in range(4)]
            x2iB = [Buf("x2i%d" % i) for i in range(4)]
            Fw = sb(es, "Fw", [128, 16], F32)
            FwB = Buf("Fw")
            yo = [sb(es, "yo%d" % i, [128, 512], F32) for i in range(4)]
            yoB = [Buf("yo%d" % i) for i in range(4)]
            Bout = Buf("out")
            S.op("pool", lambda e: e.tensor_scalar(out=Fw[:, :], in0=vecs_sb[:, C_FW:C_FW + 16], scalar1=math.sqrt(D), scalar2=None, op0=ALU.mult),
                 reads=[Bvecs], writes=[FwB])
            it = 0
            wi = 0
            mr = Ring([0, 1, 2, 3, 4, 5])
            pend7 = None

            def stat7(q2, tb2, n):
                S.op("pe", (lambda e: e.matmul(banks[6 + tb2][:, :], lhsT=ones_bf[:, :], rhs=sq7[q2][:, :],
                                              start=(n == 0), stop=(n == 15))),
                     reads=[sq7B[q2], Bconst], writes=[bB[6 + tb2]])

            def load_act(half):
                h0_ = half * 1024
                for part in range(4):
                    S.dma("sp", v3(acth[:, :], 44)[:, part * 11:(part + 1) * 11, :],
                          act_scr.rearrange("k p t -> p k t")[:, part * 11:(part + 1) * 11, h0_:h0_ + 1024],
                          reads=[Bact], writes=[acthB], key="acth")

            fctr = [0]

            def final_pass(half):
                h0_ = half * 1024
                for tb2 in range(2):
                    stats_to_rstd(6 + tb2, rsh[half][tb2], rshB[half][tb2])
                yield
                for tb2 in range(2):
                    c0 = h0_ + tb2 * 512
                    for n in range(16):
                        s3 = fctr[0] % 4
                        fctr[0] += 1
                        S.dma("sp", x2i[s3][:, :], x2T_scr[n, :, c0:c0 + 512], reads=[Bx2h[half]], writes=[x2iB[s3]], key=("x2i", s3))
                        S.op("dve", (lambda e, s3=s3, n=n, tb2=tb2: e.scalar_tensor_tensor(
                            out=yo[s3][:, :], in0=x2i[s3][:, :], scalar=Fw[:, n:n + 1], in1=rsh[half][tb2][:, :],
                            op0=ALU.mult, op1=ALU.mult)), reads=[x2iB[s3], FwB, rshB[half][tb2]], writes=[yoB[s3]])
                        S.dma("act", outT[n, :, c0:c0 + 512], yo[s3][:, :], reads=[yoB[s3]], writes=[Bout], key=("yo", s3))
                        yield

            load_act(0)
            fp_gen = None
            for half in range(2):
                h0 = half * 1024
                for nt in range(8):
                    sl = wi % 2
                    wi += 1
                    for part in range(4):
                        S.dma("pool", v3(wd[sl][:, :], 44)[:, part * 11:(part + 1) * 11, :],
                              w_down_v[:, part * 11:(part + 1) * 11, nt * 256:(nt + 1) * 256], writes=[wdB[sl]], key=("wd", sl))
                    for nn in range(2):
                        n = nt * 2 + nn
                        for tb2 in range(2):
                            s3 = it % 3
                            q2 = it % 2
                            it += 1
                            c0 = h0 + tb2 * 512
                            S.dma("sp", x1i[s3][:, :], x1T_scr[n, :, c0:c0 + 512], reads=[Bx1], writes=[x1iB[s3]], key=("x1i", s3))
                            bi = mr.next()
                            mm_acc(bi, lambda k, sl=sl, nn=nn: wd[sl][:, k * 256 + nn * 128:k * 256 + (nn + 1) * 128],
                                   lambda k, tb2=tb2: acth[:, k * 1024 + tb2 * 512:k * 1024 + (tb2 + 1) * 512], 44, [wdB[sl], acthB])
                            S.op("dve", (lambda e, bi=bi, n=n, s3=s3: e.scalar_tensor_tensor(
                                out=x2o[s3][:, :], in0=banks[bi][:, :], scalar=mod_sb[:, 80 + n:81 + n], in1=x1i[s3][:, :],
                                op0=ALU.mult, op1=ALU.add)), reads=[bB[bi], x1iB[s3], Bmod], writes=[x2oB[s3]])
                            S.op("act", (lambda e, s3=s3, q2=q2: e.activation(out=sq7[q2][:, :], in_=x2o[s3][:, :], func=AF.Square)),
                                 reads=[x2oB[s3]], writes=[sq7B[q2]])
                            if pend7 is not None:
                                stat7(*pend7)
                            pend7 = (q2, tb2, n)
                            S.dma("act", x2T_scr[n, :, c0:c0 + 512], x2o[s3][:, :], reads=[x2oB[s3]], writes=[Bx2h[half]], key=("x2o", s3))
                            if fp_gen is not None:
                                next(fp_gen, None)
                stat7(*pend7)
                pend7 = None
                if half == 0:
                    load_act(1)
                    fp_gen = final_pass(0)
                    next(fp_gen)
                else:
                    for _ in fp_gen:
                        pass
                    for _ in final_pass(1):
                        pass
            S.flush()
        return nc


def _pack_cols(v):
    v = np.asarray(v, np.float32).reshape(-1)
    return np.ascontiguousarray(v.reshape(-1, 128).T)


def _token_order(c):
    own = np.arange(TOWN * c, TOWN * (c + 1))
    others = np.concatenate([np.arange(0, TOWN * c), np.arange(TOWN * (c + 1), S_ALL)])
    r0 = 32 * c
    rows_before = np.arange(r0 - 4, r0) if c > 0 else np.arange(4, 8)
    rows_after = np.arange(r0 + 32, r0 + 36) if c < NCORE - 1 else np.arange(248, 252)
    halo_rows = np.concatenate([rows_before, rows_after])
    halo = (halo_rows[:, None] * GRID_W + np.arange(GRID_W)[None, :]).reshape(-1)
    return own, others, halo, halo_rows


def _rope_tables():
    t = np.arange(S_ALL)
    row = (t // GRID_W).astype(np.float32)
    col = (t % GRID_W).astype(np.float32)
    inv = (np.float32(10000.0) ** (-(np.arange(0, 64, 2, dtype=np.float32) / np.float32(64.0)))).astype(np.float32)
    ang = np.concatenate([row[:, None] * inv[None], col[:, None] * inv[None]], axis=-1).astype(np.float32)
    cos = np.cos(ang).astype(np.float32)
    sin = np.sin(ang).astype(np.float32)
    cosT = np.repeat(cos.T, 2, axis=0)
    sinT = np.repeat(sin.T, 2, axis=0)
    sinT[0::2] *= -1.0
    return np.ascontiguousarray(cosT), np.ascontiguousarray(sinT)


def _na_table(c, rpb, halo_rows):
    tab = np.full((8, 20, 128, 640), MASK_NEG, np.float32)
    rows_tot = S_ALL // GRID_W
    for kc in range(20):
        if kc < 16:
            lc = kc
            grow = 32 * c + 2 * kc + np.arange(2)
        elif kc < 18:
            lc = kc - 18
            grow = halo_rows[(kc - 16) * 2:(kc - 16) * 2 + 2]
        else:
            lc = kc - 2
            grow = halo_rows[4 + (kc - 18) * 2:4 + (kc - 18) * 2 + 2]
        key_r = np.repeat(grow, GRID_W)
        key_c = np.tile(np.arange(GRID_W), 2)
        blo = max(0, lc - 2)
        bhi = min(15, lc + 2)
        for b in range(blo, bhi + 1):
            qr = np.repeat(32 * c + 2 * b + np.arange(2), GRID_W)
            qc = np.tile(np.arange(GRID_W), 2)
            rs = np.clip(qr - NA_ROWS // 2, 0, rows_tot - NA_ROWS)
            cs_ = np.clip(qc - NA_COLS // 2, 0, GRID_W - NA_COLS)
            inr = (key_r[:, None] >= rs[None, :]) & (key_r[:, None] < rs[None, :] + NA_ROWS)
            inc = (key_c[:, None] >= cs_[None, :]) & (key_c[:, None] < cs_[None, :] + NA_COLS)
            valid = inr & inc
            if kc >= 16:
                own_lo = 32 * c + 2 * (b - 2)
                own_hi = 32 * c + 2 * (b + 2) + 1
                lo = max(own_lo, 32 * c)
                hi = min(own_hi, 32 * c + 31)
                dup = (key_r >= lo) & (key_r <= hi)
                valid &= ~dup[:, None]
            rel_r = np.clip(key_r[:, None] - qr[None, :] + (NA_ROWS - 1), 0, 2 * NA_ROWS - 2)
            rel_c = np.clip(key_c[:, None] - qc[None, :] + (NA_COLS - 1), 0, 2 * NA_COLS - 2)
            j0 = (b - blo) * 128
            for h in range(8):
                bias = rpb[h][rel_r, rel_c]
                tab[h, kc, :, j0:j0 + 128] = np.where(valid, bias, np.float32(MASK_NEG))
    return tab


def prep_inputs(inp):
    x = np.asarray(inp["x"], np.float32)[0]
    xTfull = np.ascontiguousarray(x.T)
    cosT, sinT = _rope_tables()
    rm = np.zeros((128, 128), np.float32)
    for k in range(128):
        rm[k, k ^ 1] = 1.0
    vec = np.zeros((128, NV), np.float32)
    vec[:, 0:16] = _pack_cols(inp["c"])
    vec[:, 16:112] = _pack_cols(inp["b_ada"])
    vec[:, 112:128] = _pack_cols(inp["norm1_w"])
    vec[:, 128:144] = _pack_cols(inp["norm2_w"])
    vec[:, 144:160] = _pack_cols(inp["final_w"])
    vec[:, 160:161] = _pack_cols(inp["q_norm_w"])
    vec[:, 161:162] = _pack_cols(inp["k_norm_w"])
    rpb = np.asarray(inp["nat_rpb"], np.float32)[0]
    shared = {
        "vecs": vec, "rmat": rm,
        "w_ada": np.ascontiguousarray(np.asarray(inp["w_ada"], np.float32)[0]),
        "w_in": np.ascontiguousarray(np.asarray(inp["w_in"], np.float32)[0]),
        "w_oa": np.ascontiguousarray(np.asarray(inp["w_oa"], np.float32)[0]),
        "w_ob": np.ascontiguousarray(np.asarray(inp["w_ob"], np.float32)[0]),
        "w_out": np.ascontiguousarray(np.asarray(inp["w_out"], np.float32)[0]),
        "w_gate": np.ascontiguousarray(np.asarray(inp["w_ffn_gate"], np.float32)[0]),
        "w_up": np.ascontiguousarray(np.asarray(inp["w_ffn_up"], np.float32)[0]),
        "w_down": np.ascontiguousarray(np.asarray(inp["w_ffn_down"], np.float32)[0]),
    }
    maps = []
    for c in range(NCORE):
        own, others, halo, halo_rows = _token_order(c)
        order = np.concatenate([own, others, halo])
        m = dict(shared)
        m["xT"] = np.ascontiguousarray(xTfull[:, order])
        o2 = order[:S_ALL]
        m["cosT"] = np.ascontiguousarray(cosT[:, o2])
        m["sinT"] = np.ascontiguousarray(sinT[:, o2])
        m["natab"] = _na_table(c, rpb, halo_rows)
        maps.append(m)
    return maps


_NC_CACHE = {}


def kernel(**inputs):
    maps = prep_inputs(inputs)
    if "nc" not in _NC_CACHE:
        _NC_CACHE["nc"] = build_program()
    nc = _NC_CACHE["nc"]
    res = run_bass_kernel_spmd(nc, maps, core_ids=list(range(NCORE)))
    outs = []
    for c in range(NCORE):
        o = np.asarray(res.results[c]["outT"], np.float32).reshape(D, TOWN)
        outs.append(o.T)
    return np.ascontiguousarray(np.concatenate(outs, axis=0)[None].astype(np.float32))
```

```python
import math
from contextlib import ExitStack

import numpy as np
import concourse.bass as bass
import concourse.mybir as mybir
from concourse.bass_utils import run_bass_kernel_spmd

F32 = mybir.dt.float32
BF16 = mybir.dt.bfloat16
AF = mybir.ActivationFunctionType
ALU = mybir.AluOpType

D = 2048
S_ALL = 16384
NCORE = 8
TOWN = 2048
THALO = 512
TLOC = TOWN + THALO
NTOK_IN = S_ALL + THALO
D_IN = 8704
D_FF = 5632
EPS = 1e-6
NV = 162
GRID_W = 64
NA_ROWS = 8
NA_COLS = 16
MASK_NEG = -30000.0


class Buf:
    __slots__ = ("name", "w", "rs", "rd")

    def __init__(self, name):
        self.name = name
        self.w = None
        self.rs = {}
        self.rd = []


class Op:
    __slots__ = ("eng", "fn", "deps", "signaled", "token", "key", "idx", "epoch")

    def __init__(self, eng, fn, key, idx, epoch):
        self.eng = eng
        self.fn = fn
        self.key = key
        self.deps = set()
        self.signaled = key is not None
        self.token = None
        self.idx = idx
        self.epoch = epoch


ENGS = ("pe", "act", "dve", "pool", "sp")


class Sched:
    def __init__(self, nc, es):
        self.nc = nc
        self.es = es
        self.sem = {e: es.enter_context(nc.semaphore("s_" + e)) for e in ENGS}
        self.cnt = {e: 0 for e in ENGS}
        self.keysem = {}
        self.keyidx = {}
        self.sempool = []
        self.poolcnt = []
        self.poolkind = []
        self.waited = {e: {} for e in ENGS}
        self.epoch = 0
        self.ops = {e: [] for e in ENGS}
        self.nidx = {e: 0 for e in ENGS}
        self.allops = []

    def _mk(self, eng, fn, reads, writes, key):
        op = Op(eng, fn, key, self.nidx[eng], self.epoch)
        self.nidx[eng] += 1
        deps = op.deps
        for b in reads:
            if b.w is not None:
                deps.add(b.w)
        for b in writes:
            if b.w is not None:
                deps.add(b.w)
            deps.update(b.rs.values())
            deps.update(b.rd)
        for d in list(deps):
            if d.key is None and key is None and d.eng == eng and (eng == "pe" or op.idx - d.idx > 2):
                deps.discard(d)
        for d in deps:
            d.signaled = True
        for b in reads:
            if key is not None:
                b.rd.append(op)
            else:
                b.rs[eng] = op
        for b in writes:
            b.w = op
            b.rs = {}
            b.rd = []
        self.ops[eng].append(op)
        self.allops.append(op)
        return op

    def op(self, eng, fn, reads=(), writes=()):
        return self._mk(eng, fn, reads, writes, None)

    def dma(self, eng, out, in_, reads=(), writes=(), key=None):
        assert key is not None
        kind = "sw" if eng == "pool" else "hw"
        key = (kind, key)
        if key not in self.keysem:
            free = [i for i in range(len(self.sempool)) if self.poolkind[i] == kind and i not in self.keyidx.values()]
            if free:
                i = free[0]
            else:
                i = len(self.sempool)
                self.sempool.append(self.es.enter_context(self.nc.semaphore("k%s%d" % (kind, i))))
                self.poolcnt.append(0)
                self.poolkind.append(kind)
            self.keysem[key] = self.sempool[i]
            self.keyidx[key] = i
        return self._mk(eng, lambda e: e.dma_start(out=out, in_=in_), reads, writes, key)

    def flush(self, final=False):
        nc = self.nc
        last = {}
        for e in ENGS:
            for op in reversed(self.ops[e]):
                if op.key is None and op.fn is not None:
                    op.signaled = True
                    last[e] = op
                    break
        self.dmawait = {}
        for op in self.allops:
            dw = None
            for d in op.deps:
                if d.key is not None and d.epoch == self.epoch:
                    if dw is None:
                        dw = {}
                    i = self.keyidx[d.key]
                    dw[i] = self.poolcnt[i]
            if dw:
                self.dmawait[op] = dw
            if op.key is not None:
                i = self.keyidx[op.key]
                self.poolcnt[i] += 16
                op.token = (self.sempool[i], self.poolcnt[i])
            elif op.signaled:
                self.cnt[op.eng] += 1
                op.token = (self.sem[op.eng], self.cnt[op.eng])
        bar = [(self.sem[e], self.cnt[e]) for e in ENGS if self.cnt[e] > 0]
        bar += [(self.sempool[i], self.poolcnt[i]) for i in range(len(self.sempool)) if self.poolcnt[i] > 0]
        ops = self.ops
        dmawait = self.dmawait
        keyidx = dict(self.keyidx)
        waited = self.waited
        sems = self.sem
        epoch = self.epoch

        def emit(ename, e):
            w = waited[ename]
            for op in ops[ename]:
                for d in op.deps:
                    if d.epoch != epoch:
                        continue
                    if d.key is None and d.eng == ename:
                        if ename == "pe" or op.idx - d.idx > 2:
                            continue
                    s, v = d.token
                    if d.key is not None:
                        v = dmawait[op][keyidx[d.key]]
                    if w.get(id(s), 0) < v:
                        e.wait_ge(s, v)
                        w[id(s)] = v
                if op.fn is None:
                    continue
                ins = op.fn(e)
                if op.key is not None:
                    ins.then_inc(op.token[0], 16)
                elif op.signaled:
                    ins.then_inc(sems[ename], 1)
            for s, v in bar:
                k = id(s)
                if w.get(k, 0) < v:
                    e.wait_ge(s, v)
                    w[k] = v

        with nc.Block() as block:
            @block.tensor
            def _(e):
                emit("pe", e)

            @block.scalar
            def _(e):
                emit("act", e)

            @block.vector
            def _(e):
                emit("dve", e)

            @block.gpsimd
            def _(e):
                emit("pool", e)

            @block.sync
            def _(e):
                emit("sp", e)

        self.ops = {e: [] for e in ENGS}
        self.allops = []
        self.epoch += 1
        self.keysem = {}
        self.keyidx = {}


class Ring:
    def __init__(self, items):
        self.items = items
        self.i = 0

    def next(self):
        it = self.items[self.i % len(self.items)]
        self.i += 1
        return it


def v3(ap, k):
    return ap.rearrange("p (k t) -> p k t", k=k)


def build_program(debug_outs=(), stop_after=99):
    nc = bass.Bass("TRN2", target_bir_lowering=False)

    def din(name, shape, dt=F32):
        return nc.dram_tensor(name, list(shape), dt, kind="ExternalInput").ap()

    def dscr(name, shape, dt=BF16):
        kind = "ExternalOutput" if name in debug_outs else "Internal"
        return nc.dram_tensor(name, list(shape), dt, kind=kind).ap()

    xT = din("xT", [D, NTOK_IN])
    cosT = din("cosT", [128, S_ALL])
    sinT = din("sinT", [128, S_ALL])
    vecs = din("vecs", [128, NV])
    rmat = din("rmat", [128, 128])
    natab = din("natab", [8, 20, 128, 640])
    w_ada = din("w_ada", [D, 6 * D])
    w_in = din("w_in", [D, D_IN])
    w_oa = din("w_oa", [1024, D])
    w_ob = din("w_ob", [1024, D])
    w_out = din("w_out", [D, D])
    w_gate = din("w_gate", [D, D_FF])
    w_up = din("w_up", [D, D_FF])
    w_down = din("w_down", [D_FF, D])
    outT = nc.dram_tensor("outT", [16, 128, TOWN], F32, kind="ExternalOutput").ap()

    hT_scr = dscr("hT_scr", [128, 16 * TLOC])
    KT_scr = dscr("KT_scr", [2, 128, S_ALL])
    V_scr = dscr("V_scr", [2, 128, 128 * 128])
    qT_scr = dscr("qT_scr", [8, 128, TOWN])
    qBT_scr = dscr("qBT_scr", [8, 128, TOWN])
    kBT_scr = dscr("kBT_scr", [8, 128, TLOC])
    vB_scr = dscr("vB_scr", [8, 128, 20 * 128])
    sg_scr = dscr("sg_scr", [32, 128, TOWN])
    bv_scr = dscr("bv_scr", [128, 16], F32)
    yT_scr = dscr("yT_scr", [16, 128, TOWN])
    x1T_scr = dscr("x1T_scr", [16, 128, TOWN], F32)
    h2T_scr = dscr("h2T_scr", [128, 16 * TOWN])
    act_scr = dscr("act_scr", [44, 128, TOWN])
    x2T_scr = dscr("x2T_scr", [16, 128, TOWN], F32)

    w_ada_v = w_ada.rearrange("(k p) n -> p k n", p=128)
    w_in_v = w_in.rearrange("(k p) n -> p k n", p=128)
    xT_v = xT.rearrange("(k p) t -> p k t", p=128)

    with ExitStack() as ges:
        S = Sched(nc, ges)

        def sb(es, name, shape, dt):
            return es.enter_context(nc.sbuf_tensor(name, list(shape), dt))

        vecs_sb = sb(ges, "vecs_sb", [128, NV], F32)
        mod_sb = sb(ges, "mod_sb", [128, 96], F32)
        A1 = sb(ges, "A1", [128, 16], F32)
        A2 = sb(ges, "A2", [128, 16], F32)
        B1bf = sb(ges, "B1bf", [128, 16], BF16)
        B2bf = sb(ges, "B2bf", [128, 16], BF16)
        ones_bf = sb(ges, "ones_bf", [128, 128], BF16)
        rmat_bf = sb(ges, "rmat_bf", [128, 128], BF16)
        kw = sb(ges, "kw", [128, 1], F32)
        epsA = sb(ges, "epsA", [128, 1], F32)
        epsB = sb(ges, "epsB", [128, 1], F32)
        bvcol = sb(ges, "bvcol", [128, 16], F32)
        csil = sb(ges, "csil", [128, 16], BF16)
        dbl = [ges.enter_context(nc.psum_tensor("dbank%d" % i, [128, 1024], F32)) for i in range(4)]
        banks = [dbl[i // 2][:, (i % 2) * 512:(i % 2) * 512 + 512] for i in range(8)]
        bB = [Buf("bank%d" % i) for i in range(8)]
        Bvecs, Bmod, Bconst, Bcsil, Bbv = Buf("vecs"), Buf("mod"), Buf("const"), Buf("csil"), Buf("bv")

        C_C, C_BADA, C_N1, C_N2, C_FW, C_QW, C_KW = 0, 16, 112, 128, 144, 160, 161

        with ExitStack() as es:
            wts = [sb(es, "wada%d" % i, [128, 16 * 512], BF16) for i in range(3)]
            wB = [Buf("wada%d" % i) for i in range(3)]
            S.dma("sp", vecs_sb[:, :], vecs[:, :], writes=[Bvecs], key="vecs")
            S.dma("pool", rmat_bf[:, :], rmat[:, :], writes=[Bconst], key="rmat")
            S.op("dve", lambda e: e.memset(ones_bf[:, :], 1.0), writes=[Bconst])
            S.op("dve", lambda e: e.memset(epsA[:, :], D * EPS), writes=[Bconst])
            S.op("dve", lambda e: e.memset(epsB[:, :], 128 * EPS), writes=[Bconst])
            S.op("act", lambda e: e.activation(out=csil[:, :], in_=vecs_sb[:, C_C:C_C + 16], func=AF.Silu),
                 reads=[Bvecs], writes=[Bcsil])
            ps_mod = banks[7]
            for i in range(24):
                sl = i % 3
                wt = wts[sl]
                S.dma("pool", v3(wt[:, :], 16), w_ada_v[:, :, i * 512:(i + 1) * 512], writes=[wB[sl]],
                      key=("wada", sl))
                for jj in range(4):
                    j = i * 4 + jj
                    for k in range(16):
                        S.op("pe", (lambda e, wt=wt, k=k, jj=jj, j=j: e.matmul(
                            ps_mod[:, j:j + 1], lhsT=wt[:, k * 512 + jj * 128:k * 512 + (jj + 1) * 128],
                            rhs=csil[:, k:k + 1], start=(k == 0), stop=(k == 15))),
                            reads=[wB[sl], Bcsil], writes=[bB[7]])
            S.op("dve", lambda e: e.tensor_tensor(out=mod_sb[:, :], in0=ps_mod[:, 0:96],
                                                  in1=vecs_sb[:, C_BADA:C_BADA + 96], op=ALU.add),
                 reads=[bB[7], Bvecs], writes=[Bmod])
            for (A, sc0, nw0) in ((A1, 16, C_N1), (A2, 64, C_N2)):
                S.op("dve", (lambda e, A=A, sc0=sc0, nw0=nw0: e.scalar_tensor_tensor(
                    out=A[:, :], in0=mod_sb[:, sc0:sc0 + 16], scalar=1.0, in1=vecs_sb[:, nw0:nw0 + 16],
                    op0=ALU.add, op1=ALU.mult)), reads=[Bmod, Bvecs], writes=[Bmod])
                S.op("pool", (lambda e, A=A: e.tensor_scalar(out=A[:, :], in0=A[:, :], scalar1=math.sqrt(D),
                                                            scalar2=None, op0=ALU.mult)),
                     reads=[Bmod], writes=[Bmod])
            S.op("dve", lambda e: e.tensor_copy(out=B1bf[:, :], in_=mod_sb[:, 0:16]), reads=[Bmod], writes=[Bmod])
            S.op("dve", lambda e: e.tensor_copy(out=B2bf[:, :], in_=mod_sb[:, 48:64]), reads=[Bmod], writes=[Bmod])
            S.op("pool", lambda e: e.tensor_scalar(out=kw[:, :], in0=vecs_sb[:, C_KW:C_KW + 1],
                                                   scalar1=math.sqrt(128.0), scalar2=None, op0=ALU.mult),
                 reads=[Bvecs], writes=[Bmod])
            S.flush()
        if stop_after <= 0:
            return nc

        def norm_rope_multi(items, use_sqrt=False):
            chain_part1(items)
            chain_part2(items, use_sqrt)

        def chain_part1(items):
            for it_ in items:
                T = it_["T"]
                S.op("act", lambda e, it_=it_, T=T: e.activation(out=T["sq"][:, :], in_=it_["ps"][:, :], func=AF.Square,
                                                               bias=it_["bias"], scale=1.0),
                     reads=[it_["psB"]] + list(it_["extra"]), writes=[T["sqB"]])
                S.op("dve", lambda e, it_=it_, T=T: e.tensor_scalar(out=T["raw"][:, :], in0=it_["ps"][:, :], scalar1=it_["bias"],
                                                                  scalar2=None, op0=ALU.add),
                     reads=[it_["psB"], T["sqB"]] + list(it_["extra"]), writes=[T["rawB"]])

        def chain_part2(items, use_sqrt=False):
            for it_ in items:
                T = it_["T"]
                S.op("pe", lambda e, it_=it_, T=T: e.matmul(banks[it_["sumb"]][:, :], lhsT=ones_bf[:, :], rhs=T["sq"][:, :],
                                                          start=True, stop=True),
                     reads=[T["sqB"], Bconst], writes=[bB[it_["sumb"]]])
            for it_ in items:
                T = it_["T"]
                S.op("act", lambda e, it_=it_, T=T: e.activation(out=T["rs"][:, :], in_=banks[it_["sumb"]][:, :],
                                                               func=(AF.Sqrt if use_sqrt else AF.Ln),
                                                               bias=epsB[:, 0:1], scale=1.0),
                     reads=[bB[it_["sumb"]], Bconst], writes=[T["rsB"]])
            for it_ in items:
                T = it_["T"]
                if use_sqrt:
                    S.op("dve", lambda e, T=T: e.reciprocal(out=T["rs"][:, :], in_=T["rs"][:, :]),
                         reads=[T["rsB"]], writes=[T["rsB"]])
                else:
                    S.op("act", lambda e, T=T: e.activation(out=T["rs"][:, :], in_=T["rs"][:, :], func=AF.Exp, scale=-0.5),
                         reads=[T["rsB"]], writes=[T["rsB"]])
            for it_ in items:
                T = it_["T"]
                S.op("dve", lambda e, it_=it_, T=T: e.scalar_tensor_tensor(out=T["n"][:, :], in0=T["raw"][:, :], scalar=it_["w"],
                                                                         in1=T["rs"][:, :], op0=ALU.mult, op1=ALU.mult),
                     reads=[T["rawB"], T["rsB"], Bmod, Bvecs], writes=[T["nB"]])
            for it_ in items:
                T = it_["T"]
                S.op("pe", lambda e, it_=it_, T=T: e.matmul(banks[it_["rotb"]][:, :], lhsT=rmat_bf[:, :], rhs=T["n"][:, :],
                                                          start=True, stop=True),
                     reads=[T["nB"], Bconst], writes=[bB[it_["rotb"]]])
            for it_ in items:
                T = it_["T"]
                S.op("pool", lambda e, it_=it_, T=T: e.tensor_tensor(out=T["t1"][:, :], in0=T["n"][:, :], in1=it_["cos"], op=ALU.mult),
                     reads=[T["nB"], it_["tabB"]], writes=[T["t1B"]])
            for it_ in items:
                T = it_["T"]
                S.op("dve", lambda e, it_=it_, T=T: e.tensor_tensor(out=T["t2"][:, :], in0=banks[it_["rotb"]][:, :], in1=it_["sin"],
                                                                  op=ALU.mult),
                     reads=[bB[it_["rotb"]], it_["tabB"]], writes=[T["t2B"]])
            for it_ in items:
                T = it_["T"]
                S.op("pool", lambda e, it_=it_, T=T: e.tensor_tensor(out=it_["out"], in0=T["t1"][:, :], in1=T["t2"][:, :], op=ALU.add),
                     reads=[T["t1B"], T["t2B"]], writes=[it_["outB"]])
                if it_.get("after") is not None:
                    it_["after"]()

        def norm_rope(ps, psB, bias_ap, w_ap, cos_ap, sin_ap, tabB, T, out_ap, outB, extra_reads=()):
            norm_rope_multi([dict(ps=ps, psB=psB, bias=bias_ap, w=w_ap, cos=cos_ap, sin=sin_ap, tabB=tabB, T=T, out=out_ap,
                                  outB=outB, extra=extra_reads, sumb=3, rotb=4, after=None)])

        def chain_tiles_alt(es, T, pfx):
            T2 = dict(T)
            T2["raw"] = sb(es, pfx + "raw", [128, 512], F32)
            T2["sq"] = sb(es, pfx + "sq", [128, 512], BF16)
            T2["rawB"] = Buf(pfx + "raw")
            T2["sqB"] = Buf(pfx + "sq")
            return T2

        def mk_chain_tiles(es, pfx):
            T = {}
            T["raw"] = sb(es, pfx + "raw", [128, 512], F32)
            T["sq"] = sb(es, pfx + "sq", [128, 512], BF16)
            T["rs"] = sb(es, pfx + "rs", [128, 512], F32)
            T["n"] = sb(es, pfx + "n", [128, 512], BF16)
            T["t1"] = sb(es, pfx + "t1", [128, 512], F32)
            T["t2"] = sb(es, pfx + "t2", [128, 512], F32)
            for k in ("raw", "sq", "rs", "n", "t1", "t2"):
                T[k + "B"] = Buf(pfx + k)
            return T

        def bias_cols(wt, wB_, ncol, Bbf, dst_ap, dstB, col_stride):
            for c in range(ncol):
                for k in range(16):
                    S.op("pe", (lambda e, c=c, k=k: e.matmul(
                        banks[7][:, c:c + 1], lhsT=wt[:, k * col_stride + c * 128:k * col_stride + (c + 1) * 128],
                        rhs=Bbf[:, k:k + 1], start=(k == 0), stop=(k == 15))),
                        reads=[wB_, Bmod], writes=[bB[7]])
            S.op("dve", lambda e: e.tensor_copy(out=dst_ap, in_=banks[7][:, 0:ncol]), reads=[bB[7]], writes=[dstB])

        with ExitStack() as es:
            xs = [sb(es, "xs%d" % i, [128, 16 * 512], F32) for i in range(2)]
            xsB = [Buf("xs%d" % i) for i in range(2)]
            sq = sb(es, "sq", [128, 16 * 512], BF16)
            sqB = Buf("sq")
            xsP = [[Buf("xs%d_%d" % (i, j)) for j in range(4)] for i in range(2)]
            sqP = [Buf("sq_%d" % j) for j in range(4)]
            hT = [sb(es, "hT%d" % i, [128, 16 * 512], BF16) for i in range(2)]
            hTB = [Buf("hT%d" % i) for i in range(2)]
            wkv = sb(es, "wkv", [128, 16 * 512], BF16)
            wkvB = Buf("wkv")
            rstd = sb(es, "rstd", [128, 512], F32)
            rstdB = Buf("rstd")
            cs = [sb(es, "cs%d" % i, [128, 1024], F32) for i in range(2)]
            csB = [Buf("cs%d" % i) for i in range(2)]
            kout = [sb(es, "kout%d" % i, [128, 512], BF16) for i in range(2)]
            koutB = [Buf("kout%d" % i) for i in range(2)]
            vout = [sb(es, "vout%d" % i, [128, 1024], BF16) for i in range(2)]
            voutB = [Buf("vout%d" % i) for i in range(2)]
            bk = sb(es, "bk", [128, 2], F32)
            bkB = Buf("bk")
            TT = [mk_chain_tiles(es, "c1a"), mk_chain_tiles(es, "c1b")]
            TTp = [TT, [chain_tiles_alt(es, TT[0], "c1c"), chain_tiles_alt(es, TT[1], "c1d")]]
            BKT = [Buf("KT0"), Buf("KT1")]
            BV = [Buf("V0"), Buf("V1")]
            BhT = Buf("hTscr")

            S.dma("pool", v3(wkv[:, :], 16), w_in_v[:, :, 1024:1536], writes=[wkvB], key="wkv")
            bias_cols(wkv, wkvB, 2, B1bf, bk[:, 0:2], bkB, 512)
            for c in range(2):
                for k in range(16):
                    S.op("pe", (lambda e, c=c, k=k: e.matmul(
                        banks[7][:, 8 + c:9 + c], lhsT=wkv[:, k * 512 + 256 + c * 128:k * 512 + 256 + (c + 1) * 128],
                        rhs=B1bf[:, k:k + 1], start=(k == 0), stop=(k == 15))),
                        reads=[wkvB, Bmod], writes=[bB[7]])
            S.op("dve", lambda e: e.tensor_copy(out=bvcol[:, 0:2], in_=banks[7][:, 8:10]), reads=[bB[7]], writes=[Bbv])

            NB1 = 33

            def stageA1a(tb):
                sl = tb % 2
                t0 = tb * 512
                for p_ in range(4):
                    S.dma("sp", v3(xs[sl][:, :], 16)[:, 4 * p_:4 * p_ + 4, :], xT_v[:, 4 * p_:4 * p_ + 4, t0:t0 + 512],
                          writes=[xsP[sl][p_]], key=("xs", sl, p_))
                    S.op("act", (lambda e, p_=p_: e.activation(out=sq[:, p_ * 2048:(p_ + 1) * 2048],
                                                              in_=xs[sl][:, p_ * 2048:(p_ + 1) * 2048], func=AF.Square)),
                         reads=[xsP[sl][p_]], writes=[sqP[p_]])

            def stageA1b(tb):
                for k in range(16):
                    S.op("pe", (lambda e, k=k: e.matmul(banks[0][:, :], lhsT=ones_bf[:, :],
                                                       rhs=sq[:, k * 512:(k + 1) * 512],
                                                       start=(k == 0), stop=(k == 15))),
                         reads=[sqP[k // 4], Bconst], writes=[bB[0]])

            def stageA2(tb):
                sl = tb % 2
                S.op("act", lambda e: e.activation(out=rstd[:, :], in_=banks[0][:, :], func=AF.Ln,
                                                   bias=epsA[:, 0:1], scale=1.0),
                     reads=[bB[0], Bconst], writes=[rstdB])
                S.op("act", lambda e: e.activation(out=rstd[:, :], in_=rstd[:, :], func=AF.Exp, scale=-0.5),
                     reads=[rstdB], writes=[rstdB])
                for k in range(16):
                    S.op("dve", (lambda e, k=k: e.scalar_tensor_tensor(
                        out=hT[sl][:, k * 512:(k + 1) * 512], in0=xs[sl][:, k * 512:(k + 1) * 512],
                        scalar=A1[:, k:k + 1], in1=rstd[:, :], op0=ALU.mult, op1=ALU.mult)),
                        reads=[xsP[sl][k // 4], rstdB, Bmod], writes=[hTB[sl]])

            def stageB(tb):
                sl = tb % 2
                t0 = tb * 512
                own = tb < 4 or tb == 32
                if own:
                    lt0 = t0 if tb < 4 else TOWN
                    S.dma("pool", v3(hT_scr[:, :], 16)[:, :, lt0:lt0 + 512], v3(hT[sl][:, :], 16),
                          reads=[hTB[sl]], writes=[BhT], key="hTst")
                if tb == 32:
                    return None
                S.dma("sp", cs[sl][:, 0:512], cosT[:, t0:t0 + 512], writes=[csB[sl]], key=("cs", sl))
                S.dma("sp", cs[sl][:, 512:1024], sinT[:, t0:t0 + 512], writes=[csB[sl]], key=("cs", sl))
                for g in range(2):
                    for k in range(16):
                        S.op("pe", (lambda e, g=g, k=k: e.matmul(
                            banks[1 + g][:, :], lhsT=wkv[:, k * 512 + g * 128:k * 512 + (g + 1) * 128],
                            rhs=hT[sl][:, k * 512:(k + 1) * 512], start=(k == 0), stop=(k == 15))),
                            reads=[wkvB, hTB[sl]], writes=[bB[1 + g]])
                for s in range(4):
                    bank = 5 + s // 2
                    c0 = (s % 2) * 256
                    for k in range(16):
                        S.op("pe", (lambda e, s=s, k=k, bank=bank, c0=c0: e.matmul(
                            banks[bank][:, c0:c0 + 256], lhsT=hT[sl][:, k * 512 + s * 128:k * 512 + (s + 1) * 128],
                            rhs=wkv[:, k * 512 + 256:k * 512 + 512], start=(k == 0), stop=(k == 15))),
                            reads=[wkvB, hTB[sl]], writes=[bB[bank]])
                vo = vout[sl]
                S.op("act", lambda e: e.activation(out=vo[:, 0:512], in_=banks[5][:, :], func=AF.Copy),
                     reads=[bB[5]], writes=[voutB[sl]])
                S.op("dve", lambda e: e.tensor_copy(out=vo[:, 512:1024], in_=banks[6][:, :]),
                     reads=[bB[6]], writes=[voutB[sl]])
                for g in range(2):
                    S.dma("pool", v3(V_scr[g, :, tb * 512:(tb + 1) * 512], 4),
                          v3(vo[:, :], 4)[:, :, g * 128:(g + 1) * 128],
                          reads=[voutB[sl]], writes=[BV[g]], key=("vst", sl))
                items = []
                for g in range(2):
                    ko = kout[g]
                    items.append(dict(
                        ps=banks[1 + g], psB=bB[1 + g], bias=bk[:, g:g + 1], w=kw[:, 0:1], cos=cs[sl][:, 0:512],
                        sin=cs[sl][:, 512:1024], tabB=csB[sl], T=TTp[sl][g], out=ko[:, :], outB=koutB[g], extra=[bkB],
                        sumb=(3, 7)[g], rotb=(4, 3)[g],
                        after=(lambda g=g, ko=ko: S.dma("pool", KT_scr[g, :, t0:t0 + 512], ko[:, :], reads=[koutB[g]],
                                                        writes=[BKT[g]], key=("kst", g)))))
                chain_part1(items)
                return items

            stageA1a(0)
            stageA1b(0)
            stageA2(0)
            stageA1a(1)
            stageA1b(1)
            pending = None
            for tb in range(NB1):
                if tb + 1 < NB1:
                    stageA2(tb + 1)
                if tb + 2 < NB1:
                    stageA1a(tb + 2)
                items = stageB(tb)
                if tb + 2 < NB1:
                    stageA1b(tb + 2)
                if pending is not None:
                    chain_part2(pending)
                pending = items
            if pending is not None:
                chain_part2(pending)
            S.flush()
        if stop_after <= 1:
            return nc

        def load_wtile(dst, dstB, view, c0, ncols, nk, key):
            S.dma("pool", v3(dst[:, 0:nk * ncols], nk), view[:, :, c0:c0 + ncols], writes=[dstB], key=key)

        def mm_acc(bank_i, lhs_fn, rhs_fn, nk, reads, out_ap=None):
            for k in range(nk):
                S.op("pe", (lambda e, k=k: e.matmul(out_ap if out_ap is not None else banks[bank_i][:, :],
                                                   lhsT=lhs_fn(k), rhs=rhs_fn(k), start=(k == 0), stop=(k == nk - 1))),
                     reads=reads, writes=[bB[bank_i]])

        BqT, BqBT, BkBT, BvB, Bsg = Buf("qT"), Buf("qBT"), Buf("kBT"), Buf("vB"), Buf("sg")
        with ExitStack() as es:
            hTo = sb(es, "hTo", [128, 16 * TLOC], BF16)
            hToB = Buf("hTo")
            wts = [sb(es, "w2_%d" % i, [128, 16 * 512], BF16) for i in range(3)]
            wtB = [Buf("w2_%d" % i) for i in range(3)]
            cso = sb(es, "cso", [128, 2 * TOWN], F32)
            csoB = Buf("cso")
            TT2 = [mk_chain_tiles(es, "c2a"), mk_chain_tiles(es, "c2b")]
            TT2p = [TT2, [chain_tiles_alt(es, TT2[0], "c2c"), chain_tiles_alt(es, TT2[1], "c2d")]]
            qitems = []
            qpend = [None]
            qpair = [0]
            ot = [sb(es, "ot%d" % i, [128, 512], BF16) for i in range(4)]
            otB = [Buf("ot%d" % i) for i in range(4)]
            bc = [sb(es, "bc%d" % i, [128, 4], F32) for i in range(2)]
            bcB = [Buf("bc%d" % i) for i in range(2)]
            hToP = [Buf("hTo%d" % i) for i in range(5)]
            for tbp in range(5):
                S.dma("sp", v3(hTo[:, :], 16)[:, :, tbp * 512:(tbp + 1) * 512], v3(hT_scr[:, :], 16)[:, :, tbp * 512:(tbp + 1) * 512],
                      reads=[BhT], writes=[hToP[tbp]], key=("hTo", tbp))
            S.dma("sp", cso[:, 0:TOWN], cosT[:, 0:TOWN], writes=[csoB], key="cso")
            S.dma("sp", cso[:, TOWN:2 * TOWN], sinT[:, 0:TOWN], writes=[csoB], key="cso")
            order = [0, 1, 3, 4, 5, 6, 7, 8] + list(range(9, 17))
            mb = Ring([0, 1, 2, 5, 6])
            oti = 0
            for n_i, ti in enumerate(order):
                sl = n_i % 3
                wt, wB_ = wts[sl], wtB[sl]
                load_wtile(wt, wB_, w_in_v, ti * 512, 512, 16, ("w2", sl))
                if ti in (7, 8):
                    hv = ti - 7
                    bias_cols(wt, wB_, 4, B1bf, bvcol[:, 2 + 4 * hv:6 + 4 * hv], Bbv, 512)
                    for s_ in range(TLOC // 128):
                        bi = mb.next()
                        mm_acc(bi, lambda k, s_=s_: hTo[:, k * TLOC + s_ * 128:k * TLOC + (s_ + 1) * 128],
                               lambda k, wt=wt: wt[:, k * 512:(k + 1) * 512], 16, [hToP[s_ // 4], wB_])
                        o_, oB_ = ot[oti % 4], otB[oti % 4]
                        oti += 1
                        if s_ % 2 == 0:
                            S.op("act", (lambda e, o_=o_, bi=bi: e.activation(out=o_[:, :], in_=banks[bi][:, :], func=AF.Copy)),
                                 reads=[bB[bi]], writes=[oB_])
                        else:
                            S.op("dve", (lambda e, o_=o_, bi=bi: e.tensor_copy(out=o_[:, :], in_=banks[bi][:, :])),
                                 reads=[bB[bi]], writes=[oB_])
                        S.dma("sp", vB_scr[4 * hv:4 * hv + 4, :, s_ * 128:(s_ + 1) * 128].rearrange("h p d -> p h d"),
                              v3(o_[:, :], 4), reads=[oB_], writes=[BvB], key=("ot", oti % 4))
                    continue
                bcs, bcsB = bc[n_i % 2], bcB[n_i % 2]
                bias_cols(wt, wB_, 4, B1bf, bcs[:, 0:4], bcsB, 512)
                for cc in range(4):
                    ntb = 5 if ti in (5, 6) else 4
                    for tb in range(ntb):
                        bi = mb.next()
                        mm_acc(bi, lambda k, wt=wt, cc=cc: wt[:, k * 512 + cc * 128:k * 512 + (cc + 1) * 128],
                               lambda k, tb=tb: hTo[:, k * TLOC + tb * 512:k * TLOC + (tb + 1) * 512], 16, [hToP[tb], wB_])
                        o_, oB_ = ot[oti % 4], otB[oti % 4]
                        oti += 1
                        okey = ("ot", oti % 4)
                        if ti in (0, 1):
                            h = ti * 4 + cc
                            qitems.append(dict(
                                ps=banks[bi], psB=bB[bi], bias=bcs[:, cc:cc + 1], w=vecs_sb[:, C_QW:C_QW + 1],
                                cos=cso[:, tb * 512:(tb + 1) * 512], sin=cso[:, TOWN + tb * 512:TOWN + (tb + 1) * 512],
                                tabB=csoB, T=TT2p[qpair[0] % 2][tb % 2], out=o_[:, :], outB=oB_, extra=[bcsB],
                                sumb=(3, 7)[tb % 2], rotb=(4, 3)[tb % 2],
                                after=(lambda h=h, tb=tb, o_=o_, oB_=oB_, okey=okey: S.dma(
                                    "sp", qT_scr[h, :, tb * 512:(tb + 1) * 512], o_[:, :], reads=[oB_], writes=[BqT], key=okey))))
                            if len(qitems) == 2:
                                chain_part1(qitems)
                                if qpend[0] is not None:
                                    chain_part2(qpend[0])
                                qpend[0] = qitems
                                qitems = []
                                qpair[0] += 1
                                if ti == 1 and cc == 3 and tb == 3:
                                    chain_part2(qpend[0])
                                    qpend[0] = None
                        elif ti in (3, 4, 5, 6):
                            S.op("act", (lambda e, o_=o_, bi=bi, cc=cc, bcs=bcs: e.activation(
                                out=o_[:, :], in_=banks[bi][:, :], func=AF.Identity, bias=bcs[:, cc:cc + 1], scale=1.0)),
                                reads=[bB[bi], bcsB], writes=[oB_])
                            if ti in (3, 4):
                                h = (ti - 3) * 4 + cc
                                S.dma("sp", qBT_scr[h, :, tb * 512:(tb + 1) * 512], o_[:, :], reads=[oB_], writes=[BqBT], key=okey)
                            else:
                                h = (ti - 5) * 4 + cc
                                S.dma("sp", kBT_scr[h, :, tb * 512:(tb + 1) * 512], o_[:, :], reads=[oB_], writes=[BkBT], key=okey)
                        else:
                            gi = (ti - 9) * 4 + cc
                            S.op("act", (lambda e, o_=o_, bi=bi, cc=cc, bcs=bcs: e.activation(
                                out=o_[:, :], in_=banks[bi][:, :], func=AF.Sigmoid, bias=bcs[:, cc:cc + 1], scale=1.0)),
                                reads=[bB[bi], bcsB], writes=[oB_])
                            S.dma("sp", sg_scr[gi, :, tb * 512:(tb + 1) * 512], o_[:, :], reads=[oB_], writes=[Bsg], key=okey)
            S.flush()
        if stop_after <= 2:
            return nc

        ByT = Buf("yT")

        def attn_epilogue(Obank, Lbank, bias_ap, E, dst_ap, key):
            S.op("act", lambda e: e.activation(out=E["rec"][:, :], in_=banks[Lbank][:, :], func=AF.Ln), reads=[bB[Lbank]], writes=[E["recB"]])
            S.op("act", lambda e: e.activation(out=E["rec"][:, :], in_=E["rec"][:, :], func=AF.Exp, scale=-1.0), reads=[E["recB"]], writes=[E["recB"]])
            S.op("dve", lambda e: e.tensor_tensor(out=E["o"][:, :], in0=banks[Obank][:, :], in1=E["rec"][:, :], op=ALU.mult),
                 reads=[bB[Obank], E["recB"]], writes=[E["oB"]])
            y_, yB_ = E["y"][E["i"] % 2], E["yB"][E["i"] % 2]
            E["i"] += 1
            S.op("dve", lambda e: e.tensor_scalar(out=y_[:, :], in0=E["o"][:, :], scalar1=bias_ap, scalar2=None, op0=ALU.add),
                 reads=[E["oB"], Bbv], writes=[yB_])
            S.dma("pool", dst_ap, y_[:, :], reads=[yB_], writes=[ByT], key=(key, E["i"] % 2))

        def mk_epi(es, pfx):
            E = {"rec": sb(es, pfx + "rec", [128, 512], F32), "o": sb(es, pfx + "o", [128, 512], F32),
                 "y": [sb(es, pfx + "y%d" % i, [128, 512], BF16) for i in range(2)],
                 "recB": Buf("rec"), "oB": Buf("o"), "yB": [Buf("y0"), Buf("y1")], "i": 0}
            return E

        with ExitStack() as es:
            KTs = [sb(es, "KTs%d" % g, [128, S_ALL], BF16) for g in range(2)]
            Vs = [sb(es, "Vs%d" % g, [128, S_ALL], BF16) for g in range(2)]
            qs = [sb(es, "qs%d" % g, [128, 4 * TOWN], BF16) for g in range(2)]
            KTsB = [[Buf("KTs%d_%d" % (g_, p_)) for p_ in range(4)] for g_ in range(2)]
            VsB = [[Buf("Vs%d_%d" % (g_, p_)) for p_ in range(4)] for g_ in range(2)]
            qsB = [Buf("qs0"), Buf("qs1")]
            Pt = [sb(es, "Pt%d" % i, [128, 512], BF16) for i in range(4)]
            PtB = [Buf("Pt%d" % i) for i in range(4)]
            E = mk_epi(es, "e3")
            for g in range(2):
                S.dma("sp", qs[g][:, :].rearrange("p (h t) -> p h t", h=4), qT_scr[4 * g:4 * g + 4].rearrange("h p t -> p h t"),
                      reads=[BqT], writes=[qsB[g]], key=("qs", g))
                for part in range(4):
                    c0 = part * 4096
                    S.dma("sp", KTs[g][:, c0:c0 + 4096], KT_scr[g, :, c0:c0 + 4096], reads=[BKT[g]], writes=[KTsB[g][part]], key=("KTs", g, part))
                    S.dma("sp", Vs[g][:, c0:c0 + 4096], V_scr[g, :, c0:c0 + 4096], reads=[BV[g]], writes=[VsB[g][part]], key=("Vs", g, part))
            item = 0
            NKC = S_ALL // 128
            for g in range(2):
                for qb in range(4):
                    for hh in range(4):
                        h = 4 * g + hh
                        Ob, Lb = 4 + item % 2, 6 + item % 2
                        item += 1
                        q_ap = qs[g][:, hh * TOWN + qb * 512:hh * TOWN + (qb + 1) * 512]

                        def s_mm(kc, g=g, q_ap=q_ap):
                            bi = kc % 4
                            S.op("pe", (lambda e: e.matmul(banks[bi][:, :], lhsT=KTs[g][:, kc * 128:(kc + 1) * 128], rhs=q_ap,
                                                          start=True, stop=True)),
                                 reads=[KTsB[g][kc // 32], qsB[g]], writes=[bB[bi]])
                            S.op("act", (lambda e: e.activation(out=Pt[bi][:, :], in_=banks[bi][:, :], func=AF.Exp)),
                                 reads=[bB[bi]], writes=[PtB[bi]])

                        def pv_mm(kc, g=g, Ob=Ob, Lb=Lb):
                            bi = kc % 4
                            S.op("pe", (lambda e: e.matmul(banks[Ob][:, :], lhsT=Vs[g][:, kc * 128:(kc + 1) * 128], rhs=Pt[bi][:, :],
                                                          start=(kc == 0), stop=(kc == NKC - 1))),
                                 reads=[VsB[g][kc // 32], PtB[bi]], writes=[bB[Ob]])
                            S.op("pe", (lambda e: e.matmul(banks[Lb][:, :], lhsT=ones_bf[:, :], rhs=Pt[bi][:, :],
                                                          start=(kc == 0), stop=(kc == NKC - 1))),
                                 reads=[PtB[bi], Bconst], writes=[bB[Lb]])

                        s_mm(0)
                        s_mm(1)
                        for kc in range(NKC):
                            if kc + 2 < NKC:
                                s_mm(kc + 2)
                            pv_mm(kc)
                        attn_epilogue(Ob, Lb, bvcol[:, g:g + 1], E, yT_scr[h, :, qb * 512:(qb + 1) * 512], "y3")
            S.flush()
        if stop_after <= 3:
            return nc

        def lc_of(kc):
            return kc if kc < 16 else (kc - 18 if kc < 18 else kc - 2)

        def kc_of(lc):
            return lc if 0 <= lc < 16 else (lc + 18 if lc < 0 else lc + 2)

        with ExitStack() as es:
            kBs = [sb(es, "kBs%d" % i, [128, TLOC], BF16) for i in range(2)]
            vBs = [sb(es, "vBs%d" % i, [128, TLOC], BF16) for i in range(2)]
            qBs = [sb(es, "qBs%d" % i, [128, TOWN], BF16) for i in range(2)]
            hdB = [Buf("hd0"), Buf("hd1")]
            tab = sb(es, "tab", [128, 20 * 640], F32)
            tabB = [Buf("tab%d" % i) for i in range(20)]
            Pa = [sb(es, "Pa%d" % i, [128, 20 * 640], BF16) for i in range(2)]
            PaB = [[Buf("Pa%d_%d" % (i, j)) for j in range(20)] for i in range(2)]
            tmp = [sb(es, "natmp%d" % i, [128, 640], F32) for i in range(2)]
            tmpB = [Buf("natmp0"), Buf("natmp1")]
            E = mk_epi(es, "e4")
            scale = 1.0 / math.sqrt(128.0)
            item = 0
            for h in range(8):
                sl = h % 2
                S.dma("sp", kBs[sl][:, :], kBT_scr[h], reads=[BkBT], writes=[hdB[sl]], key=("hd", sl))
                S.dma("sp", vBs[sl][:, :], vB_scr[h], reads=[BvB], writes=[hdB[sl]], key=("hd", sl))
                S.dma("sp", qBs[sl][:, :], qBT_scr[h], reads=[BqBT], writes=[hdB[sl]], key=("hd", sl))
                for kc in range(20):
                    S.dma("sp", tab[:, kc * 640:(kc + 1) * 640], natab[h, kc], writes=[tabB[kc]], key=("tab", kc % 4))
                for kc in range(20):
                    lc = lc_of(kc)
                    blo, bhi = max(0, lc - 2), min(15, lc + 2)
                    nq = (bhi - blo + 1) * 128
                    q0 = blo * 128
                    dd = dbl[kc % 2]
                    n1 = min(nq, 512)
                    S.op("pe", (lambda e, kc=kc, dd=dd, n1=n1, q0=q0, sl=sl: e.matmul(
                        dd[:, 0:n1], lhsT=kBs[sl][:, kc * 128:(kc + 1) * 128], rhs=qBs[sl][:, q0:q0 + n1], start=True, stop=True)),
                        reads=[hdB[sl]], writes=[bB[2 * (kc % 2)]])
                    rd = [bB[2 * (kc % 2)]]
                    if nq > 512:
                        S.op("pe", (lambda e, kc=kc, dd=dd, nq=nq, q0=q0, sl=sl: e.matmul(
                            dd[:, 512:nq], lhsT=kBs[sl][:, kc * 128:(kc + 1) * 128], rhs=qBs[sl][:, q0 + 512:q0 + nq],
                            start=True, stop=True)),
                            reads=[hdB[sl]], writes=[bB[2 * (kc % 2) + 1]])
                        rd.append(bB[2 * (kc % 2) + 1])
                    tm, tmB_ = tmp[kc % 2], tmpB[kc % 2]
                    S.op("dve", (lambda e, kc=kc, dd=dd, n1=n1, tm=tm: e.scalar_tensor_tensor(
                        out=tm[:, 0:n1], in0=dd[:, 0:n1], scalar=scale, in1=tab[:, kc * 640:kc * 640 + n1],
                        op0=ALU.mult, op1=ALU.add)), reads=[rd[0], tabB[kc]], writes=[tmB_])
                    if nq > 512:
                        S.op("dve", (lambda e, kc=kc, dd=dd, nq=nq, tm=tm: e.scalar_tensor_tensor(
                            out=tm[:, 512:nq], in0=dd[:, 512:nq], scalar=scale, in1=tab[:, kc * 640 + 512:kc * 640 + nq],
                            op0=ALU.mult, op1=ALU.add)), reads=[rd[1], tabB[kc]], writes=[tmB_])
                    S.op("act", (lambda e, kc=kc, nq=nq, tm=tm, sl=sl: e.activation(
                        out=Pa[sl][:, kc * 640:kc * 640 + nq], in_=tm[:, 0:nq], func=AF.Exp)),
                        reads=[tmB_], writes=[PaB[sl][kc]])
                for qb in range(4):
                    Ob, Lb = 4 + item % 2, 6 + item % 2
                    item += 1
                    for bq in range(4):
                        b = qb * 4 + bq
                        lcs = list(range(b - 2, b + 3))
                        for i_, lc in enumerate(lcs):
                            kc = kc_of(lc)
                            j0 = (b - max(0, lc - 2)) * 128
                            for (bank_i, lhs) in ((Ob, None), (Lb, ones_bf)):
                                S.op("pe", (lambda e, kc=kc, j0=j0, bank_i=bank_i, lhs=lhs, bq=bq, i_=i_, sl=sl: e.matmul(
                                    banks[bank_i][:, bq * 128:(bq + 1) * 128],
                                    lhsT=(vBs[sl][:, kc * 128:(kc + 1) * 128] if lhs is None else lhs[:, :]),
                                    rhs=Pa[sl][:, kc * 640 + j0:kc * 640 + j0 + 128], start=(i_ == 0), stop=(i_ == 4))),
                                    reads=[hdB[sl], PaB[sl][kc], Bconst], writes=[bB[bank_i]])
                    attn_epilogue(Ob, Lb, bvcol[:, 2 + h:3 + h], E, yT_scr[8 + h, :, qb * 512:(qb + 1) * 512], "y4")
            S.flush()
        if stop_after <= 4:
            return nc

        m_scr = dscr("m_scr", [16, 128, TOWN])
        Bm = Buf("m")
        Bx1, Bh2 = Buf("x1"), Buf("h2")
        with ExitStack() as es:
            woa = sb(es, "woa", [128, 8 * D], BF16)
            wob = sb(es, "wob", [128, 8 * D], BF16)
            woP = [Buf("woab%d" % i) for i in range(4)]
            yab = [sb(es, "yab%d" % i, [128, 16 * 512], BF16) for i in range(2)]
            yabB = [Buf("yab0"), Buf("yab1")]
            sgt = [sb(es, "sgt%d" % i, [128, 1024], BF16) for i in range(3)]
            sgtB = [Buf("sgt%d" % i) for i in range(3)]
            t1 = [sb(es, "m_t1%d" % i, [128, 512], F32) for i in range(2)]
            t2 = [sb(es, "m_t2%d" % i, [128, 512], F32) for i in range(2)]
            t1B = [Buf("t1a"), Buf("t1b")]
            t2B = [Buf("t2a"), Buf("t2b")]
            mo = [sb(es, "mo%d" % i, [128, 512], BF16) for i in range(3)]
            moB = [Buf("mo%d" % i) for i in range(3)]
            for part in range(4):
                S.dma("pool", v3(woa[:, :], 8)[:, :, part * 512:(part + 1) * 512],
                      w_oa.rearrange("(k p) n -> p k n", p=128)[:, :, part * 512:(part + 1) * 512], writes=[woP[part]], key=("woab", part))
                S.dma("pool", v3(wob[:, :], 8)[:, :, part * 512:(part + 1) * 512],
                      w_ob.rearrange("(k p) n -> p k n", p=128)[:, :, part * 512:(part + 1) * 512], writes=[woP[part]], key=("woab", part))
            it = 0
            for tb in range(4):
                sl = tb % 2
                S.dma("sp", v3(yab[sl][:, :], 16), yT_scr.rearrange("k p t -> p k t")[:, :, tb * 512:(tb + 1) * 512],
                      reads=[ByT], writes=[yabB[sl]], key=("yab", sl))
                for n in range(16):
                    s3 = it % 3
                    S.dma("sp", sgt[s3][:, 0:512], sg_scr[n, :, tb * 512:(tb + 1) * 512], reads=[Bsg], writes=[sgtB[s3]], key=("sgt", s3))
                    S.dma("sp", sgt[s3][:, 512:1024], sg_scr[16 + n, :, tb * 512:(tb + 1) * 512], reads=[Bsg], writes=[sgtB[s3]], key=("sgt", s3))
                    ba, bb_ = (0, 1) if it % 2 == 0 else (2, 5)
                    mm_acc(ba, lambda k, n=n: woa[:, k * D + n * 128:k * D + (n + 1) * 128],
                           lambda k, sl=sl: yab[sl][:, k * 512:(k + 1) * 512], 8, [woP[n // 4], yabB[sl]])
                    mm_acc(bb_, lambda k, n=n: wob[:, k * D + n * 128:k * D + (n + 1) * 128],
                           lambda k, sl=sl: yab[sl][:, (8 + k) * 512:(9 + k) * 512], 8, [woP[n // 4], yabB[sl]])
                    i2 = it % 2
                    S.op("dve", (lambda e, i2=i2, ba=ba, s3=s3: e.tensor_tensor(out=t1[i2][:, :], in0=banks[ba][:, :], in1=sgt[s3][:, 0:512], op=ALU.mult)),
                         reads=[bB[ba], sgtB[s3]], writes=[t1B[i2]])
                    S.op("dve", (lambda e, i2=i2, bb_=bb_, s3=s3: e.tensor_tensor(out=t2[i2][:, :], in0=banks[bb_][:, :], in1=sgt[s3][:, 512:1024], op=ALU.mult)),
                         reads=[bB[bb_], sgtB[s3]], writes=[t2B[i2]])
                    S.op("pool", (lambda e, i2=i2, s3=s3: e.tensor_tensor(out=mo[s3][:, :], in0=t1[i2][:, :], in1=t2[i2][:, :], op=ALU.add)),
                         reads=[t1B[i2], t2B[i2]], writes=[moB[s3]])
                    S.dma("pool", m_scr[n, :, tb * 512:(tb + 1) * 512], mo[s3][:, :], reads=[moB[s3]], writes=[Bm], key=("mo", s3))
                    it += 1
            S.flush()

        def stats_to_rstd(bank_i, rs_tile, rsB):
            S.op("act", lambda e: e.activation(out=rs_tile[:, :], in_=banks[bank_i][:, :], func=AF.Ln, bias=epsA[:, 0:1], scale=1.0),
                 reads=[bB[bank_i], Bconst], writes=[rsB])
            S.op("act", lambda e: e.activation(out=rs_tile[:, :], in_=rs_tile[:, :], func=AF.Exp, scale=-0.5), reads=[rsB], writes=[rsB])

        with ExitStack() as es:
            wout = sb(es, "wout", [128, 16 * D], BF16)
            woutP = [Buf("wout%d" % i) for i in range(4)]
            x1cB = [Buf("x1c%d" % i) for i in range(16)]
            mb_ = [sb(es, "mblk%d" % i, [128, 16 * 512], BF16) for i in range(2)]
            mbB = [Buf("mblk0"), Buf("mblk1")]
            xin = [sb(es, "xin%d" % i, [128, 512], F32) for i in range(3)]
            xinB = [Buf("xin%d" % i) for i in range(3)]
            x1b = sb(es, "x1b", [128, 16 * 512], F32)
            x1bB = Buf("x1b")
            sq5 = [sb(es, "sq5_%d" % i, [128, 512], BF16) for i in range(2)]
            sq5B = [Buf("sq5a"), Buf("sq5b")]
            rs5 = sb(es, "rs5", [128, 512], F32)
            rs5B = Buf("rs5")
            h2o = sb(es, "h2o", [128, 16 * 512], BF16)
            h2oB = Buf("h2o")
            for part in range(4):
                S.dma("pool", v3(wout[:, :], 16)[:, :, part * 512:(part + 1) * 512],
                      w_out.rearrange("(k p) n -> p k n", p=128)[:, :, part * 512:(part + 1) * 512], writes=[woutP[part]], key=("wout", part))
            it = 0
            mr = Ring([0, 1, 2, 3])
            pend5 = None

            def stat5(n, q2):
                S.op("pe", (lambda e: e.matmul(banks[7][:, :], lhsT=ones_bf[:, :], rhs=sq5[q2][:, :], start=(n == 0), stop=(n == 15))),
                     reads=[sq5B[q2], Bconst], writes=[bB[7]])

            for tb in range(4):
                sl = tb % 2
                S.dma("sp", v3(mb_[sl][:, :], 16), m_scr.rearrange("k p t -> p k t")[:, :, tb * 512:(tb + 1) * 512],
                      reads=[Bm], writes=[mbB[sl]], key=("mblk", sl))
                for n in range(16):
                    s3 = it % 3
                    it += 1
                    S.dma("sp", xin[s3][:, :], xT_v[:, n, tb * 512:(tb + 1) * 512], writes=[xinB[s3]], key=("xin", s3))
                    bi = mr.next()
                    mm_acc(bi, lambda k, n=n: wout[:, k * D + n * 128:k * D + (n + 1) * 128],
                           lambda k, sl=sl: mb_[sl][:, k * 512:(k + 1) * 512], 16, [woutP[n // 4], mbB[sl]])
                    S.op("dve", (lambda e, bi=bi, n=n, s3=s3: e.scalar_tensor_tensor(
                        out=x1b[:, n * 512:(n + 1) * 512], in0=banks[bi][:, :], scalar=mod_sb[:, 32 + n:33 + n], in1=xin[s3][:, :],
                        op0=ALU.mult, op1=ALU.add)), reads=[bB[bi], xinB[s3], Bmod], writes=[x1cB[n]])
                    q2 = n % 2
                    S.op("act", (lambda e, n=n, q2=q2: e.activation(out=sq5[q2][:, :], in_=x1b[:, n * 512:(n + 1) * 512], func=AF.Square)),
                         reads=[x1cB[n]], writes=[sq5B[q2]])
                    if pend5 is not None:
                        stat5(*pend5)
                    pend5 = (n, q2)
                stat5(*pend5)
                pend5 = None
                stats_to_rstd(7, rs5, rs5B)
                for k in range(16):
                    S.op("dve", (lambda e, k=k: e.scalar_tensor_tensor(
                        out=h2o[:, k * 512:(k + 1) * 512], in0=x1b[:, k * 512:(k + 1) * 512], scalar=A2[:, k:k + 1], in1=rs5[:, :],
                        op0=ALU.mult, op1=ALU.mult)), reads=[x1cB[k], rs5B, Bmod], writes=[h2oB])
                S.dma("pool", v3(h2T_scr[:, :], 16)[:, :, tb * 512:(tb + 1) * 512], v3(h2o[:, :], 16), reads=[h2oB], writes=[Bh2], key="h2st")
                for j4 in range(4):
                    S.dma("pool", x1T_scr.rearrange("k p t -> p k t")[:, 4 * j4:4 * j4 + 4, tb * 512:(tb + 1) * 512],
                          v3(x1b[:, :], 16)[:, 4 * j4:4 * j4 + 4, :],
                          reads=x1cB[4 * j4:4 * j4 + 4], writes=[Bx1], key=("x1st", j4))
            S.flush()
        if stop_after <= 5:
            return nc

        Bact = Buf("act")
        w_gate_v = w_gate.rearrange("(k p) n -> p k n", p=128)
        w_up_v = w_up.rearrange("(k p) n -> p k n", p=128)
        with ExitStack() as es:
            h2 = sb(es, "h2", [128, 16 * TOWN], BF16)
            h2B = Buf("h2")
            wg = [sb(es, "wg%d" % i, [128, 16 * 512], BF16) for i in range(2)]
            wu = [sb(es, "wu%d" % i, [128, 16 * 512], BF16) for i in range(2)]
            wgB = [Buf("wg0"), Buf("wg1")]
            wuB = [Buf("wu0"), Buf("wu1")]
            bg = [sb(es, "bg%d" % i, [128, 4], F32) for i in range(2)]
            bu = [sb(es, "bu%d" % i, [128, 4], F32) for i in range(2)]
            bgB = [Buf("bg0"), Buf("bg1")]
            buB = [Buf("bu0"), Buf("bu1")]
            sgl = [sb(es, "sgl%d" % i, [128, 512], F32) for i in range(2)]
            sglB = [Buf("sgl0"), Buf("sgl1")]
            ao = [sb(es, "ao%d" % i, [128, 512], BF16) for i in range(3)]
            aoB = [Buf("ao%d" % i) for i in range(3)]
            h2P = [Buf("h2_%d" % i) for i in range(4)]
            for tbp in range(4):
                S.dma("sp", v3(h2[:, :], 16)[:, :, tbp * 512:(tbp + 1) * 512], v3(h2T_scr[:, :], 16)[:, :, tbp * 512:(tbp + 1) * 512],
                      reads=[Bh2], writes=[h2P[tbp]], key=("h2ld", tbp))
            it = 0
            for ti in range(11):
                sl = ti % 2
                load_wtile(wg[sl], wgB[sl], w_gate_v, ti * 512, 512, 16, ("wg", sl))
                load_wtile(wu[sl], wuB[sl], w_up_v, ti * 512, 512, 16, ("wu", sl))
                bias_cols(wg[sl], wgB[sl], 4, B2bf, bg[sl][:, 0:4], bgB[sl], 512)
                bias_cols(wu[sl], wuB[sl], 4, B2bf, bu[sl][:, 0:4], buB[sl], 512)
                for cc in range(4):
                    for tb in range(4):
                        bG, bU = ((0, 1), (2, 3), (4, 5))[it % 3]
                        i2, i3 = it % 2, it % 3
                        it += 1
                        mm_acc(bG, lambda k, sl=sl, cc=cc: wg[sl][:, k * 512 + cc * 128:k * 512 + (cc + 1) * 128],
                               lambda k, tb=tb: h2[:, k * TOWN + tb * 512:k * TOWN + (tb + 1) * 512], 16, [wgB[sl], h2P[tb]])
                        mm_acc(bU, lambda k, sl=sl, cc=cc: wu[sl][:, k * 512 + cc * 128:k * 512 + (cc + 1) * 128],
                               lambda k, tb=tb: h2[:, k * TOWN + tb * 512:k * TOWN + (tb + 1) * 512], 16, [wuB[sl], h2P[tb]])
                        S.op("act", (lambda e, bG=bG, i2=i2, sl=sl, cc=cc: e.activation(
                            out=sgl[i2][:, :], in_=banks[bG][:, :], func=AF.Silu, bias=bg[sl][:, cc:cc + 1], scale=1.0)),
                            reads=[bB[bG], bgB[sl]], writes=[sglB[i2]])
                        S.op("dve", (lambda e, bU=bU, i2=i2, i3=i3, sl=sl, cc=cc: e.scalar_tensor_tensor(
                            out=ao[i3][:, :], in0=banks[bU][:, :], scalar=bu[sl][:, cc:cc + 1], in1=sgl[i2][:, :],
                            op0=ALU.add, op1=ALU.mult)), reads=[bB[bU], buB[sl], sglB[i2]], writes=[aoB[i3]])
                        S.dma("sp", act_scr[ti * 4 + cc, :, tb * 512:(tb + 1) * 512], ao[i3][:, :], reads=[aoB[i3]], writes=[Bact],
                              key=("ao", i3))
            S.flush()
        if stop_after <= 6:
            return nc

        Bx2h = [Buf("x2a"), Buf("x2b")]
        w_down_v = w_down.rearrange("(k p) n -> p k n", p=128)
        with ExitStack() as es:
            acth = sb(es, "acth", [128, 44 * 1024], BF16)
            acthB = Buf("acth")
            wd = [sb(es, "wd%d" % i, [128, 44 * 256], BF16) for i in range(2)]
            wdB = [Buf("wd0"), Buf("wd1")]
            x1i = [sb(es, "x1i%d" % i, [128, 512], F32) for i in range(3)]
            x1iB = [Buf("x1i%d" % i) for i in range(3)]
            x2o = [sb(es, "x2o%d" % i, [128, 512], F32) for i in range(3)]
            x2oB = [Buf("x2o%d" % i) for i in range(3)]
            sq7 = [sb(es, "sq7_%d" % i, [128, 512], BF16) for i in range(2)]
            sq7B = [Buf("sq7a"), Buf("sq7b")]
            rsh = [[sb(es, "rsh%d_%d" % (h_, i), [128, 512], F32) for i in range(2)] for h_ in range(2)]
            rshB = [[Buf("rsh%d_%d" % (h_, i)) for i in range(2)] for h_ in range(2)]
            x2i = [sb(es, "x2i%d" % i, [128, 512], F32) for i in range(4)]
            x2iB = [Buf("x2i%d" % i) for i in range(4)]
            Fw = sb(es, "Fw", [128, 16], F32)
            FwB = Buf("Fw")
            yo = [sb(es, "yo%d" % i, [128, 512], F32) for i in range(4)]
            yoB = [Buf("yo%d" % i) for i in range(4)]
            Bout = Buf("out")
            S.op("pool", lambda e: e.tensor_scalar(out=Fw[:, :], in0=vecs_sb[:, C_FW:C_FW + 16], scalar1=math.sqrt(D), scalar2=None, op0=ALU.mult),
                 reads=[Bvecs], writes=[FwB])
            it = 0
            wi = 0
            mr = Ring([0, 1, 2, 3, 4, 5])
            pend7 = None

            def stat7(q2, tb2, n):
                S.op("pe", (lambda e: e.matmul(banks[6 + tb2][:, :], lhsT=ones_bf[:, :], rhs=sq7[q2][:, :],
                                              start=(n == 0), stop=(n == 15))),
                     reads=[sq7B[q2], Bconst], writes=[bB[6 + tb2]])

            def load_act(half):
                h0_ = half * 1024
                for part in range(4):
                    S.dma("sp", v3(acth[:, :], 44)[:, part * 11:(part + 1) * 11, :],
                          act_scr.rearrange("k p t -> p k t")[:, part * 11:(part + 1) * 11, h0_:h0_ + 1024],
                          reads=[Bact], writes=[acthB], key="acth")

            fctr = [0]

            def final_pass(half):
                h0_ = half * 1024
                for tb2 in range(2):
                    stats_to_rstd(6 + tb2, rsh[half][tb2], rshB[half][tb2])
                yield
                for tb2 in range(2):
                    c0 = h0_ + tb2 * 512
                    for n in range(16):
                        s3 = fctr[0] % 4
                        fctr[0] += 1
                        S.dma("sp", x2i[s3][:, :], x2T_scr[n, :, c0:c0 + 512], reads=[Bx2h[half]], writes=[x2iB[s3]], key=("x2i", s3))
                        S.op("dve", (lambda e, s3=s3, n=n, tb2=tb2: e.scalar_tensor_tensor(
                            out=yo[s3][:, :], in0=x2i[s3][:, :], scalar=Fw[:, n:n + 1], in1=rsh[half][tb2][:, :],
                            op0=ALU.mult, op1=ALU.mult)), reads=[x2iB[s3], FwB, rshB[half][tb2]], writes=[yoB[s3]])
                        S.dma("act", outT[n, :, c0:c0 + 512], yo[s3][:, :], reads=[yoB[s3]], writes=[Bout], key=("yo", s3))
                        yield

            load_act(0)
            fp_gen = None
            for half in range(2):
                h0 = half * 1024
                for nt in range(8):
                    sl = wi % 2
                    wi += 1
                    for part in range(4):
                        S.dma("pool", v3(wd[sl][:, :], 44)[:, part * 11:(part + 1) * 11, :],
                              w_down_v[:, part * 11:(part + 1) * 11, nt * 256:(nt + 1) * 256], writes=[wdB[sl]], key=("wd", sl))
                    for nn in range(2):
                        n = nt * 2 + nn
                        for tb2 in range(2):
                            s3 = it % 3
                            q2 = it % 2
                            it += 1
                            c0 = h0 + tb2 * 512
                            S.dma("sp", x1i[s3][:, :], x1T_scr[n, :, c0:c0 + 512], reads=[Bx1], writes=[x1iB[s3]], key=("x1i", s3))
                            bi = mr.next()
                            mm_acc(bi, lambda k, sl=sl, nn=nn: wd[sl][:, k * 256 + nn * 128:k * 256 + (nn + 1) * 128],
                                   lambda k, tb2=tb2: acth[:, k * 1024 + tb2 * 512:k * 1024 + (tb2 + 1) * 512], 44, [wdB[sl], acthB])
                            S.op("dve", (lambda e, bi=bi, n=n, s3=s3: e.scalar_tensor_tensor(
                                out=x2o[s3][:, :], in0=banks[bi][:, :], scalar=mod_sb[:, 80 + n:81 + n], in1=x1i[s3][:, :],
                                op0=ALU.mult, op1=ALU.add)), reads=[bB[bi], x1iB[s3], Bmod], writes=[x2oB[s3]])
                            S.op("act", (lambda e, s3=s3, q2=q2: e.activation(out=sq7[q2][:, :], in_=x2o[s3][:, :], func=AF.Square)),
                                 reads=[x2oB[s3]], writes=[sq7B[q2]])
                            if pend7 is not None:
                                stat7(*pend7)
                            pend7 = (q2, tb2, n)
                            S.dma("act", x2T_scr[n, :, c0:c0 + 512], x2o[s3][:, :], reads=[x2oB[s3]], writes=[Bx2h[half]], key=("x2o", s3))
                            if fp_gen is not None:
                                next(fp_gen, None)
                stat7(*pend7)
                pend7 = None
                if half == 0:
                    load_act(1)
                    fp_gen = final_pass(0)
                    next(fp_gen)
                else:
                    for _ in fp_gen:
                        pass
                    for _ in final_pass(1):
                        pass
            S.flush()
        return nc


def _pack_cols(v):
    v = np.asarray(v, np.float32).reshape(-1)
    return np.ascontiguousarray(v.reshape(-1, 128).T)


def _token_order(c):
    own = np.arange(TOWN * c, TOWN * (c + 1))
    others = np.concatenate([np.arange(0, TOWN * c), np.arange(TOWN * (c + 1), S_ALL)])
    r0 = 32 * c
    rows_before = np.arange(r0 - 4, r0) if c > 0 else np.arange(4, 8)
    rows_after = np.arange(r0 + 32, r0 + 36) if c < NCORE - 1 else np.arange(248, 252)
    halo_rows = np.concatenate([rows_before, rows_after])
    halo = (halo_rows[:, None] * GRID_W + np.arange(GRID_W)[None, :]).reshape(-1)
    return own, others, halo, halo_rows


def _rope_tables():
    t = np.arange(S_ALL)
    row = (t // GRID_W).astype(np.float32)
    col = (t % GRID_W).astype(np.float32)
    inv = (np.float32(10000.0) ** (-(np.arange(0, 64, 2, dtype=np.float32) / np.float32(64.0)))).astype(np.float32)
    ang = np.concatenate([row[:, None] * inv[None], col[:, None] * inv[None]], axis=-1).astype(np.float32)
    cos = np.cos(ang).astype(np.float32)
    sin = np.sin(ang).astype(np.float32)
    cosT = np.repeat(cos.T, 2, axis=0)
    sinT = np.repeat(sin.T, 2, axis=0)
    sinT[0::2] *= -1.0
    return np.ascontiguousarray(cosT), np.ascontiguousarray(sinT)


def _na_table(c, rpb, halo_rows):
    tab = np.full((8, 20, 128, 640), MASK_NEG, np.float32)
    rows_tot = S_ALL // GRID_W
    for kc in range(20):
        if kc < 16:
            lc = kc
            grow = 32 * c + 2 * kc + np.arange(2)
        elif kc < 18:
            lc = kc - 18
            grow = halo_rows[(kc - 16) * 2:(kc - 16) * 2 + 2]
        else:
            lc = kc - 2
            grow = halo_rows[4 + (kc - 18) * 2:4 + (kc - 18) * 2 + 2]
        key_r = np.repeat(grow, GRID_W)
        key_c = np.tile(np.arange(GRID_W), 2)
        blo = max(0, lc - 2)
        bhi = min(15, lc + 2)
        for b in range(blo, bhi + 1):
            qr = np.repeat(32 * c + 2 * b + np.arange(2), GRID_W)
            qc = np.tile(np.arange(GRID_W), 2)
            rs = np.clip(qr - NA_ROWS // 2, 0, rows_tot - NA_ROWS)
            cs_ = np.clip(qc - NA_COLS // 2, 0, GRID_W - NA_COLS)
            inr = (key_r[:, None] >= rs[None, :]) & (key_r[:, None] < rs[None, :] + NA_ROWS)
            inc = (key_c[:, None] >= cs_[None, :]) & (key_c[:, None] < cs_[None, :] + NA_COLS)
            valid = inr & inc
            if kc >= 16:
                own_lo = 32 * c + 2 * (b - 2)
                own_hi = 32 * c + 2 * (b + 2) + 1
                lo = max(own_lo, 32 * c)
                hi = min(own_hi, 32 * c + 31)
                dup = (key_r >= lo) & (key_r <= hi)
                valid &= ~dup[:, None]
            rel_r = np.clip(key_r[:, None] - qr[None, :] + (NA_ROWS - 1), 0, 2 * NA_ROWS - 2)
            rel_c = np.clip(key_c[:, None] - qc[None, :] + (NA_COLS - 1), 0, 2 * NA_COLS - 2)
            j0 = (b - blo) * 128
            for h in range(8):
                bias = rpb[h][rel_r, rel_c]
                tab[h, kc, :, j0:j0 + 128] = np.where(valid, bias, np.float32(MASK_NEG))
    return tab


def prep_inputs(inp):
    x = np.asarray(inp["x"], np.float32)[0]
    xTfull = np.ascontiguousarray(x.T)
    cosT, sinT = _rope_tables()
    rm = np.zeros((128, 128), np.float32)
    for k in range(128):
        rm[k, k ^ 1] = 1.0
    vec = np.zeros((128, NV), np.float32)
    vec[:, 0:16] = _pack_cols(inp["c"])
    vec[:, 16:112] = _pack_cols(inp["b_ada"])
    vec[:, 112:128] = _pack_cols(inp["norm1_w"])
    vec[:, 128:144] = _pack_cols(inp["norm2_w"])
    vec[:, 144:160] = _pack_cols(inp["final_w"])
    vec[:, 160:161] = _pack_cols(inp["q_norm_w"])
    vec[:, 161:162] = _pack_cols(inp["k_norm_w"])
    rpb = np.asarray(inp["nat_rpb"], np.float32)[0]
    shared = {
        "vecs": vec, "rmat": rm,
        "w_ada": np.ascontiguousarray(np.asarray(inp["w_ada"], np.float32)[0]),
        "w_in": np.ascontiguousarray(np.asarray(inp["w_in"], np.float32)[0]),
        "w_oa": np.ascontiguousarray(np.asarray(inp["w_oa"], np.float32)[0]),
        "w_ob": np.ascontiguousarray(np.asarray(inp["w_ob"], np.float32)[0]),
        "w_out": np.ascontiguousarray(np.asarray(inp["w_out"], np.float32)[0]),
        "w_gate": np.ascontiguousarray(np.asarray(inp["w_ffn_gate"], np.float32)[0]),
        "w_up": np.ascontiguousarray(np.asarray(inp["w_ffn_up"], np.float32)[0]),
        "w_down": np.ascontiguousarray(np.asarray(inp["w_ffn_down"], np.float32)[0]),
    }
    maps = []
    for c in range(NCORE):
        own, others, halo, halo_rows = _token_order(c)
        order = np.concatenate([own, others, halo])
        m = dict(shared)
        m["xT"] = np.ascontiguousarray(xTfull[:, order])
        o2 = order[:S_ALL]
        m["cosT"] = np.ascontiguousarray(cosT[:, o2])
        m["sinT"] = np.ascontiguousarray(sinT[:, o2])
        m["natab"] = _na_table(c, rpb, halo_rows)
        maps.append(m)
    return maps


_NC_CACHE = {}


def kernel(**inputs):
    maps = prep_inputs(inputs)
    if "nc" not in _NC_CACHE:
        _NC_CACHE["nc"] = build_program()
    nc = _NC_CACHE["nc"]
    res = run_bass_kernel_spmd(nc, maps, core_ids=list(range(NCORE)))
    outs = []
    for c in range(NCORE):
        o = np.asarray(res.results[c]["outT"], np.float32).reshape(D, TOWN)
        outs.append(o.T)
    return np.ascontiguousarray(np.concatenate(outs, axis=0)[None].astype(np.float32))
```

```python
import math
from contextlib import ExitStack

import numpy as np
import concourse.bass as bass
import concourse.mybir as mybir
from concourse.bass_utils import run_bass_kernel_spmd

F32 = mybir.dt.float32
BF16 = mybir.dt.bfloat16
AF = mybir.ActivationFunctionType
ALU = mybir.AluOpType

D = 2048
S_ALL = 16384
NCORE = 8
TOWN = 2048
THALO = 512
TLOC = TOWN + THALO
NTOK_IN = S_ALL + THALO
D_IN = 8704
D_FF = 5632
EPS = 1e-6
NV = 162
GRID_W = 64
NA_ROWS = 8
NA_COLS = 16
MASK_NEG = -30000.0


class Buf:
    __slots__ = ("name", "w", "rs", "rd")

    def __init__(self, name):
        self.name = name
        self.w = None
        self.rs = {}
        self.rd = []


class Op:
    __slots__ = ("eng", "fn", "deps", "signaled", "token", "key", "idx", "epoch")

    def __init__(self, eng, fn, key, idx, epoch):
        self.eng = eng
        self.fn = fn
        self.key = key
        self.deps = set()
        self.signaled = key is not None
        self.token = None
        self.idx = idx
        self.epoch = epoch


ENGS = ("pe", "act", "dve", "pool", "sp")


class Sched:
    def __init__(self, nc, es):
        self.nc = nc
        self.es = es
        self.sem = {e: es.enter_context(nc.semaphore("s_" + e)) for e in ENGS}
        self.cnt = {e: 0 for e in ENGS}
        self.keysem = {}
        self.keyidx = {}
        self.sempool = []
        self.poolcnt = []
        self.poolkind = []
        self.waited = {e: {} for e in ENGS}
        self.epoch = 0
        self.ops = {e: [] for e in ENGS}
        self.nidx = {e: 0 for e in ENGS}
        self.allops = []

    def _mk(self, eng, fn, reads, writes, key):
        op = Op(eng, fn, key, self.nidx[eng], self.epoch)
        self.nidx[eng] += 1
        deps = op.deps
        raw = set()
        for b in reads:
            if b.w is not None:
                deps.add(b.w)
                raw.add(b.w)
        for b in writes:
            if b.w is not None:
                deps.add(b.w)
            deps.update(b.rs.values())
            deps.update(b.rd)
        for d in list(deps):
            if d.key is None and key is None and d.eng == eng:
                if eng == "pe" or (d not in raw and op.idx - d.idx > 2):
                    deps.discard(d)
        for d in deps:
            d.signaled = True
        for b in reads:
            if key is not None:
                b.rd.append(op)
            else:
                b.rs[eng] = op
        for b in writes:
            b.w = op
            b.rs = {}
            b.rd = []
        self.ops[eng].append(op)
        self.allops.append(op)
        return op

    def op(self, eng, fn, reads=(), writes=()):
        return self._mk(eng, fn, reads, writes, None)

    def dma(self, eng, out, in_, reads=(), writes=(), key=None):
        assert key is not None
        kind = "sw" if eng == "pool" else "hw"
        key = (kind, key)
        if key not in self.keysem:
            free = [i for i in range(len(self.sempool)) if self.poolkind[i] == kind and i not in self.keyidx.values()]
            if free:
                i = free[0]
            else:
                i = len(self.sempool)
                self.sempool.append(self.es.enter_context(self.nc.semaphore("k%s%d" % (kind, i))))
                self.poolcnt.append(0)
                self.poolkind.append(kind)
            self.keysem[key] = self.sempool[i]
            self.keyidx[key] = i
        return self._mk(eng, lambda e: e.dma_start(out=out, in_=in_), reads, writes, key)

    def flush(self, final=False):
        nc = self.nc
        last = {}
        for e in ENGS:
            for op in reversed(self.ops[e]):
                if op.key is None and op.fn is not None:
                    op.signaled = True
                    last[e] = op
                    break
        self.dmawait = {}
        for op in self.allops:
            dw = None
            for d in op.deps:
                if d.key is not None and d.epoch == self.epoch:
                    if dw is None:
                        dw = {}
                    i = self.keyidx[d.key]
                    dw[i] = self.poolcnt[i]
            if dw:
                self.dmawait[op] = dw
            if op.key is not None:
                i = self.keyidx[op.key]
                self.poolcnt[i] += 16
                op.token = (self.sempool[i], self.poolcnt[i])
            elif op.signaled:
                self.cnt[op.eng] += 1
                op.token = (self.sem[op.eng], self.cnt[op.eng])
        bar = [(self.sem[e], self.cnt[e]) for e in ENGS if self.cnt[e] > 0]
        bar += [(self.sempool[i], self.poolcnt[i]) for i in range(len(self.sempool)) if self.poolcnt[i] > 0]
        ops = self.ops
        dmawait = self.dmawait
        keyidx = dict(self.keyidx)
        waited = self.waited
        sems = self.sem
        epoch = self.epoch

        def emit(ename, e):
            w = waited[ename]
            for op in ops[ename]:
                for d in op.deps:
                    if d.epoch != epoch:
                        continue
                    s, v = d.token
                    if d.key is not None:
                        v = dmawait[op][keyidx[d.key]]
                    if w.get(id(s), 0) < v:
                        e.wait_ge(s, v)
                        w[id(s)] = v
                if op.fn is None:
                    continue
                ins = op.fn(e)
                if op.key is not None:
                    ins.then_inc(op.token[0], 16)
                elif op.signaled:
                    ins.then_inc(sems[ename], 1)
            for s, v in bar:
                k = id(s)
                if w.get(k, 0) < v:
                    e.wait_ge(s, v)
                    w[k] = v

        with nc.Block() as block:
            @block.tensor
            def _(e):
                emit("pe", e)

            @block.scalar
            def _(e):
                emit("act", e)

            @block.vector
            def _(e):
                emit("dve", e)

            @block.gpsimd
            def _(e):
                emit("pool", e)

            @block.sync
            def _(e):
                emit("sp", e)

        self.ops = {e: [] for e in ENGS}
        self.allops = []
        self.epoch += 1
        self.keysem = {}
        self.keyidx = {}


class Ring:
    def __init__(self, items):
        self.items = items
        self.i = 0

    def next(self):
        it = self.items[self.i % len(self.items)]
        self.i += 1
        return it


def v3(ap, k):
    return ap.rearrange("p (k t) -> p k t", k=k)


def build_program(debug_outs=(), stop_after=99):
    nc = bass.Bass("TRN2", target_bir_lowering=False)

    def din(name, shape, dt=F32):
        return nc.dram_tensor(name, list(shape), dt, kind="ExternalInput").ap()

    def dscr(name, shape, dt=BF16):
        kind = "ExternalOutput" if name in debug_outs else "Internal"
        return nc.dram_tensor(name, list(shape), dt, kind=kind).ap()

    xT = din("xT", [D, NTOK_IN])
    cosT = din("cosT", [128, S_ALL])
    sinT = din("sinT", [128, S_ALL])
    vecs = din("vecs", [128, NV])
    rmat = din("rmat", [128, 128])
    natab = din("natab", [8, 20, 128, 640])
    w_ada = din("w_ada", [D, 6 * D])
    w_in = din("w_in", [D, D_IN])
    w_oa = din("w_oa", [1024, D])
    w_ob = din("w_ob", [1024, D])
    w_out = din("w_out", [D, D])
    w_gate = din("w_gate", [D, D_FF])
    w_up = din("w_up", [D, D_FF])
    w_down = din("w_down", [D_FF, D])
    outT = nc.dram_tensor("outT", [16, 128, TOWN], F32, kind="ExternalOutput").ap()

    hT_scr = dscr("hT_scr", [128, 16 * TLOC])
    KT_scr = dscr("KT_scr", [2, 128, S_ALL])
    V_scr = dscr("V_scr", [2, 128, 128 * 128])
    qT_scr = dscr("qT_scr", [8, 128, TOWN])
    qBT_scr = dscr("qBT_scr", [8, 128, TOWN])
    kBT_scr = dscr("kBT_scr", [8, 128, TLOC])
    vB_scr = dscr("vB_scr", [8, 128, 20 * 128])
    sg_scr = dscr("sg_scr", [32, 128, TOWN])
    bv_scr = dscr("bv_scr", [128, 16], F32)
    yT_scr = dscr("yT_scr", [16, 128, TOWN])
    x1T_scr = dscr("x1T_scr", [16, 128, TOWN], F32)
    h2T_scr = dscr("h2T_scr", [128, 16 * TOWN])
    act_scr = dscr("act_scr", [44, 128, TOWN])
    x2T_scr = dscr("x2T_scr", [16, 128, TOWN], F32)

    w_ada_v = w_ada.rearrange("(k p) n -> p k n", p=128)
    w_in_v = w_in.rearrange("(k p) n -> p k n", p=128)
    xT_v = xT.rearrange("(k p) t -> p k t", p=128)

    with ExitStack() as ges:
        S = Sched(nc, ges)

        def sb(es, name, shape, dt):
            return es.enter_context(nc.sbuf_tensor(name, list(shape), dt))

        vecs_sb = sb(ges, "vecs_sb", [128, NV], F32)
        mod_sb = sb(ges, "mod_sb", [128, 96], F32)
        A1 = sb(ges, "A1", [128, 16], F32)
        A2 = sb(ges, "A2", [128, 16], F32)
        B1bf = sb(ges, "B1bf", [128, 16], BF16)
        B2bf = sb(ges, "B2bf", [128, 16], BF16)
        ones_bf = sb(ges, "ones_bf", [128, 128], BF16)
        rmat_bf = sb(ges, "rmat_bf", [128, 128], BF16)
        kw = sb(ges, "kw", [128, 1], F32)
        epsA = sb(ges, "epsA", [128, 1], F32)
        epsB = sb(ges, "epsB", [128, 1], F32)
        bvcol = sb(ges, "bvcol", [128, 16], F32)
        csil = sb(ges, "csil", [128, 16], BF16)
        dbl = [ges.enter_context(nc.psum_tensor("dbank%d" % i, [128, 1024], F32)) for i in range(4)]
        banks = [dbl[i // 2][:, (i % 2) * 512:(i % 2) * 512 + 512] for i in range(8)]
        bB = [Buf("bank%d" % i) for i in range(8)]
        Bvecs, Bmod, Bconst, Bcsil, Bbv = Buf("vecs"), Buf("mod"), Buf("const"), Buf("csil"), Buf("bv")

        C_C, C_BADA, C_N1, C_N2, C_FW, C_QW, C_KW = 0, 16, 112, 128, 144, 160, 161

        with ExitStack() as es:
            wts = [sb(es, "wada%d" % i, [128, 16 * 512], BF16) for i in range(3)]
            wB = [Buf("wada%d" % i) for i in range(3)]
            S.dma("sp", vecs_sb[:, :], vecs[:, :], writes=[Bvecs], key="vecs")
            S.dma("pool", rmat_bf[:, :], rmat[:, :], writes=[Bconst], key="rmat")
            S.op("dve", lambda e: e.memset(ones_bf[:, :], 1.0), writes=[Bconst])
            S.op("dve", lambda e: e.memset(epsA[:, :], D * EPS), writes=[Bconst])
            S.op("dve", lambda e: e.memset(epsB[:, :], 128 * EPS), writes=[Bconst])
            S.op("act", lambda e: e.activation(out=csil[:, :], in_=vecs_sb[:, C_C:C_C + 16], func=AF.Silu),
                 reads=[Bvecs], writes=[Bcsil])
            ps_mod = banks[7]
            for i in range(24):
                sl = i % 3
                wt = wts[sl]
                S.dma("pool", v3(wt[:, :], 16), w_ada_v[:, :, i * 512:(i + 1) * 512], writes=[wB[sl]],
                      key=("wada", sl))
                for jj in range(4):
                    j = i * 4 + jj
                    for k in range(16):
                        S.op("pe", (lambda e, wt=wt, k=k, jj=jj, j=j: e.matmul(
                            ps_mod[:, j:j + 1], lhsT=wt[:, k * 512 + jj * 128:k * 512 + (jj + 1) * 128],
                            rhs=csil[:, k:k + 1], start=(k == 0), stop=(k == 15))),
                            reads=[wB[sl], Bcsil], writes=[bB[7]])
            S.op("dve", lambda e: e.tensor_tensor(out=mod_sb[:, :], in0=ps_mod[:, 0:96],
                                                  in1=vecs_sb[:, C_BADA:C_BADA + 96], op=ALU.add),
                 reads=[bB[7], Bvecs], writes=[Bmod])
            for (A, sc0, nw0) in ((A1, 16, C_N1), (A2, 64, C_N2)):
                S.op("dve", (lambda e, A=A, sc0=sc0, nw0=nw0: e.scalar_tensor_tensor(
                    out=A[:, :], in0=mod_sb[:, sc0:sc0 + 16], scalar=1.0, in1=vecs_sb[:, nw0:nw0 + 16],
                    op0=ALU.add, op1=ALU.mult)), reads=[Bmod, Bvecs], writes=[Bmod])
                S.op("pool", (lambda e, A=A: e.tensor_scalar(out=A[:, :], in0=A[:, :], scalar1=math.sqrt(D),
                                                            scalar2=None, op0=ALU.mult)),
                     reads=[Bmod], writes=[Bmod])
            S.op("dve", lambda e: e.tensor_copy(out=B1bf[:, :], in_=mod_sb[:, 0:16]), reads=[Bmod], writes=[Bmod])
            S.op("dve", lambda e: e.tensor_copy(out=B2bf[:, :], in_=mod_sb[:, 48:64]), reads=[Bmod], writes=[Bmod])
            S.op("pool", lambda e: e.tensor_scalar(out=kw[:, :], in0=vecs_sb[:, C_KW:C_KW + 1],
                                                   scalar1=math.sqrt(128.0), scalar2=None, op0=ALU.mult),
                 reads=[Bvecs], writes=[Bmod])
            S.flush()
        if stop_after <= 0:
            return nc

        def norm_rope_multi(items, use_sqrt=False):
            chain_part1(items)
            chain_part2(items, use_sqrt)

        def chain_part1(items):
            for it_ in items:
                T = it_["T"]
                S.op("act", lambda e, it_=it_, T=T: e.activation(out=T["sq"][:, :], in_=it_["ps"][:, :], func=AF.Square,
                                                               bias=it_["bias"], scale=1.0),
                     reads=[it_["psB"]] + list(it_["extra"]), writes=[T["sqB"]])
                S.op("dve", lambda e, it_=it_, T=T: e.tensor_scalar(out=T["raw"][:, :], in0=it_["ps"][:, :], scalar1=it_["bias"],
                                                                  scalar2=None, op0=ALU.add),
                     reads=[it_["psB"], T["sqB"]] + list(it_["extra"]), writes=[T["rawB"]])

        def chain_part2(items, use_sqrt=False):
            for it_ in items:
                T = it_["T"]
                S.op("pe", lambda e, it_=it_, T=T: e.matmul(banks[it_["sumb"]][:, :], lhsT=ones_bf[:, :], rhs=T["sq"][:, :],
                                                          start=True, stop=True),
                     reads=[T["sqB"], Bconst], writes=[bB[it_["sumb"]]])
            for it_ in items:
                T = it_["T"]
                S.op("act", lambda e, it_=it_, T=T: e.activation(out=T["rs"][:, :], in_=banks[it_["sumb"]][:, :],
                                                               func=(AF.Sqrt if use_sqrt else AF.Ln),
                                                               bias=epsB[:, 0:1], scale=1.0),
                     reads=[bB[it_["sumb"]], Bconst], writes=[T["rsB"]])
            for it_ in items:
                T = it_["T"]
                if use_sqrt:
                    S.op("dve", lambda e, T=T: e.reciprocal(out=T["rs"][:, :], in_=T["rs"][:, :]),
                         reads=[T["rsB"]], writes=[T["rsB"]])
                else:
                    S.op("act", lambda e, T=T: e.activation(out=T["rs"][:, :], in_=T["rs"][:, :], func=AF.Exp, scale=-0.5),
                         reads=[T["rsB"]], writes=[T["rsB"]])
            for it_ in items:
                T = it_["T"]
                S.op("dve", lambda e, it_=it_, T=T: e.scalar_tensor_tensor(out=T["n"][:, :], in0=T["raw"][:, :], scalar=it_["w"],
                                                                         in1=T["rs"][:, :], op0=ALU.mult, op1=ALU.mult),
                     reads=[T["rawB"], T["rsB"], Bmod, Bvecs], writes=[T["nB"]])
            for it_ in items:
                T = it_["T"]
                S.op("pe", lambda e, it_=it_, T=T: e.matmul(banks[it_["rotb"]][:, :], lhsT=rmat_bf[:, :], rhs=T["n"][:, :],
                                                          start=True, stop=True),
                     reads=[T["nB"], Bconst], writes=[bB[it_["rotb"]]])
            for it_ in items:
                T = it_["T"]
                S.op("pool", lambda e, it_=it_, T=T: e.tensor_tensor(out=T["t1"][:, :], in0=T["n"][:, :], in1=it_["cos"], op=ALU.mult),
                     reads=[T["nB"], it_["tabB"]], writes=[T["t1B"]])
            for it_ in items:
                T = it_["T"]
                S.op("dve", lambda e, it_=it_, T=T: e.tensor_tensor(out=T["t2"][:, :], in0=banks[it_["rotb"]][:, :], in1=it_["sin"],
                                                                  op=ALU.mult),
                     reads=[bB[it_["rotb"]], it_["tabB"]], writes=[T["t2B"]])
            for it_ in items:
                T = it_["T"]
                S.op("pool", lambda e, it_=it_, T=T: e.tensor_tensor(out=it_["out"], in0=T["t1"][:, :], in1=T["t2"][:, :], op=ALU.add),
                     reads=[T["t1B"], T["t2B"]], writes=[it_["outB"]])
                if it_.get("after") is not None:
                    it_["after"]()

        def norm_rope(ps, psB, bias_ap, w_ap, cos_ap, sin_ap, tabB, T, out_ap, outB, extra_reads=()):
            norm_rope_multi([dict(ps=ps, psB=psB, bias=bias_ap, w=w_ap, cos=cos_ap, sin=sin_ap, tabB=tabB, T=T, out=out_ap,
                                  outB=outB, extra=extra_reads, sumb=3, rotb=4, after=None)])

        def chain_tiles_alt(es, T, pfx):
            T2 = dict(T)
            T2["raw"] = sb(es, pfx + "raw", [128, 512], F32)
            T2["sq"] = sb(es, pfx + "sq", [128, 512], BF16)
            T2["rawB"] = Buf(pfx + "raw")
            T2["sqB"] = Buf(pfx + "sq")
            return T2

        def mk_chain_tiles(es, pfx):
            T = {}
            T["raw"] = sb(es, pfx + "raw", [128, 512], F32)
            T["sq"] = sb(es, pfx + "sq", [128, 512], BF16)
            T["rs"] = sb(es, pfx + "rs", [128, 512], F32)
            T["n"] = sb(es, pfx + "n", [128, 512], BF16)
            T["t1"] = sb(es, pfx + "t1", [128, 512], F32)
            T["t2"] = sb(es, pfx + "t2", [128, 512], F32)
            for k in ("raw", "sq", "rs", "n", "t1", "t2"):
                T[k + "B"] = Buf(pfx + k)
            return T

        def bias_cols(wt, wB_, ncol, Bbf, dst_ap, dstB, col_stride):
            for c in range(ncol):
                for k in range(16):
                    S.op("pe", (lambda e, c=c, k=k: e.matmul(
                        banks[7][:, c:c + 1], lhsT=wt[:, k * col_stride + c * 128:k * col_stride + (c + 1) * 128],
                        rhs=Bbf[:, k:k + 1], start=(k == 0), stop=(k == 15))),
                        reads=[wB_, Bmod], writes=[bB[7]])
            S.op("dve", lambda e: e.tensor_copy(out=dst_ap, in_=banks[7][:, 0:ncol]), reads=[bB[7]], writes=[dstB])

        with ExitStack() as es:
            xs = [sb(es, "xs%d" % i, [128, 16 * 512], F32) for i in range(2)]
            xsB = [Buf("xs%d" % i) for i in range(2)]
            sq = sb(es, "sq", [128, 16 * 512], BF16)
            sqB = Buf("sq")
            xsP = [[Buf("xs%d_%d" % (i, j)) for j in range(4)] for i in range(2)]
            sqP = [Buf("sq_%d" % j) for j in range(4)]
            hT = [sb(es, "hT%d" % i, [128, 16 * 512], BF16) for i in range(2)]
            hTB = [Buf("hT%d" % i) for i in range(2)]
            wkv = sb(es, "wkv", [128, 16 * 512], BF16)
            wkvB = Buf("wkv")
            rstd = sb(es, "rstd", [128, 512], F32)
            rstdB = Buf("rstd")
            cs = [sb(es, "cs%d" % i, [128, 1024], F32) for i in range(2)]
            csB = [Buf("cs%d" % i) for i in range(2)]
            kout = [sb(es, "kout%d" % i, [128, 512], BF16) for i in range(2)]
            koutB = [Buf("kout%d" % i) for i in range(2)]
            vout = [sb(es, "vout%d" % i, [128, 1024], BF16) for i in range(2)]
            voutB = [Buf("vout%d" % i) for i in range(2)]
            bk = sb(es, "bk", [128, 2], F32)
            bkB = Buf("bk")
            TT = [mk_chain_tiles(es, "c1a"), mk_chain_tiles(es, "c1b")]
            TTp = [TT, [chain_tiles_alt(es, TT[0], "c1c"), chain_tiles_alt(es, TT[1], "c1d")]]
            BKT = [Buf("KT0"), Buf("KT1")]
            BV = [Buf("V0"), Buf("V1")]
            BhT = Buf("hTscr")

            S.dma("pool", v3(wkv[:, :], 16), w_in_v[:, :, 1024:1536], writes=[wkvB], key="wkv")
            bias_cols(wkv, wkvB, 2, B1bf, bk[:, 0:2], bkB, 512)
            for c in range(2):
                for k in range(16):
                    S.op("pe", (lambda e, c=c, k=k: e.matmul(
                        banks[7][:, 8 + c:9 + c], lhsT=wkv[:, k * 512 + 256 + c * 128:k * 512 + 256 + (c + 1) * 128],
                        rhs=B1bf[:, k:k + 1], start=(k == 0), stop=(k == 15))),
                        reads=[wkvB, Bmod], writes=[bB[7]])
            S.op("dve", lambda e: e.tensor_copy(out=bvcol[:, 0:2], in_=banks[7][:, 8:10]), reads=[bB[7]], writes=[Bbv])

            NB1 = 33

            def stageA1a(tb):
                sl = tb % 2
                t0 = tb * 512
                for p_ in range(4):
                    S.dma("sp", v3(xs[sl][:, :], 16)[:, 4 * p_:4 * p_ + 4, :], xT_v[:, 4 * p_:4 * p_ + 4, t0:t0 + 512],
                          writes=[xsP[sl][p_]], key=("xs", sl, p_))
                    S.op("act", (lambda e, p_=p_: e.activation(out=sq[:, p_ * 2048:(p_ + 1) * 2048],
                                                              in_=xs[sl][:, p_ * 2048:(p_ + 1) * 2048], func=AF.Square)),
                         reads=[xsP[sl][p_]], writes=[sqP[p_]])

            def stageA1b(tb):
                for k in range(16):
                    S.op("pe", (lambda e, k=k: e.matmul(banks[0][:, :], lhsT=ones_bf[:, :],
                                                       rhs=sq[:, k * 512:(k + 1) * 512],
                                                       start=(k == 0), stop=(k == 15))),
                         reads=[sqP[k // 4], Bconst], writes=[bB[0]])

            def stageA2(tb):
                sl = tb % 2
                S.op("act", lambda e: e.activation(out=rstd[:, :], in_=banks[0][:, :], func=AF.Ln,
                                                   bias=epsA[:, 0:1], scale=1.0),
                     reads=[bB[0], Bconst], writes=[rstdB])
                S.op("act", lambda e: e.activation(out=rstd[:, :], in_=rstd[:, :], func=AF.Exp, scale=-0.5),
                     reads=[rstdB], writes=[rstdB])
                for k in range(16):
                    S.op("dve", (lambda e, k=k: e.scalar_tensor_tensor(
                        out=hT[sl][:, k * 512:(k + 1) * 512], in0=xs[sl][:, k * 512:(k + 1) * 512],
                        scalar=A1[:, k:k + 1], in1=rstd[:, :], op0=ALU.mult, op1=ALU.mult)),
                        reads=[xsP[sl][k // 4], rstdB, Bmod], writes=[hTB[sl]])

            def stageB(tb):
                sl = tb % 2
                t0 = tb * 512
                own = tb < 4 or tb == 32
                if own:
                    lt0 = t0 if tb < 4 else TOWN
                    S.dma("pool", v3(hT_scr[:, :], 16)[:, :, lt0:lt0 + 512], v3(hT[sl][:, :], 16),
                          reads=[hTB[sl]], writes=[BhT], key="hTst")
                if tb == 32:
                    return None
                S.dma("sp", cs[sl][:, 0:512], cosT[:, t0:t0 + 512], writes=[csB[sl]], key=("cs", sl))
                S.dma("sp", cs[sl][:, 512:1024], sinT[:, t0:t0 + 512], writes=[csB[sl]], key=("cs", sl))
                for g in range(2):
                    for k in range(16):
                        S.op("pe", (lambda e, g=g, k=k: e.matmul(
                            banks[1 + g][:, :], lhsT=wkv[:, k * 512 + g * 128:k * 512 + (g + 1) * 128],
                            rhs=hT[sl][:, k * 512:(k + 1) * 512], start=(k == 0), stop=(k == 15))),
                            reads=[wkvB, hTB[sl]], writes=[bB[1 + g]])
                for s in range(4):
                    bank = 5 + s // 2
                    c0 = (s % 2) * 256
                    for k in range(16):
                        S.op("pe", (lambda e, s=s, k=k, bank=bank, c0=c0: e.matmul(
                            banks[bank][:, c0:c0 + 256], lhsT=hT[sl][:, k * 512 + s * 128:k * 512 + (s + 1) * 128],
                            rhs=wkv[:, k * 512 + 256:k * 512 + 512], start=(k == 0), stop=(k == 15))),
                            reads=[wkvB, hTB[sl]], writes=[bB[bank]])
                vo = vout[sl]
                S.op("act", lambda e: e.activation(out=vo[:, 0:512], in_=banks[5][:, :], func=AF.Copy),
                     reads=[bB[5]], writes=[voutB[sl]])
                S.op("dve", lambda e: e.tensor_copy(out=vo[:, 512:1024], in_=banks[6][:, :]),
                     reads=[bB[6]], writes=[voutB[sl]])
                for g in range(2):
                    S.dma("pool", v3(V_scr[g, :, tb * 512:(tb + 1) * 512], 4),
                          v3(vo[:, :], 4)[:, :, g * 128:(g + 1) * 128],
                          reads=[voutB[sl]], writes=[BV[g]], key=("vst", sl))
                items = []
                for g in range(2):
                    ko = kout[g]
                    items.append(dict(
                        ps=banks[1 + g], psB=bB[1 + g], bias=bk[:, g:g + 1], w=kw[:, 0:1], cos=cs[sl][:, 0:512],
                        sin=cs[sl][:, 512:1024], tabB=csB[sl], T=TTp[sl][g], out=ko[:, :], outB=koutB[g], extra=[bkB],
                        sumb=(3, 7)[g], rotb=(4, 3)[g],
                        after=(lambda g=g, ko=ko: S.dma("pool", KT_scr[g, :, t0:t0 + 512], ko[:, :], reads=[koutB[g]],
                                                        writes=[BKT[g]], key=("kst", g)))))
                chain_part1(items)
                return items

            stageA1a(0)
            stageA1b(0)
            stageA2(0)
            stageA1a(1)
            stageA1b(1)
            pending = None
            for tb in range(NB1):
                if tb + 1 < NB1:
                    stageA2(tb + 1)
                if tb + 2 < NB1:
                    stageA1a(tb + 2)
                items = stageB(tb)
                if tb + 2 < NB1:
                    stageA1b(tb + 2)
                if pending is not None:
                    chain_part2(pending)
                pending = items
            if pending is not None:
                chain_part2(pending)
            S.flush()
        if stop_after <= 1:
            return nc

        def load_wtile(dst, dstB, view, c0, ncols, nk, key):
            S.dma("pool", v3(dst[:, 0:nk * ncols], nk), view[:, :, c0:c0 + ncols], writes=[dstB], key=key)

        def mm_acc(bank_i, lhs_fn, rhs_fn, nk, reads, out_ap=None):
            for k in range(nk):
                S.op("pe", (lambda e, k=k: e.matmul(out_ap if out_ap is not None else banks[bank_i][:, :],
                                                   lhsT=lhs_fn(k), rhs=rhs_fn(k), start=(k == 0), stop=(k == nk - 1))),
                     reads=reads, writes=[bB[bank_i]])

        BqT, BqBT, BkBT, BvB, Bsg = Buf("qT"), Buf("qBT"), Buf("kBT"), Buf("vB"), Buf("sg")
        with ExitStack() as es:
            hTo = sb(es, "hTo", [128, 16 * TLOC], BF16)
            hToB = Buf("hTo")
            wts = [sb(es, "w2_%d" % i, [128, 16 * 512], BF16) for i in range(3)]
            wtB = [Buf("w2_%d" % i) for i in range(3)]
            cso = sb(es, "cso", [128, 2 * TOWN], F32)
            csoB = Buf("cso")
            TT2 = [mk_chain_tiles(es, "c2a"), mk_chain_tiles(es, "c2b")]
            TT2p = [TT2, [chain_tiles_alt(es, TT2[0], "c2c"), chain_tiles_alt(es, TT2[1], "c2d")]]
            qitems = []
            qpend = [None]
            qpair = [0]
            ot = [sb(es, "ot%d" % i, [128, 512], BF16) for i in range(4)]
            otB = [Buf("ot%d" % i) for i in range(4)]
            bc = [sb(es, "bc%d" % i, [128, 4], F32) for i in range(2)]
            bcB = [Buf("bc%d" % i) for i in range(2)]
            hToP = [Buf("hTo%d" % i) for i in range(5)]
            for tbp in range(5):
                S.dma("sp", v3(hTo[:, :], 16)[:, :, tbp * 512:(tbp + 1) * 512], v3(hT_scr[:, :], 16)[:, :, tbp * 512:(tbp + 1) * 512],
                      reads=[BhT], writes=[hToP[tbp]], key=("hTo", tbp))
            S.dma("sp", cso[:, 0:TOWN], cosT[:, 0:TOWN], writes=[csoB], key="cso")
            S.dma("sp", cso[:, TOWN:2 * TOWN], sinT[:, 0:TOWN], writes=[csoB], key="cso")
            order = [0, 1, 3, 4, 5, 6, 7, 8] + list(range(9, 17))
            mb = Ring([0, 1, 2, 5, 6])
            oti = 0
            for n_i, ti in enumerate(order):
                sl = n_i % 3
                wt, wB_ = wts[sl], wtB[sl]
                load_wtile(wt, wB_, w_in_v, ti * 512, 512, 16, ("w2", sl))
                if ti in (7, 8):
                    hv = ti - 7
                    bias_cols(wt, wB_, 4, B1bf, bvcol[:, 2 + 4 * hv:6 + 4 * hv], Bbv, 512)
                    for s_ in range(TLOC // 128):
                        bi = mb.next()
                        mm_acc(bi, lambda k, s_=s_: hTo[:, k * TLOC + s_ * 128:k * TLOC + (s_ + 1) * 128],
                               lambda k, wt=wt: wt[:, k * 512:(k + 1) * 512], 16, [hToP[s_ // 4], wB_])
                        o_, oB_ = ot[oti % 4], otB[oti % 4]
                        oti += 1
                        if s_ % 2 == 0:
                            S.op("act", (lambda e, o_=o_, bi=bi: e.activation(out=o_[:, :], in_=banks[bi][:, :], func=AF.Copy)),
                                 reads=[bB[bi]], writes=[oB_])
                        else:
                            S.op("dve", (lambda e, o_=o_, bi=bi: e.tensor_copy(out=o_[:, :], in_=banks[bi][:, :])),
                                 reads=[bB[bi]], writes=[oB_])
                        S.dma("sp", vB_scr[4 * hv:4 * hv + 4, :, s_ * 128:(s_ + 1) * 128].rearrange("h p d -> p h d"),
                              v3(o_[:, :], 4), reads=[oB_], writes=[BvB], key=("ot", oti % 4))
                    continue
                bcs, bcsB = bc[n_i % 2], bcB[n_i % 2]
                bias_cols(wt, wB_, 4, B1bf, bcs[:, 0:4], bcsB, 512)
                for cc in range(4):
                    ntb = 5 if ti in (5, 6) else 4
                    for tb in range(ntb):
                        bi = mb.next()
                        mm_acc(bi, lambda k, wt=wt, cc=cc: wt[:, k * 512 + cc * 128:k * 512 + (cc + 1) * 128],
                               lambda k, tb=tb: hTo[:, k * TLOC + tb * 512:k * TLOC + (tb + 1) * 512], 16, [hToP[tb], wB_])
                        o_, oB_ = ot[oti % 4], otB[oti % 4]
                        oti += 1
                        okey = ("ot", oti % 4)
                        if ti in (0, 1):
                            h = ti * 4 + cc
                            qitems.append(dict(
                                ps=banks[bi], psB=bB[bi], bias=bcs[:, cc:cc + 1], w=vecs_sb[:, C_QW:C_QW + 1],
                                cos=cso[:, tb * 512:(tb + 1) * 512], sin=cso[:, TOWN + tb * 512:TOWN + (tb + 1) * 512],
                                tabB=csoB, T=TT2p[qpair[0] % 2][tb % 2], out=o_[:, :], outB=oB_, extra=[bcsB],
                                sumb=(3, 7)[tb % 2], rotb=(4, 3)[tb % 2],
                                after=(lambda h=h, tb=tb, o_=o_, oB_=oB_, okey=okey: S.dma(
                                    "sp", qT_scr[h, :, tb * 512:(tb + 1) * 512], o_[:, :], reads=[oB_], writes=[BqT], key=okey))))
                            if len(qitems) == 2:
                                chain_part1(qitems)
                                if qpend[0] is not None:
                                    chain_part2(qpend[0])
                                qpend[0] = qitems
                                qitems = []
                                qpair[0] += 1
                                if ti == 1 and cc == 3 and tb == 3:
                                    chain_part2(qpend[0])
                                    qpend[0] = None
                        elif ti in (3, 4, 5, 6):
                            S.op("act", (lambda e, o_=o_, bi=bi, cc=cc, bcs=bcs: e.activation(
                                out=o_[:, :], in_=banks[bi][:, :], func=AF.Identity, bias=bcs[:, cc:cc + 1], scale=1.0)),
                                reads=[bB[bi], bcsB], writes=[oB_])
                            if ti in (3, 4):
                                h = (ti - 3) * 4 + cc
                                S.dma("sp", qBT_scr[h, :, tb * 512:(tb + 1) * 512], o_[:, :], reads=[oB_], writes=[BqBT], key=okey)
                            else:
                                h = (ti - 5) * 4 + cc
                                S.dma("sp", kBT_scr[h, :, tb * 512:(tb + 1) * 512], o_[:, :], reads=[oB_], writes=[BkBT], key=okey)
                        else:
                            gi = (ti - 9) * 4 + cc
                            S.op("act", (lambda e, o_=o_, bi=bi, cc=cc, bcs=bcs: e.activation(
                                out=o_[:, :], in_=banks[bi][:, :], func=AF.Sigmoid, bias=bcs[:, cc:cc + 1], scale=1.0)),
                                reads=[bB[bi], bcsB], writes=[oB_])
                            S.dma("sp", sg_scr[gi, :, tb * 512:(tb + 1) * 512], o_[:, :], reads=[oB_], writes=[Bsg], key=okey)
            S.flush()
        if stop_after <= 2:
            return nc

        ByT = Buf("yT")

        def attn_epilogue(Obank, Lbank, bias_ap, E, dst_ap, key):
            S.op("act", lambda e: e.activation(out=E["rec"][:, :], in_=banks[Lbank][:, :], func=AF.Ln), reads=[bB[Lbank]], writes=[E["recB"]])
            S.op("act", lambda e: e.activation(out=E["rec"][:, :], in_=E["rec"][:, :], func=AF.Exp, scale=-1.0), reads=[E["recB"]], writes=[E["recB"]])
            S.op("dve", lambda e: e.tensor_tensor(out=E["o"][:, :], in0=banks[Obank][:, :], in1=E["rec"][:, :], op=ALU.mult),
                 reads=[bB[Obank], E["recB"]], writes=[E["oB"]])
            y_, yB_ = E["y"][E["i"] % 2], E["yB"][E["i"] % 2]
            E["i"] += 1
            S.op("dve", lambda e: e.tensor_scalar(out=y_[:, :], in0=E["o"][:, :], scalar1=bias_ap, scalar2=None, op0=ALU.add),
                 reads=[E["oB"], Bbv], writes=[yB_])
            S.dma("pool", dst_ap, y_[:, :], reads=[yB_], writes=[ByT], key=(key, E["i"] % 2))

        def mk_epi(es, pfx):
            E = {"rec": sb(es, pfx + "rec", [128, 512], F32), "o": sb(es, pfx + "o", [128, 512], F32),
                 "y": [sb(es, pfx + "y%d" % i, [128, 512], BF16) for i in range(2)],
                 "recB": Buf("rec"), "oB": Buf("o"), "yB": [Buf("y0"), Buf("y1")], "i": 0}
            return E

        with ExitStack() as es:
            KTs = [sb(es, "KTs%d" % g, [128, S_ALL], BF16) for g in range(2)]
            Vs = [sb(es, "Vs%d" % g, [128, S_ALL], BF16) for g in range(2)]
            qs = [sb(es, "qs%d" % g, [128, 4 * TOWN], BF16) for g in range(2)]
            KTsB = [[Buf("KTs%d_%d" % (g_, p_)) for p_ in range(4)] for g_ in range(2)]
            VsB = [[Buf("Vs%d_%d" % (g_, p_)) for p_ in range(4)] for g_ in range(2)]
            qsB = [Buf("qs0"), Buf("qs1")]
            Pt = [sb(es, "Pt%d" % i, [128, 512], BF16) for i in range(4)]
            PtB = [Buf("Pt%d" % i) for i in range(4)]
            E = mk_epi(es, "e3")
            for g in range(2):
                S.dma("sp", qs[g][:, :].rearrange("p (h t) -> p h t", h=4), qT_scr[4 * g:4 * g + 4].rearrange("h p t -> p h t"),
                      reads=[BqT], writes=[qsB[g]], key=("qs", g))
                for part in range(4):
                    c0 = part * 4096
                    S.dma("sp", KTs[g][:, c0:c0 + 4096], KT_scr[g, :, c0:c0 + 4096], reads=[BKT[g]], writes=[KTsB[g][part]], key=("KTs", g, part))
                    S.dma("sp", Vs[g][:, c0:c0 + 4096], V_scr[g, :, c0:c0 + 4096], reads=[BV[g]], writes=[VsB[g][part]], key=("Vs", g, part))
            item = 0
            NKC = S_ALL // 128
            for g in range(2):
                for qb in range(4):
                    for hh in range(4):
                        h = 4 * g + hh
                        Ob, Lb = 4 + item % 2, 6 + item % 2
                        item += 1
                        q_ap = qs[g][:, hh * TOWN + qb * 512:hh * TOWN + (qb + 1) * 512]

                        def s_mm(kc, g=g, q_ap=q_ap):
                            bi = kc % 4
                            S.op("pe", (lambda e: e.matmul(banks[bi][:, :], lhsT=KTs[g][:, kc * 128:(kc + 1) * 128], rhs=q_ap,
                                                          start=True, stop=True)),
                                 reads=[KTsB[g][kc // 32], qsB[g]], writes=[bB[bi]])
                            S.op("act", (lambda e: e.activation(out=Pt[bi][:, :], in_=banks[bi][:, :], func=AF.Exp)),
                                 reads=[bB[bi]], writes=[PtB[bi]])

                        def pv_mm(kc, g=g, Ob=Ob, Lb=Lb):
                            bi = kc % 4
                            S.op("pe", (lambda e: e.matmul(banks[Ob][:, :], lhsT=Vs[g][:, kc * 128:(kc + 1) * 128], rhs=Pt[bi][:, :],
                                                          start=(kc == 0), stop=(kc == NKC - 1))),
                                 reads=[VsB[g][kc // 32], PtB[bi]], writes=[bB[Ob]])
                            S.op("pe", (lambda e: e.matmul(banks[Lb][:, :], lhsT=ones_bf[:, :], rhs=Pt[bi][:, :],
                                                          start=(kc == 0), stop=(kc == NKC - 1))),
                                 reads=[PtB[bi], Bconst], writes=[bB[Lb]])

                        s_mm(0)
                        s_mm(1)
                        for kc in range(NKC):
                            if kc + 2 < NKC:
                                s_mm(kc + 2)
                            pv_mm(kc)
                        attn_epilogue(Ob, Lb, bvcol[:, g:g + 1], E, yT_scr[h, :, qb * 512:(qb + 1) * 512], "y3")
            S.flush()
        if stop_after <= 3:
            return nc

        def lc_of(kc):
            return kc if kc < 16 else (kc - 18 if kc < 18 else kc - 2)

        def kc_of(lc):
            return lc if 0 <= lc < 16 else (lc + 18 if lc < 0 else lc + 2)

        with ExitStack() as es:
            kBs = [sb(es, "kBs%d" % i, [128, TLOC], BF16) for i in range(2)]
            vBs = [sb(es, "vBs%d" % i, [128, TLOC], BF16) for i in range(2)]
            qBs = [sb(es, "qBs%d" % i, [128, TOWN], BF16) for i in range(2)]
            hdB = [Buf("hd0"), Buf("hd1")]
            tab = sb(es, "tab", [128, 20 * 640], F32)
            tabB = [Buf("tab%d" % i) for i in range(20)]
            Pa = [sb(es, "Pa%d" % i, [128, 20 * 640], BF16) for i in range(2)]
            PaB = [[Buf("Pa%d_%d" % (i, j)) for j in range(20)] for i in range(2)]
            tmp = [sb(es, "natmp%d" % i, [128, 640], F32) for i in range(2)]
            tmpB = [Buf("natmp0"), Buf("natmp1")]
            E = mk_epi(es, "e4")
            scale = 1.0 / math.sqrt(128.0)
            item = 0
            for h in range(8):
                sl = h % 2
                S.dma("sp", kBs[sl][:, :], kBT_scr[h], reads=[BkBT], writes=[hdB[sl]], key=("hd", sl))
                S.dma("sp", vBs[sl][:, :], vB_scr[h], reads=[BvB], writes=[hdB[sl]], key=("hd", sl))
                S.dma("sp", qBs[sl][:, :], qBT_scr[h], reads=[BqBT], writes=[hdB[sl]], key=("hd", sl))
                for kc in range(20):
                    S.dma("sp", tab[:, kc * 640:(kc + 1) * 640], natab[h, kc], writes=[tabB[kc]], key=("tab", kc % 4))
                for kc in range(20):
                    lc = lc_of(kc)
                    blo, bhi = max(0, lc - 2), min(15, lc + 2)
                    nq = (bhi - blo + 1) * 128
                    q0 = blo * 128
                    dd = dbl[kc % 2]
                    n1 = min(nq, 512)
                    S.op("pe", (lambda e, kc=kc, dd=dd, n1=n1, q0=q0, sl=sl: e.matmul(
                        dd[:, 0:n1], lhsT=kBs[sl][:, kc * 128:(kc + 1) * 128], rhs=qBs[sl][:, q0:q0 + n1], start=True, stop=True)),
                        reads=[hdB[sl]], writes=[bB[2 * (kc % 2)]])
                    rd = [bB[2 * (kc % 2)]]
                    if nq > 512:
                        S.op("pe", (lambda e, kc=kc, dd=dd, nq=nq, q0=q0, sl=sl: e.matmul(
                            dd[:, 512:nq], lhsT=kBs[sl][:, kc * 128:(kc + 1) * 128], rhs=qBs[sl][:, q0 + 512:q0 + nq],
                            start=True, stop=True)),
                            reads=[hdB[sl]], writes=[bB[2 * (kc % 2) + 1]])
                        rd.append(bB[2 * (kc % 2) + 1])
                    tm, tmB_ = tmp[kc % 2], tmpB[kc % 2]
                    S.op("dve", (lambda e, kc=kc, dd=dd, n1=n1, tm=tm: e.scalar_tensor_tensor(
                        out=tm[:, 0:n1], in0=dd[:, 0:n1], scalar=scale, in1=tab[:, kc * 640:kc * 640 + n1],
                        op0=ALU.mult, op1=ALU.add)), reads=[rd[0], tabB[kc]], writes=[tmB_])
                    if nq > 512:
                        S.op("dve", (lambda e, kc=kc, dd=dd, nq=nq, tm=tm: e.scalar_tensor_tensor(
                            out=tm[:, 512:nq], in0=dd[:, 512:nq], scalar=scale, in1=tab[:, kc * 640 + 512:kc * 640 + nq],
                            op0=ALU.mult, op1=ALU.add)), reads=[rd[1], tabB[kc]], writes=[tmB_])
                    S.op("act", (lambda e, kc=kc, nq=nq, tm=tm, sl=sl: e.activation(
                        out=Pa[sl][:, kc * 640:kc * 640 + nq], in_=tm[:, 0:nq], func=AF.Exp)),
                        reads=[tmB_], writes=[PaB[sl][kc]])
                for qb in range(4):
                    Ob, Lb = 4 + item % 2, 6 + item % 2
                    item += 1
                    for bq in range(4):
                        b = qb * 4 + bq
                        lcs = list(range(b - 2, b + 3))
                        for i_, lc in enumerate(lcs):
                            kc = kc_of(lc)
                            j0 = (b - max(0, lc - 2)) * 128
                            for (bank_i, lhs) in ((Ob, None), (Lb, ones_bf)):
                                S.op("pe", (lambda e, kc=kc, j0=j0, bank_i=bank_i, lhs=lhs, bq=bq, i_=i_, sl=sl: e.matmul(
                                    banks[bank_i][:, bq * 128:(bq + 1) * 128],
                                    lhsT=(vBs[sl][:, kc * 128:(kc + 1) * 128] if lhs is None else lhs[:, :]),
                                    rhs=Pa[sl][:, kc * 640 + j0:kc * 640 + j0 + 128], start=(i_ == 0), stop=(i_ == 4))),
                                    reads=[hdB[sl], PaB[sl][kc], Bconst], writes=[bB[bank_i]])
                    attn_epilogue(Ob, Lb, bvcol[:, 2 + h:3 + h], E, yT_scr[8 + h, :, qb * 512:(qb + 1) * 512], "y4")
            S.flush()
        if stop_after <= 4:
            return nc

        m_scr = dscr("m_scr", [16, 128, TOWN])
        Bm = Buf("m")
        Bx1, Bh2 = Buf("x1"), Buf("h2")
        with ExitStack() as es:
            woa = sb(es, "woa", [128, 8 * D], BF16)
            wob = sb(es, "wob", [128, 8 * D], BF16)
            woP = [Buf("woab%d" % i) for i in range(4)]
            yab = [sb(es, "yab%d" % i, [128, 16 * 512], BF16) for i in range(2)]
            yabB = [Buf("yab0"), Buf("yab1")]
            sgt = [sb(es, "sgt%d" % i, [128, 1024], BF16) for i in range(3)]
            sgtB = [Buf("sgt%d" % i) for i in range(3)]
            t1 = [sb(es, "m_t1%d" % i, [128, 512], F32) for i in range(2)]
            t2 = [sb(es, "m_t2%d" % i, [128, 512], F32) for i in range(2)]
            t1B = [Buf("t1a"), Buf("t1b")]
            t2B = [Buf("t2a"), Buf("t2b")]
            mo = [sb(es, "mo%d" % i, [128, 512], BF16) for i in range(3)]
            moB = [Buf("mo%d" % i) for i in range(3)]
            for part in range(4):
                S.dma("pool", v3(woa[:, :], 8)[:, :, part * 512:(part + 1) * 512],
                      w_oa.rearrange("(k p) n -> p k n", p=128)[:, :, part * 512:(part + 1) * 512], writes=[woP[part]], key=("woab", part))
                S.dma("pool", v3(wob[:, :], 8)[:, :, part * 512:(part + 1) * 512],
                      w_ob.rearrange("(k p) n -> p k n", p=128)[:, :, part * 512:(part + 1) * 512], writes=[woP[part]], key=("woab", part))
            it = 0
            for tb in range(4):
                sl = tb % 2
                S.dma("sp", v3(yab[sl][:, :], 16), yT_scr.rearrange("k p t -> p k t")[:, :, tb * 512:(tb + 1) * 512],
                      reads=[ByT], writes=[yabB[sl]], key=("yab", sl))
                for n in range(16):
                    s3 = it % 3
                    S.dma("sp", sgt[s3][:, 0:512], sg_scr[n, :, tb * 512:(tb + 1) * 512], reads=[Bsg], writes=[sgtB[s3]], key=("sgt", s3))
                    S.dma("sp", sgt[s3][:, 512:1024], sg_scr[16 + n, :, tb * 512:(tb + 1) * 512], reads=[Bsg], writes=[sgtB[s3]], key=("sgt", s3))
                    ba, bb_ = (0, 1) if it % 2 == 0 else (2, 5)
                    mm_acc(ba, lambda k, n=n: woa[:, k * D + n * 128:k * D + (n + 1) * 128],
                           lambda k, sl=sl: yab[sl][:, k * 512:(k + 1) * 512], 8, [woP[n // 4], yabB[sl]])
                    mm_acc(bb_, lambda k, n=n: wob[:, k * D + n * 128:k * D + (n + 1) * 128],
                           lambda k, sl=sl: yab[sl][:, (8 + k) * 512:(9 + k) * 512], 8, [woP[n // 4], yabB[sl]])
                    i2 = it % 2
                    S.op("dve", (lambda e, i2=i2, ba=ba, s3=s3: e.tensor_tensor(out=t1[i2][:, :], in0=banks[ba][:, :], in1=sgt[s3][:, 0:512], op=ALU.mult)),
                         reads=[bB[ba], sgtB[s3]], writes=[t1B[i2]])
                    S.op("dve", (lambda e, i2=i2, bb_=bb_, s3=s3: e.tensor_tensor(out=t2[i2][:, :], in0=banks[bb_][:, :], in1=sgt[s3][:, 512:1024], op=ALU.mult)),
                         reads=[bB[bb_], sgtB[s3]], writes=[t2B[i2]])
                    S.op("pool", (lambda e, i2=i2, s3=s3: e.tensor_tensor(out=mo[s3][:, :], in0=t1[i2][:, :], in1=t2[i2][:, :], op=ALU.add)),
                         reads=[t1B[i2], t2B[i2]], writes=[moB[s3]])
                    S.dma("pool", m_scr[n, :, tb * 512:(tb + 1) * 512], mo[s3][:, :], reads=[moB[s3]], writes=[Bm], key=("mo", s3))
                    it += 1
            S.flush()

        def stats_to_rstd(bank_i, rs_tile, rsB):
            S.op("act", lambda e: e.activation(out=rs_tile[:, :], in_=banks[bank_i][:, :], func=AF.Ln, bias=epsA[:, 0:1], scale=1.0),
                 reads=[bB[bank_i], Bconst], writes=[rsB])
            S.op("act", lambda e: e.activation(out=rs_tile[:, :], in_=rs_tile[:, :], func=AF.Exp, scale=-0.5), reads=[rsB], writes=[rsB])

        with ExitStack() as es:
            wout = sb(es, "wout", [128, 16 * D], BF16)
            woutP = [Buf("wout%d" % i) for i in range(4)]
            x1cB = [Buf("x1c%d" % i) for i in range(16)]
            mb_ = [sb(es, "mblk%d" % i, [128, 16 * 512], BF16) for i in range(2)]
            mbB = [Buf("mblk0"), Buf("mblk1")]
            xin = [sb(es, "xin%d" % i, [128, 512], F32) for i in range(3)]
            xinB = [Buf("xin%d" % i) for i in range(3)]
            x1b = sb(es, "x1b", [128, 16 * 512], F32)
            x1bB = Buf("x1b")
            sq5 = [sb(es, "sq5_%d" % i, [128, 512], BF16) for i in range(2)]
            sq5B = [Buf("sq5a"), Buf("sq5b")]
            rs5 = sb(es, "rs5", [128, 512], F32)
            rs5B = Buf("rs5")
            h2o = sb(es, "h2o", [128, 16 * 512], BF16)
            h2oB = Buf("h2o")
            for part in range(4):
                S.dma("pool", v3(wout[:, :], 16)[:, :, part * 512:(part + 1) * 512],
                      w_out.rearrange("(k p) n -> p k n", p=128)[:, :, part * 512:(part + 1) * 512], writes=[woutP[part]], key=("wout", part))
            it = 0
            mr = Ring([0, 1, 2, 3])
            pend5 = None

            def stat5(n, q2):
                S.op("pe", (lambda e: e.matmul(banks[7][:, :], lhsT=ones_bf[:, :], rhs=sq5[q2][:, :], start=(n == 0), stop=(n == 15))),
                     reads=[sq5B[q2], Bconst], writes=[bB[7]])

            for tb in range(4):
                sl = tb % 2
                S.dma("sp", v3(mb_[sl][:, :], 16), m_scr.rearrange("k p t -> p k t")[:, :, tb * 512:(tb + 1) * 512],
                      reads=[Bm], writes=[mbB[sl]], key=("mblk", sl))
                for n in range(16):
                    s3 = it % 3
                    it += 1
                    S.dma("sp", xin[s3][:, :], xT_v[:, n, tb * 512:(tb + 1) * 512], writes=[xinB[s3]], key=("xin", s3))
                    bi = mr.next()
                    mm_acc(bi, lambda k, n=n: wout[:, k * D + n * 128:k * D + (n + 1) * 128],
                           lambda k, sl=sl: mb_[sl][:, k * 512:(k + 1) * 512], 16, [woutP[n // 4], mbB[sl]])
                    S.op("dve", (lambda e, bi=bi, n=n, s3=s3: e.scalar_tensor_tensor(
                        out=x1b[:, n * 512:(n + 1) * 512], in0=banks[bi][:, :], scalar=mod_sb[:, 32 + n:33 + n], in1=xin[s3][:, :],
                        op0=ALU.mult, op1=ALU.add)), reads=[bB[bi], xinB[s3], Bmod], writes=[x1cB[n]])
                    q2 = n % 2
                    S.op("act", (lambda e, n=n, q2=q2: e.activation(out=sq5[q2][:, :], in_=x1b[:, n * 512:(n + 1) * 512], func=AF.Square)),
                         reads=[x1cB[n]], writes=[sq5B[q2]])
                    if pend5 is not None:
                        stat5(*pend5)
                    pend5 = (n, q2)
                stat5(*pend5)
                pend5 = None
                stats_to_rstd(7, rs5, rs5B)
                for k in range(16):
                    S.op("dve", (lambda e, k=k: e.scalar_tensor_tensor(
                        out=h2o[:, k * 512:(k + 1) * 512], in0=x1b[:, k * 512:(k + 1) * 512], scalar=A2[:, k:k + 1], in1=rs5[:, :],
                        op0=ALU.mult, op1=ALU.mult)), reads=[x1cB[k], rs5B, Bmod], writes=[h2oB])
                S.dma("pool", v3(h2T_scr[:, :], 16)[:, :, tb * 512:(tb + 1) * 512], v3(h2o[:, :], 16), reads=[h2oB], writes=[Bh2], key="h2st")
                for j4 in range(4):
                    S.dma("pool", x1T_scr.rearrange("k p t -> p k t")[:, 4 * j4:4 * j4 + 4, tb * 512:(tb + 1) * 512],
                          v3(x1b[:, :], 16)[:, 4 * j4:4 * j4 + 4, :],
                          reads=x1cB[4 * j4:4 * j4 + 4], writes=[Bx1], key=("x1st", j4))
            S.flush()
        if stop_after <= 5:
            return nc

        Bact = Buf("act")
        w_gate_v = w_gate.rearrange("(k p) n -> p k n", p=128)
        w_up_v = w_up.rearrange("(k p) n -> p k n", p=128)
        with ExitStack() as es:
            h2 = sb(es, "h2", [128, 16 * TOWN], BF16)
            h2B = Buf("h2")
            wg = [sb(es, "wg%d" % i, [128, 16 * 512], BF16) for i in range(2)]
            wu = [sb(es, "wu%d" % i, [128, 16 * 512], BF16) for i in range(2)]
            wgB = [Buf("wg0"), Buf("wg1")]
            wuB = [Buf("wu0"), Buf("wu1")]
            bg = [sb(es, "bg%d" % i, [128, 4], F32) for i in range(2)]
            bu = [sb(es, "bu%d" % i, [128, 4], F32) for i in range(2)]
            bgB = [Buf("bg0"), Buf("bg1")]
            buB = [Buf("bu0"), Buf("bu1")]
            sgl = [sb(es, "sgl%d" % i, [128, 512], F32) for i in range(2)]
            sglB = [Buf("sgl0"), Buf("sgl1")]
            ao = [sb(es, "ao%d" % i, [128, 512], BF16) for i in range(3)]
            aoB = [Buf("ao%d" % i) for i in range(3)]
            h2P = [Buf("h2_%d" % i) for i in range(4)]
            for tbp in range(4):
                S.dma("sp", v3(h2[:, :], 16)[:, :, tbp * 512:(tbp + 1) * 512], v3(h2T_scr[:, :], 16)[:, :, tbp * 512:(tbp + 1) * 512],
                      reads=[Bh2], writes=[h2P[tbp]], key=("h2ld", tbp))
            it = 0
            for ti in range(11):
                sl = ti % 2
                load_wtile(wg[sl], wgB[sl], w_gate_v, ti * 512, 512, 16, ("wg", sl))
                load_wtile(wu[sl], wuB[sl], w_up_v, ti * 512, 512, 16, ("wu", sl))
                bias_cols(wg[sl], wgB[sl], 4, B2bf, bg[sl][:, 0:4], bgB[sl], 512)
                bias_cols(wu[sl], wuB[sl], 4, B2bf, bu[sl][:, 0:4], buB[sl], 512)
                for cc in range(4):
                    for tb in range(4):
                        bG, bU = ((0, 1), (2, 3), (4, 5))[it % 3]
                        i2, i3 = it % 2, it % 3
                        it += 1
                        mm_acc(bG, lambda k, sl=sl, cc=cc: wg[sl][:, k * 512 + cc * 128:k * 512 + (cc + 1) * 128],
                               lambda k, tb=tb: h2[:, k * TOWN + tb * 512:k * TOWN + (tb + 1) * 512], 16, [wgB[sl], h2P[tb]])
                        mm_acc(bU, lambda k, sl=sl, cc=cc: wu[sl][:, k * 512 + cc * 128:k * 512 + (cc + 1) * 128],
                               lambda k, tb=tb: h2[:, k * TOWN + tb * 512:k * TOWN + (tb + 1) * 512], 16, [wuB[sl], h2P[tb]])
                        S.op("act", (lambda e, bG=bG, i2=i2, sl=sl, cc=cc: e.activation(
                            out=sgl[i2][:, :], in_=banks[bG][:, :], func=AF.Silu, bias=bg[sl][:, cc:cc + 1], scale=1.0)),
                            reads=[bB[bG], bgB[sl]], writes=[sglB[i2]])
                        S.op("dve", (lambda e, bU=bU, i2=i2, i3=i3, sl=sl, cc=cc: e.scalar_tensor_tensor(
                            out=ao[i3][:, :], in0=banks[bU][:, :], scalar=bu[sl][:, cc:cc + 1], in1=sgl[i2][:, :],
                            op0=ALU.add, op1=ALU.mult)), reads=[bB[bU], buB[sl], sglB[i2]], writes=[aoB[i3]])
                        S.dma("sp", act_scr[ti * 4 + cc, :, tb * 512:(tb + 1) * 512], ao[i3][:, :], reads=[aoB[i3]], writes=[Bact],
                              key=("ao", i3))
            S.flush()
        if stop_after <= 6:
            return nc

        Bx2h = [Buf("x2a"), Buf("x2b")]
        w_down_v = w_down.rearrange("(k p) n -> p k n", p=128)
        with ExitStack() as es:
            acth = sb(es, "acth", [128, 44 * 1024], BF16)
            acthB = Buf("acth")
            wd = [sb(es, "wd%d" % i, [128, 44 * 256], BF16) for i in range(2)]
            wdB = [Buf("wd0"), Buf("wd1")]
            x1i = [sb(es, "x1i%d" % i, [128, 512], F32) for i in range(3)]
            x1iB = [Buf("x1i%d" % i) for i in range(3)]
            x2o = [sb(es, "x2o%d" % i, [128, 512], F32) for i in range(3)]
            x2oB = [Buf("x2o%d" % i) for i in range(3)]
            sq7 = [sb(es, "sq7_%d" % i, [128, 512], BF16) for i in range(2)]
            sq7B = [Buf("sq7a"), Buf("sq7b")]
            rsh = [[sb(es, "rsh%d_%d" % (h_, i), [128, 512], F32) for i in range(2)] for h_ in range(2)]
            rshB = [[Buf("rsh%d_%d" % (h_, i)) for i in range(2)] for h_ in range(2)]
            x2i = [sb(es, "x2i%d" % i, [128, 512], F32) for i in range(4)]
            x2iB = [Buf("x2i%d" % i) for i in range(4)]
            Fw = sb(es, "Fw", [128, 16], F32)
            FwB = Buf("Fw")
            yo = [sb(es, "yo%d" % i, [128, 512], F32) for i in range(4)]
            yoB = [Buf("yo%d" % i) for i in range(4)]
            Bout = Buf("out")
            S.op("pool", lambda e: e.tensor_scalar(out=Fw[:, :], in0=vecs_sb[:, C_FW:C_FW + 16], scalar1=math.sqrt(D), scalar2=None, op0=ALU.mult),
                 reads=[Bvecs], writes=[FwB])
            it = 0
            wi = 0
            mr = Ring([0, 1, 2, 3, 4, 5])
            pend7 = None

            def stat7(q2, tb2, n):
                S.op("pe", (lambda e: e.matmul(banks[6 + tb2][:, :], lhsT=ones_bf[:, :], rhs=sq7[q2][:, :],
                                              start=(n == 0), stop=(n == 15))),
                     reads=[sq7B[q2], Bconst], writes=[bB[6 + tb2]])

            def load_act(half):
                h0_ = half * 1024
                for part in range(4):
                    S.dma("sp", v3(acth[:, :], 44)[:, part * 11:(part + 1) * 11, :],
                          act_scr.rearrange("k p t -> p k t")[:, part * 11:(part + 1) * 11, h0_:h0_ + 1024],
                          reads=[Bact], writes=[acthB], key="acth")

            fctr = [0]

            def final_pass(half):
                h0_ = half * 1024
                for tb2 in range(2):
                    stats_to_rstd(6 + tb2, rsh[half][tb2], rshB[half][tb2])
                yield
                for tb2 in range(2):
                    c0 = h0_ + tb2 * 512
                    for n in range(16):
                        s3 = fctr[0] % 4
                        fctr[0] += 1
                        S.dma("sp", x2i[s3][:, :], x2T_scr[n, :, c0:c0 + 512], reads=[Bx2h[half]], writes=[x2iB[s3]], key=("x2i", s3))
                        S.op("dve", (lambda e, s3=s3, n=n, tb2=tb2: e.scalar_tensor_tensor(
                            out=yo[s3][:, :], in0=x2i[s3][:, :], scalar=Fw[:, n:n + 1], in1=rsh[half][tb2][:, :],
                            op0=ALU.mult, op1=ALU.mult)), reads=[x2iB[s3], FwB, rshB[half][tb2]], writes=[yoB[s3]])
                        S.dma("act", outT[n, :, c0:c0 + 512], yo[s3][:, :], reads=[yoB[s3]], writes=[Bout], key=("yo", s3))
                        yield

            load_act(0)
            fp_gen = None
            for half in range(2):
                h0 = half * 1024
                for nt in range(8):
                    sl = wi % 2
                    wi += 1
                    for part in range(4):
                        S.dma("pool", v3(wd[sl][:, :], 44)[:, part * 11:(part + 1) * 11, :],
                              w_down_v[:, part * 11:(part + 1) * 11, nt * 256:(nt + 1) * 256], writes=[wdB[sl]], key=("wd", sl))
                    for nn in range(2):
                        n = nt * 2 + nn
                        for tb2 in range(2):
                            s3 = it % 3
                            q2 = it % 2
                            it += 1
                            c0 = h0 + tb2 * 512
                            S.dma("sp", x1i[s3][:, :], x1T_scr[n, :, c0:c0 + 512], reads=[Bx1], writes=[x1iB[s3]], key=("x1i", s3))
                            bi = mr.next()
                            mm_acc(bi, lambda k, sl=sl, nn=nn: wd[sl][:, k * 256 + nn * 128:k * 256 + (nn + 1) * 128],
                                   lambda k, tb2=tb2: acth[:, k * 1024 + tb2 * 512:k * 1024 + (tb2 + 1) * 512], 44, [wdB[sl], acthB])
                            S.op("dve", (lambda e, bi=bi, n=n, s3=s3: e.scalar_tensor_tensor(
                                out=x2o[s3][:, :], in0=banks[bi][:, :], scalar=mod_sb[:, 80 + n:81 + n], in1=x1i[s3][:, :],
                                op0=ALU.mult, op1=ALU.add)), reads=[bB[bi], x1iB[s3], Bmod], writes=[x2oB[s3]])
                            S.op("act", (lambda e, s3=s3, q2=q2: e.activation(out=sq7[q2][:, :], in_=x2o[s3][:, :], func=AF.Square)),
                                 reads=[x2oB[s3]], writes=[sq7B[q2]])
                            if pend7 is not None:
                                stat7(*pend7)
                            pend7 = (q2, tb2, n)
                            S.dma("act", x2T_scr[n, :, c0:c0 + 512], x2o[s3][:, :], reads=[x2oB[s3]], writes=[Bx2h[half]], key=("x2o", s3))
                            if fp_gen is not None:
                                next(fp_gen, None)
                stat7(*pend7)
                pend7 = None
                if half == 0:
                    load_act(1)
                    fp_gen = final_pass(0)
                    next(fp_gen)
                else:
                    for _ in fp_gen:
                        pass
                    for _ in final_pass(1):
                        pass
            S.flush()
        return nc


def _pack_cols(v):
    v = np.asarray(v, np.float32).reshape(-1)
    return np.ascontiguousarray(v.reshape(-1, 128).T)


def _token_order(c):
    own = np.arange(TOWN * c, TOWN * (c + 1))
    others = np.concatenate([np.arange(0, TOWN * c), np.arange(TOWN * (c + 1), S_ALL)])
    r0 = 32 * c
    rows_before = np.arange(r0 - 4, r0) if c > 0 else np.arange(4, 8)
    rows_after = np.arange(r0 + 32, r0 + 36) if c < NCORE - 1 else np.arange(248, 252)
    halo_rows = np.concatenate([rows_before, rows_after])
    halo = (halo_rows[:, None] * GRID_W + np.arange(GRID_W)[None, :]).reshape(-1)
    return own, others, halo, halo_rows


def _rope_tables():
    t = np.arange(S_ALL)
    row = (t // GRID_W).astype(np.float32)
    col = (t % GRID_W).astype(np.float32)
    inv = (np.float32(10000.0) ** (-(np.arange(0, 64, 2, dtype=np.float32) / np.float32(64.0)))).astype(np.float32)
    ang = np.concatenate([row[:, None] * inv[None], col[:, None] * inv[None]], axis=-1).astype(np.float32)
    cos = np.cos(ang).astype(np.float32)
    sin = np.sin(ang).astype(np.float32)
    cosT = np.repeat(cos.T, 2, axis=0)
    sinT = np.repeat(sin.T, 2, axis=0)
    sinT[0::2] *= -1.0
    return np.ascontiguousarray(cosT), np.ascontiguousarray(sinT)


def _na_table(c, rpb, halo_rows):
    tab = np.full((8, 20, 128, 640), MASK_NEG, np.float32)
    rows_tot = S_ALL // GRID_W
    for kc in range(20):
        if kc < 16:
            lc = kc
            grow = 32 * c + 2 * kc + np.arange(2)
        elif kc < 18:
            lc = kc - 18
            grow = halo_rows[(kc - 16) * 2:(kc - 16) * 2 + 2]
        else:
            lc = kc - 2
            grow = halo_rows[4 + (kc - 18) * 2:4 + (kc - 18) * 2 + 2]
        key_r = np.repeat(grow, GRID_W)
        key_c = np.tile(np.arange(GRID_W), 2)
        blo = max(0, lc - 2)
        bhi = min(15, lc + 2)
        for b in range(blo, bhi + 1):
            qr = np.repeat(32 * c + 2 * b + np.arange(2), GRID_W)
            qc = np.tile(np.arange(GRID_W), 2)
            rs = np.clip(qr - NA_ROWS // 2, 0, rows_tot - NA_ROWS)
            cs_ = np.clip(qc - NA_COLS // 2, 0, GRID_W - NA_COLS)
            inr = (key_r[:, None] >= rs[None, :]) & (key_r[:, None] < rs[None, :] + NA_ROWS)
            inc = (key_c[:, None] >= cs_[None, :]) & (key_c[:, None] < cs_[None, :] + NA_COLS)
            valid = inr & inc
            if kc >= 16:
                own_lo = 32 * c + 2 * (b - 2)
                own_hi = 32 * c + 2 * (b + 2) + 1
                lo = max(own_lo, 32 * c)
                hi = min(own_hi, 32 * c + 31)
                dup = (key_r >= lo) & (key_r <= hi)
                valid &= ~dup[:, None]
            rel_r = np.clip(key_r[:, None] - qr[None, :] + (NA_ROWS - 1), 0, 2 * NA_ROWS - 2)
            rel_c = np.clip(key_c[:, None] - qc[None, :] + (NA_COLS - 1), 0, 2 * NA_COLS - 2)
            j0 = (b - blo) * 128
            for h in range(8):
                bias = rpb[h][rel_r, rel_c]
                tab[h, kc, :, j0:j0 + 128] = np.where(valid, bias, np.float32(MASK_NEG))
    return tab


def prep_inputs(inp):
    x = np.asarray(inp["x"], np.float32)[0]
    xTfull = np.ascontiguousarray(x.T)
    cosT, sinT = _rope_tables()
    rm = np.zeros((128, 128), np.float32)
    for k in range(128):
        rm[k, k ^ 1] = 1.0
    vec = np.zeros((128, NV), np.float32)
    vec[:, 0:16] = _pack_cols(inp["c"])
    vec[:, 16:112] = _pack_cols(inp["b_ada"])
    vec[:, 112:128] = _pack_cols(inp["norm1_w"])
    vec[:, 128:144] = _pack_cols(inp["norm2_w"])
    vec[:, 144:160] = _pack_cols(inp["final_w"])
    vec[:, 160:161] = _pack_cols(inp["q_norm_w"])
    vec[:, 161:162] = _pack_cols(inp["k_norm_w"])
    rpb = np.asarray(inp["nat_rpb"], np.float32)[0]
    shared = {
        "vecs": vec, "rmat": rm,
        "w_ada": np.ascontiguousarray(np.asarray(inp["w_ada"], np.float32)[0]),
        "w_in": np.ascontiguousarray(np.asarray(inp["w_in"], np.float32)[0]),
        "w_oa": np.ascontiguousarray(np.asarray(inp["w_oa"], np.float32)[0]),
        "w_ob": np.ascontiguousarray(np.asarray(inp["w_ob"], np.float32)[0]),
        "w_out": np.ascontiguousarray(np.asarray(inp["w_out"], np.float32)[0]),
        "w_gate": np.ascontiguousarray(np.asarray(inp["w_ffn_gate"], np.float32)[0]),
        "w_up": np.ascontiguousarray(np.asarray(inp["w_ffn_up"], np.float32)[0]),
        "w_down": np.ascontiguousarray(np.asarray(inp["w_ffn_down"], np.float32)[0]),
    }
    maps = []
    for c in range(NCORE):
        own, others, halo, halo_rows = _token_order(c)
        order = np.concatenate([own, others, halo])
        m = dict(shared)
        m["xT"] = np.ascontiguousarray(xTfull[:, order])
        o2 = order[:S_ALL]
        m["cosT"] = np.ascontiguousarray(cosT[:, o2])
        m["sinT"] = np.ascontiguousarray(sinT[:, o2])
        m["natab"] = _na_table(c, rpb, halo_rows)
        maps.append(m)
    return maps


_NC_CACHE = {}


def kernel(**inputs):
    maps = prep_inputs(inputs)
    if "nc" not in _NC_CACHE:
        _NC_CACHE["nc"] = build_program()
    nc = _NC_CACHE["nc"]
    res = run_bass_kernel_spmd(nc, maps, core_ids=list(range(NCORE)))
    outs = []
    for c in range(NCORE):
        o = np.asarray(res.results[c]["outT"], np.float32).reshape(D, TOWN)
        outs.append(o.T)
    return np.ascontiguousarray(np.concatenate(outs, axis=0)[None].astype(np.float32))
```
